# Optimizing a Trainium2 kernel written in Bass

```python
import math
import jax, jax.numpy as jnp
from jax import lax
import numpy as np

D_MODEL = 1024
BATCH = 16
SEQ = 2048
DEPTH = 4

CHUNK = 64

MIX_WIDTH = D_MODEL
POOL_WIDTH = MIX_WIDTH // 2
SSM_WIDTH = MIX_WIDTH - POOL_WIDTH
POOL_WINDOWS = (2, 4, 8, 16)
N_POOL_GROUPS = len(POOL_WINDOWS)
POOL_GC = POOL_WIDTH // N_POOL_GROUPS
SSM_GC = 16
SSM_GROUPS = SSM_WIDTH // SSM_GC
SSM_STATE = 64
NORM_EPS = 1e-5
DT_MIN = 1e-3
DT_MAX = 1e-1

kernel_name = "hybrid_pool_s5_parallel_groups"


def rmsnorm(x, g):
    xf = x.astype(jnp.float32)
    y = xf * lax.rsqrt(jnp.mean(xf * xf, axis=-1, keepdims=True) + NORM_EPS)
    return (y * g.astype(jnp.float32)).astype(x.dtype)


def pool_branch(u, pool_w, pool_scale):
    bsz, seq, _ = u.shape
    uf = u.astype(jnp.float32).reshape(bsz, seq, N_POOL_GROUPS, POOL_GC)
    cs = jnp.cumsum(uf, axis=1)
    cs0 = jnp.concatenate([jnp.zeros_like(cs[:, :1]), cs], axis=1)
    ks = jnp.array(POOL_WINDOWS, dtype=jnp.int32)
    t1 = jnp.arange(seq, dtype=jnp.int32)[:, None] + 1
    lag_idx = jnp.maximum(t1 - ks[None, :], 0)
    count = jnp.minimum(t1, ks[None, :]).astype(jnp.float32)
    g_idx = jnp.arange(N_POOL_GROUPS)[None, :]
    lagged = cs0[:, lag_idx, g_idx, :]
    mean = (cs0[:, 1:] - lagged) / count[None, :, :, None]
    pooled = (mean - uf).astype(u.dtype)
    y = jnp.einsum('blgc,gcd->blgd', pooled, pool_w)
    return y.reshape(bsz, seq, POOL_WIDTH) * pool_scale


def _complex_affine_combine(e1, e2):
    a1r, a1i, b1r, b1i = e1
    a2r, a2i, b2r, b2i = e2
    ar = a2r * a1r - a2i * a1i
    ai = a2r * a1i + a2i * a1r
    br = a2r * b1r - a2i * b1i + b2r
    bi = a2r * b1i + a2i * b1r + b2i
    return (ar, ai, br, bi)


def ssm_branch(u, a_re, a_im, log_dt, b_re, b_im, c_re, c_im, d_skip, glu_w, glu_b):
    bsz, seq, _ = u.shape
    f32 = jnp.float32
    uf = u.astype(f32).reshape(bsz, seq, SSM_GROUPS, SSM_GC)
    ar = a_re.astype(f32)
    ai = a_im.astype(f32)
    dt = jnp.exp(log_dt.astype(f32))[:, None]
    mag = jnp.exp(ar * dt)
    ang = ai * dt
    lb_re = mag * jnp.cos(ang)
    lb_im = mag * jnp.sin(ang)
    den = ar * ar + ai * ai
    n_re = lb_re - 1.0
    n_im = lb_im
    f_re = (n_re * ar + n_im * ai) / den
    f_im = (n_im * ar - n_re * ai) / den
    br = b_re.astype(f32)
    bi = b_im.astype(f32)
    bb_re = f_re[..., None] * br - f_im[..., None] * bi
    bb_im = f_re[..., None] * bi + f_im[..., None] * br
    bu_re = jnp.einsum('blgc,gpc->blgp', uf, bb_re)
    bu_im = jnp.einsum('blgc,gpc->blgp', uf, bb_im)
    a_full_re = jnp.broadcast_to(lb_re, bu_re.shape)
    a_full_im = jnp.broadcast_to(lb_im, bu_im.shape)
    _, _, s_re, s_im = lax.associative_scan(
        _complex_affine_combine, (a_full_re, a_full_im, bu_re, bu_im), axis=1)
    y = (jnp.einsum('blgp,gcp->blgc', s_re, c_re.astype(f32))
         - jnp.einsum('blgp,gcp->blgc', s_im, c_im.astype(f32)))
    y = y.reshape(bsz, seq, SSM_WIDTH) + d_skip.astype(f32) * uf.reshape(bsz, seq, SSM_WIDTH)
    y = jax.nn.gelu(y).astype(u.dtype)
    return y * jax.nn.sigmoid(y @ glu_w + glu_b)


def setup_inputs(seed: int = 0) -> dict:
    key = jax.random.key(seed)
    ks = jax.random.split(key, 20)
    f32 = jnp.float32
    E, D = MIX_WIDTH, D_MODEL
    G, P, C = SSM_GROUPS, SSM_STATE, SSM_GC
    x = jax.random.normal(ks[0], (BATCH, SEQ, D), f32)
    norm_g = 1.0 + 0.02 * jax.random.normal(ks[1], (DEPTH, D), f32)
    w_in = jax.random.normal(ks[2], (DEPTH, D, 2 * E), f32) * D ** -0.5
    pool_w = jax.random.normal(ks[3], (DEPTH, N_POOL_GROUPS, POOL_GC, POOL_GC), f32) * POOL_GC ** -0.5
    pool_scale = 1.0 + 0.02 * jax.random.normal(ks[4], (DEPTH, POOL_WIDTH), f32)
    a_re = -0.5 + 0.01 * jax.random.normal(ks[5], (DEPTH, G, P), f32)
    a_im = (math.pi * jnp.arange(P, dtype=f32))[None, None, :] + 0.01 * jax.random.normal(ks[6], (DEPTH, G, P), f32)
    log_dt = jax.random.uniform(ks[7], (DEPTH, G), f32, math.log(DT_MIN), math.log(DT_MAX))
    b_re = jax.random.normal(ks[8], (DEPTH, G, P, C), f32) * (2.0 * C) ** -0.5
    b_im = jax.random.normal(ks[9], (DEPTH, G, P, C), f32) * (2.0 * C) ** -0.5
    c_re = jax.random.normal(ks[10], (DEPTH, G, C, P), f32) * (2.0 * P) ** -0.5
    c_im = jax.random.normal(ks[11], (DEPTH, G, C, P), f32) * (2.0 * P) ** -0.5
    d_skip = jax.random.normal(ks[12], (DEPTH, SSM_WIDTH), f32)
    glu_w = jax.random.normal(ks[13], (DEPTH, SSM_WIDTH, SSM_WIDTH), f32) * SSM_WIDTH ** -0.5
    glu_b = 0.01 * jax.random.normal(ks[14], (DEPTH, SSM_WIDTH), f32)
    w_out = jax.random.normal(ks[15], (DEPTH, E, D), f32) * E ** -0.5
    final_g = 1.0 + 0.02 * jax.random.normal(ks[16], (D,), f32)
    return {"x": x, "norm_g": norm_g, "w_in": w_in, "pool_w": pool_w, "pool_scale": pool_scale,
            "a_re": a_re, "a_im": a_im, "log_dt": log_dt, "b_re": b_re, "b_im": b_im,
            "c_re": c_re, "c_im": c_im, "d_skip": d_skip, "glu_w": glu_w, "glu_b": glu_b,
            "w_out": w_out, "final_g": final_g}


def reference(x, norm_g, w_in, pool_w, pool_scale, a_re, a_im, log_dt, b_re, b_im,
              c_re, c_im, d_skip, glu_w, glu_b, w_out, final_g):
    for l in range(DEPTH):
        h = rmsnorm(x, norm_g[l])
        z = h @ w_in[l]
        u_pool = z[..., :POOL_WIDTH]
        u_ssm = z[..., POOL_WIDTH:MIX_WIDTH]
        gate = jax.nn.silu(z[..., MIX_WIDTH:])
        y_pool = pool_branch(u_pool, pool_w[l], pool_scale[l])
        y_ssm = ssm_branch(u_ssm, a_re[l], a_im[l], log_dt[l], b_re[l], b_im[l],
                           c_re[l], c_im[l], d_skip[l], glu_w[l], glu_b[l])
        y = jnp.concatenate([y_pool, y_ssm.astype(y_pool.dtype)], axis=-1) * gate
        x = x + (y @ w_out[l]).astype(x.dtype)
    return rmsnorm(x, final_g)
```

```python
import math
from contextlib import ExitStack

import numpy as np
import concourse.bass as bass
import concourse.mybir as mybir
from concourse.bass_utils import run_bass_kernel_spmd

F32 = mybir.dt.float32
BF16 = mybir.dt.bfloat16
I32 = mybir.dt.int32
AF = mybir.ActivationFunctionType
ALU = mybir.AluOpType

NCORES = 8
DEPTH = 4
D = 1024
SEQ = 2048
NSEQ = 2
ST = 1024
NK = ST // 8
G = 32
TWO_PI = float(2.0 * math.pi)
INV_2PI = float(1.0 / (2.0 * math.pi))
HALF_PI = float(math.pi / 2.0)
EPS = 1e-5
POOL_WINDOWS = (2, 4, 8, 16)


class Res:
    __slots__ = ("name", "w", "r")

    def __init__(self, name):
        self.name = name
        self.w = {}
        self.r = {}


class Prog:
    ENG = ("pe", "act", "dve", "pool", "sp")

    def __init__(self):
        self.ops = {e: [] for e in self.ENG}
        self.cnt = {}
        self.seen = {e: {} for e in self.ENG}

    def op(self, eng, fn, reads=(), writes=(), dma=None, sig=True, acc=()):
        need = {}
        for r in acc:
            for k, v in r.r.items():
                if need.get(k, 0) < v:
                    need[k] = v
        for r in reads:
            for k, v in r.w.items():
                if need.get(k, 0) < v:
                    need[k] = v
        for r in writes:
            for k, v in r.w.items():
                if need.get(k, 0) < v:
                    need[k] = v
            for k, v in r.r.items():
                if need.get(k, 0) < v:
                    need[k] = v
        waits = []
        seen = self.seen[eng]
        for k, v in need.items():
            if eng == "pe" and k == "pe":
                continue
            if seen.get(k, 0) >= v:
                continue
            seen[k] = v
            waits.append((k, v))
        tok = None
        inc = 1
        if sig:
            key = dma if dma is not None else eng
            inc = 16 if dma is not None else 1
            self.cnt[key] = self.cnt.get(key, 0) + inc
            tok = (key, self.cnt[key])
            for r in reads:
                if r.r.get(key, 0) < tok[1]:
                    r.r[key] = tok[1]
            for r in writes:
                r.w = {key: tok[1]}
                r.r = {}
            for r in acc:
                if r.w.get(key, 0) < tok[1]:
                    r.w[key] = tok[1]
        self.ops[eng].append((waits, fn, tok, inc))
        return tok

    def emit(self, nc, block, sems):
        def mk(name):
            def body(e):
                for waits, fn, tok, inc in self.ops[name]:
                    for k, v in waits:
                        e.wait_ge(sems[k], v)
                    if fn is None:
                        continue
                    ins = fn(e)
                    if tok is not None:
                        ins.then_inc(sems[tok[0]], inc)
            return body

        block.tensor(mk("pe"))
        block.scalar(mk("act"))
        block.vector(mk("dve"))
        block.gpsimd(mk("pool"))
        block.sync(mk("sp"))


def build_program(debug=False, n_layers=DEPTH, n_sub=4):
    nc = bass.Bass("TRN2", target_bir_lowering=False)
    P = Prog()
    es = ExitStack()

    def dram_in(name, shape, dt=F32):
        return nc.dram_tensor(name, list(shape), dt, kind="ExternalInput").ap()

    def dram_out(name, shape, dt=F32):
        return nc.dram_tensor(name, list(shape), dt, kind="ExternalOutput").ap()

    def dram_scr(name, shape, dt):
        kind = "ExternalOutput" if debug else "Internal"
        return nc.dram_tensor(name, list(shape), dt, kind=kind).ap()

    def sb(name, shape, dt, stack=None):
        return (stack or es).enter_context(nc.sbuf_tensor(name, list(shape), dt))

    x_d = dram_in("x", [NSEQ, SEQ, D])
    w_in_d = dram_in("w_in", [DEPTH, D, 2 * D])
    w_out_d = dram_in("w_out", [DEPTH, D, D])
    glu_w_d = dram_in("glu_w", [DEPTH, 512, 512])
    pool_w_d = dram_in("pool_w", [DEPTH, 128, 4, 128])
    smallp_d = dram_in("smallp", [128, DEPTH, 16])
    finalg_d = dram_in("finalg", [128, 8])
    ssm_small_d = dram_in("ssm_small", [DEPTH, 128, 4, 32])
    ssm_bc_d = dram_in("ssm_bc", [DEPTH, 128, 4, 512])
    out_d = dram_out("out", [NSEQ, SEQ, D])

    ssmW_d = dram_scr("ssmW", [DEPTH, 8, 128, 5 * 4 * 128], BF16)
    tabs_d = dram_scr("tabs", [DEPTH, 8, 128, 2, 4, 256], F32)
    poolW_d = dram_scr("poolW", [DEPTH, 128, 4 * 2 * 128], BF16)
    R_SSMW = [Res(f"ssmW{l}") for l in range(DEPTH)]
    R_TABS = [Res(f"tabs{l}") for l in range(DEPTH)]
    R_POOLW = [Res(f"poolW{l}") for l in range(DEPTH)]
    R_OUT = Res("out")

    identF = sb("identF", [128, 128], F32)
    identB = sb("identB", [128, 128], BF16)
    onesB = sb("onesB", [128, 128], BF16)
    Rb_all = sb("Rb_all", [128, DEPTH, 32], F32)
    smallp = sb("smallp_sb", [128, DEPTH, 16], F32)
    finalg = sb("finalg_sb", [128, 8], F32)
    psum = es.enter_context(nc.psum_tensor("psum", [128, 8, 512], F32))
    R_CONST = Res("const")
    R_RB = Res("Rb")
    R_SMALLP = Res("smallp")
    R_PS = [Res(f"ps{i}") for i in range(8)]
    ps_rr = [0]

    def next_ps():
        i = ps_rr[0]
        ps_rr[0] = (i + 1) % 8
        return i

    P.op("pool", lambda e: e.memset(identF[:], 1.0), writes=[R_CONST])
    P.op("pool", lambda e: e.affine_select(out=identF[:], in_=identF[:], pattern=[[-1, 128]],
                                           compare_op=ALU.is_equal, fill=0.0, base=0, channel_multiplier=1),
         reads=[R_CONST], writes=[R_CONST])
    P.op("pool", lambda e: e.tensor_copy(out=identB[:], in_=identF[:]), reads=[R_CONST], writes=[R_CONST])
    P.op("pool", lambda e: e.memset(onesB[:], 1.0 / 1024.0), writes=[R_CONST])
    P.op("sp", lambda e: e.dma_start(out=smallp[:], in_=smallp_d[:]), writes=[R_SMALLP], dma="d_small")
    P.op("sp", lambda e: e.dma_start(out=finalg[:], in_=finalg_d[:]), writes=[R_SMALLP], dma="d_small")

    with ExitStack() as ps_:
        def pb(name, shape, dt):
            return sb("pl_" + name, shape, dt, ps_)
        mask = pb("mask", [128, 128], F32)
        iotaKi = pb("iotaKi", [128, 256], I32)
        iotaK = pb("iotaK", [128, 256], F32)
        JTi = pb("JTi", [128, 7, 32], I32)
        JT = pb("JT", [128, 7, 32], F32)
        sm = pb("sm", [128, 4, 32], F32)
        bc = pb("bc", [128, 4, 512], F32)
        pw = pb("pw", [128, 4, 128], F32)
        PWo = pb("PWo", [128, 4, 2, 128], BF16)
        dtt = pb("dtt", [128, 32], F32)
        ard = pb("ard", [128, 32], F32)
        ang = pb("ang", [128, 32], F32)
        ARJ = pb("ARJ", [128, 7, 32], F32)
        ANJ = pb("ANJ", [128, 7, 32], F32)
        MAGP = pb("MAGP", [128, 7, 32], F32)
        MAGN = pb("MAGN", [128, 7, 32], F32)
        NIj = pb("NIj", [128, 7, 32], I32)
        RRj = pb("RRj", [128, 7, 32], F32)
        SN = pb("SN", [128, 7, 32], F32)
        CS = pb("CS", [128, 7, 32], F32)
        EXre = pb("EXre", [128, 32, 8], F32)
        EXim = pb("EXim", [128, 32, 8], F32)
        EYre = pb("EYre", [128, 32, 8], F32)
        EYim = pb("EYim", [128, 32, 8], F32)
        s1 = pb("s1", [128, 32], F32)
        s2 = pb("s2", [128, 32], F32)
        s3 = pb("s3", [128, 32], F32)
        s4 = pb("s4", [128, 32], F32)
        fre = pb("fre", [128, 32], F32)
        fim = pb("fim", [128, 32], F32)
        Rt = pb("Rt", [128, 32], F32)
        nRt = pb("nRt", [128, 32], F32)
        TH = pb("TH", [128, 32], F32)
        NI8 = pb("NI8", [128, 32], I32)
        bbre = pb("bbre", [128, 32, 16], F32)
        bbim = pb("bbim", [128, 32, 16], F32)
        u1 = pb("u1", [128, 32, 16], F32)
        T1 = pb("T1", [128, 32, 8, 16], F32)
        T2 = pb("T2", [128, 32, 8, 16], F32)
        XBm = pb("XBm", [128, 32, 8, 16], F32)
        Gm = pb("Gm", [128, 32, 8, 16], F32)
        YA = pb("YA", [128, 32, 8, 16], F32)
        YB = pb("YB", [128, 32, 8, 16], F32)
        W5 = pb("W5", [128, 8, 5, 4, 128], BF16)
        tmpT = pb("tmpT", [128, 4, 128], F32)
        def v256(t):
            return t[:].rearrange("p g t c -> p (g t c)").rearrange("p (a k) -> p a k", k=256)
        ANGb = v256(T1)
        NIb = v256(T2).bitcast(I32)
        RRb = v256(YA)
        COSh = v256(YB)
        SINh = pb("SINh", [128, 16, 256], F32)

        R_PC = Res("pl_const")
        R_IN = Res("pl_in")
        R_A = Res("pl_a")
        R_E = Res("pl_E")
        R_X = Res("pl_X")
        R_T = Res("pl_T")
        R_W5 = Res("pl_W5")
        R_TMPT = Res("pl_tmpT")
        R_ANG = Res("pl_ang")
        R_TAB = Res("pl_tab")
        R_PW = Res("pl_pw")

        P.op("pool", lambda e: e.memset(mask[:], 1.0), writes=[R_PC])
        P.op("pool", lambda e: e.affine_select(out=mask[:], in_=mask[:], pattern=[[16, 8], [0, 16]],
                                               compare_op=ALU.is_ge, fill=0.0, base=15, channel_multiplier=-1),
             reads=[R_PC], writes=[R_PC])
        P.op("pool", lambda e: e.iota(iotaKi[:], pattern=[[1, 256]], base=0, channel_multiplier=0), writes=[R_PC])
        P.op("pool", lambda e: e.tensor_copy(out=iotaK[:], in_=iotaKi[:]), reads=[R_PC], writes=[R_PC])
        P.op("pool", lambda e: e.iota(JTi[:], pattern=[[-1, 7], [0, 32]], base=7, channel_multiplier=0), writes=[R_PC])
        P.op("pool", lambda e: e.tensor_copy(out=JT[:], in_=JTi[:]), reads=[R_PC], writes=[R_PC])

        def bc3(ap32):
            return ap32.unsqueeze(1).to_broadcast([128, 7, 32])

        def bcc(ap32):
            return ap32.unsqueeze(2).to_broadcast([128, 32, 16])

        def bE(apE, lo, hi):
            return apE[lo:hi].unsqueeze(3).to_broadcast([hi - lo, 32, 8, 16])

        def bB(apB, lo, hi):
            return apB[lo:hi].unsqueeze(2).to_broadcast([hi - lo, 32, 8, 16])

        def bR(apR, lo, hi):
            return apR[lo:hi].rearrange("p (a g) -> p a g", g=4).unsqueeze(3).to_broadcast([hi - lo, 8, 4, 128])

        for l in range(n_layers):
            P.op("sp", lambda e, l=l: e.dma_start(out=sm[:], in_=ssm_small_d[l]), writes=[R_IN], dma="d_plin")
            P.op("sp", lambda e, l=l: e.dma_start(out=bc[:], in_=ssm_bc_d[l]), writes=[R_IN], dma="d_plin")
            P.op("sp", lambda e, l=l: e.dma_start(out=pw[:], in_=pool_w_d[l]), writes=[R_PW], dma="d_plpw")
            are, aim, ldt, dvec = sm[:, 0, :], sm[:, 1, :], sm[:, 2, :], sm[:, 3, :]
            bre = bc[:, 0, :].rearrange("p (g c) -> p g c", c=16)
            bim = bc[:, 1, :].rearrange("p (g c) -> p g c", c=16)
            cre = bc[:, 2, :].rearrange("p (g c) -> p g c", c=16)
            cim = bc[:, 3, :].rearrange("p (g c) -> p g c", c=16)

            for g4 in range(4):
                P.op("pool", lambda e, g4=g4: e.tensor_scalar(out=PWo[:, g4, 0, :], in0=pw[:, g4, :],
                                                            scalar1=1.0 / POOL_WINDOWS[g4], scalar2=None, op0=ALU.mult),
                     reads=[R_PW], writes=[R_PW])
            P.op("pool", lambda e: e.tensor_scalar(out=PWo[:, :, 1, :], in0=pw[:, :, :], scalar1=-1.0, scalar2=None, op0=ALU.mult),
                 reads=[R_PW], writes=[R_PW])
            P.op("sp", lambda e, l=l: e.dma_start(out=poolW_d[l], in_=PWo[:].rearrange("p a b c -> p (a b c)")),
                 reads=[R_PW], writes=[R_POOLW[l]], dma=f"d_plpo{l}")

            P.op("act", lambda e: e.activation(out=dtt[:], in_=ldt, func=AF.Exp), reads=[R_IN], writes=[R_A])
            P.op("dve", lambda e: e.tensor_tensor(out=ard[:], in0=are, in1=dtt[:], op=ALU.mult), reads=[R_IN, R_A], writes=[R_A])
            P.op("dve", lambda e: e.tensor_tensor(out=ang[:], in0=aim, in1=dtt[:], op=ALU.mult), reads=[R_IN, R_A], writes=[R_A])
            P.op("dve", lambda e: e.tensor_tensor(out=ARJ[:], in0=JT[:], in1=bc3(ard[:]), op=ALU.mult), reads=[R_PC, R_A], writes=[R_A])
            P.op("dve", lambda e: e.tensor_tensor(out=ANJ[:], in0=JT[:], in1=bc3(ang[:]), op=ALU.mult), reads=[R_PC, R_A], writes=[R_A])
            P.op("act", lambda e: e.activation(out=MAGP[:], in_=ARJ[:], func=AF.Exp), reads=[R_A], writes=[R_A])
            P.op("act", lambda e: e.activation(out=MAGN[:], in_=ARJ[:], func=AF.Exp, scale=-1.0), reads=[R_A], writes=[R_A])
            P.op("act", lambda e: e.activation(out=NIj[:], in_=ANJ[:], func=AF.Copy, scale=INV_2PI), reads=[R_A], writes=[R_A])
            P.op("dve", lambda e: e.scalar_tensor_tensor(out=RRj[:], in0=NIj[:], scalar=-TWO_PI, in1=ANJ[:], op0=ALU.mult, op1=ALU.add),
                 reads=[R_A], writes=[R_A])
            P.op("act", lambda e: e.activation(out=SN[:], in_=RRj[:], func=AF.Sin), reads=[R_A], writes=[R_A])
            P.op("dve", lambda e: e.tensor_scalar(out=RRj[:], in0=ANJ[:], scalar1=HALF_PI, scalar2=None, op0=ALU.add), reads=[R_A], writes=[R_A])
            P.op("act", lambda e: e.activation(out=NIj[:], in_=RRj[:], func=AF.Copy, scale=INV_2PI), reads=[R_A], writes=[R_A])
            P.op("dve", lambda e: e.scalar_tensor_tensor(out=RRj[:], in0=NIj[:], scalar=-TWO_PI, in1=RRj[:], op0=ALU.mult, op1=ALU.add),
                 reads=[R_A], writes=[R_A])
            P.op("act", lambda e: e.activation(out=CS[:], in_=RRj[:], func=AF.Sin), reads=[R_A], writes=[R_A])
            def Ev(t):
                return t[:].rearrange("p g t -> p t g")[:, 0:7, :]
            P.op("dve", lambda e: e.tensor_tensor(out=Ev(EXre), in0=MAGP[:], in1=CS[:], op=ALU.mult), reads=[R_A], writes=[R_E])
            P.op("dve", lambda e: e.tensor_tensor(out=Ev(EXim), in0=MAGP[:], in1=SN[:], op=ALU.mult), reads=[R_A], writes=[R_E])
            P.op("dve", lambda e: e.tensor_tensor(out=Ev(EYre), in0=MAGN[:], in1=CS[:], op=ALU.mult), reads=[R_A], writes=[R_E])
            P.op("dve", lambda e: e.scalar_tensor_tensor(out=Ev(EYim), in0=MAGN[:], scalar=-1.0, in1=SN[:], op0=ALU.mult, op1=ALU.mult),
                 reads=[R_A], writes=[R_E])
            P.op("dve", lambda e: e.memset(EXre[:, :, 7:8], 1.0), writes=[R_E])
            P.op("dve", lambda e: e.memset(EXim[:, :, 7:8], 0.0), writes=[R_E])
            P.op("dve", lambda e: e.memset(EYre[:, :, 7:8], 1.0), writes=[R_E])
            P.op("dve", lambda e: e.memset(EYim[:, :, 7:8], 0.0), writes=[R_E])
            lre, lim = EXre[:, :, 6], EXim[:, :, 6]
            P.op("dve", lambda e: e.tensor_scalar(out=s1[:], in0=lre, scalar1=-1.0, scalar2=None, op0=ALU.add), reads=[R_E], writes=[R_A])
            P.op("dve", lambda e: e.tensor_tensor(out=s2[:], in0=are, in1=are, op=ALU.mult), reads=[R_IN], writes=[R_A])
            P.op("dve", lambda e: e.tensor_tensor(out=s3[:], in0=aim, in1=aim, op=ALU.mult), reads=[R_IN], writes=[R_A])
            P.op("dve", lambda e: e.tensor_tensor(out=s2[:], in0=s2[:], in1=s3[:], op=ALU.add), reads=[R_A], writes=[R_A])
            P.op("dve", lambda e: e.reciprocal(out=s2[:], in_=s2[:]), reads=[R_A], writes=[R_A])
            P.op("dve", lambda e: e.tensor_tensor(out=s3[:], in0=s1[:], in1=are, op=ALU.mult), reads=[R_A, R_IN], writes=[R_A])
            P.op("dve", lambda e: e.tensor_tensor(out=s4[:], in0=lim, in1=aim, op=ALU.mult), reads=[R_E, R_IN], writes=[R_A])
            P.op("dve", lambda e: e.tensor_tensor(out=s3[:], in0=s3[:], in1=s4[:], op=ALU.add), reads=[R_A], writes=[R_A])
            P.op("dve", lambda e: e.tensor_tensor(out=fre[:], in0=s3[:], in1=s2[:], op=ALU.mult), reads=[R_A], writes=[R_A])
            P.op("dve", lambda e: e.tensor_tensor(out=s3[:], in0=lim, in1=are, op=ALU.mult), reads=[R_E, R_IN], writes=[R_A])
            P.op("dve", lambda e: e.tensor_tensor(out=s4[:], in0=s1[:], in1=aim, op=ALU.mult), reads=[R_A, R_IN], writes=[R_A])
            P.op("dve", lambda e: e.tensor_tensor(out=s3[:], in0=s3[:], in1=s4[:], op=ALU.subtract), reads=[R_A], writes=[R_A])
            P.op("dve", lambda e: e.tensor_tensor(out=fim[:], in0=s3[:], in1=s2[:], op=ALU.mult), reads=[R_A], writes=[R_A])
            P.op("dve", lambda e: e.tensor_tensor(out=bbre[:], in0=bre, in1=bcc(fre[:]), op=ALU.mult), reads=[R_A, R_IN], writes=[R_A])
            P.op("dve", lambda e: e.tensor_tensor(out=u1[:], in0=bim, in1=bcc(fim[:]), op=ALU.mult), reads=[R_A, R_IN], writes=[R_A])
            P.op("dve", lambda e: e.tensor_tensor(out=bbre[:], in0=bbre[:], in1=u1[:], op=ALU.subtract), reads=[R_A], writes=[R_A])
            P.op("dve", lambda e: e.tensor_tensor(out=bbim[:], in0=bim, in1=bcc(fre[:]), op=ALU.mult), reads=[R_A, R_IN], writes=[R_A])
            P.op("dve", lambda e: e.tensor_tensor(out=u1[:], in0=bre, in1=bcc(fim[:]), op=ALU.mult), reads=[R_A, R_IN], writes=[R_A])
            P.op("dve", lambda e: e.tensor_tensor(out=bbim[:], in0=bbim[:], in1=u1[:], op=ALU.add), reads=[R_A], writes=[R_A])
            P.op("act", lambda e: e.activation(out=Rt[:], in_=ard[:], func=AF.Exp, scale=8.0), reads=[R_A], writes=[R_A])
            P.op("act", lambda e: e.activation(out=nRt[:], in_=Rt[:], func=AF.Copy, scale=-1.0), reads=[R_A], writes=[R_A])
            P.op("act", lambda e, l=l: e.activation(out=Rb_all[:, l, :], in_=Rt[:], func=AF.Copy), reads=[R_A], writes=[R_RB])
            P.op("dve", lambda e: e.tensor_scalar(out=s1[:], in0=ang[:], scalar1=8.0, scalar2=None, op0=ALU.mult), reads=[R_A], writes=[R_A])
            P.op("act", lambda e: e.activation(out=NI8[:], in_=s1[:], func=AF.Copy, scale=INV_2PI), reads=[R_A], writes=[R_A])
            P.op("dve", lambda e: e.scalar_tensor_tensor(out=TH[:], in0=NI8[:], scalar=-TWO_PI, in1=s1[:], op0=ALU.mult, op1=ALU.add),
                 reads=[R_A], writes=[R_A])

            P.op("dve", lambda e: e.tensor_tensor(out=T1[0:64], in0=bE(EXre, 0, 64), in1=bB(bbre, 0, 64), op=ALU.mult), reads=[R_E, R_A], writes=[R_T])
            P.op("dve", lambda e: e.tensor_tensor(out=T2[0:64], in0=bE(EXim, 0, 64), in1=bB(bbim, 0, 64), op=ALU.mult), reads=[R_E, R_A], writes=[R_T])
            P.op("dve", lambda e: e.tensor_tensor(out=XBm[0:64], in0=T1[0:64], in1=T2[0:64], op=ALU.subtract), reads=[R_T], writes=[R_X])
            P.op("dve", lambda e: e.tensor_tensor(out=T1[64:128], in0=bE(EXre, 64, 128), in1=bB(bbim, 64, 128), op=ALU.mult), reads=[R_E, R_A], writes=[R_T])
            P.op("dve", lambda e: e.tensor_tensor(out=T2[64:128], in0=bE(EXim, 64, 128), in1=bB(bbre, 64, 128), op=ALU.mult), reads=[R_E, R_A], writes=[R_T])
            P.op("dve", lambda e: e.tensor_tensor(out=XBm[64:128], in0=T1[64:128], in1=T2[64:128], op=ALU.add), reads=[R_T], writes=[R_X])
            P.op("dve", lambda e: e.tensor_tensor(out=T1[:], in0=bE(EYre, 0, 128), in1=bB(cre, 0, 128), op=ALU.mult), reads=[R_E, R_IN, R_X], writes=[R_T])
            P.op("dve", lambda e: e.tensor_tensor(out=T2[:], in0=bE(EYim, 0, 128), in1=bB(cim, 0, 128), op=ALU.mult), reads=[R_E, R_IN], writes=[R_T])
            P.op("dve", lambda e: e.tensor_tensor(out=YA[:], in0=T1[:], in1=T2[:], op=ALU.subtract), reads=[R_T], writes=[R_T])
            P.op("dve", lambda e: e.tensor_tensor(out=T1[:], in0=bE(EYre, 0, 128), in1=bB(cim, 0, 128), op=ALU.mult), reads=[R_E, R_IN], writes=[R_T])
            P.op("dve", lambda e: e.tensor_tensor(out=T2[:], in0=bE(EYim, 0, 128), in1=bB(cre, 0, 128), op=ALU.mult), reads=[R_E, R_IN], writes=[R_T])
            P.op("dve", lambda e: e.scalar_tensor_tensor(out=YB[:], in0=T1[:], scalar=-1.0, in1=T2[:], op0=ALU.mult, op1=ALU.subtract),
                 reads=[R_T], writes=[R_T])
            P.op("pool", lambda e: e.tensor_copy(out=Gm[0:64], in_=YA[0:64]), reads=[R_T], writes=[R_X])
            P.op("pool", lambda e: e.tensor_copy(out=Gm[64:128], in_=YB[64:128]), reads=[R_T], writes=[R_X])
            def w5k(kind, lo, hi):
                return W5[lo:hi, :, kind, :, :]

            def flat(t, lo, hi):
                return t[lo:hi].rearrange("p (a g) t c -> p a g (t c)", g=4)
            P.op("dve", lambda e: e.tensor_tensor(out=w5k(3, 0, 64), in0=flat(YA, 0, 64), in1=bR(Rt, 0, 64), op=ALU.mult), reads=[R_T, R_A], writes=[R_W5])
            P.op("dve", lambda e: e.tensor_tensor(out=w5k(3, 64, 128), in0=flat(YB, 64, 128), in1=bR(Rt, 64, 128), op=ALU.mult), reads=[R_T, R_A], writes=[R_W5])
            P.op("dve", lambda e: e.tensor_tensor(out=w5k(4, 0, 64), in0=flat(YB, 0, 64), in1=bR(Rt, 0, 64), op=ALU.mult), reads=[R_T, R_A], writes=[R_W5])
            P.op("dve", lambda e: e.tensor_tensor(out=w5k(4, 64, 128), in0=flat(YA, 64, 128), in1=bR(nRt, 64, 128), op=ALU.mult), reads=[R_T, R_A], writes=[R_W5])

            for gb in range(8):
                pi = next_ps()

                def f_toep(e, gb=gb, pi=pi):
                    ins = None
                    for g4 in range(4):
                        g = gb * 4 + g4
                        ins = e.matmul(psum[:, pi, g4 * 128:(g4 + 1) * 128],
                                       lhsT=XBm[:, g].rearrange("p t c -> p (t c)"),
                                       rhs=Gm[:, g].rearrange("p t c -> p (t c)"), start=True, stop=True)
                    return ins
                P.op("pe", f_toep, reads=[R_X], writes=[R_PS[pi]])
                P.op("dve", lambda e, pi=pi: e.tensor_tensor(out=tmpT[:], in0=psum[:, pi, :].rearrange("p (g n) -> p g n", g=4),
                                                            in1=mask[:].unsqueeze(1).to_broadcast([128, 4, 128]), op=ALU.mult),
                     reads=[R_PS[pi], R_PC], writes=[R_TMPT])
                for g4 in range(4):
                    P.op("dve", lambda e, gb=gb, g4=g4: e.scalar_tensor_tensor(
                        out=W5[:, gb, 0, g4, :], in0=identF[:], scalar=sm[:, 3, gb * 4 + g4:gb * 4 + g4 + 1], in1=tmpT[:, g4, :],
                        op0=ALU.mult, op1=ALU.add), reads=[R_TMPT, R_CONST, R_IN], writes=[R_W5])
                pi2 = next_ps()

                def f_tr(e, gb=gb, pi2=pi2):
                    ins = None
                    for g4 in range(4):
                        g = gb * 4 + g4
                        ins = e.transpose(out=psum[:, pi2, g4 * 128:(g4 + 1) * 128],
                                          in_=XBm[:, g].rearrange("p t c -> p (t c)"), identity=identF[:])
                    return ins
                P.op("pe", f_tr, reads=[R_X, R_CONST], writes=[R_PS[pi2]])
                pv = psum[:, pi2, :].rearrange("p (g n) -> p g n", g=4)
                P.op("act", lambda e, gb=gb, pv=pv: e.activation(out=W5[:, gb, 1, :, :], in_=pv, func=AF.Copy),
                     reads=[R_PS[pi2]], writes=[R_W5])
                P.op("act", lambda e, gb=gb, pv=pv: e.activation(out=W5[:, gb, 2, :, 0:64], in_=pv[:, :, 64:128], func=AF.Copy),
                     reads=[R_PS[pi2]], writes=[R_W5])
                P.op("act", lambda e, gb=gb, pv=pv: e.activation(out=W5[:, gb, 2, :, 64:128], in_=pv[:, :, 0:64], func=AF.Copy, scale=-1.0),
                     reads=[R_PS[pi2]], writes=[R_W5])
            P.op("sp", lambda e, l=l: e.dma_start(out=ssmW_d[l].rearrange("a p n -> p a n"),
                                                  in_=W5[:].rearrange("p a k g n -> p a (k g n)")),
                 reads=[R_W5], writes=[R_SSMW[l]], dma=f"d_plssm{l}")

            tv = tabs_d[l].rearrange("a p two g k -> p a two g k")
            for hf in range(2):
                gs = slice(hf * 16, hf * 16 + 16)
                P.op("dve", lambda e, gs=gs: e.tensor_tensor(out=ANGb, in0=TH[:, gs].unsqueeze(2).to_broadcast([128, 16, 256]),
                                                            in1=iotaK[:].unsqueeze(1).to_broadcast([128, 16, 256]), op=ALU.mult),
                     reads=[R_A, R_PC], writes=[R_T])
                P.op("act", lambda e: e.activation(out=NIb, in_=ANGb, func=AF.Copy, scale=INV_2PI), reads=[R_T], writes=[R_T])
                P.op("dve", lambda e: e.scalar_tensor_tensor(out=RRb, in0=NIb, scalar=-TWO_PI, in1=ANGb, op0=ALU.mult, op1=ALU.add),
                     reads=[R_T], writes=[R_T])
                P.op("act", lambda e: e.activation(out=SINh[:], in_=RRb, func=AF.Sin), reads=[R_T], writes=[R_TAB])
                P.op("dve", lambda e: e.tensor_scalar(out=ANGb, in0=ANGb, scalar1=HALF_PI, scalar2=None, op0=ALU.add), reads=[R_T], writes=[R_T])
                P.op("act", lambda e: e.activation(out=NIb, in_=ANGb, func=AF.Copy, scale=INV_2PI), reads=[R_T], writes=[R_T])
                P.op("dve", lambda e: e.scalar_tensor_tensor(out=RRb, in0=NIb, scalar=-TWO_PI, in1=ANGb, op0=ALU.mult, op1=ALU.add),
                     reads=[R_T], writes=[R_T])
                P.op("act", lambda e: e.activation(out=COSh, in_=RRb, func=AF.Sin), reads=[R_T], writes=[R_T])
                P.op("sp", lambda e, tv=tv, hf=hf: e.dma_start(out=tv[:, hf * 4:hf * 4 + 4, 0], in_=COSh.rearrange("p (a g) k -> p a g k", g=4)),
                     reads=[R_T], writes=[R_TABS[l]], dma=f"d_pltab{l}")
                P.op("sp", lambda e, tv=tv, hf=hf: e.dma_start(out=tv[:, hf * 4:hf * 4 + 4, 1], in_=SINh[:].rearrange("p (a g) k -> p a g k", g=4)),
                     reads=[R_TAB], writes=[R_TABS[l]], dma=f"d_pltab{l}")

    wbf_in = dram_scr("wbf_in", [DEPTH, D, 2 * D], BF16)
    wbf_out = dram_scr("wbf_out", [DEPTH, D, D], BF16)
    wbf_glu = dram_scr("wbf_glu", [DEPTH, 512, 512], BF16)
    R_WBF = [Res(f"wbf{l}") for l in range(DEPTH)]
    for l in range(n_layers):
        for a in range(8):
            P.op("pool", lambda e, l=l, a=a: e.dma_start(out=wbf_in[l, a * 128:(a + 1) * 128, :], in_=w_in_d[l, a * 128:(a + 1) * 128, :]),
                 writes=[R_WBF[l]], dma=f"d_wbf{l}")
        for a in range(4):
            P.op("pool", lambda e, l=l, a=a: e.dma_start(out=wbf_out[l, a * 256:(a + 1) * 256, :], in_=w_out_d[l, a * 256:(a + 1) * 256, :]),
                 writes=[R_WBF[l]], dma=f"d_wbf{l}")
        P.op("pool", lambda e, l=l: e.dma_start(out=wbf_glu[l], in_=glu_w_d[l]), writes=[R_WBF[l]], dma=f"d_wbf{l}")

    if debug == "prologue":
        rb_d = dram_out("rb_dbg", [128, DEPTH, 32])
        P.op("sp", lambda e: e.dma_start(out=rb_d[:], in_=Rb_all[:]), reads=[R_RB], writes=[R_OUT], dma="d_out")
        fin = [R_OUT] + R_SSMW[:n_layers] + R_TABS[:n_layers] + R_POOLW[:n_layers] + R_WBF[:n_layers]
        P.op("sp", None, reads=fin, sig=False)
        sems = {k: es.enter_context(nc.semaphore(k)) for k in P.cnt}
        with nc.Block() as block:
            P.emit(nc, block, sems)
        es.close()
        return nc

    xres = sb("xres", [128, 8, ST], F32)
    h = sb("h", [128, 8, ST], BF16)
    gate = sb("gate", [128, 8, ST], BF16)
    ycat = sb("ycat", [128, 8, ST], BF16)
    upool = sb("upool", [128, 4, 16 + ST], BF16)
    spool = sb("spool", [128, 4, ST], BF16)
    ysT = spool
    pwk = sb("pwk", [128, 3, 16 + ST], BF16)
    ZYf = sb("ZY", [128, 4096], BF16)
    ZY = ZYf[:].rearrange("p (t f) -> p t f", t=8)
    Zs = ZYf[:].rearrange("p (g t c) -> p g t c", g=32, t=8)
    U = sb("U", [128, 32, 128], BF16)
    xsq = sb("xsq", [128, 8, 512], BF16)
    rs = sb("rs", [128, 512], F32)
    t1 = sb("t1", [128, 4, 128], F32)
    t2 = sb("t2", [128, 4, 128], F32)
    Q = sb("Q", [128, 4, 129], F32)
    Pc = sb("Pc", [128, 4, 128], BF16)
    Ps = sb("Ps", [128, 4, 128], BF16)
    sig = sb("sig", [128, 2, 512], BF16)
    xs = sb("xs", [128, 2, D], F32)
    carry = sb("carry", [128, DEPTH, 32], F32)
    halo = sb("halo", [128, DEPTH, 4, 16], BF16)
    cfix = sb("cfix", [128, 4, 16], F32)
    epsT = sb("epsT", [128, 1], F32)
    NWS, NSS = 4, 2
    WS = [sb(f"WS{i}", [128, 8 * 512], BF16) for i in range(NWS)]
    SSw = [sb(f"SSw{i}", [128, 5, 4, 128], BF16) for i in range(NSS)]
    SSt = [sb(f"SSt{i}", [128, 2, 4, 128], F32) for i in range(NSS)]
    R_WS = [Res(f"WS{i}") for i in range(NWS)]
    R_SS = [Res(f"SS{i}") for i in range(NSS)]
    R_XRES, R_ZY, R_U, R_XSQ, R_RS = Res("xres"), Res("ZY"), Res("U"), Res("xsq"), Res("rs")
    R_H = [Res("h0"), Res("h1")]
    R_GATE = [Res("g0"), Res("g1")]
    R_YCAT = [Res("yc0"), Res("yc1")]
    R_UPOOL, R_SPOOL, R_PWK = Res("upool"), Res("spool"), Res("pwk")
    R_T1, R_T2, R_Q, R_PCS = Res("t1"), Res("t2"), Res("Q"), Res("PcPs")
    R_SIG = [Res("sig0"), Res("sig1")]
    R_XS = [Res("xs0"), Res("xs1")]
    R_CARRY = [Res(f"carry{l}") for l in range(DEPTH)]
    R_HALO = [Res(f"halo{l}") for l in range(DEPTH)]
    R_C2 = Res("const2")
    ws_rr, ss_rr, xs_rr, ev_rr = [0], [0], [0], [0]

    def psB(pi):
        return psum[:, pi, :].bitcast(BF16)

    def evac_eng():
        ev_rr[0] ^= 1
        return "act" if ev_rr[0] else "dve"

    def copy_op(eng, out, in_, reads, acc):
        if eng == "act":
            P.op("act", lambda e: e.activation(out=out, in_=in_, func=AF.Copy), reads=reads, acc=acc)
        else:
            P.op(eng, lambda e: e.tensor_copy(out=out, in_=in_), reads=reads, acc=acc)

    P.op("pool", lambda e: e.memset(epsT[:], EPS), writes=[R_C2])
    P.op("pool", lambda e: e.memset(cfix[:], 1.0), writes=[R_C2])
    for f in range(4):
        w = POOL_WINDOWS[f]
        for t in range(w - 1):
            P.op("pool", lambda e, f=f, t=t, w=w: e.memset(cfix[:, f, t:t + 1], float(w) / float(t + 1)), writes=[R_C2])

    def load_ws(src_ap, nparts, reads):
        i = ws_rr[0]
        ws_rr[0] = (i + 1) % NWS
        dst = WS[i][:, 0:nparts * 512].rearrange("p (a n) -> p a n", n=512)
        P.op("sp", lambda e: e.dma_start(out=dst, in_=src_ap), reads=reads, writes=[R_WS[i]], dma=f"d_ws{i}")
        return i

    def wsv(i):
        return WS[i][:].rearrange("p (a n) -> p a n", n=512)

    for st in range(n_sub):
        seq, half = st // 2, st % 2
        t0 = half * ST
        if half == 0:
            P.op("pool", lambda e: e.memset(carry[:], 0.0), writes=R_CARRY)
            P.op("pool", lambda e: e.memset(halo[:], 0.0), writes=R_HALO)
        for tt in range(8):
            si = xs_rr[0]
            xs_rr[0] ^= 1
            P.op("sp", lambda e, si=si, tt=tt, seq=seq, t0=t0: e.dma_start(out=xs[:, si, :], in_=x_d[seq, t0 + tt * 128:t0 + (tt + 1) * 128, :]),
                 writes=[R_XS[si]], dma=f"d_xs{si}")
            for fh in range(2):
                pi = next_ps()

                def f_xt(e, si=si, fh=fh, pi=pi):
                    ins = None
                    for f4 in range(4):
                        f = fh * 4 + f4
                        ins = e.transpose(out=psum[:, pi, f4 * 128:(f4 + 1) * 128], in_=xs[:, si, f * 128:(f + 1) * 128], identity=identF[:])
                    return ins
                P.op("pe", f_xt, reads=[R_XS[si], R_CONST], writes=[R_PS[pi]])
                copy_op(evac_eng(), xres[:, fh * 4:(fh + 1) * 4, tt * 128:(tt + 1) * 128],
                        psum[:, pi, :].rearrange("p (f n) -> p f n", f=4), [R_PS[pi]], [R_XRES])

        def rmsnorm_stats(n):
            nr = slice(n * 512, (n + 1) * 512)
            P.op("act", lambda e: e.activation(out=xsq[:], in_=xres[:, :, nr], func=AF.Square), reads=[R_XRES], writes=[R_XSQ])
            pi = next_ps()

            def f_ms(e, pi=pi):
                ins = None
                for f in range(8):
                    ins = e.matmul(psum[:, pi, :], lhsT=onesB[:], rhs=xsq[:, f, :], start=(f == 0), stop=(f == 7))
                return ins
            P.op("pe", f_ms, reads=[R_XSQ, R_CONST], writes=[R_PS[pi]])
            P.op("act", lambda e, pi=pi: e.activation(out=rs[:], in_=psum[:, pi, :], func=AF.Sqrt, bias=epsT[:], scale=1.0),
                 reads=[R_PS[pi], R_C2], writes=[R_RS])
            P.op("dve", lambda e: e.reciprocal(out=rs[:], in_=rs[:]), reads=[R_RS], writes=[R_RS])
            return nr

        for l in range(n_layers):
            k0 = half * NK
            for n in range(2):
                nr = rmsnorm_stats(n)
                for f in range(8):
                    P.op("dve", lambda e, f=f, nr=nr, l=l: e.scalar_tensor_tensor(
                        out=h[:, f, nr], in0=xres[:, f, nr], scalar=smallp[:, l, f:f + 1], in1=rs[:], op0=ALU.mult, op1=ALU.mult),
                        reads=[R_XRES, R_RS, R_SMALLP], writes=[R_H[n]])
            win = wbf_in[l].rearrange("(a p) n -> p a n", p=128)
            for sg in range(2):
                wi = load_ws(win[:, :, 1024 + sg * 512:1024 + (sg + 1) * 512], 8, [R_WBF[l]])
                for f4 in range(4):
                    fo = sg * 4 + f4
                    for n in range(2):
                        nr = slice(n * 512, (n + 1) * 512)
                        pi = next_ps()

                        def f_mm(e, wi=wi, f4=f4, nr=nr, pi=pi):
                            ins = None
                            for kc in range(8):
                                ins = e.matmul(psum[:, pi, :], lhsT=wsv(wi)[:, kc, f4 * 128:(f4 + 1) * 128], rhs=h[:, kc, nr],
                                               start=(kc == 0), stop=(kc == 7))
                            return ins
                        P.op("pe", f_mm, reads=[R_WS[wi], R_H[n]], writes=[R_PS[pi]])
                        P.op("act", lambda e, fo=fo, nr=nr, pi=pi: e.activation(out=gate[:, fo, nr], in_=psum[:, pi, :], func=AF.Silu),
                             reads=[R_PS[pi]], writes=[R_GATE[n]])
            wg = load_ws(wbf_glu[l].rearrange("(a p) n -> p a n", p=128), 4, [R_WBF[l]])
            P.op("sp", lambda e, wg=wg, l=l: e.dma_start(out=WS[wg][:, 2048:3072], in_=poolW_d[l]),
                 reads=[R_POOLW[l]], writes=[R_WS[wg]], dma=f"d_ws{wg}")
            pwv = WS[wg][:, 2048:3072].rearrange("p (g k n) -> p g k n", g=4, k=2)
            wi = load_ws(win[:, :, 0:512], 8, [R_WBF[l]])
            P.op("pool", lambda e, l=l: e.tensor_copy(out=upool[:, :, 0:16], in_=halo[:, l, :, :]), reads=[R_HALO[l]], writes=[R_UPOOL])
            for f in range(4):
                for n in range(2):
                    nr = slice(n * 512, (n + 1) * 512)
                    pi = next_ps()

                    def f_mm(e, wi=wi, f=f, nr=nr, pi=pi):
                        ins = None
                        for kc in range(8):
                            ins = e.matmul(psum[:, pi, :], lhsT=wsv(wi)[:, kc, f * 128:(f + 1) * 128], rhs=h[:, kc, nr],
                                           start=(kc == 0), stop=(kc == 7))
                        return ins
                    P.op("pe", f_mm, reads=[R_WS[wi], R_H[n]], writes=[R_PS[pi]])
                    P.op("dve", lambda e, f=f, n=n, pi=pi: e.tensor_copy(out=upool[:, f, 16 + n * 512:16 + (n + 1) * 512], in_=psum[:, pi, :]),
                         reads=[R_PS[pi]], writes=[R_UPOOL])
            P.op("pool", lambda e, l=l: e.tensor_copy(out=halo[:, l, :, :], in_=upool[:, :, ST:ST + 16]), reads=[R_UPOOL], writes=[R_HALO[l]])
            LT = 16 + ST
            P.op("pool", lambda e: e.tensor_tensor(out=spool[:, 0, :], in0=upool[:, 0, 16:LT], in1=upool[:, 0, 15:LT - 1], op=ALU.add),
                 reads=[R_UPOOL], writes=[R_SPOOL])
            for f in (1, 2, 3):
                P.op("pool", lambda e, f=f: e.tensor_tensor(out=pwk[:, 0, 1:LT], in0=upool[:, f, 1:LT], in1=upool[:, f, 0:LT - 1], op=ALU.add),
                     reads=[R_UPOOL], writes=[R_PWK])
                if f == 1:
                    P.op("pool", lambda e: e.tensor_tensor(out=spool[:, 1, :], in0=pwk[:, 0, 16:LT], in1=pwk[:, 0, 14:LT - 2], op=ALU.add),
                         reads=[R_PWK], writes=[R_SPOOL])
                    continue
                P.op("pool", lambda e: e.tensor_tensor(out=pwk[:, 1, 3:LT], in0=pwk[:, 0, 3:LT], in1=pwk[:, 0, 1:LT - 2], op=ALU.add),
                     reads=[R_PWK], writes=[R_PWK])
                if f == 2:
                    P.op("pool", lambda e: e.tensor_tensor(out=spool[:, 2, :], in0=pwk[:, 1, 16:LT], in1=pwk[:, 1, 12:LT - 4], op=ALU.add),
                         reads=[R_PWK], writes=[R_SPOOL])
                    continue
                P.op("pool", lambda e: e.tensor_tensor(out=pwk[:, 2, 7:LT], in0=pwk[:, 1, 7:LT], in1=pwk[:, 1, 3:LT - 4], op=ALU.add),
                     reads=[R_PWK], writes=[R_PWK])
                P.op("pool", lambda e: e.tensor_tensor(out=spool[:, 3, :], in0=pwk[:, 2, 16:LT], in1=pwk[:, 2, 8:LT - 8], op=ALU.add),
                     reads=[R_PWK], writes=[R_SPOOL])
            if half == 0:
                P.op("pool", lambda e: e.tensor_tensor(out=spool[:, :, 0:16], in0=spool[:, :, 0:16], in1=cfix[:], op=ALU.mult),
                     reads=[R_SPOOL, R_C2], writes=[R_SPOOL])
            for f in range(4):
                for n in range(2):
                    nr = slice(n * 512, (n + 1) * 512)
                    pi = next_ps()

                    def f_pm(e, f=f, n=n, nr=nr, pi=pi, pwv=pwv):
                        e.matmul(psum[:, pi, :], lhsT=pwv[:, f, 0, :], rhs=spool[:, f, nr], start=True, stop=False)
                        return e.matmul(psum[:, pi, :], lhsT=pwv[:, f, 1, :], rhs=upool[:, f, 16 + n * 512:16 + (n + 1) * 512], start=False, stop=True)
                    P.op("pe", f_pm, reads=[R_WS[wg], R_SPOOL, R_UPOOL], writes=[R_PS[pi]])
                    P.op("dve", lambda e, f=f, nr=nr, pi=pi, l=l: e.scalar_tensor_tensor(
                        out=ycat[:, f, nr], in0=psum[:, pi, :], scalar=smallp[:, l, 8 + f:9 + f], in1=gate[:, f, nr], op0=ALU.mult, op1=ALU.mult),
                        reads=[R_PS[pi], R_GATE[n], R_SMALLP], writes=[R_YCAT[n]])
            wi = load_ws(win[:, :, 512:1024], 8, [R_WBF[l]])
            for tau in range(8):
                pi = next_ps()

                def f_mm(e, wi=wi, tau=tau, pi=pi):
                    ins = None
                    for kc in range(8):
                        ins = e.matmul(psum[:, pi, :], lhsT=h[:, kc, :].rearrange("p (k t) -> p t k", t=8)[:, tau, :], rhs=wsv(wi)[:, kc, :],
                                       start=(kc == 0), stop=(kc == 7))
                    return ins
                P.op("pe", f_mm, reads=[R_WS[wi], R_H[0], R_H[1]], writes=[R_PS[pi]])
                copy_op(evac_eng(), Zs[:, :, tau, :], psum[:, pi, :].rearrange("p (g c) -> p g c", c=16), [R_PS[pi]], [R_ZY])
            for gq in range(4):
                pi = next_ps()

                def f_tr(e, gq=gq, pi=pi):
                    ins = None
                    for g8 in range(8):
                        g = gq * 8 + g8
                        ins = e.transpose(out=psB(pi)[:, g8 * 128:(g8 + 1) * 128], in_=Zs[:, g].rearrange("p t c -> p (t c)"), identity=identB[:])
                    return ins
                P.op("pe", f_tr, reads=[R_ZY, R_CONST], writes=[R_PS[pi]])
                copy_op(evac_eng(), U[:, gq * 8:(gq + 1) * 8, :], psB(pi).rearrange("p (g k) -> p g k", g=8), [R_PS[pi]], [R_U])
            for gb in range(8):
                si = ss_rr[0]
                ss_rr[0] = (si + 1) % NSS
                P.op("sp", lambda e, si=si, l=l, gb=gb: e.dma_start(out=SSw[si][:].rearrange("p a g n -> p (a g n)"), in_=ssmW_d[l, gb]),
                     reads=[R_SSMW[l]], writes=[R_SS[si]], dma=f"d_ss{si}")
                P.op("sp", lambda e, si=si, l=l, gb=gb, k0=k0: e.dma_start(out=SSt[si][:], in_=tabs_d[l, gb][:, :, :, k0:k0 + NK]),
                     reads=[R_TABS[l]], writes=[R_SS[si]], dma=f"d_ss{si}")
                pa, pb_ = next_ps(), next_ps()

                def f_v(e, si=si, gb=gb, pa=pa, pb_=pb_):
                    ins = None
                    for kind, pi in ((1, pa), (2, pb_)):
                        for g4 in range(4):
                            ins = e.matmul(psum[:, pi, g4 * 128:(g4 + 1) * 128], lhsT=SSw[si][:, kind, g4, :], rhs=U[:, gb * 4 + g4, :],
                                           start=True, stop=True)
                    return ins
                P.op("pe", f_v, reads=[R_SS[si], R_U], writes=[R_PS[pa], R_PS[pb_]])
                pv = lambda pi: psum[:, pi, :].rearrange("p (g k) -> p g k", g=4)
                P.op("dve", lambda e, si=si, pa=pa: e.tensor_tensor(out=t1[:], in0=pv(pa), in1=SSt[si][:, 0], op=ALU.mult),
                     reads=[R_PS[pa], R_SS[si]], writes=[R_T1])
                P.op("dve", lambda e, si=si, pb_=pb_: e.tensor_tensor(out=t2[:], in0=pv(pb_), in1=SSt[si][:, 1], op=ALU.mult),
                     reads=[R_PS[pb_], R_SS[si]], writes=[R_T2])
                P.op("pool", lambda e: e.tensor_tensor(out=t1[:], in0=t1[:], in1=t2[:], op=ALU.add), reads=[R_T1, R_T2], writes=[R_T1])
                P.op("pool", lambda e, l=l, gb=gb: e.tensor_copy(out=Q[:, :, 0], in_=carry[:, l, gb * 4:(gb + 1) * 4]),
                     reads=[R_CARRY[l]], writes=[R_Q])
                for g4 in range(4):
                    g = gb * 4 + g4
                    P.op("dve", lambda e, g4=g4, g=g, l=l: e.tensor_tensor_scan(
                        out=Q[:, g4, 1:129], data0=Rb_all[:, l, g:g + 1].to_broadcast([128, 128]), data1=t1[:, g4, :],
                        initial=carry[:, l, g:g + 1], op0=ALU.mult, op1=ALU.add),
                        reads=[R_T1, R_RB, R_CARRY[l]], writes=[R_Q])
                P.op("pool", lambda e, l=l, gb=gb: e.tensor_copy(out=carry[:, l, gb * 4:(gb + 1) * 4], in_=Q[:, :, 128]),
                     reads=[R_Q], writes=[R_CARRY[l]])
                P.op("pool", lambda e, si=si: e.tensor_tensor(out=Pc[:], in0=Q[:, :, 0:128], in1=SSt[si][:, 0], op=ALU.mult),
                     reads=[R_Q, R_SS[si]], writes=[R_PCS])
                P.op("pool", lambda e, si=si: e.tensor_tensor(out=Ps[:], in0=Q[:, :, 0:128], in1=SSt[si][:, 1], op=ALU.mult),
                     reads=[R_Q, R_SS[si]], writes=[R_PCS])
                py = next_ps()

                def f_y(e, si=si, gb=gb, py=py):
                    ins = None
                    for g4 in range(4):
                        o = psum[:, py, g4 * 128:(g4 + 1) * 128]
                        e.matmul(o, lhsT=U[:, gb * 4 + g4, :], rhs=SSw[si][:, 0, g4, :], start=True, stop=False)
                        e.matmul(o, lhsT=Pc[:, g4, :], rhs=SSw[si][:, 3, g4, :], start=False, stop=False)
                        ins = e.matmul(o, lhsT=Ps[:, g4, :], rhs=SSw[si][:, 4, g4, :], start=False, stop=True)
                    return ins
                P.op("pe", f_y, reads=[R_SS[si], R_U, R_PCS], writes=[R_PS[py]])
                P.op("act", lambda e, gb=gb, py=py: e.activation(
                    out=ZY[:, :, gb * 64:(gb + 1) * 64].rearrange("p t (g c) -> p g t c", g=4),
                    in_=psum[:, py, :].rearrange("p (g t c) -> p g t c", g=4, t=8), func=AF.Gelu_apprx_tanh),
                    reads=[R_PS[py]], writes=[R_ZY])
            for f in range(4):
                pi = next_ps()

                def f_tr(e, f=f, pi=pi):
                    ins = None
                    for tau in range(8):
                        ins = e.transpose(out=psB(pi)[:, tau * 128:(tau + 1) * 128], in_=ZY[:, tau, f * 128:(f + 1) * 128], identity=identB[:])
                    return ins
                P.op("pe", f_tr, reads=[R_ZY, R_CONST], writes=[R_PS[pi]])
                copy_op(evac_eng(), ysT[:, f, :].rearrange("p (k t) -> p t k", t=8), psB(pi).rearrange("p (t k) -> p t k", t=8),
                        [R_PS[pi]], [R_SPOOL])
            gluv = WS[wg][:, 0:2048].rearrange("p (a n) -> p a n", n=512)
            for fo in range(4):
                for n in range(2):
                    nr = slice(n * 512, (n + 1) * 512)
                    pi = next_ps()
                    sgi = (fo * 2 + n) % 2

                    def f_mm(e, fo=fo, nr=nr, pi=pi, gluv=gluv):
                        ins = None
                        for fi in range(4):
                            ins = e.matmul(psum[:, pi, :], lhsT=gluv[:, fi, fo * 128:(fo + 1) * 128], rhs=ysT[:, fi, nr],
                                           start=(fi == 0), stop=(fi == 3))
                        return ins
                    P.op("pe", f_mm, reads=[R_WS[wg], R_SPOOL], writes=[R_PS[pi]])
                    P.op("act", lambda e, fo=fo, pi=pi, sgi=sgi, l=l: e.activation(out=sig[:, sgi, :], in_=psum[:, pi, :], func=AF.Sigmoid,
                                                                                 bias=smallp[:, l, 12 + fo:13 + fo], scale=1.0),
                         reads=[R_PS[pi], R_SMALLP], writes=[R_SIG[sgi]])
                    P.op("dve", lambda e, fo=fo, nr=nr, sgi=sgi: e.tensor_tensor(out=sig[:, sgi, :], in0=sig[:, sgi, :], in1=ysT[:, fo, nr], op=ALU.mult),
                         reads=[R_SIG[sgi], R_SPOOL], writes=[R_SIG[sgi]])
                    P.op("pool", lambda e, fo=fo, nr=nr, sgi=sgi: e.tensor_tensor(out=ycat[:, 4 + fo, nr], in0=sig[:, sgi, :], in1=gate[:, 4 + fo, nr], op=ALU.mult),
                         reads=[R_SIG[sgi], R_GATE[n]], writes=[R_YCAT[n]])
            wov = wbf_out[l].rearrange("(a p) n -> p a n", p=128)
            for so in range(2):
                wi = load_ws(wov[:, :, so * 512:(so + 1) * 512], 8, [R_WBF[l]])
                for f4 in range(4):
                    fo = so * 4 + f4
                    for n in range(2):
                        nr = slice(n * 512, (n + 1) * 512)
                        pi = next_ps()

                        def f_mm(e, wi=wi, f4=f4, nr=nr, pi=pi):
                            ins = None
                            for kc in range(8):
                                ins = e.matmul(psum[:, pi, :], lhsT=wsv(wi)[:, kc, f4 * 128:(f4 + 1) * 128], rhs=ycat[:, kc, nr],
                                               start=(kc == 0), stop=(kc == 7))
                            return ins
                        P.op("pe", f_mm, reads=[R_WS[wi], R_YCAT[n]], writes=[R_PS[pi]])
                        P.op("dve", lambda e, fo=fo, nr=nr, pi=pi: e.tensor_tensor(out=xres[:, fo, nr], in0=xres[:, fo, nr], in1=psum[:, pi, :], op=ALU.add),
                             reads=[R_PS[pi], R_XRES], writes=[R_XRES])

        for n in range(2):
            nr = rmsnorm_stats(n)
            for f in range(8):
                P.op("dve", lambda e, f=f, nr=nr: e.scalar_tensor_tensor(
                    out=xres[:, f, nr], in0=xres[:, f, nr], scalar=finalg[:, f:f + 1], in1=rs[:], op0=ALU.mult, op1=ALU.mult),
                    reads=[R_XRES, R_RS, R_SMALLP], writes=[R_XRES])
        for tt in range(8):
            si = xs_rr[0]
            xs_rr[0] ^= 1
            for fh in range(2):
                pi = next_ps()

                def f_ot(e, tt=tt, fh=fh, pi=pi):
                    ins = None
                    for f4 in range(4):
                        f = fh * 4 + f4
                        ins = e.transpose(out=psum[:, pi, f4 * 128:(f4 + 1) * 128], in_=xres[:, f, tt * 128:(tt + 1) * 128], identity=identF[:])
                    return ins
                P.op("pe", f_ot, reads=[R_XRES, R_CONST], writes=[R_PS[pi]])
                copy_op(evac_eng(), xs[:, si, fh * 512:(fh + 1) * 512], psum[:, pi, :], [R_PS[pi]], [R_XS[si]])
            P.op("sp", lambda e, si=si, tt=tt, seq=seq, t0=t0: e.dma_start(out=out_d[seq, t0 + tt * 128:t0 + (tt + 1) * 128, :], in_=xs[:, si, :]),
                 reads=[R_XS[si]], writes=[R_OUT], dma="d_out")

    P.op("sp", None, reads=[R_OUT], sig=False)
    sems = {k: es.enter_context(nc.semaphore(k)) for k in P.cnt}
    with nc.Block() as block:
        P.emit(nc, block, sems)
    es.close()
    return nc


def prep_inputs(inp):
    f = np.float32
    shared = {}
    shared["w_in"] = np.ascontiguousarray(inp["w_in"], dtype=f)
    shared["w_out"] = np.ascontiguousarray(inp["w_out"], dtype=f)
    shared["glu_w"] = np.ascontiguousarray(inp["glu_w"], dtype=f)
    shared["pool_w"] = np.ascontiguousarray(np.transpose(inp["pool_w"], (0, 2, 1, 3)), dtype=f)
    ng = np.transpose(np.asarray(inp["norm_g"], f).reshape(DEPTH, 8, 128), (2, 0, 1))
    psc = np.transpose(np.asarray(inp["pool_scale"], f).reshape(DEPTH, 4, 128), (2, 0, 1))
    gb = np.transpose(np.asarray(inp["glu_b"], f).reshape(DEPTH, 4, 128), (2, 0, 1))
    shared["smallp"] = np.ascontiguousarray(np.concatenate([ng, psc, gb], axis=2), dtype=f)
    shared["finalg"] = np.ascontiguousarray(np.asarray(inp["final_g"], f).reshape(8, 128).T)

    def dup(a):
        t = np.transpose(np.asarray(a, f), (0, 2, 1))
        return np.concatenate([t, t], axis=1)
    are = dup(inp["a_re"])
    aim = dup(inp["a_im"])
    ldt = np.broadcast_to(np.asarray(inp["log_dt"], f)[:, None, :], (DEPTH, 128, G))
    dv = np.asarray(inp["d_skip"], f).reshape(DEPTH, G, 16)
    dvec = np.broadcast_to(np.transpose(dv, (0, 2, 1))[:, None, :, :], (DEPTH, 8, 16, G)).reshape(DEPTH, 128, G)
    shared["ssm_small"] = np.ascontiguousarray(np.stack([are, aim, ldt, dvec], axis=2), dtype=f)

    def dupb(a):
        t = np.transpose(np.asarray(a, f), (0, 2, 1, 3)).reshape(DEPTH, 64, G * 16)
        return np.concatenate([t, t], axis=1)

    def dupc(a):
        t = np.transpose(np.asarray(a, f), (0, 3, 1, 2)).reshape(DEPTH, 64, G * 16)
        return np.concatenate([t, t], axis=1)
    shared["ssm_bc"] = np.ascontiguousarray(
        np.stack([dupb(inp["b_re"]), dupb(inp["b_im"]), dupc(inp["c_re"]), dupc(inp["c_im"])], axis=2), dtype=f)
    x = np.ascontiguousarray(inp["x"], dtype=f)
    maps = []
    for c in range(NCORES):
        m = dict(shared)
        m["x"] = x[c * NSEQ:(c + 1) * NSEQ]
        maps.append(m)
    return maps


def kernel(**inputs):
    maps = prep_inputs(inputs)
    nc = build_program()
    res = run_bass_kernel_spmd(nc, maps, core_ids=list(range(NCORES)))
    out = np.concatenate([np.asarray(r["out"], dtype=np.float32) for r in res.results], axis=0)
    return out
```

```python
import math
from contextlib import ExitStack

import numpy as np
import concourse.bass as bass
import concourse.mybir as mybir
from concourse.bass_utils import run_bass_kernel_spmd

F32 = mybir.dt.float32
BF16 = mybir.dt.bfloat16
I32 = mybir.dt.int32
AF = mybir.ActivationFunctionType
ALU = mybir.AluOpType

NCORES = 8
DEPTH = 4
D = 1024
SEQ = 2048
NSEQ = 2
ST = 1024
NK = ST // 8
G = 32
TWO_PI = float(2.0 * math.pi)
INV_2PI = float(1.0 / (2.0 * math.pi))
HALF_PI = float(math.pi / 2.0)
EPS = 1e-5
POOL_WINDOWS = (2, 4, 8, 16)


class Res:
    __slots__ = ("name", "w", "r")

    def __init__(self, name):
        self.name = name
        self.w = {}
        self.r = {}


class Prog:
    ENG = ("pe", "act", "dve", "pool", "sp")

    def __init__(self):
        self.ops = {e: [] for e in self.ENG}
        self.cnt = {}
        self.seen = {e: {} for e in self.ENG}

    def op(self, eng, fn, reads=(), writes=(), dma=None, sig=True, acc=()):
        need = {}
        for r in acc:
            for k, v in r.r.items():
                if need.get(k, 0) < v:
                    need[k] = v
        for r in reads:
            for k, v in r.w.items():
                if need.get(k, 0) < v:
                    need[k] = v
        for r in writes:
            for k, v in r.w.items():
                if need.get(k, 0) < v:
                    need[k] = v
            for k, v in r.r.items():
                if need.get(k, 0) < v:
                    need[k] = v
        waits = []
        seen = self.seen[eng]
        for k, v in need.items():
            if eng == "pe" and k == "pe":
                continue
            if seen.get(k, 0) >= v:
                continue
            seen[k] = v
            waits.append((k, v))
        tok = None
        inc = 1
        if sig:
            key = dma if dma is not None else eng
            inc = 16 if dma is not None else 1
            self.cnt[key] = self.cnt.get(key, 0) + inc
            tok = (key, self.cnt[key])
            for r in reads:
                if r.r.get(key, 0) < tok[1]:
                    r.r[key] = tok[1]
            for r in writes:
                r.w = {key: tok[1]}
            for r in acc:
                if r.w.get(key, 0) < tok[1]:
                    r.w[key] = tok[1]
        self.ops[eng].append((waits, fn, tok, inc))
        return tok

    def emit(self, nc, block, sems):
        def mk(name):
            def body(e):
                for waits, fn, tok, inc in self.ops[name]:
                    for k, v in waits:
                        e.wait_ge(sems[k], v)
                    if fn is None:
                        continue
                    ins = fn(e)
                    if tok is not None:
                        ins.then_inc(sems[tok[0]], inc)
            return body

        block.tensor(mk("pe"))
        block.scalar(mk("act"))
        block.vector(mk("dve"))
        block.gpsimd(mk("pool"))
        block.sync(mk("sp"))


def build_program(debug=False, n_layers=DEPTH, n_sub=4):
    nc = bass.Bass("TRN2", target_bir_lowering=False)
    P = Prog()
    es = ExitStack()

    def dram_in(name, shape, dt=F32):
        return nc.dram_tensor(name, list(shape), dt, kind="ExternalInput").ap()

    def dram_out(name, shape, dt=F32):
        return nc.dram_tensor(name, list(shape), dt, kind="ExternalOutput").ap()

    def dram_scr(name, shape, dt):
        kind = "ExternalOutput" if debug else "Internal"
        return nc.dram_tensor(name, list(shape), dt, kind=kind).ap()

    def sb(name, shape, dt, stack=None):
        return (stack or es).enter_context(nc.sbuf_tensor(name, list(shape), dt))

    x_d = dram_in("x", [NSEQ, SEQ, D])
    w_in_d = dram_in("w_in", [DEPTH, D, 2 * D])
    w_out_d = dram_in("w_out", [DEPTH, D, D])
    glu_w_d = dram_in("glu_w", [DEPTH, 512, 512])
    pool_w_d = dram_in("pool_w", [DEPTH, 128, 4, 128])
    smallp_d = dram_in("smallp", [128, DEPTH, 16])
    finalg_d = dram_in("finalg", [128, 8])
    ssm_small_d = dram_in("ssm_small", [DEPTH, 128, 4, 32])
    ssm_bc_d = dram_in("ssm_bc", [DEPTH, 128, 4, 512])
    out_d = dram_out("out", [NSEQ, SEQ, D])

    ssmW_d = dram_scr("ssmW", [DEPTH, 8, 128, 5 * 4 * 128], BF16)
    tabs_d = dram_scr("tabs", [DEPTH, 8, 128, 2, 4, 256], F32)
    poolW_d = dram_scr("poolW", [DEPTH, 128, 4 * 2 * 128], BF16)
    R_SSMW = [Res(f"ssmW{l}") for l in range(DEPTH)]
    R_TABS = [Res(f"tabs{l}") for l in range(DEPTH)]
    R_POOLW = [Res(f"poolW{l}") for l in range(DEPTH)]
    R_OUT = Res("out")

    identF = sb("identF", [128, 128], F32)
    identB = sb("identB", [128, 128], BF16)
    onesB = sb("onesB", [128, 128], BF16)
    Rb_all = sb("Rb_all", [128, DEPTH, 32], F32)
    smallp = sb("smallp_sb", [128, DEPTH, 16], F32)
    finalg = sb("finalg_sb", [128, 8], F32)
    psum = es.enter_context(nc.psum_tensor("psum", [128, 8, 512], F32))
    R_CONST = Res("const")
    R_RB = Res("Rb")
    R_SMALLP = Res("smallp")
    R_PS = [Res(f"ps{i}") for i in range(8)]
    ps_rr = [0]

    def next_ps():
        i = ps_rr[0]
        ps_rr[0] = (i + 1) % 8
        return i

    P.op("pool", lambda e: e.memset(identF[:], 1.0), writes=[R_CONST])
    P.op("pool", lambda e: e.affine_select(out=identF[:], in_=identF[:], pattern=[[-1, 128]],
                                           compare_op=ALU.is_equal, fill=0.0, base=0, channel_multiplier=1),
         reads=[R_CONST], writes=[R_CONST])
    P.op("pool", lambda e: e.tensor_copy(out=identB[:], in_=identF[:]), reads=[R_CONST], writes=[R_CONST])
    P.op("pool", lambda e: e.memset(onesB[:], 1.0 / 1024.0), writes=[R_CONST])
    P.op("sp", lambda e: e.dma_start(out=smallp[:], in_=smallp_d[:]), writes=[R_SMALLP], dma="d_small")
    P.op("sp", lambda e: e.dma_start(out=finalg[:], in_=finalg_d[:]), writes=[R_SMALLP], dma="d_small")

    with ExitStack() as ps_:
        def pb(name, shape, dt):
            return sb("pl_" + name, shape, dt, ps_)
        mask = pb("mask", [128, 128], F32)
        iotaKi = pb("iotaKi", [128, 256], I32)
        iotaK = pb("iotaK", [128, 256], F32)
        JTi = pb("JTi", [128, 7, 32], I32)
        JT = pb("JT", [128, 7, 32], F32)
        sm = pb("sm", [128, 4, 32], F32)
        bc = pb("bc", [128, 4, 512], F32)
        pw = pb("pw", [128, 4, 128], F32)
        PWo = pb("PWo", [128, 4, 2, 128], BF16)
        dtt = pb("dtt", [128, 32], F32)
        ard = pb("ard", [128, 32], F32)
        ang = pb("ang", [128, 32], F32)
        ARJ = pb("ARJ", [128, 7, 32], F32)
        ANJ = pb("ANJ", [128, 7, 32], F32)
        MAGP = pb("MAGP", [128, 7, 32], F32)
        MAGN = pb("MAGN", [128, 7, 32], F32)
        NIj = pb("NIj", [128, 7, 32], I32)
        RRj = pb("RRj", [128, 7, 32], F32)
        SN = pb("SN", [128, 7, 32], F32)
        CS = pb("CS", [128, 7, 32], F32)
        EXre = pb("EXre", [128, 32, 8], F32)
        EXim = pb("EXim", [128, 32, 8], F32)
        EYre = pb("EYre", [128, 32, 8], F32)
        EYim = pb("EYim", [128, 32, 8], F32)
        s1 = pb("s1", [128, 32], F32)
        s2 = pb("s2", [128, 32], F32)
        s3 = pb("s3", [128, 32], F32)
        s4 = pb("s4", [128, 32], F32)
        fre = pb("fre", [128, 32], F32)
        fim = pb("fim", [128, 32], F32)
        Rt = pb("Rt", [128, 32], F32)
        nRt = pb("nRt", [128, 32], F32)
        TH = pb("TH", [128, 32], F32)
        NI8 = pb("NI8", [128, 32], I32)
        bbre = pb("bbre", [128, 32, 16], F32)
        bbim = pb("bbim", [128, 32, 16], F32)
        u1 = pb("u1", [128, 32, 16], F32)
        T1 = pb("T1", [128, 32, 8, 16], F32)
        T2 = pb("T2", [128, 32, 8, 16], F32)
        XBm = pb("XBm", [128, 32, 8, 16], F32)
        Gm = pb("Gm", [128, 32, 8, 16], F32)
        YA = pb("YA", [128, 32, 8, 16], F32)
        YB = pb("YB", [128, 32, 8, 16], F32)
        W5 = pb("W5", [128, 8, 5, 4, 128], BF16)
        tmpT = pb("tmpT", [128, 4, 128], F32)
        def v256(t):
            return t[:].rearrange("p g t c -> p (g t c)").rearrange("p (a k) -> p a k", k=256)
        ANGb = v256(T1)
        NIb = v256(T2).bitcast(I32)
        RRb = v256(YA)
        COSh = v256(YB)
        SINh = pb("SINh", [128, 16, 256], F32)

        R_PC = Res("pl_const")
        R_IN = Res("pl_in")
        R_A = Res("pl_a")
        R_E = Res("pl_E")
        R_X = Res("pl_X")
        R_T = Res("pl_T")
        R_W5 = Res("pl_W5")
        R_TMPT = Res("pl_tmpT")
        R_ANG = Res("pl_ang")
        R_TAB = Res("pl_tab")
        R_PW = Res("pl_pw")

        P.op("pool", lambda e: e.memset(mask[:], 1.0), writes=[R_PC])
        P.op("pool", lambda e: e.affine_select(out=mask[:], in_=mask[:], pattern=[[16, 8], [0, 16]],
                                               compare_op=ALU.is_ge, fill=0.0, base=15, channel_multiplier=-1),
             reads=[R_PC], writes=[R_PC])
        P.op("pool", lambda e: e.iota(iotaKi[:], pattern=[[1, 256]], base=0, channel_multiplier=0), writes=[R_PC])
        P.op("pool", lambda e: e.tensor_copy(out=iotaK[:], in_=iotaKi[:]), reads=[R_PC], writes=[R_PC])
        P.op("pool", lambda e: e.iota(JTi[:], pattern=[[-1, 7], [0, 32]], base=7, channel_multiplier=0), writes=[R_PC])
        P.op("pool", lambda e: e.tensor_copy(out=JT[:], in_=JTi[:]), reads=[R_PC], writes=[R_PC])

        def bc3(ap32):
            return ap32.unsqueeze(1).to_broadcast([128, 7, 32])

        def bcc(ap32):
            return ap32.unsqueeze(2).to_broadcast([128, 32, 16])

        def bE(apE, lo, hi):
            return apE[lo:hi].unsqueeze(3).to_broadcast([hi - lo, 32, 8, 16])

        def bB(apB, lo, hi):
            return apB[lo:hi].unsqueeze(2).to_broadcast([hi - lo, 32, 8, 16])

        def bR(apR, lo, hi):
            return apR[lo:hi].rearrange("p (a g) -> p a g", g=4).unsqueeze(3).to_broadcast([hi - lo, 8, 4, 128])

        for l in range(n_layers):
            P.op("sp", lambda e, l=l: e.dma_start(out=sm[:], in_=ssm_small_d[l]), writes=[R_IN], dma="d_plin")
            P.op("sp", lambda e, l=l: e.dma_start(out=bc[:], in_=ssm_bc_d[l]), writes=[R_IN], dma="d_plin")
            P.op("sp", lambda e, l=l: e.dma_start(out=pw[:], in_=pool_w_d[l]), writes=[R_PW], dma="d_plpw")
            are, aim, ldt, dvec = sm[:, 0, :], sm[:, 1, :], sm[:, 2, :], sm[:, 3, :]
            bre = bc[:, 0, :].rearrange("p (g c) -> p g c", c=16)
            bim = bc[:, 1, :].rearrange("p (g c) -> p g c", c=16)
            cre = bc[:, 2, :].rearrange("p (g c) -> p g c", c=16)
            cim = bc[:, 3, :].rearrange("p (g c) -> p g c", c=16)

            for g4 in range(4):
                P.op("pool", lambda e, g4=g4: e.tensor_scalar(out=PWo[:, g4, 0, :], in0=pw[:, g4, :],
                                                            scalar1=1.0 / POOL_WINDOWS[g4], scalar2=None, op0=ALU.mult),
                     reads=[R_PW], writes=[R_PW])
            P.op("pool", lambda e: e.tensor_scalar(out=PWo[:, :, 1, :], in0=pw[:, :, :], scalar1=-1.0, scalar2=None, op0=ALU.mult),
                 reads=[R_PW], writes=[R_PW])
            P.op("sp", lambda e, l=l: e.dma_start(out=poolW_d[l], in_=PWo[:].rearrange("p a b c -> p (a b c)")),
                 reads=[R_PW], writes=[R_POOLW[l]], dma=f"d_plpo{l}")

            P.op("act", lambda e: e.activation(out=dtt[:], in_=ldt, func=AF.Exp), reads=[R_IN], writes=[R_A])
            P.op("dve", lambda e: e.tensor_tensor(out=ard[:], in0=are, in1=dtt[:], op=ALU.mult), reads=[R_IN, R_A], writes=[R_A])
            P.op("dve", lambda e: e.tensor_tensor(out=ang[:], in0=aim, in1=dtt[:], op=ALU.mult), reads=[R_IN, R_A], writes=[R_A])
            P.op("dve", lambda e: e.tensor_tensor(out=ARJ[:], in0=JT[:], in1=bc3(ard[:]), op=ALU.mult), reads=[R_PC, R_A], writes=[R_A])
            P.op("dve", lambda e: e.tensor_tensor(out=ANJ[:], in0=JT[:], in1=bc3(ang[:]), op=ALU.mult), reads=[R_PC, R_A], writes=[R_A])
            P.op("act", lambda e: e.activation(out=MAGP[:], in_=ARJ[:], func=AF.Exp), reads=[R_A], writes=[R_A])
            P.op("act", lambda e: e.activation(out=MAGN[:], in_=ARJ[:], func=AF.Exp, scale=-1.0), reads=[R_A], writes=[R_A])
            P.op("act", lambda e: e.activation(out=NIj[:], in_=ANJ[:], func=AF.Copy, scale=INV_2PI), reads=[R_A], writes=[R_A])
            P.op("dve", lambda e: e.scalar_tensor_tensor(out=RRj[:], in0=NIj[:], scalar=-TWO_PI, in1=ANJ[:], op0=ALU.mult, op1=ALU.add),
                 reads=[R_A], writes=[R_A])
            P.op("act", lambda e: e.activation(out=SN[:], in_=RRj[:], func=AF.Sin), reads=[R_A], writes=[R_A])
            P.op("dve", lambda e: e.tensor_scalar(out=RRj[:], in0=ANJ[:], scalar1=HALF_PI, scalar2=None, op0=ALU.add), reads=[R_A], writes=[R_A])
            P.op("act", lambda e: e.activation(out=NIj[:], in_=RRj[:], func=AF.Copy, scale=INV_2PI), reads=[R_A], writes=[R_A])
            P.op("dve", lambda e: e.scalar_tensor_tensor(out=RRj[:], in0=NIj[:], scalar=-TWO_PI, in1=RRj[:], op0=ALU.mult, op1=ALU.add),
                 reads=[R_A], writes=[R_A])
            P.op("act", lambda e: e.activation(out=CS[:], in_=RRj[:], func=AF.Sin), reads=[R_A], writes=[R_A])
            def Ev(t):
                return t[:].rearrange("p g t -> p t g")[:, 0:7, :]
            P.op("dve", lambda e: e.tensor_tensor(out=Ev(EXre), in0=MAGP[:], in1=CS[:], op=ALU.mult), reads=[R_A], writes=[R_E])
            P.op("dve", lambda e: e.tensor_tensor(out=Ev(EXim), in0=MAGP[:], in1=SN[:], op=ALU.mult), reads=[R_A], writes=[R_E])
            P.op("dve", lambda e: e.tensor_tensor(out=Ev(EYre), in0=MAGN[:], in1=CS[:], op=ALU.mult), reads=[R_A], writes=[R_E])
            P.op("dve", lambda e: e.scalar_tensor_tensor(out=Ev(EYim), in0=MAGN[:], scalar=-1.0, in1=SN[:], op0=ALU.mult, op1=ALU.mult),
                 reads=[R_A], writes=[R_E])
            P.op("dve", lambda e: e.memset(EXre[:, :, 7:8], 1.0), writes=[R_E])
            P.op("dve", lambda e: e.memset(EXim[:, :, 7:8], 0.0), writes=[R_E])
            P.op("dve", lambda e: e.memset(EYre[:, :, 7:8], 1.0), writes=[R_E])
            P.op("dve", lambda e: e.memset(EYim[:, :, 7:8], 0.0), writes=[R_E])
            lre, lim = EXre[:, :, 6], EXim[:, :, 6]
            P.op("dve", lambda e: e.tensor_scalar(out=s1[:], in0=lre, scalar1=-1.0, scalar2=None, op0=ALU.add), reads=[R_E], writes=[R_A])
            P.op("dve", lambda e: e.tensor_tensor(out=s2[:], in0=are, in1=are, op=ALU.mult), reads=[R_IN], writes=[R_A])
            P.op("dve", lambda e: e.tensor_tensor(out=s3[:], in0=aim, in1=aim, op=ALU.mult), reads=[R_IN], writes=[R_A])
            P.op("dve", lambda e: e.tensor_tensor(out=s2[:], in0=s2[:], in1=s3[:], op=ALU.add), reads=[R_A], writes=[R_A])
            P.op("dve", lambda e: e.reciprocal(out=s2[:], in_=s2[:]), reads=[R_A], writes=[R_A])
            P.op("dve", lambda e: e.tensor_tensor(out=s3[:], in0=s1[:], in1=are, op=ALU.mult), reads=[R_A, R_IN], writes=[R_A])
            P.op("dve", lambda e: e.tensor_tensor(out=s4[:], in0=lim, in1=aim, op=ALU.mult), reads=[R_E, R_IN], writes=[R_A])
            P.op("dve", lambda e: e.tensor_tensor(out=s3[:], in0=s3[:], in1=s4[:], op=ALU.add), reads=[R_A], writes=[R_A])
            P.op("dve", lambda e: e.tensor_tensor(out=fre[:], in0=s3[:], in1=s2[:], op=ALU.mult), reads=[R_A], writes=[R_A])
            P.op("dve", lambda e: e.tensor_tensor(out=s3[:], in0=lim, in1=are, op=ALU.mult), reads=[R_E, R_IN], writes=[R_A])
            P.op("dve", lambda e: e.tensor_tensor(out=s4[:], in0=s1[:], in1=aim, op=ALU.mult), reads=[R_A, R_IN], writes=[R_A])
            P.op("dve", lambda e: e.tensor_tensor(out=s3[:], in0=s3[:], in1=s4[:], op=ALU.subtract), reads=[R_A], writes=[R_A])
            P.op("dve", lambda e: e.tensor_tensor(out=fim[:], in0=s3[:], in1=s2[:], op=ALU.mult), reads=[R_A], writes=[R_A])
            P.op("dve", lambda e: e.tensor_tensor(out=bbre[:], in0=bre, in1=bcc(fre[:]), op=ALU.mult), reads=[R_A, R_IN], writes=[R_A])
            P.op("dve", lambda e: e.tensor_tensor(out=u1[:], in0=bim, in1=bcc(fim[:]), op=ALU.mult), reads=[R_A, R_IN], writes=[R_A])
            P.op("dve", lambda e: e.tensor_tensor(out=bbre[:], in0=bbre[:], in1=u1[:], op=ALU.subtract), reads=[R_A], writes=[R_A])
            P.op("dve", lambda e: e.tensor_tensor(out=bbim[:], in0=bim, in1=bcc(fre[:]), op=ALU.mult), reads=[R_A, R_IN], writes=[R_A])
            P.op("dve", lambda e: e.tensor_tensor(out=u1[:], in0=bre, in1=bcc(fim[:]), op=ALU.mult), reads=[R_A, R_IN], writes=[R_A])
            P.op("dve", lambda e: e.tensor_tensor(out=bbim[:], in0=bbim[:], in1=u1[:], op=ALU.add), reads=[R_A], writes=[R_A])
            P.op("act", lambda e: e.activation(out=Rt[:], in_=ard[:], func=AF.Exp, scale=8.0), reads=[R_A], writes=[R_A])
            P.op("act", lambda e: e.activation(out=nRt[:], in_=Rt[:], func=AF.Copy, scale=-1.0), reads=[R_A], writes=[R_A])
            P.op("act", lambda e, l=l: e.activation(out=Rb_all[:, l, :], in_=Rt[:], func=AF.Copy), reads=[R_A], writes=[R_RB])
            P.op("dve", lambda e: e.tensor_scalar(out=s1[:], in0=ang[:], scalar1=8.0, scalar2=None, op0=ALU.mult), reads=[R_A], writes=[R_A])
            P.op("act", lambda e: e.activation(out=NI8[:], in_=s1[:], func=AF.Copy, scale=INV_2PI), reads=[R_A], writes=[R_A])
            P.op("dve", lambda e: e.scalar_tensor_tensor(out=TH[:], in0=NI8[:], scalar=-TWO_PI, in1=s1[:], op0=ALU.mult, op1=ALU.add),
                 reads=[R_A], writes=[R_A])

            P.op("dve", lambda e: e.tensor_tensor(out=T1[0:64], in0=bE(EXre, 0, 64), in1=bB(bbre, 0, 64), op=ALU.mult), reads=[R_E, R_A], writes=[R_T])
            P.op("dve", lambda e: e.tensor_tensor(out=T2[0:64], in0=bE(EXim, 0, 64), in1=bB(bbim, 0, 64), op=ALU.mult), reads=[R_E, R_A], writes=[R_T])
            P.op("dve", lambda e: e.tensor_tensor(out=XBm[0:64], in0=T1[0:64], in1=T2[0:64], op=ALU.subtract), reads=[R_T], writes=[R_X])
            P.op("dve", lambda e: e.tensor_tensor(out=T1[64:128], in0=bE(EXre, 64, 128), in1=bB(bbim, 64, 128), op=ALU.mult), reads=[R_E, R_A], writes=[R_T])
            P.op("dve", lambda e: e.tensor_tensor(out=T2[64:128], in0=bE(EXim, 64, 128), in1=bB(bbre, 64, 128), op=ALU.mult), reads=[R_E, R_A], writes=[R_T])
            P.op("dve", lambda e: e.tensor_tensor(out=XBm[64:128], in0=T1[64:128], in1=T2[64:128], op=ALU.add), reads=[R_T], writes=[R_X])
            P.op("dve", lambda e: e.tensor_tensor(out=T1[:], in0=bE(EYre, 0, 128), in1=bB(cre, 0, 128), op=ALU.mult), reads=[R_E, R_IN, R_X], writes=[R_T])
            P.op("dve", lambda e: e.tensor_tensor(out=T2[:], in0=bE(EYim, 0, 128), in1=bB(cim, 0, 128), op=ALU.mult), reads=[R_E, R_IN], writes=[R_T])
            P.op("dve", lambda e: e.tensor_tensor(out=YA[:], in0=T1[:], in1=T2[:], op=ALU.subtract), reads=[R_T], writes=[R_T])
            P.op("dve", lambda e: e.tensor_tensor(out=T1[:], in0=bE(EYre, 0, 128), in1=bB(cim, 0, 128), op=ALU.mult), reads=[R_E, R_IN], writes=[R_T])
            P.op("dve", lambda e: e.tensor_tensor(out=T2[:], in0=bE(EYim, 0, 128), in1=bB(cre, 0, 128), op=ALU.mult), reads=[R_E, R_IN], writes=[R_T])
            P.op("dve", lambda e: e.scalar_tensor_tensor(out=YB[:], in0=T1[:], scalar=-1.0, in1=T2[:], op0=ALU.mult, op1=ALU.subtract),
                 reads=[R_T], writes=[R_T])
            P.op("pool", lambda e: e.tensor_copy(out=Gm[0:64], in_=YA[0:64]), reads=[R_T], writes=[R_X])
            P.op("pool", lambda e: e.tensor_copy(out=Gm[64:128], in_=YB[64:128]), reads=[R_T], writes=[R_X])
            def w5k(kind, lo, hi):
                return W5[lo:hi, :, kind, :, :]

            def flat(t, lo, hi):
                return t[lo:hi].rearrange("p (a g) t c -> p a g (t c)", g=4)
            P.op("dve", lambda e: e.tensor_tensor(out=w5k(3, 0, 64), in0=flat(YA, 0, 64), in1=bR(Rt, 0, 64), op=ALU.mult), reads=[R_T, R_A], writes=[R_W5])
            P.op("dve", lambda e: e.tensor_tensor(out=w5k(3, 64, 128), in0=flat(YB, 64, 128), in1=bR(Rt, 64, 128), op=ALU.mult), reads=[R_T, R_A], writes=[R_W5])
            P.op("dve", lambda e: e.tensor_tensor(out=w5k(4, 0, 64), in0=flat(YB, 0, 64), in1=bR(Rt, 0, 64), op=ALU.mult), reads=[R_T, R_A], writes=[R_W5])
            P.op("dve", lambda e: e.tensor_tensor(out=w5k(4, 64, 128), in0=flat(YA, 64, 128), in1=bR(nRt, 64, 128), op=ALU.mult), reads=[R_T, R_A], writes=[R_W5])

            for gb in range(8):
                pi = next_ps()

                def f_toep(e, gb=gb, pi=pi):
                    ins = None
                    for g4 in range(4):
                        g = gb * 4 + g4
                        ins = e.matmul(psum[:, pi, g4 * 128:(g4 + 1) * 128],
                                       lhsT=XBm[:, g].rearrange("p t c -> p (t c)"),
                                       rhs=Gm[:, g].rearrange("p t c -> p (t c)"), start=True, stop=True)
                    return ins
                P.op("pe", f_toep, reads=[R_X], writes=[R_PS[pi]])
                P.op("dve", lambda e, pi=pi: e.tensor_tensor(out=tmpT[:], in0=psum[:, pi, :].rearrange("p (g n) -> p g n", g=4),
                                                            in1=mask[:].unsqueeze(1).to_broadcast([128, 4, 128]), op=ALU.mult),
                     reads=[R_PS[pi], R_PC], writes=[R_TMPT])
                for g4 in range(4):
                    P.op("dve", lambda e, gb=gb, g4=g4: e.scalar_tensor_tensor(
                        out=W5[:, gb, 0, g4, :], in0=identF[:], scalar=sm[:, 3, gb * 4 + g4:gb * 4 + g4 + 1], in1=tmpT[:, g4, :],
                        op0=ALU.mult, op1=ALU.add), reads=[R_TMPT, R_CONST, R_IN], writes=[R_W5])
                pi2 = next_ps()

                def f_tr(e, gb=gb, pi2=pi2):
                    ins = None
                    for g4 in range(4):
                        g = gb * 4 + g4
                        ins = e.transpose(out=psum[:, pi2, g4 * 128:(g4 + 1) * 128],
                                          in_=XBm[:, g].rearrange("p t c -> p (t c)"), identity=identF[:])
                    return ins
                P.op("pe", f_tr, reads=[R_X, R_CONST], writes=[R_PS[pi2]])
                pv = psum[:, pi2, :].rearrange("p (g n) -> p g n", g=4)
                P.op("act", lambda e, gb=gb, pv=pv: e.activation(out=W5[:, gb, 1, :, :], in_=pv, func=AF.Copy),
                     reads=[R_PS[pi2]], writes=[R_W5])
                P.op("act", lambda e, gb=gb, pv=pv: e.activation(out=W5[:, gb, 2, :, 0:64], in_=pv[:, :, 64:128], func=AF.Copy),
                     reads=[R_PS[pi2]], writes=[R_W5])
                P.op("act", lambda e, gb=gb, pv=pv: e.activation(out=W5[:, gb, 2, :, 64:128], in_=pv[:, :, 0:64], func=AF.Copy, scale=-1.0),
                     reads=[R_PS[pi2]], writes=[R_W5])
            P.op("sp", lambda e, l=l: e.dma_start(out=ssmW_d[l].rearrange("a p n -> p a n"),
                                                  in_=W5[:].rearrange("p a k g n -> p a (k g n)")),
                 reads=[R_W5], writes=[R_SSMW[l]], dma=f"d_plssm{l}")

            tv = tabs_d[l].rearrange("a p two g k -> p a two g k")
            for hf in range(2):
                gs = slice(hf * 16, hf * 16 + 16)
                P.op("dve", lambda e, gs=gs: e.tensor_tensor(out=ANGb, in0=TH[:, gs].unsqueeze(2).to_broadcast([128, 16, 256]),
                                                            in1=iotaK[:].unsqueeze(1).to_broadcast([128, 16, 256]), op=ALU.mult),
                     reads=[R_A, R_PC], writes=[R_T])
                P.op("act", lambda e: e.activation(out=NIb, in_=ANGb, func=AF.Copy, scale=INV_2PI), reads=[R_T], writes=[R_T])
                P.op("dve", lambda e: e.scalar_tensor_tensor(out=RRb, in0=NIb, scalar=-TWO_PI, in1=ANGb, op0=ALU.mult, op1=ALU.add),
                     reads=[R_T], writes=[R_T])
                P.op("act", lambda e: e.activation(out=SINh[:], in_=RRb, func=AF.Sin), reads=[R_T], writes=[R_TAB])
                P.op("dve", lambda e: e.tensor_scalar(out=ANGb, in0=ANGb, scalar1=HALF_PI, scalar2=None, op0=ALU.add), reads=[R_T], writes=[R_T])
                P.op("act", lambda e: e.activation(out=NIb, in_=ANGb, func=AF.Copy, scale=INV_2PI), reads=[R_T], writes=[R_T])
                P.op("dve", lambda e: e.scalar_tensor_tensor(out=RRb, in0=NIb, scalar=-TWO_PI, in1=ANGb, op0=ALU.mult, op1=ALU.add),
                     reads=[R_T], writes=[R_T])
                P.op("act", lambda e: e.activation(out=COSh, in_=RRb, func=AF.Sin), reads=[R_T], writes=[R_T])
                P.op("sp", lambda e, tv=tv, hf=hf: e.dma_start(out=tv[:, hf * 4:hf * 4 + 4, 0], in_=COSh.rearrange("p (a g) k -> p a g k", g=4)),
                     reads=[R_T], writes=[R_TABS[l]], dma=f"d_pltab{l}")
                P.op("sp", lambda e, tv=tv, hf=hf: e.dma_start(out=tv[:, hf * 4:hf * 4 + 4, 1], in_=SINh[:].rearrange("p (a g) k -> p a g k", g=4)),
                     reads=[R_TAB], writes=[R_TABS[l]], dma=f"d_pltab{l}")

    wbf_in = dram_scr("wbf_in", [DEPTH, D, 2 * D], BF16)
    wbf_out = dram_scr("wbf_out", [DEPTH, D, D], BF16)
    wbf_glu = dram_scr("wbf_glu", [DEPTH, 512, 512], BF16)
    R_WBF = [Res(f"wbf{l}") for l in range(DEPTH)]
    for l in range(n_layers):
        for a in range(8):
            P.op("pool", lambda e, l=l, a=a: e.dma_start(out=wbf_in[l, a * 128:(a + 1) * 128, :], in_=w_in_d[l, a * 128:(a + 1) * 128, :]),
                 writes=[R_WBF[l]], dma=f"d_wbf{l}")
        for a in range(4):
            P.op("pool", lambda e, l=l, a=a: e.dma_start(out=wbf_out[l, a * 256:(a + 1) * 256, :], in_=w_out_d[l, a * 256:(a + 1) * 256, :]),
                 writes=[R_WBF[l]], dma=f"d_wbf{l}")
        P.op("pool", lambda e, l=l: e.dma_start(out=wbf_glu[l], in_=glu_w_d[l]), writes=[R_WBF[l]], dma=f"d_wbf{l}")

    if debug == "prologue":
        rb_d = dram_out("rb_dbg", [128, DEPTH, 32])
        P.op("sp", lambda e: e.dma_start(out=rb_d[:], in_=Rb_all[:]), reads=[R_RB], writes=[R_OUT], dma="d_out")
        fin = [R_OUT] + R_SSMW[:n_layers] + R_TABS[:n_layers] + R_POOLW[:n_layers] + R_WBF[:n_layers]
        P.op("sp", None, reads=fin, sig=False)
        sems = {k: es.enter_context(nc.semaphore(k)) for k in P.cnt}
        with nc.Block() as block:
            P.emit(nc, block, sems)
        es.close()
        return nc

    xres = sb("xres", [128, 8, ST], F32)
    h = sb("h", [128, 8, ST], BF16)
    gate = sb("gate", [128, 8, ST], BF16)
    ycat = sb("ycat", [128, 8, ST], BF16)
    upool = sb("upool", [128, 4, 16 + ST], BF16)
    spool = sb("spool", [128, 4, ST], BF16)
    ysT = spool
    xsq = spool[:].rearrange("p a n -> p (a n)").rearrange("p (a n) -> p a n", n=512)
    ZYf = sb("ZY", [128, 4096], BF16)
    ZY = ZYf[:].rearrange("p (t f) -> p t f", t=8)
    Zs = ZYf[:].rearrange("p (g t c) -> p g t c", g=32, t=8)
    U = sb("U", [128, 32, 128], BF16)
    rs = sb("rs", [128, 2, 512], F32)
    t1 = sb("t1", [128, 4, 128], F32)
    t2 = sb("t2", [128, 4, 128], F32)
    NQ = 3
    Q = [sb(f"Q{i}", [128, 4, 129], F32) for i in range(NQ)]
    Pc = [sb(f"Pc{i}", [128, 4, 128], BF16) for i in range(NQ)]
    Ps = [sb(f"Ps{i}", [128, 4, 128], BF16) for i in range(NQ)]
    sig = sb("sig", [128, 2, 512], BF16)
    xs = sb("xs", [128, 2, D], F32)
    pwk = xs[:].rearrange("p a n -> p (a n)").bitcast(BF16)[:, 0:3 * (16 + ST)].rearrange("p (a n) -> p a n", a=3)
    carry = sb("carry", [128, DEPTH, 32], F32)
    halo = sb("halo", [128, DEPTH, 4, 16], BF16)
    cfix = sb("cfix", [128, 4, 16], F32)
    epsT = sb("epsT", [128, 1], F32)
    NWS, NSW, NSTB = 4, 3, 3
    WS = [sb(f"WS{i}", [128, 8 * 512], BF16) for i in range(NWS)]
    SSw = [sb(f"SSw{i}", [128, 5, 4, 128], BF16) for i in range(NSW)]
    SSt = [sb(f"SSt{i}", [128, 2, 4, 128], F32) for i in range(NSTB)]
    R_WS = [Res(f"WS{i}") for i in range(NWS)]
    R_SW = [Res(f"SW{i}") for i in range(NSW)]
    R_STB = [Res(f"STB{i}") for i in range(NSTB)]
    R_ZY, R_U = Res("ZY"), Res("U")
    R_XRES = [Res("xres0"), Res("xres1")]
    R_RS = [Res("rs0"), Res("rs1")]
    R_H = [Res("h0"), Res("h1")]
    R_GATE = [Res("g0"), Res("g1")]
    R_YCAT = [Res("yc0"), Res("yc1")]
    R_UPOOL, R_SPOOL = Res("upool"), Res("spool")
    R_T1, R_T2 = Res("t1"), Res("t2")
    R_Q = [Res(f"Q{i}") for i in range(NQ)]
    R_PCS = [Res(f"PcPs{i}") for i in range(NQ)]
    R_SIG = [Res("sig0"), Res("sig1")]
    R_XS = [Res("xs0"), Res("xs1")]
    R_CARRY = [[Res(f"carry{l}_{gb}") for gb in range(8)] for l in range(DEPTH)]
    R_HALO = [Res(f"halo{l}") for l in range(DEPTH)]
    R_C2 = Res("const2")
    ws_rr, sw_rr, stb_rr, xs_rr, ev_rr = [0], [0], [0], [0], [0]

    def psB(pi):
        return psum[:, pi, :].bitcast(BF16)

    def evac_eng():
        ev_rr[0] ^= 1
        return "act" if ev_rr[0] else "dve"

    def copy_op(eng, out, in_, reads, acc):
        if eng == "act":
            P.op("act", lambda e: e.activation(out=out, in_=in_, func=AF.Copy), reads=reads, acc=acc)
        else:
            P.op(eng, lambda e: e.tensor_copy(out=out, in_=in_), reads=reads, acc=acc)

    P.op("pool", lambda e: e.memset(epsT[:], EPS), writes=[R_C2])
    P.op("pool", lambda e: e.memset(cfix[:], 1.0), writes=[R_C2])
    for f in range(4):
        w = POOL_WINDOWS[f]
        for t in range(w - 1):
            P.op("pool", lambda e, f=f, t=t, w=w: e.memset(cfix[:, f, t:t + 1], float(w) / float(t + 1)), writes=[R_C2])

    def load_ws(src_ap, nparts, reads):
        i = ws_rr[0]
        ws_rr[0] = (i + 1) % NWS
        dst = WS[i][:, 0:nparts * 512].rearrange("p (a n) -> p a n", n=512)
        P.op("sp", lambda e: e.dma_start(out=dst, in_=src_ap), reads=reads, writes=[R_WS[i]], dma=f"d_ws{i}")
        return i

    def wsv(i):
        return WS[i][:].rearrange("p (a n) -> p a n", n=512)

    def mm_group(pi, lhs_fn, rhs_fn, nk, reads):
        def f(e):
            ins = None
            for kc in range(nk):
                ins = e.matmul(psum[:, pi, :], lhsT=lhs_fn(kc), rhs=rhs_fn(kc), start=(kc == 0), stop=(kc == nk - 1))
            return ins
        P.op("pe", f, reads=reads, writes=[R_PS[pi]])

    def rmsnorm_stats(n):
        nr = slice(n * 512, (n + 1) * 512)
        P.op("act", lambda e: e.activation(out=xsq, in_=xres[:, :, nr], func=AF.Square), reads=[R_XRES[n]], writes=[R_SPOOL])
        pi = next_ps()
        mm_group(pi, lambda kc: onesB[:], lambda kc: xsq[:, kc, :], 8, [R_SPOOL, R_CONST])
        P.op("act", lambda e, pi=pi: e.activation(out=rs[:, n, :], in_=psum[:, pi, :], func=AF.Sqrt, bias=epsT[:], scale=1.0),
             reads=[R_PS[pi], R_C2], writes=[R_RS[n]])
        P.op("dve", lambda e: e.reciprocal(out=rs[:, n, :], in_=rs[:, n, :]), reads=[R_RS[n]], writes=[R_RS[n]])
        return nr

    for st in range(n_sub):
        seq, half = st // 2, st % 2
        t0 = half * ST
        if half == 0:
            P.op("pool", lambda e: e.memset(carry[:], 0.0), writes=[r for rl in R_CARRY for r in rl])
            P.op("pool", lambda e: e.memset(halo[:], 0.0), writes=R_HALO)
        for tt in range(8):
            si = xs_rr[0]
            xs_rr[0] ^= 1
            P.op("sp", lambda e, si=si, tt=tt, seq=seq, t0=t0: e.dma_start(out=xs[:, si, :], in_=x_d[seq, t0 + tt * 128:t0 + (tt + 1) * 128, :]),
                 writes=[R_XS[si]], dma=f"d_xs{si}")
            for fh in range(2):
                pi = next_ps()

                def f_xt(e, si=si, fh=fh, pi=pi):
                    ins = None
                    for f4 in range(4):
                        f = fh * 4 + f4
                        ins = e.transpose(out=psum[:, pi, f4 * 128:(f4 + 1) * 128], in_=xs[:, si, f * 128:(f + 1) * 128], identity=identF[:])
                    return ins
                P.op("pe", f_xt, reads=[R_XS[si], R_CONST], writes=[R_PS[pi]])
                copy_op(evac_eng(), xres[:, fh * 4:(fh + 1) * 4, tt * 128:(tt + 1) * 128],
                        psum[:, pi, :].rearrange("p (f n) -> p f n", f=4), [R_PS[pi]], [R_XRES[tt // 4]])

        for l in range(n_layers):
            k0 = half * NK
            win = wbf_in[l].rearrange("(a p) n -> p a n", p=128)
            for n in range(2):
                nr = rmsnorm_stats(n)
                for f in range(8):
                    P.op("dve", lambda e, f=f, nr=nr, n=n, l=l: e.scalar_tensor_tensor(
                        out=h[:, f, nr], in0=xres[:, f, nr], scalar=smallp[:, l, f:f + 1], in1=rs[:, n, :], op0=ALU.mult, op1=ALU.mult),
                        reads=[R_XRES[n], R_RS[n], R_SMALLP], acc=[R_H[n]])
            wi_s = load_ws(win[:, :, 512:1024], 8, [R_WBF[l]])
            for tau in range(8):
                pi = next_ps()
                mm_group(pi, lambda kc, tau=tau: h[:, kc, :].rearrange("p (k t) -> p t k", t=8)[:, tau, :],
                         lambda kc, wi_s=wi_s: wsv(wi_s)[:, kc, :], 8, [R_WS[wi_s], R_H[0], R_H[1]])
                copy_op(evac_eng(), Zs[:, :, tau, :], psum[:, pi, :].rearrange("p (g c) -> p g c", c=16), [R_PS[pi]], [R_ZY])
            for gq in range(4):
                pi = next_ps()

                def f_tr(e, gq=gq, pi=pi):
                    ins = None
                    for g8 in range(8):
                        g = gq * 8 + g8
                        ins = e.transpose(out=psB(pi)[:, g8 * 128:(g8 + 1) * 128], in_=Zs[:, g].rearrange("p t c -> p (t c)"), identity=identB[:])
                    return ins
                P.op("pe", f_tr, reads=[R_ZY, R_CONST], writes=[R_PS[pi]])
                copy_op(evac_eng(), U[:, gq * 8:(gq + 1) * 8, :], psB(pi).rearrange("p (g k) -> p g k", g=8), [R_PS[pi]], [R_U])

            wi_p = load_ws(win[:, :, 0:512], 8, [R_WBF[l]])
            wi_g = [load_ws(win[:, :, 1024 + sg * 512:1024 + (sg + 1) * 512], 8, [R_WBF[l]]) for sg in range(2)]
            fillers = []

            def pool_in_group(f, n, wi_p=wi_p, l=l):
                nr = slice(n * 512, (n + 1) * 512)
                pi = next_ps()
                mm_group(pi, lambda kc: wsv(wi_p)[:, kc, f * 128:(f + 1) * 128], lambda kc: h[:, kc, nr], 8, [R_WS[wi_p], R_H[n]])
                P.op("dve", lambda e: e.tensor_copy(out=upool[:, f, 16 + n * 512:16 + (n + 1) * 512], in_=psum[:, pi, :]),
                     reads=[R_PS[pi]], acc=[R_UPOOL])

            def gate_group(fo, n, wi_g=wi_g):
                nr = slice(n * 512, (n + 1) * 512)
                wi, f4 = wi_g[fo // 4], fo % 4
                pi = next_ps()
                mm_group(pi, lambda kc: wsv(wi)[:, kc, f4 * 128:(f4 + 1) * 128], lambda kc: h[:, kc, nr], 8, [R_WS[wi], R_H[n]])
                P.op("act", lambda e: e.activation(out=gate[:, fo, nr], in_=psum[:, pi, :], func=AF.Silu), reads=[R_PS[pi]], acc=[R_GATE[n]])

            def pool_sums(l=l, half=half):
                LT = 16 + ST
                R_PWK = R_XS
                P.op("pool", lambda e: e.tensor_copy(out=halo[:, l, :, :], in_=upool[:, :, ST:ST + 16]), reads=[R_UPOOL], writes=[R_HALO[l]])
                P.op("pool", lambda e: e.tensor_tensor(out=spool[:, 0, :], in0=upool[:, 0, 16:LT], in1=upool[:, 0, 15:LT - 1], op=ALU.add),
                     reads=[R_UPOOL], writes=[R_SPOOL])
                for f in (1, 2, 3):
                    P.op("pool", lambda e, f=f: e.tensor_tensor(out=pwk[:, 0, 1:LT], in0=upool[:, f, 1:LT], in1=upool[:, f, 0:LT - 1], op=ALU.add),
                         reads=[R_UPOOL], writes=R_PWK)
                    if f == 1:
                        P.op("pool", lambda e: e.tensor_tensor(out=spool[:, 1, :], in0=pwk[:, 0, 16:LT], in1=pwk[:, 0, 14:LT - 2], op=ALU.add),
                             reads=R_PWK, acc=[R_SPOOL])
                        continue
                    P.op("pool", lambda e: e.tensor_tensor(out=pwk[:, 1, 3:LT], in0=pwk[:, 0, 3:LT], in1=pwk[:, 0, 1:LT - 2], op=ALU.add),
                         reads=R_PWK, writes=R_PWK)
                    if f == 2:
                        P.op("pool", lambda e: e.tensor_tensor(out=spool[:, 2, :], in0=pwk[:, 1, 16:LT], in1=pwk[:, 1, 12:LT - 4], op=ALU.add),
                             reads=R_PWK, acc=[R_SPOOL])
                        continue
                    P.op("pool", lambda e: e.tensor_tensor(out=pwk[:, 2, 7:LT], in0=pwk[:, 1, 7:LT], in1=pwk[:, 1, 3:LT - 4], op=ALU.add),
                         reads=R_PWK, writes=R_PWK)
                    P.op("pool", lambda e: e.tensor_tensor(out=spool[:, 3, :], in0=pwk[:, 2, 16:LT], in1=pwk[:, 2, 8:LT - 8], op=ALU.add),
                         reads=R_PWK, acc=[R_SPOOL])
                if half == 0:
                    P.op("pool", lambda e: e.tensor_tensor(out=spool[:, :, 0:16], in0=spool[:, :, 0:16], in1=cfix[:], op=ALU.mult),
                         reads=[R_C2], writes=[R_SPOOL])

            P.op("pool", lambda e, l=l: e.tensor_copy(out=upool[:, :, 0:16], in_=halo[:, l, :, :]), reads=[R_HALO[l]], writes=[R_UPOOL])
            for f in range(4):
                for n in range(2):
                    fillers.append(lambda f=f, n=n: pool_in_group(f, n))
            fillers.append(pool_sums)
            for fo in range(8):
                for n in range(2):
                    fillers.append(lambda fo=fo, n=n: gate_group(fo, n))
            fillers.reverse()

            def run_fillers(k):
                for _ in range(k):
                    if fillers:
                        fillers.pop()()

            slots = {}

            def ssm_front(gb, l=l, k0=k0):
                wi = sw_rr[0]
                sw_rr[0] = (wi + 1) % NSW
                ti = stb_rr[0]
                stb_rr[0] = (ti + 1) % NSTB
                qi = gb % NQ
                slots[gb] = (wi, qi)
                P.op("sp", lambda e: e.dma_start(out=SSw[wi][:].rearrange("p a g n -> p (a g n)"), in_=ssmW_d[l, gb]),
                     reads=[R_SSMW[l]], writes=[R_SW[wi]], dma=f"d_sw{wi}")
                P.op("sp", lambda e: e.dma_start(out=SSt[ti][:], in_=tabs_d[l, gb][:, :, :, k0:k0 + NK]),
                     reads=[R_TABS[l]], writes=[R_STB[ti]], dma=f"d_stb{ti}")
                pa, pb_ = next_ps(), next_ps()

                def f_v(e):
                    ins = None
                    for kind, pi in ((1, pa), (2, pb_)):
                        for g4 in range(4):
                            ins = e.matmul(psum[:, pi, g4 * 128:(g4 + 1) * 128], lhsT=SSw[wi][:, kind, g4, :], rhs=U[:, gb * 4 + g4, :],
                                           start=True, stop=True)
                    return ins
                P.op("pe", f_v, reads=[R_SW[wi], R_U], writes=[R_PS[pa], R_PS[pb_]])

                def pv(pi):
                    return psum[:, pi, :].rearrange("p (g k) -> p g k", g=4)
                P.op("dve", lambda e: e.tensor_tensor(out=t1[:], in0=pv(pa), in1=SSt[ti][:, 0], op=ALU.mult),
                     reads=[R_PS[pa], R_STB[ti]], writes=[R_T1])
                P.op("dve", lambda e: e.tensor_tensor(out=t2[:], in0=pv(pb_), in1=SSt[ti][:, 1], op=ALU.mult),
                     reads=[R_PS[pb_], R_STB[ti]], writes=[R_T2])
                P.op("dve", lambda e: e.tensor_tensor(out=t1[:], in0=t1[:], in1=t2[:], op=ALU.add), reads=[R_T1, R_T2], writes=[R_T1])
                P.op("dve", lambda e: e.tensor_copy(out=Q[qi][:, :, 0], in_=carry[:, l, gb * 4:(gb + 1) * 4]),
                     reads=[R_CARRY[l][gb]], writes=[R_Q[qi]])
                for g4 in range(4):
                    g = gb * 4 + g4
                    P.op("dve", lambda e, g4=g4, g=g: e.tensor_tensor_scan(
                        out=Q[qi][:, g4, 1:129], data0=Rb_all[:, l, g:g + 1].to_broadcast([128, 128]), data1=t1[:, g4, :],
                        initial=carry[:, l, g:g + 1], op0=ALU.mult, op1=ALU.add),
                        reads=[R_T1, R_RB, R_CARRY[l][gb]], acc=[R_Q[qi]])
                P.op("dve", lambda e: e.tensor_copy(out=carry[:, l, gb * 4:(gb + 1) * 4], in_=Q[qi][:, :, 128]),
                     reads=[R_Q[qi]], writes=[R_CARRY[l][gb]])
                P.op("dve", lambda e: e.tensor_tensor(out=Pc[qi][:], in0=Q[qi][:, :, 0:128], in1=SSt[ti][:, 0], op=ALU.mult),
                     reads=[R_Q[qi], R_STB[ti]], writes=[R_PCS[qi]])
                P.op("dve", lambda e: e.tensor_tensor(out=Ps[qi][:], in0=Q[qi][:, :, 0:128], in1=SSt[ti][:, 1], op=ALU.mult),
                     reads=[R_Q[qi], R_STB[ti]], acc=[R_PCS[qi]])

            def ssm_back(gb):
                wi, qi = slots[gb]
                py = next_ps()

                def f_y(e):
                    ins = None
                    for g4 in range(4):
                        o = psum[:, py, g4 * 128:(g4 + 1) * 128]
                        e.matmul(o, lhsT=U[:, gb * 4 + g4, :], rhs=SSw[wi][:, 0, g4, :], start=True, stop=False)
                        e.matmul(o, lhsT=Pc[qi][:, g4, :], rhs=SSw[wi][:, 3, g4, :], start=False, stop=False)
                        ins = e.matmul(o, lhsT=Ps[qi][:, g4, :], rhs=SSw[wi][:, 4, g4, :], start=False, stop=True)
                    return ins
                P.op("pe", f_y, reads=[R_SW[wi], R_U, R_PCS[qi]], writes=[R_PS[py]])
                P.op("act", lambda e: e.activation(
                    out=ZY[:, :, gb * 64:(gb + 1) * 64].rearrange("p t (g c) -> p g t c", g=4),
                    in_=psum[:, py, :].rearrange("p (g t c) -> p g t c", g=4, t=8), func=AF.Gelu_apprx_tanh),
                    reads=[R_PS[py]], acc=[R_ZY])

            LAG = 1
            for s in range(8 + LAG):
                if s < 8:
                    ssm_front(s)
                run_fillers(3)
                if s >= LAG:
                    ssm_back(s - LAG)
            run_fillers(len(fillers))

            wg = load_ws(wbf_glu[l].rearrange("(a p) n -> p a n", p=128), 4, [R_WBF[l]])
            P.op("sp", lambda e, wg=wg, l=l: e.dma_start(out=WS[wg][:, 2048:3072], in_=poolW_d[l]),
                 reads=[R_POOLW[l]], acc=[R_WS[wg]], dma=f"d_ws{wg}")
            pwv = WS[wg][:, 2048:3072].rearrange("p (g k n) -> p g k n", g=4, k=2)
            for n in range(2):
                nr = slice(n * 512, (n + 1) * 512)
                for f in range(4):
                    pi = next_ps()

                    def f_pm(e, f=f, n=n, nr=nr, pi=pi, pwv=pwv):
                        e.matmul(psum[:, pi, :], lhsT=pwv[:, f, 0, :], rhs=spool[:, f, nr], start=True, stop=False)
                        return e.matmul(psum[:, pi, :], lhsT=pwv[:, f, 1, :], rhs=upool[:, f, 16 + n * 512:16 + (n + 1) * 512], start=False, stop=True)
                    P.op("pe", f_pm, reads=[R_WS[wg], R_SPOOL, R_UPOOL], writes=[R_PS[pi]])
                    P.op("dve", lambda e, f=f, nr=nr, pi=pi, l=l: e.scalar_tensor_tensor(
                        out=ycat[:, f, nr], in0=psum[:, pi, :], scalar=smallp[:, l, 8 + f:9 + f], in1=gate[:, f, nr], op0=ALU.mult, op1=ALU.mult),
                        reads=[R_PS[pi], R_GATE[n], R_SMALLP], acc=[R_YCAT[n]])
            for f in range(4):
                pi = next_ps()

                def f_tr(e, f=f, pi=pi):
                    ins = None
                    for tau in range(8):
                        ins = e.transpose(out=psB(pi)[:, tau * 128:(tau + 1) * 128], in_=ZY[:, tau, f * 128:(f + 1) * 128], identity=identB[:])
                    return ins
                P.op("pe", f_tr, reads=[R_ZY, R_CONST], writes=[R_PS[pi]])
                eng = evac_eng()
                o_, i_ = ysT[:, f, :].rearrange("p (k t) -> p t k", t=8), psB(pi).rearrange("p (t k) -> p t k", t=8)
                if f == 0:
                    if eng == "act":
                        P.op("act", lambda e, o_=o_, i_=i_: e.activation(out=o_, in_=i_, func=AF.Copy), reads=[R_PS[pi]], writes=[R_SPOOL])
                    else:
                        P.op("dve", lambda e, o_=o_, i_=i_: e.tensor_copy(out=o_, in_=i_), reads=[R_PS[pi]], writes=[R_SPOOL])
                else:
                    copy_op(eng, o_, i_, [R_PS[pi]], [R_SPOOL])
            gluv = WS[wg][:, 0:2048].rearrange("p (a n) -> p a n", n=512)
            wov = wbf_out[l].rearrange("(a p) n -> p a n", p=128)
            wi_o = [load_ws(wov[:, :, so * 512:(so + 1) * 512], 8, [R_WBF[l]]) for so in range(2)]

            def c1_half(n, l=l, gluv=gluv, wg=wg):
                nr = slice(n * 512, (n + 1) * 512)
                for fo in range(4):
                    pi = next_ps()
                    sgi = fo % 2
                    mm_group(pi, lambda kc, fo=fo: gluv[:, kc, fo * 128:(fo + 1) * 128], lambda kc: ysT[:, kc, nr], 4, [R_WS[wg], R_SPOOL])
                    P.op("act", lambda e, fo=fo, pi=pi, sgi=sgi: e.activation(out=sig[:, sgi, :], in_=psum[:, pi, :], func=AF.Sigmoid,
                                                                             bias=smallp[:, l, 12 + fo:13 + fo], scale=1.0),
                         reads=[R_PS[pi], R_SMALLP], writes=[R_SIG[sgi]])
                    P.op("dve", lambda e, fo=fo, sgi=sgi: e.tensor_tensor(out=sig[:, sgi, :], in0=sig[:, sgi, :], in1=ysT[:, fo, nr], op=ALU.mult),
                         reads=[R_SIG[sgi], R_SPOOL], writes=[R_SIG[sgi]])
                    P.op("pool", lambda e, fo=fo, sgi=sgi: e.tensor_tensor(out=ycat[:, 4 + fo, nr], in0=sig[:, sgi, :], in1=gate[:, 4 + fo, nr], op=ALU.mult),
                         reads=[R_SIG[sgi], R_GATE[n]], acc=[R_YCAT[n]])

            def c2_half(n, wi_o=wi_o):
                nr = slice(n * 512, (n + 1) * 512)
                for fo in range(8):
                    wi, f4 = wi_o[fo // 4], fo % 4
                    pi = next_ps()
                    mm_group(pi, lambda kc, wi=wi, f4=f4: wsv(wi)[:, kc, f4 * 128:(f4 + 1) * 128], lambda kc: ycat[:, kc, nr], 8, [R_WS[wi], R_YCAT[n]])
                    P.op("dve", lambda e, fo=fo, pi=pi: e.tensor_tensor(out=xres[:, fo, nr], in0=xres[:, fo, nr], in1=psum[:, pi, :], op=ALU.add),
                         reads=[R_PS[pi], R_XRES[n]], acc=[R_XRES[n]])
            c1_half(0)
            c1_half(1)
            c2_half(0)
            c2_half(1)

        for n in range(2):
            nr = rmsnorm_stats(n)
            for f in range(8):
                P.op("dve", lambda e, f=f, nr=nr, n=n: e.scalar_tensor_tensor(
                    out=xres[:, f, nr], in0=xres[:, f, nr], scalar=finalg[:, f:f + 1], in1=rs[:, n, :], op0=ALU.mult, op1=ALU.mult),
                    reads=[R_RS[n], R_SMALLP, R_XRES[n]], acc=[R_XRES[n]])
        for tt in range(8):
            si = xs_rr[0]
            xs_rr[0] ^= 1
            for fh in range(2):
                pi = next_ps()

                def f_ot(e, tt=tt, fh=fh, pi=pi):
                    ins = None
                    for f4 in range(4):
                        f = fh * 4 + f4
                        ins = e.transpose(out=psum[:, pi, f4 * 128:(f4 + 1) * 128], in_=xres[:, f, tt * 128:(tt + 1) * 128], identity=identF[:])
                    return ins
                P.op("pe", f_ot, reads=[R_XRES[tt // 4], R_CONST], writes=[R_PS[pi]])
                copy_op(evac_eng(), xs[:, si, fh * 512:(fh + 1) * 512], psum[:, pi, :], [R_PS[pi]], [R_XS[si]])
            P.op("sp", lambda e, si=si, tt=tt, seq=seq, t0=t0: e.dma_start(out=out_d[seq, t0 + tt * 128:t0 + (tt + 1) * 128, :], in_=xs[:, si, :]),
                 reads=[R_XS[si]], acc=[R_OUT], dma="d_out")

    P.op("sp", None, reads=[R_OUT], sig=False)
    sems = {k: es.enter_context(nc.semaphore(k)) for k in P.cnt}
    with nc.Block() as block:
        P.emit(nc, block, sems)
    es.close()
    return nc


def prep_inputs(inp):
    f = np.float32
    shared = {}
    shared["w_in"] = np.ascontiguousarray(inp["w_in"], dtype=f)
    shared["w_out"] = np.ascontiguousarray(inp["w_out"], dtype=f)
    shared["glu_w"] = np.ascontiguousarray(inp["glu_w"], dtype=f)
    shared["pool_w"] = np.ascontiguousarray(np.transpose(inp["pool_w"], (0, 2, 1, 3)), dtype=f)
    ng = np.transpose(np.asarray(inp["norm_g"], f).reshape(DEPTH, 8, 128), (2, 0, 1))
    psc = np.transpose(np.asarray(inp["pool_scale"], f).reshape(DEPTH, 4, 128), (2, 0, 1))
    gb = np.transpose(np.asarray(inp["glu_b"], f).reshape(DEPTH, 4, 128), (2, 0, 1))
    shared["smallp"] = np.ascontiguousarray(np.concatenate([ng, psc, gb], axis=2), dtype=f)
    shared["finalg"] = np.ascontiguousarray(np.asarray(inp["final_g"], f).reshape(8, 128).T)

    def dup(a):
        t = np.transpose(np.asarray(a, f), (0, 2, 1))
        return np.concatenate([t, t], axis=1)
    are = dup(inp["a_re"])
    aim = dup(inp["a_im"])
    ldt = np.broadcast_to(np.asarray(inp["log_dt"], f)[:, None, :], (DEPTH, 128, G))
    dv = np.asarray(inp["d_skip"], f).reshape(DEPTH, G, 16)
    dvec = np.broadcast_to(np.transpose(dv, (0, 2, 1))[:, None, :, :], (DEPTH, 8, 16, G)).reshape(DEPTH, 128, G)
    shared["ssm_small"] = np.ascontiguousarray(np.stack([are, aim, ldt, dvec], axis=2), dtype=f)

    def dupb(a):
        t = np.transpose(np.asarray(a, f), (0, 2, 1, 3)).reshape(DEPTH, 64, G * 16)
        return np.concatenate([t, t], axis=1)

    def dupc(a):
        t = np.transpose(np.asarray(a, f), (0, 3, 1, 2)).reshape(DEPTH, 64, G * 16)
        return np.concatenate([t, t], axis=1)
    shared["ssm_bc"] = np.ascontiguousarray(
        np.stack([dupb(inp["b_re"]), dupb(inp["b_im"]), dupc(inp["c_re"]), dupc(inp["c_im"])], axis=2), dtype=f)
    x = np.ascontiguousarray(inp["x"], dtype=f)
    maps = []
    for c in range(NCORES):
        m = dict(shared)
        m["x"] = x[c * NSEQ:(c + 1) * NSEQ]
        maps.append(m)
    return maps


def kernel(**inputs):
    maps = prep_inputs(inputs)
    nc = build_program()
    res = run_bass_kernel_spmd(nc, maps, core_ids=list(range(NCORES)))
    out = np.concatenate([np.asarray(r["out"], dtype=np.float32) for r in res.results], axis=0)
    return out
```

```python
import math
from contextlib import ExitStack

import numpy as np
import concourse.bass as bass
import concourse.mybir as mybir
from concourse.bass_utils import run_bass_kernel_spmd

F32 = mybir.dt.float32
BF16 = mybir.dt.bfloat16
I32 = mybir.dt.int32
AF = mybir.ActivationFunctionType
ALU = mybir.AluOpType

NCORES = 8
DEPTH = 4
D = 1024
SEQ = 2048
NSEQ = 2
ST = 1024
NK = ST // 8
G = 32
TWO_PI = float(2.0 * math.pi)
INV_2PI = float(1.0 / (2.0 * math.pi))
HALF_PI = float(math.pi / 2.0)
SIN_SCALE = 1.0 - 4e-5
EPS = 1e-5
POOL_WINDOWS = (2, 4, 8, 16)


class Res:
    __slots__ = ("name", "w", "r")

    def __init__(self, name):
        self.name = name
        self.w = {}
        self.r = {}


class Prog:
    ENG = ("pe", "act", "dve", "pool", "sp")

    def __init__(self):
        self.ops = {e: [] for e in self.ENG}
        self.cnt = {}
        self.seen = {e: {} for e in self.ENG}

    def op(self, eng, fn, reads=(), writes=(), dma=None, sig=True, acc=()):
        need = {}
        for r in acc:
            for k, v in r.r.items():
                if need.get(k, 0) < v:
                    need[k] = v
        for r in reads:
            for k, v in r.w.items():
                if need.get(k, 0) < v:
                    need[k] = v
        for r in writes:
            for k, v in r.w.items():
                if need.get(k, 0) < v:
                    need[k] = v
            for k, v in r.r.items():
                if need.get(k, 0) < v:
                    need[k] = v
        waits = []
        seen = self.seen[eng]
        for k, v in need.items():
            if eng == "pe" and k == "pe":
                continue
            if seen.get(k, 0) >= v:
                continue
            seen[k] = v
            waits.append((k, v))
        tok = None
        inc = 1
        if sig:
            key = dma if dma is not None else eng
            inc = 16 if dma is not None else 1
            self.cnt[key] = self.cnt.get(key, 0) + inc
            tok = (key, self.cnt[key])
            for r in reads:
                if r.r.get(key, 0) < tok[1]:
                    r.r[key] = tok[1]
            for r in writes:
                r.w = {key: tok[1]}
            for r in acc:
                if r.w.get(key, 0) < tok[1]:
                    r.w[key] = tok[1]
        self.ops[eng].append((waits, fn, tok, inc))
        return tok

    def emit(self, nc, block, sems):
        def mk(name):
            def body(e):
                for waits, fn, tok, inc in self.ops[name]:
                    for k, v in waits:
                        e.wait_ge(sems[k], v)
                    if fn is None:
                        continue
                    ins = fn(e)
                    if tok is not None:
                        ins.then_inc(sems[tok[0]], inc)
            return body

        block.tensor(mk("pe"))
        block.scalar(mk("act"))
        block.vector(mk("dve"))
        block.gpsimd(mk("pool"))
        block.sync(mk("sp"))


def build_program(debug=False, n_layers=DEPTH, n_sub=4):
    nc = bass.Bass("TRN2", target_bir_lowering=False)
    P = Prog()
    es = ExitStack()

    def dram_in(name, shape, dt=F32):
        return nc.dram_tensor(name, list(shape), dt, kind="ExternalInput").ap()

    def dram_out(name, shape, dt=F32):
        return nc.dram_tensor(name, list(shape), dt, kind="ExternalOutput").ap()

    def dram_scr(name, shape, dt):
        kind = "ExternalOutput" if debug else "Internal"
        return nc.dram_tensor(name, list(shape), dt, kind=kind).ap()

    def sb(name, shape, dt, stack=None):
        return (stack or es).enter_context(nc.sbuf_tensor(name, list(shape), dt))

    x_d = dram_in("x", [NSEQ, SEQ, D])
    w_in_d = dram_in("w_in", [DEPTH, D, 2 * D])
    w_out_d = dram_in("w_out", [DEPTH, D, D])
    glu_w_d = dram_in("glu_w", [DEPTH, 512, 512])
    pool_w_d = dram_in("pool_w", [DEPTH, 128, 4, 128])
    smallp_d = dram_in("smallp", [128, DEPTH, 16])
    finalg_d = dram_in("finalg", [128, 8])
    ssm_small_d = dram_in("ssm_small", [DEPTH, 128, 4, 32])
    ssm_bc_d = dram_in("ssm_bc", [DEPTH, 128, 4, 512])
    out_d = dram_out("out", [NSEQ, SEQ, D])

    ssmW_d = dram_scr("ssmW", [DEPTH, 8, 128, 5 * 4 * 128], BF16)
    tabs_d = dram_scr("tabs", [DEPTH, 8, 128, 2, 4, 256], F32)
    poolW_d = dram_scr("poolW", [DEPTH, 128, 4 * 2 * 128], BF16)
    R_SSMW = [Res(f"ssmW{l}") for l in range(DEPTH)]
    R_TABS = [Res(f"tabs{l}") for l in range(DEPTH)]
    R_POOLW = [Res(f"poolW{l}") for l in range(DEPTH)]
    R_OUT = Res("out")

    identF = sb("identF", [128, 128], F32)
    identB = sb("identB", [128, 128], BF16)
    onesB = sb("onesB", [128, 128], BF16)
    Rb_all = sb("Rb_all", [128, DEPTH, 32], F32)
    smallp = sb("smallp_sb", [128, DEPTH, 16], F32)
    finalg = sb("finalg_sb", [128, 8], F32)
    psum = es.enter_context(nc.psum_tensor("psum", [128, 8, 512], F32))
    R_CONST = Res("const")
    R_RB = Res("Rb")
    R_SMALLP = Res("smallp")
    R_PS = [Res(f"ps{i}") for i in range(8)]
    ps_rr = [0]

    def next_ps():
        i = ps_rr[0]
        ps_rr[0] = (i + 1) % 8
        return i

    P.op("pool", lambda e: e.memset(identF[:], 1.0), writes=[R_CONST])
    P.op("pool", lambda e: e.affine_select(out=identF[:], in_=identF[:], pattern=[[-1, 128]],
                                           compare_op=ALU.is_equal, fill=0.0, base=0, channel_multiplier=1),
         reads=[R_CONST], writes=[R_CONST])
    P.op("pool", lambda e: e.tensor_copy(out=identB[:], in_=identF[:]), reads=[R_CONST], writes=[R_CONST])
    P.op("pool", lambda e: e.memset(onesB[:], 1.0 / 1024.0), writes=[R_CONST])
    P.op("sp", lambda e: e.dma_start(out=smallp[:], in_=smallp_d[:]), writes=[R_SMALLP], dma="d_small")
    P.op("sp", lambda e: e.dma_start(out=finalg[:], in_=finalg_d[:]), writes=[R_SMALLP], dma="d_small")

    with ExitStack() as ps_:
        def pb(name, shape, dt):
            return sb("pl_" + name, shape, dt, ps_)
        mask = pb("mask", [128, 128], F32)
        iotaKi = pb("iotaKi", [128, 256], I32)
        iotaK = pb("iotaK", [128, 256], F32)
        JTi = pb("JTi", [128, 7, 32], I32)
        JT = pb("JT", [128, 7, 32], F32)
        sm = pb("sm", [128, 4, 32], F32)
        bc = pb("bc", [128, 4, 512], F32)
        pw = pb("pw", [128, 4, 128], F32)
        PWo = pb("PWo", [128, 4, 2, 128], BF16)
        dtt = pb("dtt", [128, 32], F32)
        ard = pb("ard", [128, 32], F32)
        ang = pb("ang", [128, 32], F32)
        ARJ = pb("ARJ", [128, 7, 32], F32)
        ANJ = pb("ANJ", [128, 7, 32], F32)
        MAGP = pb("MAGP", [128, 7, 32], F32)
        MAGN = pb("MAGN", [128, 7, 32], F32)
        NIj = pb("NIj", [128, 7, 32], I32)
        RRj = pb("RRj", [128, 7, 32], F32)
        SN = pb("SN", [128, 7, 32], F32)
        CS = pb("CS", [128, 7, 32], F32)
        EXre = pb("EXre", [128, 32, 8], F32)
        EXim = pb("EXim", [128, 32, 8], F32)
        EYre = pb("EYre", [128, 32, 8], F32)
        EYim = pb("EYim", [128, 32, 8], F32)
        s1 = pb("s1", [128, 32], F32)
        s2 = pb("s2", [128, 32], F32)
        s3 = pb("s3", [128, 32], F32)
        s4 = pb("s4", [128, 32], F32)
        fre = pb("fre", [128, 32], F32)
        fim = pb("fim", [128, 32], F32)
        Rt = pb("Rt", [128, 32], F32)
        nRt = pb("nRt", [128, 32], F32)
        TH = pb("TH", [128, 32], F32)
        NI8 = pb("NI8", [128, 32], I32)
        bbre = pb("bbre", [128, 32, 16], F32)
        bbim = pb("bbim", [128, 32, 16], F32)
        u1 = pb("u1", [128, 32, 16], F32)
        T1 = pb("T1", [128, 32, 8, 16], F32)
        T2 = pb("T2", [128, 32, 8, 16], F32)
        XBm = pb("XBm", [128, 32, 8, 16], F32)
        Gm = pb("Gm", [128, 32, 8, 16], F32)
        YA = pb("YA", [128, 32, 8, 16], F32)
        YB = pb("YB", [128, 32, 8, 16], F32)
        W5 = pb("W5", [128, 8, 5, 4, 128], BF16)
        tmpT = pb("tmpT", [128, 4, 128], F32)
        def v256(t):
            return t[:].rearrange("p g t c -> p (g t c)").rearrange("p (a k) -> p a k", k=256)
        ANGb = v256(T1)
        NIb = v256(T2).bitcast(I32)
        RRb = v256(YA)
        COSh = v256(YB)
        SINh = pb("SINh", [128, 16, 256], F32)

        R_PC = Res("pl_const")
        R_IN = Res("pl_in")
        R_A = Res("pl_a")
        R_E = Res("pl_E")
        R_X = Res("pl_X")
        R_T = Res("pl_T")
        R_W5 = Res("pl_W5")
        R_TMPT = Res("pl_tmpT")
        R_ANG = Res("pl_ang")
        R_TAB = Res("pl_tab")
        R_PW = Res("pl_pw")

        P.op("pool", lambda e: e.memset(mask[:], 1.0), writes=[R_PC])
        P.op("pool", lambda e: e.affine_select(out=mask[:], in_=mask[:], pattern=[[16, 8], [0, 16]],
                                               compare_op=ALU.is_ge, fill=0.0, base=15, channel_multiplier=-1),
             reads=[R_PC], writes=[R_PC])
        P.op("pool", lambda e: e.iota(iotaKi[:], pattern=[[1, 256]], base=0, channel_multiplier=0), writes=[R_PC])
        P.op("pool", lambda e: e.tensor_copy(out=iotaK[:], in_=iotaKi[:]), reads=[R_PC], writes=[R_PC])
        P.op("pool", lambda e: e.iota(JTi[:], pattern=[[-1, 7], [0, 32]], base=7, channel_multiplier=0), writes=[R_PC])
        P.op("pool", lambda e: e.tensor_copy(out=JT[:], in_=JTi[:]), reads=[R_PC], writes=[R_PC])

        def bc3(ap32):
            return ap32.unsqueeze(1).to_broadcast([128, 7, 32])

        def bcc(ap32):
            return ap32.unsqueeze(2).to_broadcast([128, 32, 16])

        def bE(apE, lo, hi):
            return apE[lo:hi].unsqueeze(3).to_broadcast([hi - lo, 32, 8, 16])

        def bB(apB, lo, hi):
            return apB[lo:hi].unsqueeze(2).to_broadcast([hi - lo, 32, 8, 16])

        def bR(apR, lo, hi):
            return apR[lo:hi].rearrange("p (a g) -> p a g", g=4).unsqueeze(3).to_broadcast([hi - lo, 8, 4, 128])

        for l in range(n_layers):
            P.op("sp", lambda e, l=l: e.dma_start(out=sm[:], in_=ssm_small_d[l]), writes=[R_IN], dma="d_plin")
            P.op("sp", lambda e, l=l: e.dma_start(out=bc[:], in_=ssm_bc_d[l]), writes=[R_IN], dma="d_plin")
            P.op("sp", lambda e, l=l: e.dma_start(out=pw[:], in_=pool_w_d[l]), writes=[R_PW], dma="d_plpw")
            are, aim, ldt, dvec = sm[:, 0, :], sm[:, 1, :], sm[:, 2, :], sm[:, 3, :]
            bre = bc[:, 0, :].rearrange("p (g c) -> p g c", c=16)
            bim = bc[:, 1, :].rearrange("p (g c) -> p g c", c=16)
            cre = bc[:, 2, :].rearrange("p (g c) -> p g c", c=16)
            cim = bc[:, 3, :].rearrange("p (g c) -> p g c", c=16)

            for g4 in range(4):
                P.op("pool", lambda e, g4=g4: e.tensor_scalar(out=PWo[:, g4, 0, :], in0=pw[:, g4, :],
                                                            scalar1=1.0 / POOL_WINDOWS[g4], scalar2=None, op0=ALU.mult),
                     reads=[R_PW], writes=[R_PW])
            P.op("pool", lambda e: e.tensor_scalar(out=PWo[:, :, 1, :], in0=pw[:, :, :], scalar1=-1.0, scalar2=None, op0=ALU.mult),
                 reads=[R_PW], writes=[R_PW])
            P.op("sp", lambda e, l=l: e.dma_start(out=poolW_d[l], in_=PWo[:].rearrange("p a b c -> p (a b c)")),
                 reads=[R_PW], writes=[R_POOLW[l]], dma=f"d_plpo{l}")

            P.op("act", lambda e: e.activation(out=dtt[:], in_=ldt, func=AF.Exp), reads=[R_IN], writes=[R_A])
            P.op("dve", lambda e: e.tensor_tensor(out=ard[:], in0=are, in1=dtt[:], op=ALU.mult), reads=[R_IN, R_A], writes=[R_A])
            P.op("dve", lambda e: e.tensor_tensor(out=ang[:], in0=aim, in1=dtt[:], op=ALU.mult), reads=[R_IN, R_A], writes=[R_A])
            P.op("dve", lambda e: e.tensor_tensor(out=ARJ[:], in0=JT[:], in1=bc3(ard[:]), op=ALU.mult), reads=[R_PC, R_A], writes=[R_A])
            P.op("dve", lambda e: e.tensor_tensor(out=ANJ[:], in0=JT[:], in1=bc3(ang[:]), op=ALU.mult), reads=[R_PC, R_A], writes=[R_A])
            P.op("act", lambda e: e.activation(out=MAGP[:], in_=ARJ[:], func=AF.Exp), reads=[R_A], writes=[R_A])
            P.op("act", lambda e: e.activation(out=MAGN[:], in_=ARJ[:], func=AF.Exp, scale=-1.0), reads=[R_A], writes=[R_A])
            P.op("act", lambda e: e.activation(out=NIj[:], in_=ANJ[:], func=AF.Copy, scale=INV_2PI), reads=[R_A], writes=[R_A])
            P.op("dve", lambda e: e.scalar_tensor_tensor(out=RRj[:], in0=NIj[:], scalar=-TWO_PI, in1=ANJ[:], op0=ALU.mult, op1=ALU.add),
                 reads=[R_A], writes=[R_A])
            P.op("act", lambda e: e.activation(out=SN[:], in_=RRj[:], func=AF.Sin, scale=SIN_SCALE), reads=[R_A], writes=[R_A])
            P.op("dve", lambda e: e.tensor_scalar(out=RRj[:], in0=ANJ[:], scalar1=HALF_PI, scalar2=None, op0=ALU.add), reads=[R_A], writes=[R_A])
            P.op("act", lambda e: e.activation(out=NIj[:], in_=RRj[:], func=AF.Copy, scale=INV_2PI), reads=[R_A], writes=[R_A])
            P.op("dve", lambda e: e.scalar_tensor_tensor(out=RRj[:], in0=NIj[:], scalar=-TWO_PI, in1=RRj[:], op0=ALU.mult, op1=ALU.add),
                 reads=[R_A], writes=[R_A])
            P.op("act", lambda e: e.activation(out=CS[:], in_=RRj[:], func=AF.Sin, scale=SIN_SCALE), reads=[R_A], writes=[R_A])
            def Ev(t):
                return t[:].rearrange("p g t -> p t g")[:, 0:7, :]
            P.op("dve", lambda e: e.tensor_tensor(out=Ev(EXre), in0=MAGP[:], in1=CS[:], op=ALU.mult), reads=[R_A], writes=[R_E])
            P.op("dve", lambda e: e.tensor_tensor(out=Ev(EXim), in0=MAGP[:], in1=SN[:], op=ALU.mult), reads=[R_A], writes=[R_E])
            P.op("dve", lambda e: e.tensor_tensor(out=Ev(EYre), in0=MAGN[:], in1=CS[:], op=ALU.mult), reads=[R_A], writes=[R_E])
            P.op("dve", lambda e: e.scalar_tensor_tensor(out=Ev(EYim), in0=MAGN[:], scalar=-1.0, in1=SN[:], op0=ALU.mult, op1=ALU.mult),
                 reads=[R_A], writes=[R_E])
            P.op("dve", lambda e: e.memset(EXre[:, :, 7:8], 1.0), writes=[R_E])
            P.op("dve", lambda e: e.memset(EXim[:, :, 7:8], 0.0), writes=[R_E])
            P.op("dve", lambda e: e.memset(EYre[:, :, 7:8], 1.0), writes=[R_E])
            P.op("dve", lambda e: e.memset(EYim[:, :, 7:8], 0.0), writes=[R_E])
            lre, lim = EXre[:, :, 6], EXim[:, :, 6]
            P.op("dve", lambda e: e.tensor_scalar(out=s1[:], in0=lre, scalar1=-1.0, scalar2=None, op0=ALU.add), reads=[R_E], writes=[R_A])
            P.op("dve", lambda e: e.tensor_tensor(out=s2[:], in0=are, in1=are, op=ALU.mult), reads=[R_IN], writes=[R_A])
            P.op("dve", lambda e: e.tensor_tensor(out=s3[:], in0=aim, in1=aim, op=ALU.mult), reads=[R_IN], writes=[R_A])
            P.op("dve", lambda e: e.tensor_tensor(out=s2[:], in0=s2[:], in1=s3[:], op=ALU.add), reads=[R_A], writes=[R_A])
            P.op("dve", lambda e: e.reciprocal(out=s2[:], in_=s2[:]), reads=[R_A], writes=[R_A])
            P.op("dve", lambda e: e.tensor_tensor(out=s3[:], in0=s1[:], in1=are, op=ALU.mult), reads=[R_A, R_IN], writes=[R_A])
            P.op("dve", lambda e: e.tensor_tensor(out=s4[:], in0=lim, in1=aim, op=ALU.mult), reads=[R_E, R_IN], writes=[R_A])
            P.op("dve", lambda e: e.tensor_tensor(out=s3[:], in0=s3[:], in1=s4[:], op=ALU.add), reads=[R_A], writes=[R_A])
            P.op("dve", lambda e: e.tensor_tensor(out=fre[:], in0=s3[:], in1=s2[:], op=ALU.mult), reads=[R_A], writes=[R_A])
            P.op("dve", lambda e: e.tensor_tensor(out=s3[:], in0=lim, in1=are, op=ALU.mult), reads=[R_E, R_IN], writes=[R_A])
            P.op("dve", lambda e: e.tensor_tensor(out=s4[:], in0=s1[:], in1=aim, op=ALU.mult), reads=[R_A, R_IN], writes=[R_A])
            P.op("dve", lambda e: e.tensor_tensor(out=s3[:], in0=s3[:], in1=s4[:], op=ALU.subtract), reads=[R_A], writes=[R_A])
            P.op("dve", lambda e: e.tensor_tensor(out=fim[:], in0=s3[:], in1=s2[:], op=ALU.mult), reads=[R_A], writes=[R_A])
            P.op("dve", lambda e: e.tensor_tensor(out=bbre[:], in0=bre, in1=bcc(fre[:]), op=ALU.mult), reads=[R_A, R_IN], writes=[R_A])
            P.op("dve", lambda e: e.tensor_tensor(out=u1[:], in0=bim, in1=bcc(fim[:]), op=ALU.mult), reads=[R_A, R_IN], writes=[R_A])
            P.op("dve", lambda e: e.tensor_tensor(out=bbre[:], in0=bbre[:], in1=u1[:], op=ALU.subtract), reads=[R_A], writes=[R_A])
            P.op("dve", lambda e: e.tensor_tensor(out=bbim[:], in0=bim, in1=bcc(fre[:]), op=ALU.mult), reads=[R_A, R_IN], writes=[R_A])
            P.op("dve", lambda e: e.tensor_tensor(out=u1[:], in0=bre, in1=bcc(fim[:]), op=ALU.mult), reads=[R_A, R_IN], writes=[R_A])
            P.op("dve", lambda e: e.tensor_tensor(out=bbim[:], in0=bbim[:], in1=u1[:], op=ALU.add), reads=[R_A], writes=[R_A])
            P.op("act", lambda e: e.activation(out=Rt[:], in_=ard[:], func=AF.Exp, scale=8.0), reads=[R_A], writes=[R_A])
            P.op("act", lambda e: e.activation(out=nRt[:], in_=Rt[:], func=AF.Copy, scale=-1.0), reads=[R_A], writes=[R_A])
            P.op("act", lambda e, l=l: e.activation(out=Rb_all[:, l, :], in_=Rt[:], func=AF.Copy), reads=[R_A], writes=[R_RB])
            P.op("dve", lambda e: e.tensor_scalar(out=s1[:], in0=ang[:], scalar1=8.0, scalar2=None, op0=ALU.mult), reads=[R_A], writes=[R_A])
            P.op("act", lambda e: e.activation(out=NI8[:], in_=s1[:], func=AF.Copy, scale=INV_2PI), reads=[R_A], writes=[R_A])
            P.op("dve", lambda e: e.scalar_tensor_tensor(out=TH[:], in0=NI8[:], scalar=-TWO_PI, in1=s1[:], op0=ALU.mult, op1=ALU.add),
                 reads=[R_A], writes=[R_A])

            P.op("dve", lambda e: e.tensor_tensor(out=T1[0:64], in0=bE(EXre, 0, 64), in1=bB(bbre, 0, 64), op=ALU.mult), reads=[R_E, R_A], writes=[R_T])
            P.op("dve", lambda e: e.tensor_tensor(out=T2[0:64], in0=bE(EXim, 0, 64), in1=bB(bbim, 0, 64), op=ALU.mult), reads=[R_E, R_A], writes=[R_T])
            P.op("dve", lambda e: e.tensor_tensor(out=XBm[0:64], in0=T1[0:64], in1=T2[0:64], op=ALU.subtract), reads=[R_T], writes=[R_X])
            P.op("dve", lambda e: e.tensor_tensor(out=T1[64:128], in0=bE(EXre, 64, 128), in1=bB(bbim, 64, 128), op=ALU.mult), reads=[R_E, R_A], writes=[R_T])
            P.op("dve", lambda e: e.tensor_tensor(out=T2[64:128], in0=bE(EXim, 64, 128), in1=bB(bbre, 64, 128), op=ALU.mult), reads=[R_E, R_A], writes=[R_T])
            P.op("dve", lambda e: e.tensor_tensor(out=XBm[64:128], in0=T1[64:128], in1=T2[64:128], op=ALU.add), reads=[R_T], writes=[R_X])
            P.op("dve", lambda e: e.tensor_tensor(out=T1[:], in0=bE(EYre, 0, 128), in1=bB(cre, 0, 128), op=ALU.mult), reads=[R_E, R_IN, R_X], writes=[R_T])
            P.op("dve", lambda e: e.tensor_tensor(out=T2[:], in0=bE(EYim, 0, 128), in1=bB(cim, 0, 128), op=ALU.mult), reads=[R_E, R_IN], writes=[R_T])
            P.op("dve", lambda e: e.tensor_tensor(out=YA[:], in0=T1[:], in1=T2[:], op=ALU.subtract), reads=[R_T], writes=[R_T])
            P.op("dve", lambda e: e.tensor_tensor(out=T1[:], in0=bE(EYre, 0, 128), in1=bB(cim, 0, 128), op=ALU.mult), reads=[R_E, R_IN], writes=[R_T])
            P.op("dve", lambda e: e.tensor_tensor(out=T2[:], in0=bE(EYim, 0, 128), in1=bB(cre, 0, 128), op=ALU.mult), reads=[R_E, R_IN], writes=[R_T])
            P.op("dve", lambda e: e.scalar_tensor_tensor(out=YB[:], in0=T1[:], scalar=-1.0, in1=T2[:], op0=ALU.mult, op1=ALU.subtract),
                 reads=[R_T], writes=[R_T])
            P.op("pool", lambda e: e.tensor_copy(out=Gm[0:64], in_=YA[0:64]), reads=[R_T], writes=[R_X])
            P.op("pool", lambda e: e.tensor_copy(out=Gm[64:128], in_=YB[64:128]), reads=[R_T], writes=[R_X])
            def w5k(kind, lo, hi):
                return W5[lo:hi, :, kind, :, :]

            def flat(t, lo, hi):
                return t[lo:hi].rearrange("p (a g) t c -> p a g (t c)", g=4)
            P.op("dve", lambda e: e.tensor_tensor(out=w5k(3, 0, 64), in0=flat(YA, 0, 64), in1=bR(Rt, 0, 64), op=ALU.mult), reads=[R_T, R_A], writes=[R_W5])
            P.op("dve", lambda e: e.tensor_tensor(out=w5k(3, 64, 128), in0=flat(YB, 64, 128), in1=bR(Rt, 64, 128), op=ALU.mult), reads=[R_T, R_A], writes=[R_W5])
            P.op("dve", lambda e: e.tensor_tensor(out=w5k(4, 0, 64), in0=flat(YB, 0, 64), in1=bR(Rt, 0, 64), op=ALU.mult), reads=[R_T, R_A], writes=[R_W5])
            P.op("dve", lambda e: e.tensor_tensor(out=w5k(4, 64, 128), in0=flat(YA, 64, 128), in1=bR(nRt, 64, 128), op=ALU.mult), reads=[R_T, R_A], writes=[R_W5])

            for gb in range(8):
                pi = next_ps()

                def f_toep(e, gb=gb, pi=pi):
                    ins = None
                    for g4 in range(4):
                        g = gb * 4 + g4
                        ins = e.matmul(psum[:, pi, g4 * 128:(g4 + 1) * 128],
                                       lhsT=XBm[:, g].rearrange("p t c -> p (t c)"),
                                       rhs=Gm[:, g].rearrange("p t c -> p (t c)"), start=True, stop=True)
                    return ins
                P.op("pe", f_toep, reads=[R_X], writes=[R_PS[pi]])
                P.op("dve", lambda e, pi=pi: e.tensor_tensor(out=tmpT[:], in0=psum[:, pi, :].rearrange("p (g n) -> p g n", g=4),
                                                            in1=mask[:].unsqueeze(1).to_broadcast([128, 4, 128]), op=ALU.mult),
                     reads=[R_PS[pi], R_PC], writes=[R_TMPT])
                for g4 in range(4):
                    P.op("dve", lambda e, gb=gb, g4=g4: e.scalar_tensor_tensor(
                        out=W5[:, gb, 0, g4, :], in0=identF[:], scalar=sm[:, 3, gb * 4 + g4:gb * 4 + g4 + 1], in1=tmpT[:, g4, :],
                        op0=ALU.mult, op1=ALU.add), reads=[R_TMPT, R_CONST, R_IN], writes=[R_W5])
                pi2 = next_ps()

                def f_tr(e, gb=gb, pi2=pi2):
                    ins = None
                    for g4 in range(4):
                        g = gb * 4 + g4
                        ins = e.transpose(out=psum[:, pi2, g4 * 128:(g4 + 1) * 128],
                                          in_=XBm[:, g].rearrange("p t c -> p (t c)"), identity=identF[:])
                    return ins
                P.op("pe", f_tr, reads=[R_X, R_CONST], writes=[R_PS[pi2]])
                pv = psum[:, pi2, :].rearrange("p (g n) -> p g n", g=4)
                P.op("act", lambda e, gb=gb, pv=pv: e.activation(out=W5[:, gb, 1, :, :], in_=pv, func=AF.Copy),
                     reads=[R_PS[pi2]], writes=[R_W5])
                P.op("act", lambda e, gb=gb, pv=pv: e.activation(out=W5[:, gb, 2, :, 0:64], in_=pv[:, :, 64:128], func=AF.Copy),
                     reads=[R_PS[pi2]], writes=[R_W5])
                P.op("act", lambda e, gb=gb, pv=pv: e.activation(out=W5[:, gb, 2, :, 64:128], in_=pv[:, :, 0:64], func=AF.Copy, scale=-1.0),
                     reads=[R_PS[pi2]], writes=[R_W5])
            P.op("sp", lambda e, l=l: e.dma_start(out=ssmW_d[l].rearrange("a p n -> p a n"),
                                                  in_=W5[:].rearrange("p a k g n -> p a (k g n)")),
                 reads=[R_W5], writes=[R_SSMW[l]], dma=f"d_plssm{l}")

            tv = tabs_d[l].rearrange("a p two g k -> p a two g k")
            for hf in range(2):
                gs = slice(hf * 16, hf * 16 + 16)
                P.op("dve", lambda e, gs=gs: e.tensor_tensor(out=ANGb, in0=TH[:, gs].unsqueeze(2).to_broadcast([128, 16, 256]),
                                                            in1=iotaK[:].unsqueeze(1).to_broadcast([128, 16, 256]), op=ALU.mult),
                     reads=[R_A, R_PC], writes=[R_T])
                P.op("act", lambda e: e.activation(out=NIb, in_=ANGb, func=AF.Copy, scale=INV_2PI), reads=[R_T], writes=[R_T])
                P.op("dve", lambda e: e.scalar_tensor_tensor(out=RRb, in0=NIb, scalar=-TWO_PI, in1=ANGb, op0=ALU.mult, op1=ALU.add),
                     reads=[R_T], writes=[R_T])
                P.op("act", lambda e: e.activation(out=SINh[:], in_=RRb, func=AF.Sin, scale=SIN_SCALE), reads=[R_T], writes=[R_TAB])
                P.op("dve", lambda e: e.tensor_scalar(out=ANGb, in0=ANGb, scalar1=HALF_PI, scalar2=None, op0=ALU.add), reads=[R_T], writes=[R_T])
                P.op("act", lambda e: e.activation(out=NIb, in_=ANGb, func=AF.Copy, scale=INV_2PI), reads=[R_T], writes=[R_T])
                P.op("dve", lambda e: e.scalar_tensor_tensor(out=RRb, in0=NIb, scalar=-TWO_PI, in1=ANGb, op0=ALU.mult, op1=ALU.add),
                     reads=[R_T], writes=[R_T])
                P.op("act", lambda e: e.activation(out=COSh, in_=RRb, func=AF.Sin, scale=SIN_SCALE), reads=[R_T], writes=[R_T])
                P.op("sp", lambda e, tv=tv, hf=hf: e.dma_start(out=tv[:, hf * 4:hf * 4 + 4, 0], in_=COSh.rearrange("p (a g) k -> p a g k", g=4)),
                     reads=[R_T], writes=[R_TABS[l]], dma=f"d_pltab{l}")
                P.op("sp", lambda e, tv=tv, hf=hf: e.dma_start(out=tv[:, hf * 4:hf * 4 + 4, 1], in_=SINh[:].rearrange("p (a g) k -> p a g k", g=4)),
                     reads=[R_TAB], writes=[R_TABS[l]], dma=f"d_pltab{l}")

    wbf_in = dram_scr("wbf_in", [DEPTH, D, 2 * D], BF16)
    wbf_out = dram_scr("wbf_out", [DEPTH, D, D], BF16)
    wbf_glu = dram_scr("wbf_glu", [DEPTH, 512, 512], BF16)
    R_WBF = [Res(f"wbf{l}") for l in range(DEPTH)]
    for l in range(n_layers):
        for a in range(8):
            P.op("pool", lambda e, l=l, a=a: e.dma_start(out=wbf_in[l, a * 128:(a + 1) * 128, :], in_=w_in_d[l, a * 128:(a + 1) * 128, :]),
                 writes=[R_WBF[l]], dma=f"d_wbf{l}")
        for a in range(4):
            P.op("pool", lambda e, l=l, a=a: e.dma_start(out=wbf_out[l, a * 256:(a + 1) * 256, :], in_=w_out_d[l, a * 256:(a + 1) * 256, :]),
                 writes=[R_WBF[l]], dma=f"d_wbf{l}")
        P.op("pool", lambda e, l=l: e.dma_start(out=wbf_glu[l], in_=glu_w_d[l]), writes=[R_WBF[l]], dma=f"d_wbf{l}")

    if debug == "prologue":
        rb_d = dram_out("rb_dbg", [128, DEPTH, 32])
        P.op("sp", lambda e: e.dma_start(out=rb_d[:], in_=Rb_all[:]), reads=[R_RB], writes=[R_OUT], dma="d_out")
        fin = [R_OUT] + R_SSMW[:n_layers] + R_TABS[:n_layers] + R_POOLW[:n_layers] + R_WBF[:n_layers]
        P.op("sp", None, reads=fin, sig=False)
        sems = {k: es.enter_context(nc.semaphore(k)) for k in P.cnt}
        with nc.Block() as block:
            P.emit(nc, block, sems)
        es.close()
        return nc

    xres = sb("xres", [128, 8, ST], F32)
    h = sb("h", [128, 8, ST], BF16)
    gate = sb("gate", [128, 8, ST], BF16)
    ycat = sb("ycat", [128, 8, ST], BF16)
    upool = sb("upool", [128, 4, 16 + ST], BF16)
    spool = sb("spool", [128, 4, ST], BF16)
    ysT = spool
    xsq = spool[:].rearrange("p a n -> p (a n)").rearrange("p (a n) -> p a n", n=512)
    ZYf = sb("ZY", [128, 4096], BF16)
    ZY = ZYf[:].rearrange("p (t f) -> p t f", t=8)
    Zs = ZYf[:].rearrange("p (g t c) -> p g t c", g=32, t=8)
    U = sb("U", [128, 32, 128], BF16)
    rs = sb("rs", [128, 2, 512], F32)
    t1 = sb("t1", [128, 4, 128], F32)
    t2 = sb("t2", [128, 4, 128], F32)
    NQ = 3
    Q = [sb(f"Q{i}", [128, 4, 129], F32) for i in range(NQ)]
    Pc = [sb(f"Pc{i}", [128, 4, 128], BF16) for i in range(NQ)]
    Ps = [sb(f"Ps{i}", [128, 4, 128], BF16) for i in range(NQ)]
    sig = sb("sig", [128, 2, 512], BF16)
    xs = sb("xs", [128, 2, D], F32)
    pwk = xs[:].rearrange("p a n -> p (a n)").bitcast(BF16)[:, 0:3 * (16 + ST)].rearrange("p (a n) -> p a n", a=3)
    carry = sb("carry", [128, DEPTH, 32], F32)
    halo = sb("halo", [128, DEPTH, 4, 16], BF16)
    cfix = sb("cfix", [128, 4, 16], F32)
    epsT = sb("epsT", [128, 1], F32)
    NWS, NSW, NSTB = 4, 3, 3
    WS = [sb(f"WS{i}", [128, 8 * 512], BF16) for i in range(NWS)]
    SSw = [sb(f"SSw{i}", [128, 5, 4, 128], BF16) for i in range(NSW)]
    SSt = [sb(f"SSt{i}", [128, 2, 4, 128], F32) for i in range(NSTB)]
    R_WS = [Res(f"WS{i}") for i in range(NWS)]
    R_SW = [Res(f"SW{i}") for i in range(NSW)]
    R_STB = [Res(f"STB{i}") for i in range(NSTB)]
    R_ZY, R_U = Res("ZY"), Res("U")
    R_XRES = [Res("xres0"), Res("xres1")]
    R_RS = [Res("rs0"), Res("rs1")]
    R_H = [Res("h0"), Res("h1")]
    R_GATE = [Res("g0"), Res("g1")]
    R_YCAT = [Res("yc0"), Res("yc1")]
    R_UPOOL, R_SPOOL = Res("upool"), Res("spool")
    R_T1, R_T2 = Res("t1"), Res("t2")
    R_Q = [Res(f"Q{i}") for i in range(NQ)]
    R_PCS = [Res(f"PcPs{i}") for i in range(NQ)]
    R_SIG = [Res("sig0"), Res("sig1")]
    R_XS = [Res("xs0"), Res("xs1")]
    R_CARRY = [[Res(f"carry{l}_{gb}") for gb in range(8)] for l in range(DEPTH)]
    R_HALO = [Res(f"halo{l}") for l in range(DEPTH)]
    R_C2 = Res("const2")
    ws_rr, sw_rr, stb_rr, xs_rr, ev_rr = [0], [0], [0], [0], [0]

    def psB(pi):
        return psum[:, pi, :].bitcast(BF16)

    def evac_eng():
        ev_rr[0] ^= 1
        return "act" if ev_rr[0] else "dve"

    def copy_op(eng, out, in_, reads, acc):
        if eng == "act":
            P.op("act", lambda e: e.activation(out=out, in_=in_, func=AF.Copy), reads=reads, acc=acc)
        else:
            P.op(eng, lambda e: e.tensor_copy(out=out, in_=in_), reads=reads, acc=acc)

    P.op("pool", lambda e: e.memset(epsT[:], EPS), writes=[R_C2])
    P.op("pool", lambda e: e.memset(cfix[:], 1.0), writes=[R_C2])
    for f in range(4):
        w = POOL_WINDOWS[f]
        for t in range(w - 1):
            P.op("pool", lambda e, f=f, t=t, w=w: e.memset(cfix[:, f, t:t + 1], float(w) / float(t + 1)), writes=[R_C2])

    def load_ws(src_ap, nparts, reads):
        i = ws_rr[0]
        ws_rr[0] = (i + 1) % NWS
        dst = WS[i][:, 0:nparts * 512].rearrange("p (a n) -> p a n", n=512)
        P.op("sp", lambda e: e.dma_start(out=dst, in_=src_ap), reads=reads, writes=[R_WS[i]], dma=f"d_ws{i}")
        return i

    def wsv(i):
        return WS[i][:].rearrange("p (a n) -> p a n", n=512)

    def mm_group(pi, lhs_fn, rhs_fn, nk, reads):
        def f(e):
            ins = None
            for kc in range(nk):
                ins = e.matmul(psum[:, pi, :], lhsT=lhs_fn(kc), rhs=rhs_fn(kc), start=(kc == 0), stop=(kc == nk - 1))
            return ins
        P.op("pe", f, reads=reads, writes=[R_PS[pi]])

    def rmsnorm_stats(n):
        nr = slice(n * 512, (n + 1) * 512)
        P.op("act", lambda e: e.activation(out=xsq, in_=xres[:, :, nr], func=AF.Square), reads=[R_XRES[n]], writes=[R_SPOOL])
        pi = next_ps()
        mm_group(pi, lambda kc: onesB[:], lambda kc: xsq[:, kc, :], 8, [R_SPOOL, R_CONST])
        P.op("act", lambda e, pi=pi: e.activation(out=rs[:, n, :], in_=psum[:, pi, :], func=AF.Ln, bias=epsT[:], scale=1.0),
             reads=[R_PS[pi], R_C2], writes=[R_RS[n]])
        P.op("act", lambda e: e.activation(out=rs[:, n, :], in_=rs[:, n, :], func=AF.Exp, scale=-0.5), reads=[R_RS[n]], writes=[R_RS[n]])
        return nr

    for st in range(n_sub):
        seq, half = st // 2, st % 2
        t0 = half * ST
        if half == 0:
            P.op("pool", lambda e: e.memset(carry[:], 0.0), writes=[r for rl in R_CARRY for r in rl])
            P.op("pool", lambda e: e.memset(halo[:], 0.0), writes=R_HALO)
        for tt in range(8):
            si = xs_rr[0]
            xs_rr[0] ^= 1
            P.op("sp", lambda e, si=si, tt=tt, seq=seq, t0=t0: e.dma_start(out=xs[:, si, :], in_=x_d[seq, t0 + tt * 128:t0 + (tt + 1) * 128, :]),
                 writes=[R_XS[si]], dma=f"d_xs{si}")
            for fh in range(2):
                pi = next_ps()

                def f_xt(e, si=si, fh=fh, pi=pi):
                    ins = None
                    for f4 in range(4):
                        f = fh * 4 + f4
                        ins = e.transpose(out=psum[:, pi, f4 * 128:(f4 + 1) * 128], in_=xs[:, si, f * 128:(f + 1) * 128], identity=identF[:])
                    return ins
                P.op("pe", f_xt, reads=[R_XS[si], R_CONST], writes=[R_PS[pi]])
                copy_op(evac_eng(), xres[:, fh * 4:(fh + 1) * 4, tt * 128:(tt + 1) * 128],
                        psum[:, pi, :].rearrange("p (f n) -> p f n", f=4), [R_PS[pi]], [R_XRES[tt // 4]])

        for l in range(n_layers):
            k0 = half * NK
            win = wbf_in[l].rearrange("(a p) n -> p a n", p=128)
            for n in range(2):
                nr = rmsnorm_stats(n)
                for f in range(8):
                    P.op("dve", lambda e, f=f, nr=nr, n=n, l=l: e.scalar_tensor_tensor(
                        out=h[:, f, nr], in0=xres[:, f, nr], scalar=smallp[:, l, f:f + 1], in1=rs[:, n, :], op0=ALU.mult, op1=ALU.mult),
                        reads=[R_XRES[n], R_RS[n], R_SMALLP], acc=[R_H[n]])
            wi_s = load_ws(win[:, :, 512:1024], 8, [R_WBF[l]])
            for tau in range(8):
                pi = next_ps()
                mm_group(pi, lambda kc, tau=tau: h[:, kc, :].rearrange("p (k t) -> p t k", t=8)[:, tau, :],
                         lambda kc, wi_s=wi_s: wsv(wi_s)[:, kc, :], 8, [R_WS[wi_s], R_H[0], R_H[1]])
                copy_op(evac_eng(), Zs[:, :, tau, :], psum[:, pi, :].rearrange("p (g c) -> p g c", c=16), [R_PS[pi]], [R_ZY])
            for gq in range(4):
                pi = next_ps()

                def f_tr(e, gq=gq, pi=pi):
                    ins = None
                    for g8 in range(8):
                        g = gq * 8 + g8
                        ins = e.transpose(out=psB(pi)[:, g8 * 128:(g8 + 1) * 128], in_=Zs[:, g].rearrange("p t c -> p (t c)"), identity=identB[:])
                    return ins
                P.op("pe", f_tr, reads=[R_ZY, R_CONST], writes=[R_PS[pi]])
                copy_op(evac_eng(), U[:, gq * 8:(gq + 1) * 8, :], psB(pi).rearrange("p (g k) -> p g k", g=8), [R_PS[pi]], [R_U])

            wi_p = load_ws(win[:, :, 0:512], 8, [R_WBF[l]])
            wi_g = [load_ws(win[:, :, 1024 + sg * 512:1024 + (sg + 1) * 512], 8, [R_WBF[l]]) for sg in range(2)]
            fillers = []

            def pool_in_group(f, n, wi_p=wi_p, l=l):
                nr = slice(n * 512, (n + 1) * 512)
                pi = next_ps()
                mm_group(pi, lambda kc: wsv(wi_p)[:, kc, f * 128:(f + 1) * 128], lambda kc: h[:, kc, nr], 8, [R_WS[wi_p], R_H[n]])
                P.op("dve", lambda e: e.tensor_copy(out=upool[:, f, 16 + n * 512:16 + (n + 1) * 512], in_=psum[:, pi, :]),
                     reads=[R_PS[pi]], acc=[R_UPOOL])

            def gate_group(fo, n, wi_g=wi_g):
                nr = slice(n * 512, (n + 1) * 512)
                wi, f4 = wi_g[fo // 4], fo % 4
                pi = next_ps()
                mm_group(pi, lambda kc: wsv(wi)[:, kc, f4 * 128:(f4 + 1) * 128], lambda kc: h[:, kc, nr], 8, [R_WS[wi], R_H[n]])
                P.op("act", lambda e: e.activation(out=gate[:, fo, nr], in_=psum[:, pi, :], func=AF.Silu), reads=[R_PS[pi]], acc=[R_GATE[n]])

            def pool_sums(l=l, half=half):
                LT = 16 + ST
                R_PWK = R_XS
                P.op("dve", lambda e: e.tensor_copy(out=halo[:, l, :, :], in_=upool[:, :, ST:ST + 16]), reads=[R_UPOOL], writes=[R_HALO[l]])
                P.op("dve", lambda e: e.tensor_tensor(out=spool[:, 0, :], in0=upool[:, 0, 16:LT], in1=upool[:, 0, 15:LT - 1], op=ALU.add),
                     reads=[R_UPOOL], writes=[R_SPOOL])
                for f in (1, 2, 3):
                    P.op("dve", lambda e, f=f: e.tensor_tensor(out=pwk[:, 0, 1:LT], in0=upool[:, f, 1:LT], in1=upool[:, f, 0:LT - 1], op=ALU.add),
                         reads=[R_UPOOL], writes=R_PWK)
                    if f == 1:
                        P.op("dve", lambda e: e.tensor_tensor(out=spool[:, 1, :], in0=pwk[:, 0, 16:LT], in1=pwk[:, 0, 14:LT - 2], op=ALU.add),
                             reads=R_PWK, acc=[R_SPOOL])
                        continue
                    P.op("dve", lambda e: e.tensor_tensor(out=pwk[:, 1, 3:LT], in0=pwk[:, 0, 3:LT], in1=pwk[:, 0, 1:LT - 2], op=ALU.add),
                         reads=R_PWK, writes=R_PWK)
                    if f == 2:
                        P.op("dve", lambda e: e.tensor_tensor(out=spool[:, 2, :], in0=pwk[:, 1, 16:LT], in1=pwk[:, 1, 12:LT - 4], op=ALU.add),
                             reads=R_PWK, acc=[R_SPOOL])
                        continue
                    P.op("dve", lambda e: e.tensor_tensor(out=pwk[:, 2, 7:LT], in0=pwk[:, 1, 7:LT], in1=pwk[:, 1, 3:LT - 4], op=ALU.add),
                         reads=R_PWK, writes=R_PWK)
                    P.op("dve", lambda e: e.tensor_tensor(out=spool[:, 3, :], in0=pwk[:, 2, 16:LT], in1=pwk[:, 2, 8:LT - 8], op=ALU.add),
                         reads=R_PWK, acc=[R_SPOOL])
                if half == 0:
                    P.op("dve", lambda e: e.tensor_tensor(out=spool[:, :, 0:16], in0=spool[:, :, 0:16], in1=cfix[:], op=ALU.mult),
                         reads=[R_C2], writes=[R_SPOOL])

            P.op("dve", lambda e, l=l: e.tensor_copy(out=upool[:, :, 0:16], in_=halo[:, l, :, :]), reads=[R_HALO[l]], writes=[R_UPOOL])
            for f in range(4):
                for n in range(2):
                    fillers.append(lambda f=f, n=n: pool_in_group(f, n))
            fillers.append(pool_sums)
            for fo in range(8):
                for n in range(2):
                    fillers.append(lambda fo=fo, n=n: gate_group(fo, n))
            fillers.reverse()

            def run_fillers(k):
                for _ in range(k):
                    if fillers:
                        fillers.pop()()

            slots = {}

            def ssm_front(gb, l=l, k0=k0):
                wi = sw_rr[0]
                sw_rr[0] = (wi + 1) % NSW
                ti = stb_rr[0]
                stb_rr[0] = (ti + 1) % NSTB
                qi = gb % NQ
                slots[gb] = (wi, qi)
                P.op("sp", lambda e: e.dma_start(out=SSw[wi][:].rearrange("p a g n -> p (a g n)"), in_=ssmW_d[l, gb]),
                     reads=[R_SSMW[l]], writes=[R_SW[wi]], dma=f"d_sw{wi}")
                P.op("sp", lambda e: e.dma_start(out=SSt[ti][:], in_=tabs_d[l, gb][:, :, :, k0:k0 + NK]),
                     reads=[R_TABS[l]], writes=[R_STB[ti]], dma=f"d_stb{ti}")
                pa, pb_ = next_ps(), next_ps()

                def f_v(e):
                    ins = None
                    for kind, pi in ((1, pa), (2, pb_)):
                        for g4 in range(4):
                            ins = e.matmul(psum[:, pi, g4 * 128:(g4 + 1) * 128], lhsT=SSw[wi][:, kind, g4, :], rhs=U[:, gb * 4 + g4, :],
                                           start=True, stop=True)
                    return ins
                P.op("pe", f_v, reads=[R_SW[wi], R_U], writes=[R_PS[pa], R_PS[pb_]])

                def pv(pi):
                    return psum[:, pi, :].rearrange("p (g k) -> p g k", g=4)
                P.op("dve", lambda e: e.tensor_tensor(out=t1[:], in0=pv(pa), in1=SSt[ti][:, 0], op=ALU.mult),
                     reads=[R_PS[pa], R_STB[ti]], writes=[R_T1])
                P.op("dve", lambda e: e.tensor_tensor(out=t2[:], in0=pv(pb_), in1=SSt[ti][:, 1], op=ALU.mult),
                     reads=[R_PS[pb_], R_STB[ti]], writes=[R_T2])
                P.op("dve", lambda e: e.tensor_tensor(out=t1[:], in0=t1[:], in1=t2[:], op=ALU.add), reads=[R_T1, R_T2], writes=[R_T1])
                P.op("dve", lambda e: e.tensor_copy(out=Q[qi][:, :, 0], in_=carry[:, l, gb * 4:(gb + 1) * 4]),
                     reads=[R_CARRY[l][gb]], writes=[R_Q[qi]])
                for g4 in range(4):
                    g = gb * 4 + g4
                    P.op("dve", lambda e, g4=g4, g=g: e.tensor_tensor_scan(
                        out=Q[qi][:, g4, 1:129], data0=Rb_all[:, l, g:g + 1].to_broadcast([128, 128]), data1=t1[:, g4, :],
                        initial=carry[:, l, g:g + 1], op0=ALU.mult, op1=ALU.add),
                        reads=[R_T1, R_RB, R_CARRY[l][gb]], acc=[R_Q[qi]])
                P.op("dve", lambda e: e.tensor_copy(out=carry[:, l, gb * 4:(gb + 1) * 4], in_=Q[qi][:, :, 128]),
                     reads=[R_Q[qi]], writes=[R_CARRY[l][gb]])
                P.op("dve", lambda e: e.tensor_tensor(out=Pc[qi][:], in0=Q[qi][:, :, 0:128], in1=SSt[ti][:, 0], op=ALU.mult),
                     reads=[R_Q[qi], R_STB[ti]], writes=[R_PCS[qi]])
                P.op("dve", lambda e: e.tensor_tensor(out=Ps[qi][:], in0=Q[qi][:, :, 0:128], in1=SSt[ti][:, 1], op=ALU.mult),
                     reads=[R_Q[qi], R_STB[ti]], acc=[R_PCS[qi]])

            def ssm_back(gb):
                wi, qi = slots[gb]
                py = next_ps()

                def f_y(e):
                    ins = None
                    for g4 in range(4):
                        o = psum[:, py, g4 * 128:(g4 + 1) * 128]
                        e.matmul(o, lhsT=U[:, gb * 4 + g4, :], rhs=SSw[wi][:, 0, g4, :], start=True, stop=False)
                        e.matmul(o, lhsT=Pc[qi][:, g4, :], rhs=SSw[wi][:, 3, g4, :], start=False, stop=False)
                        ins = e.matmul(o, lhsT=Ps[qi][:, g4, :], rhs=SSw[wi][:, 4, g4, :], start=False, stop=True)
                    return ins
                P.op("pe", f_y, reads=[R_SW[wi], R_U, R_PCS[qi]], writes=[R_PS[py]])
                P.op("act", lambda e: e.activation(
                    out=ZY[:, :, gb * 64:(gb + 1) * 64].rearrange("p t (g c) -> p g t c", g=4),
                    in_=psum[:, py, :].rearrange("p (g t c) -> p g t c", g=4, t=8), func=AF.Gelu_apprx_tanh),
                    reads=[R_PS[py]], acc=[R_ZY])

            LAG = 1
            for s in range(8 + LAG):
                if s < 8:
                    ssm_front(s)
                run_fillers(3)
                if s >= LAG:
                    ssm_back(s - LAG)
            run_fillers(len(fillers))

            wg = load_ws(wbf_glu[l].rearrange("(a p) n -> p a n", p=128), 4, [R_WBF[l]])
            P.op("sp", lambda e, wg=wg, l=l: e.dma_start(out=WS[wg][:, 2048:3072], in_=poolW_d[l]),
                 reads=[R_POOLW[l]], acc=[R_WS[wg]], dma=f"d_ws{wg}")
            pwv = WS[wg][:, 2048:3072].rearrange("p (g k n) -> p g k n", g=4, k=2)
            for n in range(2):
                nr = slice(n * 512, (n + 1) * 512)
                for f in range(4):
                    pi = next_ps()

                    def f_pm(e, f=f, n=n, nr=nr, pi=pi, pwv=pwv):
                        e.matmul(psum[:, pi, :], lhsT=pwv[:, f, 0, :], rhs=spool[:, f, nr], start=True, stop=False)
                        return e.matmul(psum[:, pi, :], lhsT=pwv[:, f, 1, :], rhs=upool[:, f, 16 + n * 512:16 + (n + 1) * 512], start=False, stop=True)
                    P.op("pe", f_pm, reads=[R_WS[wg], R_SPOOL, R_UPOOL], writes=[R_PS[pi]])
                    P.op("dve", lambda e, f=f, nr=nr, pi=pi, l=l: e.scalar_tensor_tensor(
                        out=ycat[:, f, nr], in0=psum[:, pi, :], scalar=smallp[:, l, 8 + f:9 + f], in1=gate[:, f, nr], op0=ALU.mult, op1=ALU.mult),
                        reads=[R_PS[pi], R_GATE[n], R_SMALLP], acc=[R_YCAT[n]])
            for f in range(4):
                pi = next_ps()

                def f_tr(e, f=f, pi=pi):
                    ins = None
                    for tau in range(8):
                        ins = e.transpose(out=psB(pi)[:, tau * 128:(tau + 1) * 128], in_=ZY[:, tau, f * 128:(f + 1) * 128], identity=identB[:])
                    return ins
                P.op("pe", f_tr, reads=[R_ZY, R_CONST], writes=[R_PS[pi]])
                eng = evac_eng()
                o_, i_ = ysT[:, f, :].rearrange("p (k t) -> p t k", t=8), psB(pi).rearrange("p (t k) -> p t k", t=8)
                if f == 0:
                    if eng == "act":
                        P.op("act", lambda e, o_=o_, i_=i_: e.activation(out=o_, in_=i_, func=AF.Copy), reads=[R_PS[pi]], writes=[R_SPOOL])
                    else:
                        P.op("dve", lambda e, o_=o_, i_=i_: e.tensor_copy(out=o_, in_=i_), reads=[R_PS[pi]], writes=[R_SPOOL])
                else:
                    copy_op(eng, o_, i_, [R_PS[pi]], [R_SPOOL])
            gluv = WS[wg][:, 0:2048].rearrange("p (a n) -> p a n", n=512)
            wov = wbf_out[l].rearrange("(a p) n -> p a n", p=128)
            wi_o = [load_ws(wov[:, :, so * 512:(so + 1) * 512], 8, [R_WBF[l]]) for so in range(2)]

            def c1_half(n, l=l, gluv=gluv, wg=wg):
                nr = slice(n * 512, (n + 1) * 512)
                for fo in range(4):
                    pi = next_ps()
                    sgi = fo % 2
                    mm_group(pi, lambda kc, fo=fo: gluv[:, kc, fo * 128:(fo + 1) * 128], lambda kc: ysT[:, kc, nr], 4, [R_WS[wg], R_SPOOL])
                    P.op("act", lambda e, fo=fo, pi=pi, sgi=sgi: e.activation(out=sig[:, sgi, :], in_=psum[:, pi, :], func=AF.Sigmoid,
                                                                             bias=smallp[:, l, 12 + fo:13 + fo], scale=1.0),
                         reads=[R_PS[pi], R_SMALLP], writes=[R_SIG[sgi]])
                    P.op("dve", lambda e, fo=fo, sgi=sgi: e.tensor_tensor(out=sig[:, sgi, :], in0=sig[:, sgi, :], in1=ysT[:, fo, nr], op=ALU.mult),
                         reads=[R_SIG[sgi], R_SPOOL], writes=[R_SIG[sgi]])
                    P.op("dve", lambda e, fo=fo, sgi=sgi: e.tensor_tensor(out=ycat[:, 4 + fo, nr], in0=sig[:, sgi, :], in1=gate[:, 4 + fo, nr], op=ALU.mult),
                         reads=[R_SIG[sgi], R_GATE[n]], acc=[R_YCAT[n]])

            def c2_half(n, wi_o=wi_o):
                nr = slice(n * 512, (n + 1) * 512)
                for fo in range(8):
                    wi, f4 = wi_o[fo // 4], fo % 4
                    pi = next_ps()
                    mm_group(pi, lambda kc, wi=wi, f4=f4: wsv(wi)[:, kc, f4 * 128:(f4 + 1) * 128], lambda kc: ycat[:, kc, nr], 8, [R_WS[wi], R_YCAT[n]])
                    P.op("dve", lambda e, fo=fo, pi=pi: e.tensor_tensor(out=xres[:, fo, nr], in0=xres[:, fo, nr], in1=psum[:, pi, :], op=ALU.add),
                         reads=[R_PS[pi], R_XRES[n]], acc=[R_XRES[n]])
            c1_half(0)
            c1_half(1)
            c2_half(0)
            c2_half(1)

        for n in range(2):
            nr = rmsnorm_stats(n)
            for f in range(8):
                P.op("dve", lambda e, f=f, nr=nr, n=n: e.scalar_tensor_tensor(
                    out=xres[:, f, nr], in0=xres[:, f, nr], scalar=finalg[:, f:f + 1], in1=rs[:, n, :], op0=ALU.mult, op1=ALU.mult),
                    reads=[R_RS[n], R_SMALLP, R_XRES[n]], acc=[R_XRES[n]])
        for tt in range(8):
            si = xs_rr[0]
            xs_rr[0] ^= 1
            for fh in range(2):
                pi = next_ps()

                def f_ot(e, tt=tt, fh=fh, pi=pi):
                    ins = None
                    for f4 in range(4):
                        f = fh * 4 + f4
                        ins = e.transpose(out=psum[:, pi, f4 * 128:(f4 + 1) * 128], in_=xres[:, f, tt * 128:(tt + 1) * 128], identity=identF[:])
                    return ins
                P.op("pe", f_ot, reads=[R_XRES[tt // 4], R_CONST], writes=[R_PS[pi]])
                copy_op(evac_eng(), xs[:, si, fh * 512:(fh + 1) * 512], psum[:, pi, :], [R_PS[pi]], [R_XS[si]])
            P.op("sp", lambda e, si=si, tt=tt, seq=seq, t0=t0: e.dma_start(out=out_d[seq, t0 + tt * 128:t0 + (tt + 1) * 128, :], in_=xs[:, si, :]),
                 reads=[R_XS[si]], acc=[R_OUT], dma="d_out")

    P.op("sp", None, reads=[R_OUT], sig=False)
    sems = {k: es.enter_context(nc.semaphore(k)) for k in P.cnt}
    with nc.Block() as block:
        P.emit(nc, block, sems)
    es.close()
    return nc


def prep_inputs(inp):
    f = np.float32
    shared = {}
    shared["w_in"] = np.ascontiguousarray(inp["w_in"], dtype=f)
    shared["w_out"] = np.ascontiguousarray(inp["w_out"], dtype=f)
    shared["glu_w"] = np.ascontiguousarray(inp["glu_w"], dtype=f)
    shared["pool_w"] = np.ascontiguousarray(np.transpose(inp["pool_w"], (0, 2, 1, 3)), dtype=f)
    ng = np.transpose(np.asarray(inp["norm_g"], f).reshape(DEPTH, 8, 128), (2, 0, 1))
    psc = np.transpose(np.asarray(inp["pool_scale"], f).reshape(DEPTH, 4, 128), (2, 0, 1))
    gb = np.transpose(np.asarray(inp["glu_b"], f).reshape(DEPTH, 4, 128), (2, 0, 1))
    shared["smallp"] = np.ascontiguousarray(np.concatenate([ng, psc, gb], axis=2), dtype=f)
    shared["finalg"] = np.ascontiguousarray(np.asarray(inp["final_g"], f).reshape(8, 128).T)

    def dup(a):
        t = np.transpose(np.asarray(a, f), (0, 2, 1))
        return np.concatenate([t, t], axis=1)
    are = dup(inp["a_re"])
    aim = dup(inp["a_im"])
    ldt = np.broadcast_to(np.asarray(inp["log_dt"], f)[:, None, :], (DEPTH, 128, G))
    dv = np.asarray(inp["d_skip"], f).reshape(DEPTH, G, 16)
    dvec = np.broadcast_to(np.transpose(dv, (0, 2, 1))[:, None, :, :], (DEPTH, 8, 16, G)).reshape(DEPTH, 128, G)
    shared["ssm_small"] = np.ascontiguousarray(np.stack([are, aim, ldt, dvec], axis=2), dtype=f)

    def dupb(a):
        t = np.transpose(np.asarray(a, f), (0, 2, 1, 3)).reshape(DEPTH, 64, G * 16)
        return np.concatenate([t, t], axis=1)

    def dupc(a):
        t = np.transpose(np.asarray(a, f), (0, 3, 1, 2)).reshape(DEPTH, 64, G * 16)
        return np.concatenate([t, t], axis=1)
    shared["ssm_bc"] = np.ascontiguousarray(
        np.stack([dupb(inp["b_re"]), dupb(inp["b_im"]), dupc(inp["c_re"]), dupc(inp["c_im"])], axis=2), dtype=f)
    x = np.ascontiguousarray(inp["x"], dtype=f)
    maps = []
    for c in range(NCORES):
        m = dict(shared)
        m["x"] = x[c * NSEQ:(c + 1) * NSEQ]
        maps.append(m)
    return maps


def kernel(**inputs):
    maps = prep_inputs(inputs)
    nc = build_program()
    res = run_bass_kernel_spmd(nc, maps, core_ids=list(range(NCORES)))
    out = np.concatenate([np.asarray(r["out"], dtype=np.float32) for r in res.results], axis=0)
    return out
```

```python
import math
from contextlib import ExitStack

import numpy as np
import concourse.bass as bass
import concourse.mybir as mybir
from concourse.bass_utils import run_bass_kernel_spmd

F32 = mybir.dt.float32
BF16 = mybir.dt.bfloat16
I32 = mybir.dt.int32
AF = mybir.ActivationFunctionType
ALU = mybir.AluOpType

NCORES = 8
DEPTH = 4
D = 1024
SEQ = 2048
NSEQ = 2
ST = 1024
NK = ST // 8
G = 32
TWO_PI = float(2.0 * math.pi)
INV_2PI = float(1.0 / (2.0 * math.pi))
HALF_PI = float(math.pi / 2.0)
SIN_SCALE = 1.0 - 4e-5
EPS = 1e-5
POOL_WINDOWS = (2, 4, 8, 16)


class Res:
    __slots__ = ("name", "w", "r")

    def __init__(self, name):
        self.name = name
        self.w = {}
        self.r = {}


class Prog:
    ENG = ("pe", "act", "dve", "pool", "sp")

    def __init__(self):
        self.ops = {e: [] for e in self.ENG}
        self.cnt = {}
        self.seen = {e: {} for e in self.ENG}
        self.pending = {e: {} for e in self.ENG}

    def fence(self):
        for e in self.ENG:
            self.pending[e] = dict(self.cnt)

    def op(self, eng, fn, reads=(), writes=(), dma=None, sig=True, acc=()):
        need = self.pending[eng]
        self.pending[eng] = {}
        for r in acc:
            for k, v in r.r.items():
                if need.get(k, 0) < v:
                    need[k] = v
        for r in reads:
            for k, v in r.w.items():
                if need.get(k, 0) < v:
                    need[k] = v
        for r in writes:
            for k, v in r.w.items():
                if need.get(k, 0) < v:
                    need[k] = v
            for k, v in r.r.items():
                if need.get(k, 0) < v:
                    need[k] = v
        waits = []
        seen = self.seen[eng]
        for k, v in need.items():
            if eng == "pe" and k == "pe":
                continue
            if seen.get(k, 0) >= v:
                continue
            seen[k] = v
            waits.append((k, v))
        tok = None
        inc = 1
        if sig:
            key = dma if dma is not None else eng
            inc = 16 if dma is not None else 1
            self.cnt[key] = self.cnt.get(key, 0) + inc
            tok = (key, self.cnt[key])
            for r in reads:
                if r.r.get(key, 0) < tok[1]:
                    r.r[key] = tok[1]
            for r in writes:
                r.w = {key: tok[1]}
            for r in acc:
                if r.w.get(key, 0) < tok[1]:
                    r.w[key] = tok[1]
        self.ops[eng].append((waits, fn, tok, inc))
        return tok

    def emit(self, nc, block, sems):
        def mk(name):
            def body(e):
                for waits, fn, tok, inc in self.ops[name]:
                    for k, v in waits:
                        e.wait_ge(sems[k], v)
                    if fn is None:
                        continue
                    ins = fn(e)
                    if tok is not None:
                        ins.then_inc(sems[tok[0]], inc)
            return body

        block.tensor(mk("pe"))
        block.scalar(mk("act"))
        block.vector(mk("dve"))
        block.gpsimd(mk("pool"))
        block.sync(mk("sp"))


def build_program(debug=False, n_layers=DEPTH, n_sub=4):
    nc = bass.Bass("TRN2", target_bir_lowering=False)
    P = Prog()
    es = ExitStack()

    def dram_in(name, shape, dt=F32):
        return nc.dram_tensor(name, list(shape), dt, kind="ExternalInput").ap()

    def dram_out(name, shape, dt=F32):
        return nc.dram_tensor(name, list(shape), dt, kind="ExternalOutput").ap()

    def dram_scr(name, shape, dt):
        kind = "ExternalOutput" if debug else "Internal"
        return nc.dram_tensor(name, list(shape), dt, kind=kind).ap()

    def sb(name, shape, dt, stack=None):
        return (stack or es).enter_context(nc.sbuf_tensor(name, list(shape), dt))

    x_d = dram_in("x", [NSEQ, SEQ, D])
    w_in_d = dram_in("w_in", [DEPTH, D, 2 * D])
    w_out_d = dram_in("w_out", [DEPTH, D, D])
    glu_w_d = dram_in("glu_w", [DEPTH, 512, 512])
    pool_w_d = dram_in("pool_w", [DEPTH, 128, 4, 128])
    smallp_d = dram_in("smallp", [128, DEPTH, 16])
    finalg_d = dram_in("finalg", [128, 8])
    ssm_small_d = dram_in("ssm_small", [DEPTH, 128, 4, 32])
    ssm_bc_d = dram_in("ssm_bc", [DEPTH, 128, 4, 512])
    out_d = dram_out("out", [NSEQ, SEQ, D])

    ssmW_d = dram_scr("ssmW", [DEPTH, 8, 128, 5 * 4 * 128], BF16)
    tabs_d = dram_scr("tabs", [DEPTH, 8, 128, 2, 4, 256], F32)
    poolW_d = dram_scr("poolW", [DEPTH, 128, 4 * 2 * 128], BF16)
    R_SSMW = [Res(f"ssmW{l}") for l in range(DEPTH)]
    R_TABS = [Res(f"tabs{l}") for l in range(DEPTH)]
    R_POOLW = [Res(f"poolW{l}") for l in range(DEPTH)]
    R_OUT = Res("out")

    identF = sb("identF", [128, 128], F32)
    identB = sb("identB", [128, 128], BF16)
    onesB = sb("onesB", [128, 128], BF16)
    Rb_all = sb("Rb_all", [128, DEPTH, 32], F32)
    smallp = sb("smallp_sb", [128, DEPTH, 16], F32)
    finalg = sb("finalg_sb", [128, 8], F32)
    psum = es.enter_context(nc.psum_tensor("psum", [128, 8, 512], F32))
    R_CONST = Res("const")
    R_RB = Res("Rb")
    R_SMALLP = Res("smallp")
    R_PS = [Res(f"ps{i}") for i in range(8)]
    ps_rr = [0]

    def next_ps():
        i = ps_rr[0]
        ps_rr[0] = (i + 1) % 8
        return i

    P.op("pool", lambda e: e.memset(identF[:], 1.0), writes=[R_CONST])
    P.op("pool", lambda e: e.affine_select(out=identF[:], in_=identF[:], pattern=[[-1, 128]],
                                           compare_op=ALU.is_equal, fill=0.0, base=0, channel_multiplier=1),
         reads=[R_CONST], writes=[R_CONST])
    P.op("pool", lambda e: e.tensor_copy(out=identB[:], in_=identF[:]), reads=[R_CONST], writes=[R_CONST])
    P.op("pool", lambda e: e.memset(onesB[:], 1.0 / 1024.0), writes=[R_CONST])
    P.op("sp", lambda e: e.dma_start(out=smallp[:], in_=smallp_d[:]), writes=[R_SMALLP], dma="d_small")
    P.op("sp", lambda e: e.dma_start(out=finalg[:], in_=finalg_d[:]), writes=[R_SMALLP], dma="d_small")

    with ExitStack() as ps_:
        def pb(name, shape, dt):
            return sb("pl_" + name, shape, dt, ps_)
        mask = pb("mask", [128, 128], F32)
        iotaKi = pb("iotaKi", [128, 256], I32)
        iotaK = pb("iotaK", [128, 256], F32)
        JTi = pb("JTi", [128, 7, 32], I32)
        JT = pb("JT", [128, 7, 32], F32)
        sm = pb("sm", [128, 4, 32], F32)
        bc = pb("bc", [128, 4, 512], F32)
        pw = pb("pw", [128, 4, 128], F32)
        PWo = pb("PWo", [128, 4, 2, 128], BF16)
        dtt = pb("dtt", [128, 32], F32)
        ard = pb("ard", [128, 32], F32)
        ang = pb("ang", [128, 32], F32)
        ARJ = pb("ARJ", [128, 7, 32], F32)
        ANJ = pb("ANJ", [128, 7, 32], F32)
        MAGP = pb("MAGP", [128, 7, 32], F32)
        MAGN = pb("MAGN", [128, 7, 32], F32)
        NIj = pb("NIj", [128, 7, 32], I32)
        RRj = pb("RRj", [128, 7, 32], F32)
        SN = pb("SN", [128, 7, 32], F32)
        CS = pb("CS", [128, 7, 32], F32)
        EXre = pb("EXre", [128, 32, 8], F32)
        EXim = pb("EXim", [128, 32, 8], F32)
        EYre = pb("EYre", [128, 32, 8], F32)
        EYim = pb("EYim", [128, 32, 8], F32)
        s1 = pb("s1", [128, 32], F32)
        s2 = pb("s2", [128, 32], F32)
        s3 = pb("s3", [128, 32], F32)
        s4 = pb("s4", [128, 32], F32)
        fre = pb("fre", [128, 32], F32)
        fim = pb("fim", [128, 32], F32)
        Rt = pb("Rt", [128, 32], F32)
        nRt = pb("nRt", [128, 32], F32)
        TH = pb("TH", [128, 32], F32)
        NI8 = pb("NI8", [128, 32], I32)
        ta = pb("ta", [128, 32, 16], F32)
        tb_ = pb("tb", [128, 32, 16], F32)
        tc_ = pb("tc", [128, 32, 16], F32)
        td = pb("td", [128, 32, 16], F32)
        bbA = pb("bbA", [128, 32, 16], F32)
        bbB = pb("bbB", [128, 32, 16], F32)
        cA = pb("cA", [128, 32, 16], F32)
        cB = pb("cB", [128, 32, 16], F32)
        Smat = pb("Smat", [128, 128], F32)
        hpiT = pb("hpiT", [128, 1], F32)
        T1 = pb("T1", [128, 32, 8, 16], F32)
        T2 = pb("T2", [128, 32, 8, 16], F32)
        XBm = pb("XBm", [128, 32, 8, 16], F32)
        Gm = pb("Gm", [128, 32, 8, 16], F32)
        W5 = pb("W5", [128, 8, 5, 4, 128], BF16)
        tmpT = [pb(f"tmpT{i}", [128, 4, 128], F32) for i in range(2)]
        NTB = 2
        ANGq = [pb(f"ANGq{i}", [128, 4, 256], F32) for i in range(NTB)]
        NIq = [pb(f"NIq{i}", [128, 4, 256], I32) for i in range(NTB)]
        RRq = [pb(f"RRq{i}", [128, 4, 256], F32) for i in range(NTB)]
        ABq = [pb(f"ABq{i}", [128, 4, 256], F32) for i in range(NTB)]
        CSq = [pb(f"CSq{i}", [128, 2, 4, 256], F32) for i in range(NTB)]
        R_TQ = [Res(f"pl_tq{i}") for i in range(NTB)]
        R_CSQ = [Res(f"pl_csq{i}") for i in range(NTB)]
        R_BB = Res("pl_bb")

        R_PC = Res("pl_const")
        R_IN = Res("pl_in")
        R_A = Res("pl_a")
        R_E = Res("pl_E")
        R_X = Res("pl_X")
        R_T = Res("pl_T")
        R_W5 = Res("pl_W5")
        R_TMPT = [Res("pl_tmpT0"), Res("pl_tmpT1")]
        R_ANG = Res("pl_ang")
        R_TAB = Res("pl_tab")
        R_PW = Res("pl_pw")

        P.op("pool", lambda e: e.memset(mask[:], 1.0), writes=[R_PC])
        P.op("pool", lambda e: e.affine_select(out=mask[:], in_=mask[:], pattern=[[16, 8], [0, 16]],
                                               compare_op=ALU.is_ge, fill=0.0, base=15, channel_multiplier=-1),
             reads=[R_PC], writes=[R_PC])
        P.op("pool", lambda e: e.iota(iotaKi[:], pattern=[[1, 256]], base=0, channel_multiplier=0), writes=[R_PC])
        P.op("pool", lambda e: e.tensor_copy(out=iotaK[:], in_=iotaKi[:]), reads=[R_PC], writes=[R_PC])
        P.op("pool", lambda e: e.iota(JTi[:], pattern=[[-1, 7], [0, 32]], base=7, channel_multiplier=0), writes=[R_PC])
        P.op("pool", lambda e: e.tensor_copy(out=JT[:], in_=JTi[:]), reads=[R_PC], writes=[R_PC])
        P.op("pool", lambda e: e.tensor_copy(out=Smat[:, 0:64], in_=identF[:, 64:128]), reads=[R_CONST], writes=[R_PC])
        P.op("pool", lambda e: e.tensor_scalar(out=Smat[:, 64:128], in0=identF[:, 0:64], scalar1=-1.0, scalar2=None, op0=ALU.mult),
             reads=[R_CONST, R_PC], writes=[R_PC])
        P.op("pool", lambda e: e.memset(hpiT[:], HALF_PI), reads=[R_PC], writes=[R_PC])

        def bc3(ap32):
            return ap32.unsqueeze(1).to_broadcast([128, 7, 32])

        def bcc(ap32):
            return ap32.unsqueeze(2).to_broadcast([128, 32, 16])

        def bE(apE, lo, hi):
            return apE[lo:hi].unsqueeze(3).to_broadcast([hi - lo, 32, 8, 16])

        def bB(apB, lo, hi):
            return apB[lo:hi].unsqueeze(2).to_broadcast([hi - lo, 32, 8, 16])

        def bR(apR, lo, hi):
            return apR[lo:hi].rearrange("p (a g) -> p a g", g=4).unsqueeze(3).to_broadcast([hi - lo, 8, 4, 128])

        for l in range(n_layers):
            P.op("sp", lambda e, l=l: e.dma_start(out=sm[:], in_=ssm_small_d[l]), writes=[R_IN], dma="d_plin")
            P.op("sp", lambda e, l=l: e.dma_start(out=bc[:], in_=ssm_bc_d[l]), writes=[R_IN], dma="d_plin")
            P.op("sp", lambda e, l=l: e.dma_start(out=pw[:], in_=pool_w_d[l]), writes=[R_PW], dma="d_plpw")
            are, aim, ldt, dvec = sm[:, 0, :], sm[:, 1, :], sm[:, 2, :], sm[:, 3, :]
            bre = bc[:, 0, :].rearrange("p (g c) -> p g c", c=16)
            bim = bc[:, 1, :].rearrange("p (g c) -> p g c", c=16)
            cre = bc[:, 2, :].rearrange("p (g c) -> p g c", c=16)
            cim = bc[:, 3, :].rearrange("p (g c) -> p g c", c=16)

            for g4 in range(4):
                P.op("pool", lambda e, g4=g4: e.tensor_scalar(out=PWo[:, g4, 0, :], in0=pw[:, g4, :],
                                                            scalar1=1.0 / POOL_WINDOWS[g4], scalar2=None, op0=ALU.mult),
                     reads=[R_PW], writes=[R_PW])
            P.op("pool", lambda e: e.tensor_scalar(out=PWo[:, :, 1, :], in0=pw[:, :, :], scalar1=-1.0, scalar2=None, op0=ALU.mult),
                 reads=[R_PW], writes=[R_PW])
            P.op("sp", lambda e, l=l: e.dma_start(out=poolW_d[l], in_=PWo[:].rearrange("p a b c -> p (a b c)")),
                 reads=[R_PW], writes=[R_POOLW[l]], dma=f"d_plpo{l}")

            P.op("act", lambda e: e.activation(out=dtt[:], in_=ldt, func=AF.Exp), reads=[R_IN], writes=[R_A])
            P.op("dve", lambda e: e.tensor_tensor(out=ard[:], in0=are, in1=dtt[:], op=ALU.mult), reads=[R_IN, R_A], writes=[R_A])
            P.op("dve", lambda e: e.tensor_tensor(out=ang[:], in0=aim, in1=dtt[:], op=ALU.mult), reads=[R_IN, R_A], writes=[R_A])
            P.op("dve", lambda e: e.tensor_tensor(out=ARJ[:], in0=JT[:], in1=bc3(ard[:]), op=ALU.mult), reads=[R_PC, R_A], writes=[R_A])
            P.op("dve", lambda e: e.tensor_tensor(out=ANJ[:], in0=JT[:], in1=bc3(ang[:]), op=ALU.mult), reads=[R_PC, R_A], writes=[R_A])
            P.op("act", lambda e: e.activation(out=MAGP[:], in_=ARJ[:], func=AF.Exp), reads=[R_A], writes=[R_A])
            P.op("act", lambda e: e.activation(out=MAGN[:], in_=ARJ[:], func=AF.Exp, scale=-1.0), reads=[R_A], writes=[R_A])
            P.op("act", lambda e: e.activation(out=NIj[:], in_=ANJ[:], func=AF.Copy, scale=INV_2PI), reads=[R_A], writes=[R_A])
            P.op("dve", lambda e: e.scalar_tensor_tensor(out=RRj[:], in0=NIj[:], scalar=-TWO_PI, in1=ANJ[:], op0=ALU.mult, op1=ALU.add),
                 reads=[R_A], writes=[R_A])
            P.op("act", lambda e: e.activation(out=SN[:], in_=RRj[:], func=AF.Sin, scale=SIN_SCALE), reads=[R_A], writes=[R_A])
            P.op("dve", lambda e: e.tensor_scalar(out=RRj[:], in0=ANJ[:], scalar1=HALF_PI, scalar2=None, op0=ALU.add), reads=[R_A], writes=[R_A])
            P.op("act", lambda e: e.activation(out=NIj[:], in_=RRj[:], func=AF.Copy, scale=INV_2PI), reads=[R_A], writes=[R_A])
            P.op("dve", lambda e: e.scalar_tensor_tensor(out=RRj[:], in0=NIj[:], scalar=-TWO_PI, in1=RRj[:], op0=ALU.mult, op1=ALU.add),
                 reads=[R_A], writes=[R_A])
            P.op("act", lambda e: e.activation(out=CS[:], in_=RRj[:], func=AF.Sin, scale=SIN_SCALE), reads=[R_A], writes=[R_A])
            def Ev(t):
                return t[:].rearrange("p g t -> p t g")[:, 0:7, :]
            P.op("dve", lambda e: e.tensor_tensor(out=Ev(EXre), in0=MAGP[:], in1=CS[:], op=ALU.mult), reads=[R_A], writes=[R_E])
            P.op("dve", lambda e: e.tensor_tensor(out=Ev(EXim), in0=MAGP[:], in1=SN[:], op=ALU.mult), reads=[R_A], writes=[R_E])
            P.op("dve", lambda e: e.tensor_tensor(out=Ev(EYre), in0=MAGN[:], in1=CS[:], op=ALU.mult), reads=[R_A], writes=[R_E])
            P.op("dve", lambda e: e.scalar_tensor_tensor(out=Ev(EYim), in0=MAGN[:], scalar=-1.0, in1=SN[:], op0=ALU.mult, op1=ALU.mult),
                 reads=[R_A], writes=[R_E])
            P.op("dve", lambda e: e.memset(EXre[:, :, 7:8], 1.0), writes=[R_E])
            P.op("dve", lambda e: e.memset(EXim[:, :, 7:8], 0.0), writes=[R_E])
            P.op("dve", lambda e: e.memset(EYre[:, :, 7:8], 1.0), writes=[R_E])
            P.op("dve", lambda e: e.memset(EYim[:, :, 7:8], 0.0), writes=[R_E])
            lre, lim = EXre[:, :, 6], EXim[:, :, 6]
            P.op("dve", lambda e: e.tensor_scalar(out=s1[:], in0=lre, scalar1=-1.0, scalar2=None, op0=ALU.add), reads=[R_E], writes=[R_A])
            P.op("dve", lambda e: e.tensor_tensor(out=s2[:], in0=are, in1=are, op=ALU.mult), reads=[R_IN], writes=[R_A])
            P.op("dve", lambda e: e.tensor_tensor(out=s3[:], in0=aim, in1=aim, op=ALU.mult), reads=[R_IN], writes=[R_A])
            P.op("dve", lambda e: e.tensor_tensor(out=s2[:], in0=s2[:], in1=s3[:], op=ALU.add), reads=[R_A], writes=[R_A])
            P.op("dve", lambda e: e.reciprocal(out=s2[:], in_=s2[:]), reads=[R_A], writes=[R_A])
            P.op("dve", lambda e: e.tensor_tensor(out=s3[:], in0=s1[:], in1=are, op=ALU.mult), reads=[R_A, R_IN], writes=[R_A])
            P.op("dve", lambda e: e.tensor_tensor(out=s4[:], in0=lim, in1=aim, op=ALU.mult), reads=[R_E, R_IN], writes=[R_A])
            P.op("dve", lambda e: e.tensor_tensor(out=s3[:], in0=s3[:], in1=s4[:], op=ALU.add), reads=[R_A], writes=[R_A])
            P.op("dve", lambda e: e.tensor_tensor(out=fre[:], in0=s3[:], in1=s2[:], op=ALU.mult), reads=[R_A], writes=[R_A])
            P.op("dve", lambda e: e.tensor_tensor(out=s3[:], in0=lim, in1=are, op=ALU.mult), reads=[R_E, R_IN], writes=[R_A])
            P.op("dve", lambda e: e.tensor_tensor(out=s4[:], in0=s1[:], in1=aim, op=ALU.mult), reads=[R_A, R_IN], writes=[R_A])
            P.op("dve", lambda e: e.tensor_tensor(out=s3[:], in0=s3[:], in1=s4[:], op=ALU.subtract), reads=[R_A], writes=[R_A])
            P.op("dve", lambda e: e.tensor_tensor(out=fim[:], in0=s3[:], in1=s2[:], op=ALU.mult), reads=[R_A], writes=[R_A])
            P.op("dve", lambda e: e.tensor_tensor(out=ta[:], in0=bre, in1=bcc(fre[:]), op=ALU.mult), reads=[R_A, R_IN], writes=[R_BB])
            P.op("dve", lambda e: e.tensor_tensor(out=tb_[:], in0=bim, in1=bcc(fim[:]), op=ALU.mult), reads=[R_A, R_IN], acc=[R_BB])
            P.op("dve", lambda e: e.tensor_tensor(out=tc_[:], in0=bim, in1=bcc(fre[:]), op=ALU.mult), reads=[R_A, R_IN], acc=[R_BB])
            P.op("dve", lambda e: e.tensor_tensor(out=td[:], in0=bre, in1=bcc(fim[:]), op=ALU.mult), reads=[R_A, R_IN], acc=[R_BB])
            P.op("dve", lambda e: e.tensor_tensor(out=bbA[0:64], in0=ta[0:64], in1=tb_[0:64], op=ALU.subtract), reads=[R_BB], acc=[R_BB])
            P.op("dve", lambda e: e.tensor_tensor(out=bbA[64:128], in0=tc_[64:128], in1=td[64:128], op=ALU.add), reads=[R_BB], acc=[R_BB])
            P.op("dve", lambda e: e.scalar_tensor_tensor(out=bbB[0:64], in0=tc_[0:64], scalar=-1.0, in1=td[0:64], op0=ALU.mult, op1=ALU.subtract),
                 reads=[R_BB], acc=[R_BB])
            P.op("dve", lambda e: e.tensor_tensor(out=bbB[64:128], in0=ta[64:128], in1=tb_[64:128], op=ALU.subtract), reads=[R_BB], acc=[R_BB])
            P.op("act", lambda e: e.activation(out=cA[0:64], in_=cre[0:64], func=AF.Copy), reads=[R_IN], acc=[R_BB])
            P.op("act", lambda e: e.activation(out=cA[64:128], in_=cim[64:128], func=AF.Copy, scale=-1.0), reads=[R_IN], acc=[R_BB])
            P.op("act", lambda e: e.activation(out=cB[0:64], in_=cim[0:64], func=AF.Copy, scale=-1.0), reads=[R_IN], acc=[R_BB])
            P.op("act", lambda e: e.activation(out=cB[64:128], in_=cre[64:128], func=AF.Copy, scale=-1.0), reads=[R_IN], acc=[R_BB])
            P.op("act", lambda e: e.activation(out=Rt[:], in_=ard[:], func=AF.Exp, scale=8.0), reads=[R_A], writes=[R_A])
            P.op("act", lambda e, l=l: e.activation(out=Rb_all[:, l, :], in_=Rt[:], func=AF.Copy), reads=[R_A], writes=[R_RB])
            P.op("dve", lambda e: e.tensor_scalar(out=s1[:], in0=ang[:], scalar1=8.0, scalar2=None, op0=ALU.mult), reads=[R_A], writes=[R_A])
            P.op("act", lambda e: e.activation(out=NI8[:], in_=s1[:], func=AF.Copy, scale=INV_2PI), reads=[R_A], writes=[R_A])
            P.op("dve", lambda e: e.scalar_tensor_tensor(out=TH[:], in0=NI8[:], scalar=-TWO_PI, in1=s1[:], op0=ALU.mult, op1=ALU.add),
                 reads=[R_A], writes=[R_A])

            tv = tabs_d[l].rearrange("a p two g k -> p a two g k")
            tq_rr = [0]

            def table_piece(gb, l=l, tv=tv):
                i = tq_rr[0]
                tq_rr[0] = (i + 1) % NTB
                gs = slice(gb * 4, gb * 4 + 4)
                P.op("dve", lambda e: e.tensor_tensor(out=ANGq[i][:], in0=TH[:, gs].unsqueeze(2).to_broadcast([128, 4, 256]),
                                                     in1=iotaK[:].unsqueeze(1).to_broadcast([128, 4, 256]), op=ALU.mult),
                     reads=[R_A, R_PC], writes=[R_TQ[i]])
                P.op("act", lambda e: e.activation(out=NIq[i][:], in_=ANGq[i][:], func=AF.Copy, scale=INV_2PI), reads=[R_TQ[i]], acc=[R_TQ[i]])
                P.op("dve", lambda e: e.scalar_tensor_tensor(out=RRq[i][:], in0=NIq[i][:], scalar=-TWO_PI, in1=ANGq[i][:], op0=ALU.mult, op1=ALU.add),
                     reads=[R_TQ[i]], acc=[R_TQ[i]])
                P.op("act", lambda e: e.activation(out=ABq[i][:], in_=RRq[i][:], func=AF.Abs), reads=[R_TQ[i]], acc=[R_TQ[i]])
                P.op("act", lambda e: e.activation(out=CSq[i][:, 1], in_=RRq[i][:], func=AF.Sin, scale=SIN_SCALE), reads=[R_TQ[i]], writes=[R_CSQ[i]])
                P.op("act", lambda e: e.activation(out=CSq[i][:, 0], in_=ABq[i][:], func=AF.Sin, scale=-1.0, bias=hpiT[:]),
                     reads=[R_TQ[i], R_PC], acc=[R_CSQ[i]])
                P.op("sp", lambda e: e.dma_start(out=tv[:, gb], in_=CSq[i][:]), reads=[R_CSQ[i]], acc=[R_TABS[l]], dma=f"d_pltab{l}")

            bigops = [
                lambda: P.op("dve", lambda e: e.tensor_tensor(out=T1[:], in0=bE(EXre, 0, 128), in1=bB(bbA, 0, 128), op=ALU.mult), reads=[R_E, R_BB], writes=[R_T]),
                lambda: P.op("dve", lambda e: e.tensor_tensor(out=T2[:], in0=bE(EXim, 0, 128), in1=bB(bbB, 0, 128), op=ALU.mult), reads=[R_E, R_BB], acc=[R_T]),
                lambda: P.op("dve", lambda e: e.tensor_tensor(out=XBm[:], in0=T1[:], in1=T2[:], op=ALU.add), reads=[R_T], writes=[R_X]),
                lambda: P.op("dve", lambda e: e.tensor_tensor(out=T1[:], in0=bE(EYre, 0, 128), in1=bB(cA, 0, 128), op=ALU.mult), reads=[R_E, R_BB], writes=[R_T]),
                lambda: P.op("dve", lambda e: e.tensor_tensor(out=T2[:], in0=bE(EYim, 0, 128), in1=bB(cB, 0, 128), op=ALU.mult), reads=[R_E, R_BB], acc=[R_T]),
                lambda: P.op("dve", lambda e: e.tensor_tensor(out=Gm[:], in0=T1[:], in1=T2[:], op=ALU.add), reads=[R_T], acc=[R_X]),
            ]
            for i_, bo in enumerate(bigops):
                table_piece(i_)
                bo()

            for gb in range(8):
                if gb in (2, 5):
                    table_piece(6 + (gb == 5))
                ti = gb % 2
                pi = next_ps()

                def f_toep(e, gb=gb, pi=pi):
                    ins = None
                    for g4 in range(4):
                        g = gb * 4 + g4
                        ins = e.matmul(psum[:, pi, g4 * 128:(g4 + 1) * 128],
                                       lhsT=XBm[:, g].rearrange("p t c -> p (t c)"),
                                       rhs=Gm[:, g].rearrange("p t c -> p (t c)"), start=True, stop=True)
                    return ins
                P.op("pe", f_toep, reads=[R_X], writes=[R_PS[pi]])
                P.op("dve", lambda e, pi=pi, ti=ti: e.tensor_tensor(out=tmpT[ti][:], in0=psum[:, pi, :].rearrange("p (g n) -> p g n", g=4),
                                                                   in1=mask[:].unsqueeze(1).to_broadcast([128, 4, 128]), op=ALU.mult),
                     reads=[R_PS[pi], R_PC], writes=[R_TMPT[ti]])
                for g4 in range(4):
                    P.op("dve", lambda e, gb=gb, g4=g4, ti=ti: e.scalar_tensor_tensor(
                        out=W5[:, gb, 0, g4, :], in0=identF[:], scalar=sm[:, 3, gb * 4 + g4:gb * 4 + g4 + 1], in1=tmpT[ti][:, g4, :],
                        op0=ALU.mult, op1=ALU.add), reads=[R_TMPT[ti], R_CONST, R_IN], acc=[R_W5])
                pi2 = next_ps()

                def f_tr(e, gb=gb, pi2=pi2):
                    ins = None
                    for g4 in range(4):
                        g = gb * 4 + g4
                        ins = e.transpose(out=psum[:, pi2, g4 * 128:(g4 + 1) * 128],
                                          in_=XBm[:, g].rearrange("p t c -> p (t c)"), identity=identF[:])
                    return ins
                P.op("pe", f_tr, reads=[R_X, R_CONST], writes=[R_PS[pi2]])
                pv = psum[:, pi2, :].rearrange("p (g n) -> p g n", g=4)
                P.op("act", lambda e, gb=gb, pv=pv: e.activation(out=W5[:, gb, 1, :, :], in_=pv, func=AF.Copy),
                     reads=[R_PS[pi2]], acc=[R_W5])
                P.op("act", lambda e, gb=gb, pv=pv: e.activation(out=W5[:, gb, 2, :, 0:64], in_=pv[:, :, 64:128], func=AF.Copy),
                     reads=[R_PS[pi2]], acc=[R_W5])
                P.op("act", lambda e, gb=gb, pv=pv: e.activation(out=W5[:, gb, 2, :, 64:128], in_=pv[:, :, 0:64], func=AF.Copy, scale=-1.0),
                     reads=[R_PS[pi2]], acc=[R_W5])
                pi3 = next_ps()

                def f_sw(e, gb=gb, pi3=pi3):
                    ins = None
                    for g4 in range(4):
                        g = gb * 4 + g4
                        ins = e.matmul(psum[:, pi3, g4 * 128:(g4 + 1) * 128], lhsT=Smat[:],
                                       rhs=Gm[:, g].rearrange("p t c -> p (t c)"), start=True, stop=True)
                    return ins
                P.op("pe", f_sw, reads=[R_X, R_PC], writes=[R_PS[pi3]])
                for g4 in range(4):
                    g = gb * 4 + g4
                    P.op("act", lambda e, gb=gb, g4=g4, g=g: e.activation(out=W5[:, gb, 3, g4, :], in_=Gm[:, g].rearrange("p t c -> p (t c)"),
                                                                         func=AF.Copy, scale=Rt[:, g:g + 1]),
                         reads=[R_X, R_A], acc=[R_W5])
                    P.op("act", lambda e, gb=gb, g4=g4, g=g, pi3=pi3: e.activation(out=W5[:, gb, 4, g4, :], in_=psum[:, pi3, g4 * 128:(g4 + 1) * 128],
                                                                                  func=AF.Copy, scale=Rt[:, g:g + 1]),
                         reads=[R_PS[pi3], R_A], acc=[R_W5])
            P.op("sp", lambda e, l=l: e.dma_start(out=ssmW_d[l].rearrange("a p n -> p a n"),
                                                  in_=W5[:].rearrange("p a k g n -> p a (k g n)")),
                 reads=[R_W5], writes=[R_SSMW[l]], dma=f"d_plssm{l}")

    wbf_in = dram_scr("wbf_in", [DEPTH, D, 2 * D], BF16)
    wbf_out = dram_scr("wbf_out", [DEPTH, D, D], BF16)
    wbf_glu = dram_scr("wbf_glu", [DEPTH, 512, 512], BF16)
    R_WBF = [Res(f"wbf{l}") for l in range(DEPTH)]
    for l in range(n_layers):
        for a in range(8):
            P.op("pool", lambda e, l=l, a=a: e.dma_start(out=wbf_in[l, a * 128:(a + 1) * 128, :], in_=w_in_d[l, a * 128:(a + 1) * 128, :]),
                 writes=[R_WBF[l]], dma=f"d_wbf{l}")
        for a in range(4):
            P.op("pool", lambda e, l=l, a=a: e.dma_start(out=wbf_out[l, a * 256:(a + 1) * 256, :], in_=w_out_d[l, a * 256:(a + 1) * 256, :]),
                 writes=[R_WBF[l]], dma=f"d_wbf{l}")
        P.op("pool", lambda e, l=l: e.dma_start(out=wbf_glu[l], in_=glu_w_d[l]), writes=[R_WBF[l]], dma=f"d_wbf{l}")

    if debug == "prologue":
        rb_d = dram_out("rb_dbg", [128, DEPTH, 32])
        P.op("sp", lambda e: e.dma_start(out=rb_d[:], in_=Rb_all[:]), reads=[R_RB], writes=[R_OUT], dma="d_out")
        fin = [R_OUT] + R_SSMW[:n_layers] + R_TABS[:n_layers] + R_POOLW[:n_layers] + R_WBF[:n_layers]
        P.op("sp", None, reads=fin, sig=False)
        sems = {k: es.enter_context(nc.semaphore(k)) for k in P.cnt}
        with nc.Block() as block:
            P.emit(nc, block, sems)
        es.close()
        return nc

    P.fence()
    xres = sb("xres", [128, 8, ST], F32)
    h = sb("h", [128, 8, ST], BF16)
    gate = sb("gate", [128, 8, ST], BF16)
    ycat = sb("ycat", [128, 8, ST], BF16)
    upool = sb("upool", [128, 4, 16 + ST], BF16)
    spool = sb("spool", [128, 4, ST], BF16)
    ysT = spool
    xsq = spool[:].rearrange("p a n -> p (a n)").rearrange("p (a n) -> p a n", n=512)
    ZYf = sb("ZY", [128, 4096], BF16)
    ZY = ZYf[:].rearrange("p (t f) -> p t f", t=8)
    Zs = ZYf[:].rearrange("p (g t c) -> p g t c", g=32, t=8)
    U = sb("U", [128, 32, 128], BF16)
    rs = sb("rs", [128, 2, 512], F32)
    t1 = sb("t1", [128, 4, 128], F32)
    t2 = sb("t2", [128, 4, 128], F32)
    NQ = 3
    Q = [sb(f"Q{i}", [128, 4, 129], F32) for i in range(NQ)]
    Pc = [sb(f"Pc{i}", [128, 4, 128], BF16) for i in range(NQ)]
    Ps = [sb(f"Ps{i}", [128, 4, 128], BF16) for i in range(NQ)]
    sig = sb("sig", [128, 2, 512], BF16)
    xs = sb("xs", [128, 2, D], F32)
    pwk = xs[:].rearrange("p a n -> p (a n)").bitcast(BF16)[:, 0:3 * (16 + ST)].rearrange("p (a n) -> p a n", a=3)
    carry = sb("carry", [128, DEPTH, 32], F32)
    halo = sb("halo", [128, DEPTH, 4, 16], BF16)
    cfix = sb("cfix", [128, 4, 16], F32)
    epsT = sb("epsT", [128, 1], F32)
    NWS, NSW, NSTB = 4, 3, 3
    WS = [sb(f"WS{i}", [128, 8 * 512], BF16) for i in range(NWS)]
    SSw = [sb(f"SSw{i}", [128, 5, 4, 128], BF16) for i in range(NSW)]
    SSt = [sb(f"SSt{i}", [128, 2, 4, 128], F32) for i in range(NSTB)]
    R_WS = [Res(f"WS{i}") for i in range(NWS)]
    R_SW = [Res(f"SW{i}") for i in range(NSW)]
    R_STB = [Res(f"STB{i}") for i in range(NSTB)]
    R_ZY, R_U = Res("ZY"), Res("U")
    R_XRES = [Res("xres0"), Res("xres1")]
    R_RS = [Res("rs0"), Res("rs1")]
    R_H = [Res("h0"), Res("h1")]
    R_GATE = [Res("g0"), Res("g1")]
    R_YCAT = [Res("yc0"), Res("yc1")]
    R_UPOOL, R_SPOOL = Res("upool"), Res("spool")
    R_T1, R_T2 = Res("t1"), Res("t2")
    R_Q = [Res(f"Q{i}") for i in range(NQ)]
    R_PCS = [Res(f"PcPs{i}") for i in range(NQ)]
    R_SIG = [Res("sig0"), Res("sig1")]
    R_XS = [Res("xs0"), Res("xs1")]
    R_CARRY = [[Res(f"carry{l}_{gb}") for gb in range(8)] for l in range(DEPTH)]
    R_HALO = [Res(f"halo{l}") for l in range(DEPTH)]
    R_C2 = Res("const2")
    ws_rr, sw_rr, stb_rr, xs_rr, ev_rr = [0], [0], [0], [0], [0]

    def psB(pi):
        return psum[:, pi, :].bitcast(BF16)

    def evac_eng():
        ev_rr[0] ^= 1
        return "act" if ev_rr[0] else "dve"

    def copy_op(eng, out, in_, reads, acc):
        if eng == "act":
            P.op("act", lambda e: e.activation(out=out, in_=in_, func=AF.Copy), reads=reads, acc=acc)
        else:
            P.op(eng, lambda e: e.tensor_copy(out=out, in_=in_), reads=reads, acc=acc)

    P.op("pool", lambda e: e.memset(epsT[:], EPS), writes=[R_C2])
    P.op("pool", lambda e: e.memset(cfix[:], 1.0), writes=[R_C2])
    for f in range(4):
        w = POOL_WINDOWS[f]
        for t in range(w - 1):
            P.op("pool", lambda e, f=f, t=t, w=w: e.memset(cfix[:, f, t:t + 1], float(w) / float(t + 1)), writes=[R_C2])

    def load_ws(src_ap, nparts, reads):
        i = ws_rr[0]
        ws_rr[0] = (i + 1) % NWS
        dst = WS[i][:, 0:nparts * 512].rearrange("p (a n) -> p a n", n=512)
        P.op("sp", lambda e: e.dma_start(out=dst, in_=src_ap), reads=reads, writes=[R_WS[i]], dma=f"d_ws{i}")
        return i

    def wsv(i):
        return WS[i][:].rearrange("p (a n) -> p a n", n=512)

    def mm_group(pi, lhs_fn, rhs_fn, nk, reads):
        def f(e):
            ins = None
            for kc in range(nk):
                ins = e.matmul(psum[:, pi, :], lhsT=lhs_fn(kc), rhs=rhs_fn(kc), start=(kc == 0), stop=(kc == nk - 1))
            return ins
        P.op("pe", f, reads=reads, writes=[R_PS[pi]])

    def rmsnorm_stats(n):
        nr = slice(n * 512, (n + 1) * 512)
        P.op("act", lambda e: e.activation(out=xsq, in_=xres[:, :, nr], func=AF.Square), reads=[R_XRES[n]], writes=[R_SPOOL])
        pi = next_ps()
        mm_group(pi, lambda kc: onesB[:], lambda kc: xsq[:, kc, :], 8, [R_SPOOL, R_CONST])
        P.op("act", lambda e, pi=pi: e.activation(out=rs[:, n, :], in_=psum[:, pi, :], func=AF.Ln, bias=epsT[:], scale=1.0),
             reads=[R_PS[pi], R_C2], writes=[R_RS[n]])
        P.op("act", lambda e: e.activation(out=rs[:, n, :], in_=rs[:, n, :], func=AF.Exp, scale=-0.5), reads=[R_RS[n]], writes=[R_RS[n]])
        return nr

    for st in range(n_sub):
        seq, half = st // 2, st % 2
        t0 = half * ST
        if half == 0:
            P.op("pool", lambda e: e.memset(carry[:], 0.0), writes=[r for rl in R_CARRY for r in rl])
            P.op("pool", lambda e: e.memset(halo[:], 0.0), writes=R_HALO)
        for tt in range(8):
            si = xs_rr[0]
            xs_rr[0] ^= 1
            P.op("sp", lambda e, si=si, tt=tt, seq=seq, t0=t0: e.dma_start(out=xs[:, si, :], in_=x_d[seq, t0 + tt * 128:t0 + (tt + 1) * 128, :]),
                 writes=[R_XS[si]], dma=f"d_xs{si}")
            for fh in range(2):
                pi = next_ps()

                def f_xt(e, si=si, fh=fh, pi=pi):
                    ins = None
                    for f4 in range(4):
                        f = fh * 4 + f4
                        ins = e.transpose(out=psum[:, pi, f4 * 128:(f4 + 1) * 128], in_=xs[:, si, f * 128:(f + 1) * 128], identity=identF[:])
                    return ins
                P.op("pe", f_xt, reads=[R_XS[si], R_CONST], writes=[R_PS[pi]])
                copy_op(evac_eng(), xres[:, fh * 4:(fh + 1) * 4, tt * 128:(tt + 1) * 128],
                        psum[:, pi, :].rearrange("p (f n) -> p f n", f=4), [R_PS[pi]], [R_XRES[tt // 4]])

        for l in range(n_layers):
            k0 = half * NK
            win = wbf_in[l].rearrange("(a p) n -> p a n", p=128)
            for n in range(2):
                nr = rmsnorm_stats(n)
                for f in range(8):
                    P.op("dve", lambda e, f=f, nr=nr, n=n, l=l: e.scalar_tensor_tensor(
                        out=h[:, f, nr], in0=xres[:, f, nr], scalar=smallp[:, l, f:f + 1], in1=rs[:, n, :], op0=ALU.mult, op1=ALU.mult),
                        reads=[R_XRES[n], R_RS[n], R_SMALLP], acc=[R_H[n]])
            wi_s = load_ws(win[:, :, 512:1024], 8, [R_WBF[l]])
            for tau in range(8):
                pi = next_ps()
                mm_group(pi, lambda kc, tau=tau: h[:, kc, :].rearrange("p (k t) -> p t k", t=8)[:, tau, :],
                         lambda kc, wi_s=wi_s: wsv(wi_s)[:, kc, :], 8, [R_WS[wi_s], R_H[0], R_H[1]])
                copy_op(evac_eng(), Zs[:, :, tau, :], psum[:, pi, :].rearrange("p (g c) -> p g c", c=16), [R_PS[pi]], [R_ZY])
            for gq in range(4):
                pi = next_ps()

                def f_tr(e, gq=gq, pi=pi):
                    ins = None
                    for g8 in range(8):
                        g = gq * 8 + g8
                        ins = e.transpose(out=psB(pi)[:, g8 * 128:(g8 + 1) * 128], in_=Zs[:, g].rearrange("p t c -> p (t c)"), identity=identB[:])
                    return ins
                P.op("pe", f_tr, reads=[R_ZY, R_CONST], writes=[R_PS[pi]])
                copy_op(evac_eng(), U[:, gq * 8:(gq + 1) * 8, :], psB(pi).rearrange("p (g k) -> p g k", g=8), [R_PS[pi]], [R_U])

            wi_p = load_ws(win[:, :, 0:512], 8, [R_WBF[l]])
            wi_g = [load_ws(win[:, :, 1024 + sg * 512:1024 + (sg + 1) * 512], 8, [R_WBF[l]]) for sg in range(2)]
            fillers = []

            def pool_in_group(f, n, wi_p=wi_p, l=l):
                nr = slice(n * 512, (n + 1) * 512)
                pi = next_ps()
                mm_group(pi, lambda kc: wsv(wi_p)[:, kc, f * 128:(f + 1) * 128], lambda kc: h[:, kc, nr], 8, [R_WS[wi_p], R_H[n]])
                P.op("dve", lambda e: e.tensor_copy(out=upool[:, f, 16 + n * 512:16 + (n + 1) * 512], in_=psum[:, pi, :]),
                     reads=[R_PS[pi]], acc=[R_UPOOL])

            def gate_group(fo, n, wi_g=wi_g):
                nr = slice(n * 512, (n + 1) * 512)
                wi, f4 = wi_g[fo // 4], fo % 4
                pi = next_ps()
                mm_group(pi, lambda kc: wsv(wi)[:, kc, f4 * 128:(f4 + 1) * 128], lambda kc: h[:, kc, nr], 8, [R_WS[wi], R_H[n]])
                P.op("act", lambda e: e.activation(out=gate[:, fo, nr], in_=psum[:, pi, :], func=AF.Silu), reads=[R_PS[pi]], acc=[R_GATE[n]])

            def pool_sums(l=l, half=half):
                LT = 16 + ST
                R_PWK = R_XS
                P.op("dve", lambda e: e.tensor_copy(out=halo[:, l, :, :], in_=upool[:, :, ST:ST + 16]), reads=[R_UPOOL], writes=[R_HALO[l]])
                P.op("dve", lambda e: e.tensor_tensor(out=spool[:, 0, :], in0=upool[:, 0, 16:LT], in1=upool[:, 0, 15:LT - 1], op=ALU.add),
                     reads=[R_UPOOL], writes=[R_SPOOL])
                for f in (1, 2, 3):
                    P.op("dve", lambda e, f=f: e.tensor_tensor(out=pwk[:, 0, 1:LT], in0=upool[:, f, 1:LT], in1=upool[:, f, 0:LT - 1], op=ALU.add),
                         reads=[R_UPOOL], writes=R_PWK)
                    if f == 1:
                        P.op("dve", lambda e: e.tensor_tensor(out=spool[:, 1, :], in0=pwk[:, 0, 16:LT], in1=pwk[:, 0, 14:LT - 2], op=ALU.add),
                             reads=R_PWK, acc=[R_SPOOL])
                        continue
                    P.op("dve", lambda e: e.tensor_tensor(out=pwk[:, 1, 3:LT], in0=pwk[:, 0, 3:LT], in1=pwk[:, 0, 1:LT - 2], op=ALU.add),
                         reads=R_PWK, writes=R_PWK)
                    if f == 2:
                        P.op("dve", lambda e: e.tensor_tensor(out=spool[:, 2, :], in0=pwk[:, 1, 16:LT], in1=pwk[:, 1, 12:LT - 4], op=ALU.add),
                             reads=R_PWK, acc=[R_SPOOL])
                        continue
                    P.op("dve", lambda e: e.tensor_tensor(out=pwk[:, 2, 7:LT], in0=pwk[:, 1, 7:LT], in1=pwk[:, 1, 3:LT - 4], op=ALU.add),
                         reads=R_PWK, writes=R_PWK)
                    P.op("dve", lambda e: e.tensor_tensor(out=spool[:, 3, :], in0=pwk[:, 2, 16:LT], in1=pwk[:, 2, 8:LT - 8], op=ALU.add),
                         reads=R_PWK, acc=[R_SPOOL])
                if half == 0:
                    P.op("dve", lambda e: e.tensor_tensor(out=spool[:, :, 0:16], in0=spool[:, :, 0:16], in1=cfix[:], op=ALU.mult),
                         reads=[R_C2], writes=[R_SPOOL])

            P.op("dve", lambda e, l=l: e.tensor_copy(out=upool[:, :, 0:16], in_=halo[:, l, :, :]), reads=[R_HALO[l]], writes=[R_UPOOL])
            for f in range(4):
                for n in range(2):
                    fillers.append(lambda f=f, n=n: pool_in_group(f, n))
            fillers.append(pool_sums)
            for fo in range(8):
                for n in range(2):
                    fillers.append(lambda fo=fo, n=n: gate_group(fo, n))
            fillers.reverse()

            def run_fillers(k):
                for _ in range(k):
                    if fillers:
                        fillers.pop()()

            slots = {}

            def ssm_front(gb, l=l, k0=k0):
                wi = sw_rr[0]
                sw_rr[0] = (wi + 1) % NSW
                ti = stb_rr[0]
                stb_rr[0] = (ti + 1) % NSTB
                qi = gb % NQ
                slots[gb] = (wi, qi)
                P.op("sp", lambda e: e.dma_start(out=SSw[wi][:].rearrange("p a g n -> p (a g n)"), in_=ssmW_d[l, gb]),
                     reads=[R_SSMW[l]], writes=[R_SW[wi]], dma=f"d_sw{wi}")
                P.op("sp", lambda e: e.dma_start(out=SSt[ti][:], in_=tabs_d[l, gb][:, :, :, k0:k0 + NK]),
                     reads=[R_TABS[l]], writes=[R_STB[ti]], dma=f"d_stb{ti}")
                pa, pb_ = next_ps(), next_ps()

                def f_v(e):
                    ins = None
                    for kind, pi in ((1, pa), (2, pb_)):
                        for g4 in range(4):
                            ins = e.matmul(psum[:, pi, g4 * 128:(g4 + 1) * 128], lhsT=SSw[wi][:, kind, g4, :], rhs=U[:, gb * 4 + g4, :],
                                           start=True, stop=True)
                    return ins
                P.op("pe", f_v, reads=[R_SW[wi], R_U], writes=[R_PS[pa], R_PS[pb_]])

                def pv(pi):
                    return psum[:, pi, :].rearrange("p (g k) -> p g k", g=4)
                P.op("dve", lambda e: e.tensor_tensor(out=t1[:], in0=pv(pa), in1=SSt[ti][:, 0], op=ALU.mult),
                     reads=[R_PS[pa], R_STB[ti]], writes=[R_T1])
                P.op("dve", lambda e: e.tensor_tensor(out=t2[:], in0=pv(pb_), in1=SSt[ti][:, 1], op=ALU.mult),
                     reads=[R_PS[pb_], R_STB[ti]], writes=[R_T2])
                P.op("dve", lambda e: e.tensor_tensor(out=t1[:], in0=t1[:], in1=t2[:], op=ALU.add), reads=[R_T1, R_T2], writes=[R_T1])
                P.op("dve", lambda e: e.tensor_copy(out=Q[qi][:, :, 0], in_=carry[:, l, gb * 4:(gb + 1) * 4]),
                     reads=[R_CARRY[l][gb]], writes=[R_Q[qi]])
                for g4 in range(4):
                    g = gb * 4 + g4
                    P.op("dve", lambda e, g4=g4, g=g: e.tensor_tensor_scan(
                        out=Q[qi][:, g4, 1:129], data0=Rb_all[:, l, g:g + 1].to_broadcast([128, 128]), data1=t1[:, g4, :],
                        initial=carry[:, l, g:g + 1], op0=ALU.mult, op1=ALU.add),
                        reads=[R_T1, R_RB, R_CARRY[l][gb]], acc=[R_Q[qi]])
                P.op("dve", lambda e: e.tensor_copy(out=carry[:, l, gb * 4:(gb + 1) * 4], in_=Q[qi][:, :, 128]),
                     reads=[R_Q[qi]], writes=[R_CARRY[l][gb]])
                P.op("dve", lambda e: e.tensor_tensor(out=Pc[qi][:], in0=Q[qi][:, :, 0:128], in1=SSt[ti][:, 0], op=ALU.mult),
                     reads=[R_Q[qi], R_STB[ti]], writes=[R_PCS[qi]])
                P.op("dve", lambda e: e.tensor_tensor(out=Ps[qi][:], in0=Q[qi][:, :, 0:128], in1=SSt[ti][:, 1], op=ALU.mult),
                     reads=[R_Q[qi], R_STB[ti]], acc=[R_PCS[qi]])

            def ssm_back(gb):
                wi, qi = slots[gb]
                py = next_ps()

                def f_y(e):
                    ins = None
                    for g4 in range(4):
                        o = psum[:, py, g4 * 128:(g4 + 1) * 128]
                        e.matmul(o, lhsT=U[:, gb * 4 + g4, :], rhs=SSw[wi][:, 0, g4, :], start=True, stop=False)
                        e.matmul(o, lhsT=Pc[qi][:, g4, :], rhs=SSw[wi][:, 3, g4, :], start=False, stop=False)
                        ins = e.matmul(o, lhsT=Ps[qi][:, g4, :], rhs=SSw[wi][:, 4, g4, :], start=False, stop=True)
                    return ins
                P.op("pe", f_y, reads=[R_SW[wi], R_U, R_PCS[qi]], writes=[R_PS[py]])
                P.op("act", lambda e: e.activation(
                    out=ZY[:, :, gb * 64:(gb + 1) * 64].rearrange("p t (g c) -> p g t c", g=4),
                    in_=psum[:, py, :].rearrange("p (g t c) -> p g t c", g=4, t=8), func=AF.Gelu_apprx_tanh),
                    reads=[R_PS[py]], acc=[R_ZY])

            LAG = 1
            for s in range(8 + LAG):
                if s < 8:
                    ssm_front(s)
                run_fillers(3)
                if s >= LAG:
                    ssm_back(s - LAG)
            run_fillers(len(fillers))

            wg = load_ws(wbf_glu[l].rearrange("(a p) n -> p a n", p=128), 4, [R_WBF[l]])
            P.op("sp", lambda e, wg=wg, l=l: e.dma_start(out=WS[wg][:, 2048:3072], in_=poolW_d[l]),
                 reads=[R_POOLW[l]], acc=[R_WS[wg]], dma=f"d_ws{wg}")
            pwv = WS[wg][:, 2048:3072].rearrange("p (g k n) -> p g k n", g=4, k=2)
            for n in range(2):
                nr = slice(n * 512, (n + 1) * 512)
                for f in range(4):
                    pi = next_ps()

                    def f_pm(e, f=f, n=n, nr=nr, pi=pi, pwv=pwv):
                        e.matmul(psum[:, pi, :], lhsT=pwv[:, f, 0, :], rhs=spool[:, f, nr], start=True, stop=False)
                        return e.matmul(psum[:, pi, :], lhsT=pwv[:, f, 1, :], rhs=upool[:, f, 16 + n * 512:16 + (n + 1) * 512], start=False, stop=True)
                    P.op("pe", f_pm, reads=[R_WS[wg], R_SPOOL, R_UPOOL], writes=[R_PS[pi]])
                    P.op("dve", lambda e, f=f, nr=nr, pi=pi, l=l: e.scalar_tensor_tensor(
                        out=ycat[:, f, nr], in0=psum[:, pi, :], scalar=smallp[:, l, 8 + f:9 + f], in1=gate[:, f, nr], op0=ALU.mult, op1=ALU.mult),
                        reads=[R_PS[pi], R_GATE[n], R_SMALLP], acc=[R_YCAT[n]])
            for f in range(4):
                pi = next_ps()

                def f_tr(e, f=f, pi=pi):
                    ins = None
                    for tau in range(8):
                        ins = e.transpose(out=psB(pi)[:, tau * 128:(tau + 1) * 128], in_=ZY[:, tau, f * 128:(f + 1) * 128], identity=identB[:])
                    return ins
                P.op("pe", f_tr, reads=[R_ZY, R_CONST], writes=[R_PS[pi]])
                eng = evac_eng()
                o_, i_ = ysT[:, f, :].rearrange("p (k t) -> p t k", t=8), psB(pi).rearrange("p (t k) -> p t k", t=8)
                if f == 0:
                    if eng == "act":
                        P.op("act", lambda e, o_=o_, i_=i_: e.activation(out=o_, in_=i_, func=AF.Copy), reads=[R_PS[pi]], writes=[R_SPOOL])
                    else:
                        P.op("dve", lambda e, o_=o_, i_=i_: e.tensor_copy(out=o_, in_=i_), reads=[R_PS[pi]], writes=[R_SPOOL])
                else:
                    copy_op(eng, o_, i_, [R_PS[pi]], [R_SPOOL])
            gluv = WS[wg][:, 0:2048].rearrange("p (a n) -> p a n", n=512)
            wov = wbf_out[l].rearrange("(a p) n -> p a n", p=128)
            wi_o = [load_ws(wov[:, :, so * 512:(so + 1) * 512], 8, [R_WBF[l]]) for so in range(2)]

            def c1_half(n, l=l, gluv=gluv, wg=wg):
                nr = slice(n * 512, (n + 1) * 512)
                for fo in range(4):
                    pi = next_ps()
                    sgi = fo % 2
                    mm_group(pi, lambda kc, fo=fo: gluv[:, kc, fo * 128:(fo + 1) * 128], lambda kc: ysT[:, kc, nr], 4, [R_WS[wg], R_SPOOL])
                    P.op("act", lambda e, fo=fo, pi=pi, sgi=sgi: e.activation(out=sig[:, sgi, :], in_=psum[:, pi, :], func=AF.Sigmoid,
                                                                             bias=smallp[:, l, 12 + fo:13 + fo], scale=1.0),
                         reads=[R_PS[pi], R_SMALLP], writes=[R_SIG[sgi]])
                    P.op("dve", lambda e, fo=fo, sgi=sgi: e.tensor_tensor(out=sig[:, sgi, :], in0=sig[:, sgi, :], in1=ysT[:, fo, nr], op=ALU.mult),
                         reads=[R_SIG[sgi], R_SPOOL], writes=[R_SIG[sgi]])
                    P.op("dve", lambda e, fo=fo, sgi=sgi: e.tensor_tensor(out=ycat[:, 4 + fo, nr], in0=sig[:, sgi, :], in1=gate[:, 4 + fo, nr], op=ALU.mult),
                         reads=[R_SIG[sgi], R_GATE[n]], acc=[R_YCAT[n]])

            def c2_half(n, wi_o=wi_o):
                nr = slice(n * 512, (n + 1) * 512)
                for fo in range(8):
                    wi, f4 = wi_o[fo // 4], fo % 4
                    pi = next_ps()
                    mm_group(pi, lambda kc, wi=wi, f4=f4: wsv(wi)[:, kc, f4 * 128:(f4 + 1) * 128], lambda kc: ycat[:, kc, nr], 8, [R_WS[wi], R_YCAT[n]])
                    P.op("dve", lambda e, fo=fo, pi=pi: e.tensor_tensor(out=xres[:, fo, nr], in0=xres[:, fo, nr], in1=psum[:, pi, :], op=ALU.add),
                         reads=[R_PS[pi], R_XRES[n]], acc=[R_XRES[n]])
            c1_half(0)
            c1_half(1)
            c2_half(0)
            c2_half(1)

        for n in range(2):
            nr = rmsnorm_stats(n)
            for f in range(8):
                P.op("dve", lambda e, f=f, nr=nr, n=n: e.scalar_tensor_tensor(
                    out=xres[:, f, nr], in0=xres[:, f, nr], scalar=finalg[:, f:f + 1], in1=rs[:, n, :], op0=ALU.mult, op1=ALU.mult),
                    reads=[R_RS[n], R_SMALLP, R_XRES[n]], acc=[R_XRES[n]])
        for tt in range(8):
            si = xs_rr[0]
            xs_rr[0] ^= 1
            for fh in range(2):
                pi = next_ps()

                def f_ot(e, tt=tt, fh=fh, pi=pi):
                    ins = None
                    for f4 in range(4):
                        f = fh * 4 + f4
                        ins = e.transpose(out=psum[:, pi, f4 * 128:(f4 + 1) * 128], in_=xres[:, f, tt * 128:(tt + 1) * 128], identity=identF[:])
                    return ins
                P.op("pe", f_ot, reads=[R_XRES[tt // 4], R_CONST], writes=[R_PS[pi]])
                copy_op(evac_eng(), xs[:, si, fh * 512:(fh + 1) * 512], psum[:, pi, :], [R_PS[pi]], [R_XS[si]])
            P.op("sp", lambda e, si=si, tt=tt, seq=seq, t0=t0: e.dma_start(out=out_d[seq, t0 + tt * 128:t0 + (tt + 1) * 128, :], in_=xs[:, si, :]),
                 reads=[R_XS[si]], acc=[R_OUT], dma="d_out")

    P.op("sp", None, reads=[R_OUT], sig=False)
    sems = {k: es.enter_context(nc.semaphore(k)) for k in P.cnt}
    with nc.Block() as block:
        P.emit(nc, block, sems)
    es.close()
    return nc


def prep_inputs(inp):
    f = np.float32
    shared = {}
    shared["w_in"] = np.ascontiguousarray(inp["w_in"], dtype=f)
    shared["w_out"] = np.ascontiguousarray(inp["w_out"], dtype=f)
    shared["glu_w"] = np.ascontiguousarray(inp["glu_w"], dtype=f)
    shared["pool_w"] = np.ascontiguousarray(np.transpose(inp["pool_w"], (0, 2, 1, 3)), dtype=f)
    ng = np.transpose(np.asarray(inp["norm_g"], f).reshape(DEPTH, 8, 128), (2, 0, 1))
    psc = np.transpose(np.asarray(inp["pool_scale"], f).reshape(DEPTH, 4, 128), (2, 0, 1))
    gb = np.transpose(np.asarray(inp["glu_b"], f).reshape(DEPTH, 4, 128), (2, 0, 1))
    shared["smallp"] = np.ascontiguousarray(np.concatenate([ng, psc, gb], axis=2), dtype=f)
    shared["finalg"] = np.ascontiguousarray(np.asarray(inp["final_g"], f).reshape(8, 128).T)

    def dup(a):
        t = np.transpose(np.asarray(a, f), (0, 2, 1))
        return np.concatenate([t, t], axis=1)
    are = dup(inp["a_re"])
    aim = dup(inp["a_im"])
    ldt = np.broadcast_to(np.asarray(inp["log_dt"], f)[:, None, :], (DEPTH, 128, G))
    dv = np.asarray(inp["d_skip"], f).reshape(DEPTH, G, 16)
    dvec = np.broadcast_to(np.transpose(dv, (0, 2, 1))[:, None, :, :], (DEPTH, 8, 16, G)).reshape(DEPTH, 128, G)
    shared["ssm_small"] = np.ascontiguousarray(np.stack([are, aim, ldt, dvec], axis=2), dtype=f)

    def dupb(a):
        t = np.transpose(np.asarray(a, f), (0, 2, 1, 3)).reshape(DEPTH, 64, G * 16)
        return np.concatenate([t, t], axis=1)

    def dupc(a):
        t = np.transpose(np.asarray(a, f), (0, 3, 1, 2)).reshape(DEPTH, 64, G * 16)
        return np.concatenate([t, t], axis=1)
    shared["ssm_bc"] = np.ascontiguousarray(
        np.stack([dupb(inp["b_re"]), dupb(inp["b_im"]), dupc(inp["c_re"]), dupc(inp["c_im"])], axis=2), dtype=f)
    x = np.ascontiguousarray(inp["x"], dtype=f)
    maps = []
    for c in range(NCORES):
        m = dict(shared)
        m["x"] = x[c * NSEQ:(c + 1) * NSEQ]
        maps.append(m)
    return maps


def kernel(**inputs):
    maps = prep_inputs(inputs)
    nc = build_program()
    res = run_bass_kernel_spmd(nc, maps, core_ids=list(range(NCORES)))
    out = np.concatenate([np.asarray(r["out"], dtype=np.float32) for r in res.results], axis=0)
    return out
```

```python
import math
from contextlib import ExitStack

import numpy as np
import concourse.bass as bass
import concourse.mybir as mybir
from concourse.bass_utils import run_bass_kernel_spmd

F32 = mybir.dt.float32
BF16 = mybir.dt.bfloat16
I32 = mybir.dt.int32
AF = mybir.ActivationFunctionType
ALU = mybir.AluOpType

NCORES = 8
DEPTH = 4
D = 1024
SEQ = 2048
NSEQ = 2
ST = 1024
NK = ST // 8
G = 32
TWO_PI = float(2.0 * math.pi)
INV_2PI = float(1.0 / (2.0 * math.pi))
HALF_PI = float(math.pi / 2.0)
SIN_SCALE = 1.0 - 4e-5
EPS = 1e-5
POOL_WINDOWS = (2, 4, 8, 16)


class Res:
    __slots__ = ("name", "w", "r")

    def __init__(self, name):
        self.name = name
        self.w = {}
        self.r = {}


class Prog:
    ENG = ("pe", "act", "dve", "pool", "sp")

    def __init__(self):
        self.ops = {e: [] for e in self.ENG}
        self.cnt = {}
        self.seen = {e: {} for e in self.ENG}
        self.pending = {e: {} for e in self.ENG}

    def fence(self):
        for e in self.ENG:
            self.pending[e] = dict(self.cnt)

    def op(self, eng, fn, reads=(), writes=(), dma=None, sig=True, acc=()):
        need = self.pending[eng]
        self.pending[eng] = {}
        for r in acc:
            for k, v in r.r.items():
                if need.get(k, 0) < v:
                    need[k] = v
        for r in reads:
            for k, v in r.w.items():
                if need.get(k, 0) < v:
                    need[k] = v
        for r in writes:
            for k, v in r.w.items():
                if need.get(k, 0) < v:
                    need[k] = v
            for k, v in r.r.items():
                if need.get(k, 0) < v:
                    need[k] = v
        waits = []
        seen = self.seen[eng]
        for k, v in need.items():
            if eng == "pe" and k == "pe":
                continue
            if seen.get(k, 0) >= v:
                continue
            seen[k] = v
            waits.append((k, v))
        tok = None
        inc = 1
        if sig:
            key = dma if dma is not None else eng
            inc = 16 if dma is not None else 1
            self.cnt[key] = self.cnt.get(key, 0) + inc
            tok = (key, self.cnt[key])
            for r in reads:
                if r.r.get(key, 0) < tok[1]:
                    r.r[key] = tok[1]
            for r in writes:
                r.w = {key: tok[1]}
            for r in acc:
                if r.w.get(key, 0) < tok[1]:
                    r.w[key] = tok[1]
        self.ops[eng].append((waits, fn, tok, inc))
        return tok

    def emit(self, nc, block, sems):
        def mk(name):
            def body(e):
                for waits, fn, tok, inc in self.ops[name]:
                    for k, v in waits:
                        e.wait_ge(sems[k], v)
                    if fn is None:
                        continue
                    ins = fn(e)
                    if tok is not None:
                        ins.then_inc(sems[tok[0]], inc)
            return body

        block.tensor(mk("pe"))
        block.scalar(mk("act"))
        block.vector(mk("dve"))
        block.gpsimd(mk("pool"))
        block.sync(mk("sp"))


def build_program(debug=False, n_layers=DEPTH, n_sub=4):
    nc = bass.Bass("TRN2", target_bir_lowering=False)
    P = Prog()
    es = ExitStack()

    def dram_in(name, shape, dt=F32):
        return nc.dram_tensor(name, list(shape), dt, kind="ExternalInput").ap()

    def dram_out(name, shape, dt=F32):
        return nc.dram_tensor(name, list(shape), dt, kind="ExternalOutput").ap()

    def dram_scr(name, shape, dt):
        kind = "ExternalOutput" if debug else "Internal"
        return nc.dram_tensor(name, list(shape), dt, kind=kind).ap()

    def sb(name, shape, dt, stack=None):
        return (stack or es).enter_context(nc.sbuf_tensor(name, list(shape), dt))

    x_d = dram_in("x", [NSEQ, SEQ, D])
    w_in_d = dram_in("w_in", [DEPTH, D, 2 * D])
    w_out_d = dram_in("w_out", [DEPTH, D, D])
    glu_w_d = dram_in("glu_w", [DEPTH, 512, 512])
    pool_w_d = dram_in("pool_w", [DEPTH, 128, 4, 128])
    smallp_d = dram_in("smallp", [128, DEPTH, 16])
    finalg_d = dram_in("finalg", [128, 8])
    ssm_small_d = dram_in("ssm_small", [DEPTH, 128, 4, 32])
    ssm_bc_d = dram_in("ssm_bc", [DEPTH, 128, 4, 512])
    out_d = dram_out("out", [NSEQ, SEQ, D])

    ssmW_d = dram_scr("ssmW", [DEPTH, 8, 128, 5 * 4 * 128], BF16)
    tabs_d = dram_scr("tabs", [DEPTH, 8, 128, 2, 4, 256], F32)
    poolW_d = dram_scr("poolW", [DEPTH, 128, 4 * 2 * 128], BF16)
    R_SSMW = [Res(f"ssmW{l}") for l in range(DEPTH)]
    R_TABS = [Res(f"tabs{l}") for l in range(DEPTH)]
    R_POOLW = [Res(f"poolW{l}") for l in range(DEPTH)]
    R_OUT = Res("out")

    identF = sb("identF", [128, 128], F32)
    identB = sb("identB", [128, 128], BF16)
    onesB = sb("onesB", [128, 128], BF16)
    Rb_all = sb("Rb_all", [128, DEPTH, 32], F32)
    smallp = sb("smallp_sb", [128, DEPTH, 16], F32)
    finalg = sb("finalg_sb", [128, 8], F32)
    psum = es.enter_context(nc.psum_tensor("psum", [128, 8, 512], F32))
    R_CONST = Res("const")
    R_RB = Res("Rb")
    R_SMALLP = Res("smallp")
    R_PS = [Res(f"ps{i}") for i in range(8)]
    ps_rr = [0]

    def next_ps():
        i = ps_rr[0]
        ps_rr[0] = (i + 1) % 8
        return i

    P.op("pool", lambda e: e.memset(identF[:], 1.0), writes=[R_CONST])
    P.op("pool", lambda e: e.affine_select(out=identF[:], in_=identF[:], pattern=[[-1, 128]],
                                           compare_op=ALU.is_equal, fill=0.0, base=0, channel_multiplier=1),
         reads=[R_CONST], writes=[R_CONST])
    P.op("pool", lambda e: e.tensor_copy(out=identB[:], in_=identF[:]), reads=[R_CONST], writes=[R_CONST])
    P.op("pool", lambda e: e.memset(onesB[:], 1.0 / 1024.0), writes=[R_CONST])
    P.op("sp", lambda e: e.dma_start(out=smallp[:], in_=smallp_d[:]), writes=[R_SMALLP], dma="d_small")
    P.op("sp", lambda e: e.dma_start(out=finalg[:], in_=finalg_d[:]), writes=[R_SMALLP], dma="d_small")

    wbf_in = dram_scr("wbf_in", [DEPTH, D, 2 * D], BF16)
    wbf_out = dram_scr("wbf_out", [DEPTH, D, D], BF16)
    wbf_glu = dram_scr("wbf_glu", [DEPTH, 512, 512], BF16)
    R_WBF = [Res(f"wbf{l}") for l in range(DEPTH)]
    for l in range(n_layers):
        for a in range(8):
            P.op("pool", lambda e, l=l, a=a: e.dma_start(out=wbf_in[l, a * 128:(a + 1) * 128, :], in_=w_in_d[l, a * 128:(a + 1) * 128, :]),
                 writes=[R_WBF[l]], dma=f"d_wbf{l}")
        for a in range(4):
            P.op("pool", lambda e, l=l, a=a: e.dma_start(out=wbf_out[l, a * 256:(a + 1) * 256, :], in_=w_out_d[l, a * 256:(a + 1) * 256, :]),
                 writes=[R_WBF[l]], dma=f"d_wbf{l}")
        P.op("pool", lambda e, l=l: e.dma_start(out=wbf_glu[l], in_=glu_w_d[l]), writes=[R_WBF[l]], dma=f"d_wbf{l}")

    with ExitStack() as ps_:
        def pb(name, shape, dt):
            return sb("pl_" + name, shape, dt, ps_)
        mask = pb("mask", [128, 128], F32)
        iotaKi = pb("iotaKi", [128, 256], I32)
        iotaK = pb("iotaK", [128, 256], F32)
        JTi = pb("JTi", [128, 7, 32], I32)
        JT = pb("JT", [128, 7, 32], F32)
        sm = pb("sm", [128, 4, 32], F32)
        bc = pb("bc", [128, 4, 512], F32)
        pw = pb("pw", [128, 4, 128], F32)
        PWo = pb("PWo", [128, 4, 2, 128], BF16)
        dtt = pb("dtt", [128, 32], F32)
        ard = pb("ard", [128, 32], F32)
        ang = pb("ang", [128, 32], F32)
        ARJ = pb("ARJ", [128, 7, 32], F32)
        ANJ = pb("ANJ", [128, 7, 32], F32)
        MAGP = pb("MAGP", [128, 7, 32], F32)
        MAGN = pb("MAGN", [128, 7, 32], F32)
        NIj = pb("NIj", [128, 7, 32], I32)
        RRj = pb("RRj", [128, 7, 32], F32)
        SN = pb("SN", [128, 7, 32], F32)
        CS = pb("CS", [128, 7, 32], F32)
        EXre = pb("EXre", [128, 32, 8], F32)
        EXim = pb("EXim", [128, 32, 8], F32)
        EYre = pb("EYre", [128, 32, 8], F32)
        EYim = pb("EYim", [128, 32, 8], F32)
        s1 = pb("s1", [128, 32], F32)
        s2 = pb("s2", [128, 32], F32)
        s3 = pb("s3", [128, 32], F32)
        s4 = pb("s4", [128, 32], F32)
        fre = pb("fre", [128, 32], F32)
        fim = pb("fim", [128, 32], F32)
        Rt = pb("Rt", [128, 32], F32)
        nRt = pb("nRt", [128, 32], F32)
        TH = pb("TH", [128, 32], F32)
        NI8 = pb("NI8", [128, 32], I32)
        ta = pb("ta", [128, 32, 16], F32)
        tb_ = pb("tb", [128, 32, 16], F32)
        tc_ = pb("tc", [128, 32, 16], F32)
        td = pb("td", [128, 32, 16], F32)
        bbA = pb("bbA", [128, 32, 16], F32)
        bbB = pb("bbB", [128, 32, 16], F32)
        cA = pb("cA", [128, 32, 16], F32)
        cB = pb("cB", [128, 32, 16], F32)
        Smat = pb("Smat", [128, 128], F32)
        hpiT = pb("hpiT", [128, 1], F32)
        T1 = pb("T1", [128, 32, 8, 16], F32)
        T2 = pb("T2", [128, 32, 8, 16], F32)
        XBm = pb("XBm", [128, 32, 8, 16], F32)
        Gm = pb("Gm", [128, 32, 8, 16], F32)
        W5 = pb("W5", [128, 8, 5, 4, 128], BF16)
        tmpT = [pb(f"tmpT{i}", [128, 4, 128], F32) for i in range(2)]
        NTB = 2
        ANGq = [pb(f"ANGq{i}", [128, 4, 256], F32) for i in range(NTB)]
        NIq = [pb(f"NIq{i}", [128, 4, 256], I32) for i in range(NTB)]
        RRq = [pb(f"RRq{i}", [128, 4, 256], F32) for i in range(NTB)]
        ABq = [pb(f"ABq{i}", [128, 4, 256], F32) for i in range(NTB)]
        CSq = [pb(f"CSq{i}", [128, 2, 4, 256], F32) for i in range(NTB)]
        R_TQ = [Res(f"pl_tq{i}") for i in range(NTB)]
        R_CSQ = [Res(f"pl_csq{i}") for i in range(NTB)]
        R_BB = Res("pl_bb")

        R_PC = Res("pl_const")
        R_IN = Res("pl_in")
        R_A = Res("pl_a")
        R_E = Res("pl_E")
        R_X = Res("pl_X")
        R_T = Res("pl_T")
        R_W5 = Res("pl_W5")
        R_TMPT = [Res("pl_tmpT0"), Res("pl_tmpT1")]
        R_ANG = Res("pl_ang")
        R_TAB = Res("pl_tab")
        R_PW = Res("pl_pw")

        P.op("pool", lambda e: e.memset(mask[:], 1.0), writes=[R_PC])
        P.op("pool", lambda e: e.affine_select(out=mask[:], in_=mask[:], pattern=[[16, 8], [0, 16]],
                                               compare_op=ALU.is_ge, fill=0.0, base=15, channel_multiplier=-1),
             reads=[R_PC], writes=[R_PC])
        P.op("pool", lambda e: e.iota(iotaKi[:], pattern=[[1, 256]], base=0, channel_multiplier=0), writes=[R_PC])
        P.op("pool", lambda e: e.tensor_copy(out=iotaK[:], in_=iotaKi[:]), reads=[R_PC], writes=[R_PC])
        P.op("pool", lambda e: e.iota(JTi[:], pattern=[[-1, 7], [0, 32]], base=7, channel_multiplier=0), writes=[R_PC])
        P.op("pool", lambda e: e.tensor_copy(out=JT[:], in_=JTi[:]), reads=[R_PC], writes=[R_PC])
        P.op("pool", lambda e: e.tensor_copy(out=Smat[:, 0:64], in_=identF[:, 64:128]), reads=[R_CONST], writes=[R_PC])
        P.op("pool", lambda e: e.tensor_scalar(out=Smat[:, 64:128], in0=identF[:, 0:64], scalar1=-1.0, scalar2=None, op0=ALU.mult),
             reads=[R_CONST, R_PC], writes=[R_PC])
        P.op("pool", lambda e: e.memset(hpiT[:], HALF_PI), reads=[R_PC], writes=[R_PC])

        def bc3(ap32):
            return ap32.unsqueeze(1).to_broadcast([128, 7, 32])

        def bcc(ap32):
            return ap32.unsqueeze(2).to_broadcast([128, 32, 16])

        def bE(apE, lo, hi):
            return apE[lo:hi].unsqueeze(3).to_broadcast([hi - lo, 32, 8, 16])

        def bB(apB, lo, hi):
            return apB[lo:hi].unsqueeze(2).to_broadcast([hi - lo, 32, 8, 16])

        def bR(apR, lo, hi):
            return apR[lo:hi].rearrange("p (a g) -> p a g", g=4).unsqueeze(3).to_broadcast([hi - lo, 8, 4, 128])

        for l in range(n_layers):
            P.op("sp", lambda e, l=l: e.dma_start(out=sm[:], in_=ssm_small_d[l]), writes=[R_IN], dma="d_plin")
            P.op("sp", lambda e, l=l: e.dma_start(out=bc[:], in_=ssm_bc_d[l]), writes=[R_IN], dma="d_plin")
            P.op("sp", lambda e, l=l: e.dma_start(out=pw[:], in_=pool_w_d[l]), writes=[R_PW], dma="d_plpw")
            are, aim, ldt, dvec = sm[:, 0, :], sm[:, 1, :], sm[:, 2, :], sm[:, 3, :]
            bre = bc[:, 0, :].rearrange("p (g c) -> p g c", c=16)
            bim = bc[:, 1, :].rearrange("p (g c) -> p g c", c=16)
            cre = bc[:, 2, :].rearrange("p (g c) -> p g c", c=16)
            cim = bc[:, 3, :].rearrange("p (g c) -> p g c", c=16)

            for g4 in range(4):
                P.op("pool", lambda e, g4=g4: e.tensor_scalar(out=PWo[:, g4, 0, :], in0=pw[:, g4, :],
                                                            scalar1=1.0 / POOL_WINDOWS[g4], scalar2=None, op0=ALU.mult),
                     reads=[R_PW], writes=[R_PW])
            P.op("pool", lambda e: e.tensor_scalar(out=PWo[:, :, 1, :], in0=pw[:, :, :], scalar1=-1.0, scalar2=None, op0=ALU.mult),
                 reads=[R_PW], writes=[R_PW])
            P.op("sp", lambda e, l=l: e.dma_start(out=poolW_d[l], in_=PWo[:].rearrange("p a b c -> p (a b c)")),
                 reads=[R_PW], writes=[R_POOLW[l]], dma=f"d_plpo{l}")

            P.op("act", lambda e: e.activation(out=dtt[:], in_=ldt, func=AF.Exp), reads=[R_IN], writes=[R_A])
            P.op("dve", lambda e: e.tensor_tensor(out=ard[:], in0=are, in1=dtt[:], op=ALU.mult), reads=[R_IN, R_A], writes=[R_A])
            P.op("dve", lambda e: e.tensor_tensor(out=ang[:], in0=aim, in1=dtt[:], op=ALU.mult), reads=[R_IN, R_A], writes=[R_A])
            P.op("dve", lambda e: e.tensor_tensor(out=ARJ[:], in0=JT[:], in1=bc3(ard[:]), op=ALU.mult), reads=[R_PC, R_A], writes=[R_A])
            P.op("dve", lambda e: e.tensor_tensor(out=ANJ[:], in0=JT[:], in1=bc3(ang[:]), op=ALU.mult), reads=[R_PC, R_A], writes=[R_A])
            P.op("act", lambda e: e.activation(out=MAGP[:], in_=ARJ[:], func=AF.Exp), reads=[R_A], writes=[R_A])
            P.op("act", lambda e: e.activation(out=MAGN[:], in_=ARJ[:], func=AF.Exp, scale=-1.0), reads=[R_A], writes=[R_A])
            P.op("act", lambda e: e.activation(out=NIj[:], in_=ANJ[:], func=AF.Copy, scale=INV_2PI), reads=[R_A], writes=[R_A])
            P.op("dve", lambda e: e.scalar_tensor_tensor(out=RRj[:], in0=NIj[:], scalar=-TWO_PI, in1=ANJ[:], op0=ALU.mult, op1=ALU.add),
                 reads=[R_A], writes=[R_A])
            P.op("act", lambda e: e.activation(out=SN[:], in_=RRj[:], func=AF.Sin, scale=SIN_SCALE), reads=[R_A], writes=[R_A])
            P.op("dve", lambda e: e.tensor_scalar(out=RRj[:], in0=ANJ[:], scalar1=HALF_PI, scalar2=None, op0=ALU.add), reads=[R_A], writes=[R_A])
            P.op("act", lambda e: e.activation(out=NIj[:], in_=RRj[:], func=AF.Copy, scale=INV_2PI), reads=[R_A], writes=[R_A])
            P.op("dve", lambda e: e.scalar_tensor_tensor(out=RRj[:], in0=NIj[:], scalar=-TWO_PI, in1=RRj[:], op0=ALU.mult, op1=ALU.add),
                 reads=[R_A], writes=[R_A])
            P.op("act", lambda e: e.activation(out=CS[:], in_=RRj[:], func=AF.Sin, scale=SIN_SCALE), reads=[R_A], writes=[R_A])
            def Ev(t):
                return t[:].rearrange("p g t -> p t g")[:, 0:7, :]
            P.op("dve", lambda e: e.tensor_tensor(out=Ev(EXre), in0=MAGP[:], in1=CS[:], op=ALU.mult), reads=[R_A], writes=[R_E])
            P.op("dve", lambda e: e.tensor_tensor(out=Ev(EXim), in0=MAGP[:], in1=SN[:], op=ALU.mult), reads=[R_A], writes=[R_E])
            P.op("dve", lambda e: e.tensor_tensor(out=Ev(EYre), in0=MAGN[:], in1=CS[:], op=ALU.mult), reads=[R_A], writes=[R_E])
            P.op("dve", lambda e: e.scalar_tensor_tensor(out=Ev(EYim), in0=MAGN[:], scalar=-1.0, in1=SN[:], op0=ALU.mult, op1=ALU.mult),
                 reads=[R_A], writes=[R_E])
            P.op("dve", lambda e: e.memset(EXre[:, :, 7:8], 1.0), writes=[R_E])
            P.op("dve", lambda e: e.memset(EXim[:, :, 7:8], 0.0), writes=[R_E])
            P.op("dve", lambda e: e.memset(EYre[:, :, 7:8], 1.0), writes=[R_E])
            P.op("dve", lambda e: e.memset(EYim[:, :, 7:8], 0.0), writes=[R_E])
            lre, lim = EXre[:, :, 6], EXim[:, :, 6]
            P.op("dve", lambda e: e.tensor_scalar(out=s1[:], in0=lre, scalar1=-1.0, scalar2=None, op0=ALU.add), reads=[R_E], writes=[R_A])
            P.op("dve", lambda e: e.tensor_tensor(out=s2[:], in0=are, in1=are, op=ALU.mult), reads=[R_IN], writes=[R_A])
            P.op("dve", lambda e: e.tensor_tensor(out=s3[:], in0=aim, in1=aim, op=ALU.mult), reads=[R_IN], writes=[R_A])
            P.op("dve", lambda e: e.tensor_tensor(out=s2[:], in0=s2[:], in1=s3[:], op=ALU.add), reads=[R_A], writes=[R_A])
            P.op("dve", lambda e: e.reciprocal(out=s2[:], in_=s2[:]), reads=[R_A], writes=[R_A])
            P.op("dve", lambda e: e.tensor_tensor(out=s3[:], in0=s1[:], in1=are, op=ALU.mult), reads=[R_A, R_IN], writes=[R_A])
            P.op("dve", lambda e: e.tensor_tensor(out=s4[:], in0=lim, in1=aim, op=ALU.mult), reads=[R_E, R_IN], writes=[R_A])
            P.op("dve", lambda e: e.tensor_tensor(out=s3[:], in0=s3[:], in1=s4[:], op=ALU.add), reads=[R_A], writes=[R_A])
            P.op("dve", lambda e: e.tensor_tensor(out=fre[:], in0=s3[:], in1=s2[:], op=ALU.mult), reads=[R_A], writes=[R_A])
            P.op("dve", lambda e: e.tensor_tensor(out=s3[:], in0=lim, in1=are, op=ALU.mult), reads=[R_E, R_IN], writes=[R_A])
            P.op("dve", lambda e: e.tensor_tensor(out=s4[:], in0=s1[:], in1=aim, op=ALU.mult), reads=[R_A, R_IN], writes=[R_A])
            P.op("dve", lambda e: e.tensor_tensor(out=s3[:], in0=s3[:], in1=s4[:], op=ALU.subtract), reads=[R_A], writes=[R_A])
            P.op("dve", lambda e: e.tensor_tensor(out=fim[:], in0=s3[:], in1=s2[:], op=ALU.mult), reads=[R_A], writes=[R_A])
            P.op("dve", lambda e: e.tensor_tensor(out=ta[:], in0=bre, in1=bcc(fre[:]), op=ALU.mult), reads=[R_A, R_IN], writes=[R_BB])
            P.op("dve", lambda e: e.tensor_tensor(out=tb_[:], in0=bim, in1=bcc(fim[:]), op=ALU.mult), reads=[R_A, R_IN], acc=[R_BB])
            P.op("dve", lambda e: e.tensor_tensor(out=tc_[:], in0=bim, in1=bcc(fre[:]), op=ALU.mult), reads=[R_A, R_IN], acc=[R_BB])
            P.op("dve", lambda e: e.tensor_tensor(out=td[:], in0=bre, in1=bcc(fim[:]), op=ALU.mult), reads=[R_A, R_IN], acc=[R_BB])
            P.op("dve", lambda e: e.tensor_tensor(out=bbA[0:64], in0=ta[0:64], in1=tb_[0:64], op=ALU.subtract), reads=[R_BB], acc=[R_BB])
            P.op("dve", lambda e: e.tensor_tensor(out=bbA[64:128], in0=tc_[64:128], in1=td[64:128], op=ALU.add), reads=[R_BB], acc=[R_BB])
            P.op("dve", lambda e: e.scalar_tensor_tensor(out=bbB[0:64], in0=tc_[0:64], scalar=-1.0, in1=td[0:64], op0=ALU.mult, op1=ALU.subtract),
                 reads=[R_BB], acc=[R_BB])
            P.op("dve", lambda e: e.tensor_tensor(out=bbB[64:128], in0=ta[64:128], in1=tb_[64:128], op=ALU.subtract), reads=[R_BB], acc=[R_BB])
            P.op("act", lambda e: e.activation(out=cA[0:64], in_=cre[0:64], func=AF.Copy), reads=[R_IN], acc=[R_BB])
            P.op("act", lambda e: e.activation(out=cA[64:128], in_=cim[64:128], func=AF.Copy, scale=-1.0), reads=[R_IN], acc=[R_BB])
            P.op("act", lambda e: e.activation(out=cB[0:64], in_=cim[0:64], func=AF.Copy, scale=-1.0), reads=[R_IN], acc=[R_BB])
            P.op("act", lambda e: e.activation(out=cB[64:128], in_=cre[64:128], func=AF.Copy, scale=-1.0), reads=[R_IN], acc=[R_BB])
            P.op("act", lambda e: e.activation(out=Rt[:], in_=ard[:], func=AF.Exp, scale=8.0), reads=[R_A], writes=[R_A])
            P.op("act", lambda e, l=l: e.activation(out=Rb_all[:, l, :], in_=Rt[:], func=AF.Copy), reads=[R_A], writes=[R_RB])
            P.op("dve", lambda e: e.tensor_scalar(out=s1[:], in0=ang[:], scalar1=8.0, scalar2=None, op0=ALU.mult), reads=[R_A], writes=[R_A])
            P.op("act", lambda e: e.activation(out=NI8[:], in_=s1[:], func=AF.Copy, scale=INV_2PI), reads=[R_A], writes=[R_A])
            P.op("dve", lambda e: e.scalar_tensor_tensor(out=TH[:], in0=NI8[:], scalar=-TWO_PI, in1=s1[:], op0=ALU.mult, op1=ALU.add),
                 reads=[R_A], writes=[R_A])

            tv = tabs_d[l].rearrange("a p two g k -> p a two g k")
            tq_rr = [0]

            def table_piece(gb, l=l, tv=tv):
                i = tq_rr[0]
                tq_rr[0] = (i + 1) % NTB
                gs = slice(gb * 4, gb * 4 + 4)
                P.op("dve", lambda e: e.tensor_tensor(out=ANGq[i][:], in0=TH[:, gs].unsqueeze(2).to_broadcast([128, 4, 256]),
                                                     in1=iotaK[:].unsqueeze(1).to_broadcast([128, 4, 256]), op=ALU.mult),
                     reads=[R_A, R_PC], writes=[R_TQ[i]])
                P.op("act", lambda e: e.activation(out=NIq[i][:], in_=ANGq[i][:], func=AF.Copy, scale=INV_2PI), reads=[R_TQ[i]], acc=[R_TQ[i]])
                P.op("dve", lambda e: e.scalar_tensor_tensor(out=RRq[i][:], in0=NIq[i][:], scalar=-TWO_PI, in1=ANGq[i][:], op0=ALU.mult, op1=ALU.add),
                     reads=[R_TQ[i]], acc=[R_TQ[i]])
                P.op("act", lambda e: e.activation(out=ABq[i][:], in_=RRq[i][:], func=AF.Abs), reads=[R_TQ[i]], acc=[R_TQ[i]])
                P.op("act", lambda e: e.activation(out=CSq[i][:, 1], in_=RRq[i][:], func=AF.Sin, scale=SIN_SCALE), reads=[R_TQ[i]], writes=[R_CSQ[i]])
                P.op("act", lambda e: e.activation(out=CSq[i][:, 0], in_=ABq[i][:], func=AF.Sin, scale=-1.0, bias=hpiT[:]),
                     reads=[R_TQ[i], R_PC], acc=[R_CSQ[i]])
                P.op("sp", lambda e: e.dma_start(out=tv[:, gb], in_=CSq[i][:]), reads=[R_CSQ[i]], acc=[R_TABS[l]], dma=f"d_pltab{l}")

            bigops = [
                lambda: P.op("dve", lambda e: e.tensor_tensor(out=T1[:], in0=bE(EXre, 0, 128), in1=bB(bbA, 0, 128), op=ALU.mult), reads=[R_E, R_BB], writes=[R_T]),
                lambda: P.op("dve", lambda e: e.tensor_tensor(out=T2[:], in0=bE(EXim, 0, 128), in1=bB(bbB, 0, 128), op=ALU.mult), reads=[R_E, R_BB], acc=[R_T]),
                lambda: P.op("dve", lambda e: e.tensor_tensor(out=XBm[:], in0=T1[:], in1=T2[:], op=ALU.add), reads=[R_T], writes=[R_X]),
                lambda: P.op("dve", lambda e: e.tensor_tensor(out=T1[:], in0=bE(EYre, 0, 128), in1=bB(cA, 0, 128), op=ALU.mult), reads=[R_E, R_BB], writes=[R_T]),
                lambda: P.op("dve", lambda e: e.tensor_tensor(out=T2[:], in0=bE(EYim, 0, 128), in1=bB(cB, 0, 128), op=ALU.mult), reads=[R_E, R_BB], acc=[R_T]),
                lambda: P.op("dve", lambda e: e.tensor_tensor(out=Gm[:], in0=T1[:], in1=T2[:], op=ALU.add), reads=[R_T], acc=[R_X]),
            ]
            for i_, bo in enumerate(bigops):
                table_piece(i_)
                bo()

            for gb in range(8):
                if gb in (2, 5):
                    table_piece(6 + (gb == 5))
                ti = gb % 2
                pi = next_ps()

                def f_toep(e, gb=gb, pi=pi):
                    ins = None
                    for g4 in range(4):
                        g = gb * 4 + g4
                        ins = e.matmul(psum[:, pi, g4 * 128:(g4 + 1) * 128],
                                       lhsT=XBm[:, g].rearrange("p t c -> p (t c)"),
                                       rhs=Gm[:, g].rearrange("p t c -> p (t c)"), start=True, stop=True)
                    return ins
                P.op("pe", f_toep, reads=[R_X], writes=[R_PS[pi]])
                P.op("dve", lambda e, pi=pi, ti=ti: e.tensor_tensor(out=tmpT[ti][:], in0=psum[:, pi, :].rearrange("p (g n) -> p g n", g=4),
                                                                   in1=mask[:].unsqueeze(1).to_broadcast([128, 4, 128]), op=ALU.mult),
                     reads=[R_PS[pi], R_PC], writes=[R_TMPT[ti]])
                for g4 in range(4):
                    P.op("dve", lambda e, gb=gb, g4=g4, ti=ti: e.scalar_tensor_tensor(
                        out=W5[:, gb, 0, g4, :], in0=identF[:], scalar=sm[:, 3, gb * 4 + g4:gb * 4 + g4 + 1], in1=tmpT[ti][:, g4, :],
                        op0=ALU.mult, op1=ALU.add), reads=[R_TMPT[ti], R_CONST, R_IN], acc=[R_W5])
                pi2 = next_ps()

                def f_tr(e, gb=gb, pi2=pi2):
                    ins = None
                    for g4 in range(4):
                        g = gb * 4 + g4
                        ins = e.transpose(out=psum[:, pi2, g4 * 128:(g4 + 1) * 128],
                                          in_=XBm[:, g].rearrange("p t c -> p (t c)"), identity=identF[:])
                    return ins
                P.op("pe", f_tr, reads=[R_X, R_CONST], writes=[R_PS[pi2]])
                pv = psum[:, pi2, :].rearrange("p (g n) -> p g n", g=4)
                P.op("act", lambda e, gb=gb, pv=pv: e.activation(out=W5[:, gb, 1, :, :], in_=pv, func=AF.Copy),
                     reads=[R_PS[pi2]], acc=[R_W5])
                P.op("act", lambda e, gb=gb, pv=pv: e.activation(out=W5[:, gb, 2, :, 0:64], in_=pv[:, :, 64:128], func=AF.Copy),
                     reads=[R_PS[pi2]], acc=[R_W5])
                P.op("act", lambda e, gb=gb, pv=pv: e.activation(out=W5[:, gb, 2, :, 64:128], in_=pv[:, :, 0:64], func=AF.Copy, scale=-1.0),
                     reads=[R_PS[pi2]], acc=[R_W5])
                pi3 = next_ps()

                def f_sw(e, gb=gb, pi3=pi3):
                    ins = None
                    for g4 in range(4):
                        g = gb * 4 + g4
                        ins = e.matmul(psum[:, pi3, g4 * 128:(g4 + 1) * 128], lhsT=Smat[:],
                                       rhs=Gm[:, g].rearrange("p t c -> p (t c)"), start=True, stop=True)
                    return ins
                P.op("pe", f_sw, reads=[R_X, R_PC], writes=[R_PS[pi3]])
                for g4 in range(4):
                    g = gb * 4 + g4
                    P.op("act", lambda e, gb=gb, g4=g4, g=g: e.activation(out=W5[:, gb, 3, g4, :], in_=Gm[:, g].rearrange("p t c -> p (t c)"),
                                                                         func=AF.Copy, scale=Rt[:, g:g + 1]),
                         reads=[R_X, R_A], acc=[R_W5])
                    P.op("act", lambda e, gb=gb, g4=g4, g=g, pi3=pi3: e.activation(out=W5[:, gb, 4, g4, :], in_=psum[:, pi3, g4 * 128:(g4 + 1) * 128],
                                                                                  func=AF.Copy, scale=Rt[:, g:g + 1]),
                         reads=[R_PS[pi3], R_A], acc=[R_W5])
            P.op("sp", lambda e, l=l: e.dma_start(out=ssmW_d[l].rearrange("a p n -> p a n"),
                                                  in_=W5[:].rearrange("p a k g n -> p a (k g n)")),
                 reads=[R_W5], writes=[R_SSMW[l]], dma=f"d_plssm{l}")

    if debug == "prologue":
        rb_d = dram_out("rb_dbg", [128, DEPTH, 32])
        P.op("sp", lambda e: e.dma_start(out=rb_d[:], in_=Rb_all[:]), reads=[R_RB], writes=[R_OUT], dma="d_out")
        fin = [R_OUT] + R_SSMW[:n_layers] + R_TABS[:n_layers] + R_POOLW[:n_layers] + R_WBF[:n_layers]
        P.op("sp", None, reads=fin, sig=False)
        sems = {k: es.enter_context(nc.semaphore(k)) for k in P.cnt}
        with nc.Block() as block:
            P.emit(nc, block, sems)
        es.close()
        return nc

    P.fence()
    xres = sb("xres", [128, 8, ST], F32)
    h = sb("h", [128, 8, ST], BF16)
    gate = sb("gate", [128, 8, ST], BF16)
    ycat = sb("ycat", [128, 8, ST], BF16)
    upool = sb("upool", [128, 4, 16 + ST], BF16)
    spool = sb("spool", [128, 4, ST], BF16)
    ysT = sb("ysT", [128, 4, ST], BF16)
    xsq0 = spool[:].rearrange("p a n -> p (a n)").rearrange("p (a n) -> p a n", n=512)
    xsq1 = upool[:].rearrange("p a n -> p (a n)")[:, 0:4096].rearrange("p (a n) -> p a n", n=512)
    xsqs = [xsq0, xsq1]
    ZYf = sb("ZY", [128, 4096], BF16)
    ZY = ZYf[:].rearrange("p (t f) -> p t f", t=8)
    Zs = ZYf[:].rearrange("p (g t c) -> p g t c", g=32, t=8)
    U = sb("U", [128, 32, 128], BF16)
    rs = sb("rs", [128, 2, 512], F32)
    t1 = sb("t1", [128, 4, 128], F32)
    t2 = sb("t2", [128, 4, 128], F32)
    NQ = 2
    Q = [sb(f"Q{i}", [128, 4, 129], F32) for i in range(NQ)]
    Pc = [sb(f"Pc{i}", [128, 4, 128], BF16) for i in range(NQ)]
    Ps = [sb(f"Ps{i}", [128, 4, 128], BF16) for i in range(NQ)]
    sig = sb("sig", [128, 2, 512], BF16)
    xs = sb("xs", [128, 2, D], F32)
    pwk = xs[:].rearrange("p a n -> p (a n)").bitcast(BF16)[:, 0:3 * (16 + ST)].rearrange("p (a n) -> p a n", a=3)
    carry = sb("carry", [128, DEPTH, 32], F32)
    halo = sb("halo", [128, DEPTH, 4, 16], BF16)
    cfix = sb("cfix", [128, 4, 16], F32)
    epsT = sb("epsT", [128, 1], F32)
    NWS, NSW, NSTB = 4, 3, 2
    WS = [sb(f"WS{i}", [128, 8 * 512], BF16) for i in range(NWS)]
    SSw = [sb(f"SSw{i}", [128, 5, 4, 128], BF16) for i in range(NSW)]
    SSt = [sb(f"SSt{i}", [128, 2, 4, 128], F32) for i in range(NSTB)]
    R_WS = [Res(f"WS{i}") for i in range(NWS)]
    R_SW = [Res(f"SW{i}") for i in range(NSW)]
    R_STB = [Res(f"STB{i}") for i in range(NSTB)]
    R_ZY, R_U = Res("ZY"), Res("U")
    R_XRES = [Res("xres0"), Res("xres1")]
    R_RS = [Res("rs0"), Res("rs1")]
    R_H = [Res("h0"), Res("h1")]
    R_GATE = [Res("g0"), Res("g1")]
    R_YCAT = [Res("yc0"), Res("yc1")]
    R_UPOOL, R_SPOOL, R_YST = Res("upool"), Res("spool"), Res("ysT")
    R_T1, R_T2 = Res("t1"), Res("t2")
    R_Q = [Res(f"Q{i}") for i in range(NQ)]
    R_PCS = [Res(f"PcPs{i}") for i in range(NQ)]
    R_SIG = [Res("sig0"), Res("sig1")]
    R_XS = [Res("xs0"), Res("xs1")]
    R_CARRY = [[Res(f"carry{l}_{gb}") for gb in range(8)] for l in range(DEPTH)]
    R_HALO = [Res(f"halo{l}") for l in range(DEPTH)]
    R_C2 = Res("const2")
    ws_rr, sw_rr, stb_rr, xs_rr, ev_rr = [0], [0], [0], [0], [0]

    def psB(pi):
        return psum[:, pi, :].bitcast(BF16)

    def evac_eng():
        ev_rr[0] ^= 1
        return "act" if ev_rr[0] else "dve"

    def copy_op(eng, out, in_, reads, acc):
        if eng == "act":
            P.op("act", lambda e: e.activation(out=out, in_=in_, func=AF.Copy), reads=reads, acc=acc)
        else:
            P.op(eng, lambda e: e.tensor_copy(out=out, in_=in_), reads=reads, acc=acc)

    P.op("pool", lambda e: e.memset(epsT[:], EPS), writes=[R_C2])
    P.op("pool", lambda e: e.memset(cfix[:], 1.0), writes=[R_C2])
    for f in range(4):
        w = POOL_WINDOWS[f]
        for t in range(w - 1):
            P.op("pool", lambda e, f=f, t=t, w=w: e.memset(cfix[:, f, t:t + 1], float(w) / float(t + 1)), writes=[R_C2])

    def load_ws(src_ap, nparts, reads):
        i = ws_rr[0]
        ws_rr[0] = (i + 1) % NWS
        dst = WS[i][:, 0:nparts * 512].rearrange("p (a n) -> p a n", n=512)
        P.op("sp", lambda e: e.dma_start(out=dst, in_=src_ap), reads=reads, writes=[R_WS[i]], dma=f"d_ws{i}")
        return i

    def wsv(i):
        return WS[i][:].rearrange("p (a n) -> p a n", n=512)

    def mm_group(pi, lhs_fn, rhs_fn, nk, reads):
        def f(e):
            ins = None
            for kc in range(nk):
                ins = e.matmul(psum[:, pi, :], lhsT=lhs_fn(kc), rhs=rhs_fn(kc), start=(kc == 0), stop=(kc == nk - 1))
            return ins
        P.op("pe", f, reads=reads, writes=[R_PS[pi]])

    def R_XSQ(n):
        return R_SPOOL if n == 0 else R_UPOOL

    def norm_square(n, f=None):
        nr = slice(n * 512, (n + 1) * 512)
        if f is None:
            P.op("act", lambda e: e.activation(out=xsqs[n], in_=xres[:, :, nr], func=AF.Square), reads=[R_XRES[n]], writes=[R_XSQ(n)])
        elif f == 0:
            P.op("act", lambda e: e.activation(out=xsqs[n][:, 0, :], in_=xres[:, 0, nr], func=AF.Square), reads=[R_XRES[n]], writes=[R_XSQ(n)])
        else:
            P.op("act", lambda e: e.activation(out=xsqs[n][:, f, :], in_=xres[:, f, nr], func=AF.Square), reads=[R_XRES[n]], acc=[R_XSQ(n)])

    def norm_rstd(n):
        pi = next_ps()
        mm_group(pi, lambda kc: onesB[:], lambda kc: xsqs[n][:, kc, :], 8, [R_XSQ(n), R_CONST])
        P.op("act", lambda e, pi=pi: e.activation(out=rs[:, n, :], in_=psum[:, pi, :], func=AF.Ln, bias=epsT[:], scale=1.0),
             reads=[R_PS[pi], R_C2], writes=[R_RS[n]])
        P.op("act", lambda e: e.activation(out=rs[:, n, :], in_=rs[:, n, :], func=AF.Exp, scale=-0.5), reads=[R_RS[n]], writes=[R_RS[n]])

    def norm_apply(n, lnext):
        nr = slice(n * 512, (n + 1) * 512)
        for f in range(8):
            if lnext is None:
                P.op("dve", lambda e, f=f: e.scalar_tensor_tensor(
                    out=xres[:, f, nr], in0=xres[:, f, nr], scalar=finalg[:, f:f + 1], in1=rs[:, n, :], op0=ALU.mult, op1=ALU.mult),
                    reads=[R_RS[n], R_SMALLP, R_XRES[n]], acc=[R_XRES[n]])
            else:
                P.op("dve", lambda e, f=f: e.scalar_tensor_tensor(
                    out=h[:, f, nr], in0=xres[:, f, nr], scalar=smallp[:, lnext, f:f + 1], in1=rs[:, n, :], op0=ALU.mult, op1=ALU.mult),
                    reads=[R_XRES[n], R_RS[n], R_SMALLP], acc=[R_H[n]])

    for st in range(n_sub):
        seq, half = st // 2, st % 2
        t0 = half * ST
        if half == 0:
            P.op("pool", lambda e: e.memset(carry[:], 0.0), writes=[r for rl in R_CARRY for r in rl])
            P.op("pool", lambda e: e.memset(halo[:], 0.0), writes=R_HALO)
        for tt in range(8):
            si = xs_rr[0]
            xs_rr[0] ^= 1
            P.op("sp", lambda e, si=si, tt=tt, seq=seq, t0=t0: e.dma_start(out=xs[:, si, :], in_=x_d[seq, t0 + tt * 128:t0 + (tt + 1) * 128, :]),
                 writes=[R_XS[si]], dma=f"d_xs{si}")
            for fh in range(2):
                pi = next_ps()

                def f_xt(e, si=si, fh=fh, pi=pi):
                    ins = None
                    for f4 in range(4):
                        f = fh * 4 + f4
                        ins = e.transpose(out=psum[:, pi, f4 * 128:(f4 + 1) * 128], in_=xs[:, si, f * 128:(f + 1) * 128], identity=identF[:])
                    return ins
                P.op("pe", f_xt, reads=[R_XS[si], R_CONST], writes=[R_PS[pi]])
                copy_op(evac_eng(), xres[:, fh * 4:(fh + 1) * 4, tt * 128:(tt + 1) * 128],
                        psum[:, pi, :].rearrange("p (f n) -> p f n", f=4), [R_PS[pi]], [R_XRES[tt // 4]])

        for l in range(n_layers):
            k0 = half * NK
            win = wbf_in[l].rearrange("(a p) n -> p a n", p=128)
            if l == 0:
                for n in range(2):
                    norm_square(n)
                    norm_rstd(n)
                    norm_apply(n, 0)
            wi_g = [load_ws(win[:, :, 1024 + sg * 512:1024 + (sg + 1) * 512], 8, [R_WBF[l]]) for sg in range(2)]

            def gate_group(fo, n, wi_g=wi_g):
                nr = slice(n * 512, (n + 1) * 512)
                wi, f4 = wi_g[fo // 4], fo % 4
                pi = next_ps()
                mm_group(pi, lambda kc: wsv(wi)[:, kc, f4 * 128:(f4 + 1) * 128], lambda kc: h[:, kc, nr], 8, [R_WS[wi], R_H[n]])
                P.op("act", lambda e: e.activation(out=gate[:, fo, nr], in_=psum[:, pi, :], func=AF.Silu), reads=[R_PS[pi]], acc=[R_GATE[n]])
            if l > 0:
                for fo in range(4):
                    gate_group(fo, 0)
            wi_s = load_ws(win[:, :, 512:1024], 8, [R_WBF[l]])
            for tau in range(8):
                pi = next_ps()
                mm_group(pi, lambda kc, tau=tau: h[:, kc, :].rearrange("p (k t) -> p t k", t=8)[:, tau, :],
                         lambda kc, wi_s=wi_s: wsv(wi_s)[:, kc, :], 8, [R_WS[wi_s], R_H[0], R_H[1]])
                copy_op(evac_eng(), Zs[:, :, tau, :], psum[:, pi, :].rearrange("p (g c) -> p g c", c=16), [R_PS[pi]], [R_ZY])
            for gq in range(4):
                pi = next_ps()

                def f_tr(e, gq=gq, pi=pi):
                    ins = None
                    for g8 in range(8):
                        g = gq * 8 + g8
                        ins = e.transpose(out=psB(pi)[:, g8 * 128:(g8 + 1) * 128], in_=Zs[:, g].rearrange("p t c -> p (t c)"), identity=identB[:])
                    return ins
                P.op("pe", f_tr, reads=[R_ZY, R_CONST], writes=[R_PS[pi]])
                copy_op(evac_eng(), U[:, gq * 8:(gq + 1) * 8, :], psB(pi).rearrange("p (g k) -> p g k", g=8), [R_PS[pi]], [R_U])

            wi_p = load_ws(win[:, :, 0:512], 8, [R_WBF[l]])
            fillers = []

            def pool_in_group(f, n, wi_p=wi_p, l=l):
                nr = slice(n * 512, (n + 1) * 512)
                pi = next_ps()
                mm_group(pi, lambda kc: wsv(wi_p)[:, kc, f * 128:(f + 1) * 128], lambda kc: h[:, kc, nr], 8, [R_WS[wi_p], R_H[n]])
                P.op("act", lambda e: e.activation(out=upool[:, f, 16 + n * 512:16 + (n + 1) * 512], in_=psum[:, pi, :], func=AF.Copy),
                     reads=[R_PS[pi]], acc=[R_UPOOL])

            def pool_sums(l=l, half=half):
                LT = 16 + ST
                R_PWK = R_XS
                P.op("dve", lambda e: e.tensor_copy(out=halo[:, l, :, :], in_=upool[:, :, ST:ST + 16]), reads=[R_UPOOL], writes=[R_HALO[l]])
                P.op("dve", lambda e: e.tensor_tensor(out=spool[:, 0, :], in0=upool[:, 0, 16:LT], in1=upool[:, 0, 15:LT - 1], op=ALU.add),
                     reads=[R_UPOOL], writes=[R_SPOOL])
                for f in (1, 2, 3):
                    P.op("dve", lambda e, f=f: e.tensor_tensor(out=pwk[:, 0, 1:LT], in0=upool[:, f, 1:LT], in1=upool[:, f, 0:LT - 1], op=ALU.add),
                         reads=[R_UPOOL], writes=R_PWK)
                    if f == 1:
                        P.op("dve", lambda e: e.tensor_tensor(out=spool[:, 1, :], in0=pwk[:, 0, 16:LT], in1=pwk[:, 0, 14:LT - 2], op=ALU.add),
                             reads=R_PWK, acc=[R_SPOOL])
                        continue
                    P.op("dve", lambda e: e.tensor_tensor(out=pwk[:, 1, 3:LT], in0=pwk[:, 0, 3:LT], in1=pwk[:, 0, 1:LT - 2], op=ALU.add),
                         reads=R_PWK, writes=R_PWK)
                    if f == 2:
                        P.op("dve", lambda e: e.tensor_tensor(out=spool[:, 2, :], in0=pwk[:, 1, 16:LT], in1=pwk[:, 1, 12:LT - 4], op=ALU.add),
                             reads=R_PWK, acc=[R_SPOOL])
                        continue
                    P.op("dve", lambda e: e.tensor_tensor(out=pwk[:, 2, 7:LT], in0=pwk[:, 1, 7:LT], in1=pwk[:, 1, 3:LT - 4], op=ALU.add),
                         reads=R_PWK, writes=R_PWK)
                    P.op("dve", lambda e: e.tensor_tensor(out=spool[:, 3, :], in0=pwk[:, 2, 16:LT], in1=pwk[:, 2, 8:LT - 8], op=ALU.add),
                         reads=R_PWK, acc=[R_SPOOL])
                if half == 0:
                    P.op("dve", lambda e: e.tensor_tensor(out=spool[:, :, 0:16], in0=spool[:, :, 0:16], in1=cfix[:], op=ALU.mult),
                         reads=[R_C2], writes=[R_SPOOL])

            P.op("dve", lambda e, l=l: e.tensor_copy(out=upool[:, :, 0:16], in_=halo[:, l, :, :]), reads=[R_HALO[l]], writes=[R_UPOOL])
            for fo in range(4):
                for n in range(2):
                    if l > 0 and n == 0:
                        continue
                    fillers.append(lambda fo=fo, n=n: gate_group(fo, n))
            for f in range(4):
                for n in range(2):
                    fillers.append(lambda f=f, n=n: pool_in_group(f, n))
            fillers.append(pool_sums)
            for fo in range(4, 8):
                for n in range(2):
                    fillers.append(lambda fo=fo, n=n: gate_group(fo, n))
            fillers.reverse()

            def run_fillers(k):
                for _ in range(k):
                    if fillers:
                        fillers.pop()()

            slots = {}

            def ssm_front(gb, l=l, k0=k0):
                wi = sw_rr[0]
                sw_rr[0] = (wi + 1) % NSW
                ti = stb_rr[0]
                stb_rr[0] = (ti + 1) % NSTB
                qi = gb % NQ
                slots[gb] = (wi, qi)
                P.op("sp", lambda e: e.dma_start(out=SSw[wi][:].rearrange("p a g n -> p (a g n)"), in_=ssmW_d[l, gb]),
                     reads=[R_SSMW[l]], writes=[R_SW[wi]], dma=f"d_sw{wi}")
                P.op("sp", lambda e: e.dma_start(out=SSt[ti][:], in_=tabs_d[l, gb][:, :, :, k0:k0 + NK]),
                     reads=[R_TABS[l]], writes=[R_STB[ti]], dma=f"d_stb{ti}")
                pa, pb_ = next_ps(), next_ps()

                def f_v(e):
                    ins = None
                    for kind, pi in ((1, pa), (2, pb_)):
                        for g4 in range(4):
                            ins = e.matmul(psum[:, pi, g4 * 128:(g4 + 1) * 128], lhsT=SSw[wi][:, kind, g4, :], rhs=U[:, gb * 4 + g4, :],
                                           start=True, stop=True)
                    return ins
                P.op("pe", f_v, reads=[R_SW[wi], R_U], writes=[R_PS[pa], R_PS[pb_]])

                def pv(pi):
                    return psum[:, pi, :].rearrange("p (g k) -> p g k", g=4)
                P.op("dve", lambda e: e.tensor_tensor(out=t1[:], in0=pv(pa), in1=SSt[ti][:, 0], op=ALU.mult),
                     reads=[R_PS[pa], R_STB[ti]], writes=[R_T1])
                P.op("dve", lambda e: e.tensor_tensor(out=t2[:], in0=pv(pb_), in1=SSt[ti][:, 1], op=ALU.mult),
                     reads=[R_PS[pb_], R_STB[ti]], writes=[R_T2])
                P.op("dve", lambda e: e.tensor_tensor(out=t1[:], in0=t1[:], in1=t2[:], op=ALU.add), reads=[R_T1, R_T2], writes=[R_T1])
                P.op("dve", lambda e: e.tensor_copy(out=Q[qi][:, :, 0], in_=carry[:, l, gb * 4:(gb + 1) * 4]),
                     reads=[R_CARRY[l][gb]], writes=[R_Q[qi]])
                for g4 in range(4):
                    g = gb * 4 + g4
                    P.op("dve", lambda e, g4=g4, g=g: e.tensor_tensor_scan(
                        out=Q[qi][:, g4, 1:129], data0=Rb_all[:, l, g:g + 1].to_broadcast([128, 128]), data1=t1[:, g4, :],
                        initial=carry[:, l, g:g + 1], op0=ALU.mult, op1=ALU.add),
                        reads=[R_T1, R_RB, R_CARRY[l][gb]], acc=[R_Q[qi]])
                P.op("dve", lambda e: e.tensor_copy(out=carry[:, l, gb * 4:(gb + 1) * 4], in_=Q[qi][:, :, 128]),
                     reads=[R_Q[qi]], writes=[R_CARRY[l][gb]])
                P.op("dve", lambda e: e.tensor_tensor(out=Pc[qi][:], in0=Q[qi][:, :, 0:128], in1=SSt[ti][:, 0], op=ALU.mult),
                     reads=[R_Q[qi], R_STB[ti]], writes=[R_PCS[qi]])
                P.op("dve", lambda e: e.tensor_tensor(out=Ps[qi][:], in0=Q[qi][:, :, 0:128], in1=SSt[ti][:, 1], op=ALU.mult),
                     reads=[R_Q[qi], R_STB[ti]], acc=[R_PCS[qi]])

            def ssm_back(gb):
                wi, qi = slots[gb]
                py = next_ps()

                def f_y(e):
                    ins = None
                    for g4 in range(4):
                        o = psum[:, py, g4 * 128:(g4 + 1) * 128]
                        e.matmul(o, lhsT=U[:, gb * 4 + g4, :], rhs=SSw[wi][:, 0, g4, :], start=True, stop=False)
                        e.matmul(o, lhsT=Pc[qi][:, g4, :], rhs=SSw[wi][:, 3, g4, :], start=False, stop=False)
                        ins = e.matmul(o, lhsT=Ps[qi][:, g4, :], rhs=SSw[wi][:, 4, g4, :], start=False, stop=True)
                    return ins
                P.op("pe", f_y, reads=[R_SW[wi], R_U, R_PCS[qi]], writes=[R_PS[py]])
                P.op("act", lambda e: e.activation(
                    out=ZY[:, :, gb * 64:(gb + 1) * 64].rearrange("p t (g c) -> p g t c", g=4),
                    in_=psum[:, py, :].rearrange("p (g t c) -> p g t c", g=4, t=8), func=AF.Gelu_apprx_tanh),
                    reads=[R_PS[py]], acc=[R_ZY])

            def b3_tile(f):
                pi = next_ps()

                def f_tr(e, f=f, pi=pi):
                    ins = None
                    for tau in range(8):
                        ins = e.transpose(out=psB(pi)[:, tau * 128:(tau + 1) * 128], in_=ZY[:, tau, f * 128:(f + 1) * 128], identity=identB[:])
                    return ins
                P.op("pe", f_tr, reads=[R_ZY, R_CONST], writes=[R_PS[pi]])
                copy_op(evac_eng(), ysT[:, f, :].rearrange("p (k t) -> p t k", t=8), psB(pi).rearrange("p (t k) -> p t k", t=8),
                        [R_PS[pi]], [R_YST])

            wg = None
            LAG = 1
            for s_ in range(8 + LAG):
                if s_ < 8:
                    ssm_front(s_)
                run_fillers(2)
                if s_ == 3:
                    wg = load_ws(wbf_glu[l].rearrange("(a p) n -> p a n", p=128), 4, [R_WBF[l]])
                    P.op("sp", lambda e, wg=wg, l=l: e.dma_start(out=WS[wg][:, 2048:3072], in_=poolW_d[l]),
                         reads=[R_POOLW[l]], acc=[R_WS[wg]], dma=f"d_ws{wg}")
                if s_ >= LAG:
                    ssm_back(s_ - LAG)
                    if (s_ - LAG) % 2 == 1:
                        b3_tile((s_ - LAG) // 2)
            run_fillers(len(fillers))

            pwv = WS[wg][:, 2048:3072].rearrange("p (g k n) -> p g k n", g=4, k=2)
            for n in range(2):
                nr = slice(n * 512, (n + 1) * 512)
                for f in range(4):
                    pi = next_ps()

                    def f_pm(e, f=f, n=n, nr=nr, pi=pi, pwv=pwv):
                        e.matmul(psum[:, pi, :], lhsT=pwv[:, f, 0, :], rhs=spool[:, f, nr], start=True, stop=False)
                        return e.matmul(psum[:, pi, :], lhsT=pwv[:, f, 1, :], rhs=upool[:, f, 16 + n * 512:16 + (n + 1) * 512], start=False, stop=True)
                    P.op("pe", f_pm, reads=[R_WS[wg], R_SPOOL, R_UPOOL], writes=[R_PS[pi]])
                    P.op("dve", lambda e, f=f, nr=nr, pi=pi, l=l: e.scalar_tensor_tensor(
                        out=ycat[:, f, nr], in0=psum[:, pi, :], scalar=smallp[:, l, 8 + f:9 + f], in1=gate[:, f, nr], op0=ALU.mult, op1=ALU.mult),
                        reads=[R_PS[pi], R_GATE[n], R_SMALLP], acc=[R_YCAT[n]])
            gluv = WS[wg][:, 0:2048].rearrange("p (a n) -> p a n", n=512)
            wov = wbf_out[l].rearrange("(a p) n -> p a n", p=128)
            wi_o = [load_ws(wov[:, :, so * 512:(so + 1) * 512], 8, [R_WBF[l]]) for so in range(2)]

            def c1_half(n, l=l, gluv=gluv, wg=wg):
                nr = slice(n * 512, (n + 1) * 512)
                for fo in range(4):
                    pi = next_ps()
                    sgi = fo % 2
                    mm_group(pi, lambda kc, fo=fo: gluv[:, kc, fo * 128:(fo + 1) * 128], lambda kc: ysT[:, kc, nr], 4, [R_WS[wg], R_YST])
                    P.op("act", lambda e, fo=fo, pi=pi, sgi=sgi: e.activation(out=sig[:, sgi, :], in_=psum[:, pi, :], func=AF.Sigmoid,
                                                                             bias=smallp[:, l, 12 + fo:13 + fo], scale=1.0),
                         reads=[R_PS[pi], R_SMALLP], writes=[R_SIG[sgi]])
                    P.op("dve", lambda e, fo=fo, sgi=sgi: e.tensor_tensor(out=sig[:, sgi, :], in0=sig[:, sgi, :], in1=ysT[:, fo, nr], op=ALU.mult),
                         reads=[R_SIG[sgi], R_YST], writes=[R_SIG[sgi]])
                    P.op("dve", lambda e, fo=fo, sgi=sgi: e.tensor_tensor(out=ycat[:, 4 + fo, nr], in0=sig[:, sgi, :], in1=gate[:, 4 + fo, nr], op=ALU.mult),
                         reads=[R_SIG[sgi], R_GATE[n]], acc=[R_YCAT[n]])

            lnext = l + 1 if l + 1 < n_layers else None

            def c2_half(n, wi_o=wi_o, lnext=lnext):
                nr = slice(n * 512, (n + 1) * 512)
                for fo in range(8):
                    if n == 1 and fo == 4:
                        norm_rstd(0)
                        norm_apply(0, lnext)
                    wi, f4 = wi_o[fo // 4], fo % 4
                    pi = next_ps()
                    mm_group(pi, lambda kc, wi=wi, f4=f4: wsv(wi)[:, kc, f4 * 128:(f4 + 1) * 128], lambda kc: ycat[:, kc, nr], 8, [R_WS[wi], R_YCAT[n]])
                    P.op("dve", lambda e, fo=fo, pi=pi: e.tensor_tensor(out=xres[:, fo, nr], in0=xres[:, fo, nr], in1=psum[:, pi, :], op=ALU.add),
                         reads=[R_PS[pi], R_XRES[n]], acc=[R_XRES[n]])
                    norm_square(n, fo)
            c1_half(0)
            c1_half(1)
            c2_half(0)
            c2_half(1)
            norm_rstd(1)
            norm_apply(1, lnext)

        for tt in range(8):
            si = xs_rr[0]
            xs_rr[0] ^= 1
            for fh in range(2):
                pi = next_ps()

                def f_ot(e, tt=tt, fh=fh, pi=pi):
                    ins = None
                    for f4 in range(4):
                        f = fh * 4 + f4
                        ins = e.transpose(out=psum[:, pi, f4 * 128:(f4 + 1) * 128], in_=xres[:, f, tt * 128:(tt + 1) * 128], identity=identF[:])
                    return ins
                P.op("pe", f_ot, reads=[R_XRES[tt // 4], R_CONST], writes=[R_PS[pi]])
                copy_op(evac_eng(), xs[:, si, fh * 512:(fh + 1) * 512], psum[:, pi, :], [R_PS[pi]], [R_XS[si]])
            P.op("sp", lambda e, si=si, tt=tt, seq=seq, t0=t0: e.dma_start(out=out_d[seq, t0 + tt * 128:t0 + (tt + 1) * 128, :], in_=xs[:, si, :]),
                 reads=[R_XS[si]], acc=[R_OUT], dma="d_out")

    P.op("sp", None, reads=[R_OUT], sig=False)
    sems = {k: es.enter_context(nc.semaphore(k)) for k in P.cnt}
    with nc.Block() as block:
        P.emit(nc, block, sems)
    es.close()
    return nc


def prep_inputs(inp):
    f = np.float32
    shared = {}
    shared["w_in"] = np.ascontiguousarray(inp["w_in"], dtype=f)
    shared["w_out"] = np.ascontiguousarray(inp["w_out"], dtype=f)
    shared["glu_w"] = np.ascontiguousarray(inp["glu_w"], dtype=f)
    shared["pool_w"] = np.ascontiguousarray(np.transpose(inp["pool_w"], (0, 2, 1, 3)), dtype=f)
    ng = np.transpose(np.asarray(inp["norm_g"], f).reshape(DEPTH, 8, 128), (2, 0, 1))
    psc = np.transpose(np.asarray(inp["pool_scale"], f).reshape(DEPTH, 4, 128), (2, 0, 1))
    gb = np.transpose(np.asarray(inp["glu_b"], f).reshape(DEPTH, 4, 128), (2, 0, 1))
    shared["smallp"] = np.ascontiguousarray(np.concatenate([ng, psc, gb], axis=2), dtype=f)
    shared["finalg"] = np.ascontiguousarray(np.asarray(inp["final_g"], f).reshape(8, 128).T)

    def dup(a):
        t = np.transpose(np.asarray(a, f), (0, 2, 1))
        return np.concatenate([t, t], axis=1)
    are = dup(inp["a_re"])
    aim = dup(inp["a_im"])
    ldt = np.broadcast_to(np.asarray(inp["log_dt"], f)[:, None, :], (DEPTH, 128, G))
    dv = np.asarray(inp["d_skip"], f).reshape(DEPTH, G, 16)
    dvec = np.broadcast_to(np.transpose(dv, (0, 2, 1))[:, None, :, :], (DEPTH, 8, 16, G)).reshape(DEPTH, 128, G)
    shared["ssm_small"] = np.ascontiguousarray(np.stack([are, aim, ldt, dvec], axis=2), dtype=f)

    def dupb(a):
        t = np.transpose(np.asarray(a, f), (0, 2, 1, 3)).reshape(DEPTH, 64, G * 16)
        return np.concatenate([t, t], axis=1)

    def dupc(a):
        t = np.transpose(np.asarray(a, f), (0, 3, 1, 2)).reshape(DEPTH, 64, G * 16)
        return np.concatenate([t, t], axis=1)
    shared["ssm_bc"] = np.ascontiguousarray(
        np.stack([dupb(inp["b_re"]), dupb(inp["b_im"]), dupc(inp["c_re"]), dupc(inp["c_im"])], axis=2), dtype=f)
    x = np.ascontiguousarray(inp["x"], dtype=f)
    maps = []
    for c in range(NCORES):
        m = dict(shared)
        m["x"] = x[c * NSEQ:(c + 1) * NSEQ]
        maps.append(m)
    return maps


def kernel(**inputs):
    maps = prep_inputs(inputs)
    nc = build_program()
    res = run_bass_kernel_spmd(nc, maps, core_ids=list(range(NCORES)))
    out = np.concatenate([np.asarray(r["out"], dtype=np.float32) for r in res.results], axis=0)
    return out
```

```python
import math
from contextlib import ExitStack

import numpy as np
import concourse.bass as bass
import concourse.mybir as mybir
from concourse.bass_utils import run_bass_kernel_spmd

F32 = mybir.dt.float32
BF16 = mybir.dt.bfloat16
I32 = mybir.dt.int32
AF = mybir.ActivationFunctionType
ALU = mybir.AluOpType

NCORES = 8
DEPTH = 4
D = 1024
SEQ = 2048
NSEQ = 2
ST = 1024
NK = ST // 8
G = 32
TWO_PI = float(2.0 * math.pi)
INV_2PI = float(1.0 / (2.0 * math.pi))
HALF_PI = float(math.pi / 2.0)
SIN_SCALE = 1.0 - 4e-5
EPS = 1e-5
POOL_WINDOWS = (2, 4, 8, 16)


class Res:
    __slots__ = ("name", "w", "r")

    def __init__(self, name):
        self.name = name
        self.w = {}
        self.r = {}


class Prog:
    ENG = ("pe", "act", "dve", "pool", "sp")

    def __init__(self):
        self.ops = {e: [] for e in self.ENG}
        self.cnt = {}
        self.seen = {e: {} for e in self.ENG}
        self.pending = {e: {} for e in self.ENG}

    def fence(self):
        for e in self.ENG:
            self.pending[e] = dict(self.cnt)

    def op(self, eng, fn, reads=(), writes=(), dma=None, sig=True, acc=()):
        need = self.pending[eng]
        self.pending[eng] = {}
        for r in acc:
            for k, v in r.r.items():
                if need.get(k, 0) < v:
                    need[k] = v
        for r in reads:
            for k, v in r.w.items():
                if need.get(k, 0) < v:
                    need[k] = v
        for r in writes:
            for k, v in r.w.items():
                if need.get(k, 0) < v:
                    need[k] = v
            for k, v in r.r.items():
                if need.get(k, 0) < v:
                    need[k] = v
        waits = []
        seen = self.seen[eng]
        for k, v in need.items():
            if eng == "pe" and k == "pe":
                continue
            if seen.get(k, 0) >= v:
                continue
            seen[k] = v
            waits.append((k, v))
        tok = None
        inc = 1
        if sig:
            key = dma if dma is not None else eng
            inc = 16 if dma is not None else 1
            self.cnt[key] = self.cnt.get(key, 0) + inc
            tok = (key, self.cnt[key])
            for r in reads:
                if r.r.get(key, 0) < tok[1]:
                    r.r[key] = tok[1]
            for r in writes:
                r.w = {key: tok[1]}
            for r in acc:
                if r.w.get(key, 0) < tok[1]:
                    r.w[key] = tok[1]
        self.ops[eng].append((waits, fn, tok, inc))
        return tok

    def emit(self, nc, block, sems):
        def mk(name):
            def body(e):
                for waits, fn, tok, inc in self.ops[name]:
                    for k, v in waits:
                        e.wait_ge(sems[k], v)
                    if fn is None:
                        continue
                    ins = fn(e)
                    if tok is not None:
                        ins.then_inc(sems[tok[0]], inc)
            return body

        block.tensor(mk("pe"))
        block.scalar(mk("act"))
        block.vector(mk("dve"))
        block.gpsimd(mk("pool"))
        block.sync(mk("sp"))


def build_program(debug=False, n_layers=DEPTH, n_sub=4):
    nc = bass.Bass("TRN2", target_bir_lowering=False)
    P = Prog()
    es = ExitStack()

    def dram_in(name, shape, dt=F32):
        return nc.dram_tensor(name, list(shape), dt, kind="ExternalInput").ap()

    def dram_out(name, shape, dt=F32):
        return nc.dram_tensor(name, list(shape), dt, kind="ExternalOutput").ap()

    def dram_scr(name, shape, dt):
        kind = "ExternalOutput" if debug else "Internal"
        return nc.dram_tensor(name, list(shape), dt, kind=kind).ap()

    def sb(name, shape, dt, stack=None):
        return (stack or es).enter_context(nc.sbuf_tensor(name, list(shape), dt))

    x_d = dram_in("x", [NSEQ, SEQ, D])
    w_in_d = dram_in("w_in", [DEPTH, D, 2 * D])
    w_out_d = dram_in("w_out", [DEPTH, D, D])
    glu_w_d = dram_in("glu_w", [DEPTH, 512, 512])
    pool_w_d = dram_in("pool_w", [DEPTH, 128, 4, 128])
    smallp_d = dram_in("smallp", [128, DEPTH, 16])
    finalg_d = dram_in("finalg", [128, 8])
    ssm_small_d = dram_in("ssm_small", [DEPTH, 128, 4, 32])
    ssm_bc_d = dram_in("ssm_bc", [DEPTH, 128, 4, 512])
    out_d = dram_out("out", [NSEQ, SEQ, D])

    ssmW_d = dram_scr("ssmW", [DEPTH, 8, 128, 5 * 4 * 128], BF16)
    tabs_d = dram_scr("tabs", [DEPTH, 8, 128, 2, 4, 256], F32)
    poolW_d = dram_scr("poolW", [DEPTH, 128, 4 * 2 * 128], BF16)
    R_SSMW = [Res(f"ssmW{l}") for l in range(DEPTH)]
    R_TABS = [Res(f"tabs{l}") for l in range(DEPTH)]
    R_POOLW = [Res(f"poolW{l}") for l in range(DEPTH)]
    R_OUT = Res("out")

    identF = sb("identF", [128, 128], F32)
    identB = sb("identB", [128, 128], BF16)
    onesB = sb("onesB", [128, 128], BF16)
    Rb_all = sb("Rb_all", [128, DEPTH, 32], F32)
    smallp = sb("smallp_sb", [128, DEPTH, 16], F32)
    finalg = sb("finalg_sb", [128, 8], F32)
    psum = es.enter_context(nc.psum_tensor("psum", [128, 8, 512], F32))
    R_CONST = Res("const")
    R_RB = Res("Rb")
    R_SMALLP = Res("smallp")
    R_PS = [Res(f"ps{i}") for i in range(8)]
    ps_rr = [0]

    def next_ps():
        i = ps_rr[0]
        ps_rr[0] = (i + 1) % 8
        return i

    P.op("pool", lambda e: e.memset(identF[:], 1.0), writes=[R_CONST])
    P.op("pool", lambda e: e.affine_select(out=identF[:], in_=identF[:], pattern=[[-1, 128]],
                                           compare_op=ALU.is_equal, fill=0.0, base=0, channel_multiplier=1),
         reads=[R_CONST], writes=[R_CONST])
    P.op("pool", lambda e: e.tensor_copy(out=identB[:], in_=identF[:]), reads=[R_CONST], writes=[R_CONST])
    P.op("pool", lambda e: e.memset(onesB[:], 1.0 / 1024.0), writes=[R_CONST])
    P.op("sp", lambda e: e.dma_start(out=smallp[:], in_=smallp_d[:]), writes=[R_SMALLP], dma="d_small")
    P.op("sp", lambda e: e.dma_start(out=finalg[:], in_=finalg_d[:]), writes=[R_SMALLP], dma="d_small")

    with ExitStack() as ps_:
        def pb(name, shape, dt):
            return sb("pl_" + name, shape, dt, ps_)
        mask = pb("mask", [128, 128], F32)
        iotaKi = pb("iotaKi", [128, 256], I32)
        iotaK = pb("iotaK", [128, 256], F32)
        JTi = pb("JTi", [128, 7, 32], I32)
        JT = pb("JT", [128, 7, 32], F32)
        sm = pb("sm", [128, 4, 32], F32)
        bc = pb("bc", [128, 4, 512], F32)
        pw = pb("pw", [128, 4, 128], F32)
        PWo = pb("PWo", [128, 4, 2, 128], BF16)
        dtt = pb("dtt", [128, 32], F32)
        ard = pb("ard", [128, 32], F32)
        ang = pb("ang", [128, 32], F32)
        ARJ = pb("ARJ", [128, 7, 32], F32)
        ANJ = pb("ANJ", [128, 7, 32], F32)
        MAGP = pb("MAGP", [128, 7, 32], F32)
        MAGN = pb("MAGN", [128, 7, 32], F32)
        NIj = pb("NIj", [128, 7, 32], I32)
        RRj = pb("RRj", [128, 7, 32], F32)
        SN = pb("SN", [128, 7, 32], F32)
        CS = pb("CS", [128, 7, 32], F32)
        EXre = pb("EXre", [128, 32, 8], F32)
        EXim = pb("EXim", [128, 32, 8], F32)
        EYre = pb("EYre", [128, 32, 8], F32)
        EYim = pb("EYim", [128, 32, 8], F32)
        s1 = pb("s1", [128, 32], F32)
        s2 = pb("s2", [128, 32], F32)
        s3 = pb("s3", [128, 32], F32)
        s4 = pb("s4", [128, 32], F32)
        fre = pb("fre", [128, 32], F32)
        fim = pb("fim", [128, 32], F32)
        Rt = pb("Rt", [128, 32], F32)
        nRt = pb("nRt", [128, 32], F32)
        TH = pb("TH", [128, 32], F32)
        NI8 = pb("NI8", [128, 32], I32)
        ta = pb("ta", [128, 32, 16], F32)
        tb_ = pb("tb", [128, 32, 16], F32)
        tc_ = pb("tc", [128, 32, 16], F32)
        td = pb("td", [128, 32, 16], F32)
        bbA = pb("bbA", [128, 32, 16], F32)
        bbB = pb("bbB", [128, 32, 16], F32)
        cA = pb("cA", [128, 32, 16], F32)
        cB = pb("cB", [128, 32, 16], F32)
        Smat = pb("Smat", [128, 128], F32)
        hpiT = pb("hpiT", [128, 1], F32)
        T1 = pb("T1", [128, 32, 8, 16], F32)
        T2 = pb("T2", [128, 32, 8, 16], F32)
        XBm = pb("XBm", [128, 32, 8, 16], F32)
        Gm = pb("Gm", [128, 32, 8, 16], F32)
        W5 = pb("W5", [128, 8, 5, 4, 128], BF16)
        tmpT = [pb(f"tmpT{i}", [128, 4, 128], F32) for i in range(2)]
        NTB = 2
        ANGq = [pb(f"ANGq{i}", [128, 4, 256], F32) for i in range(NTB)]
        NIq = [pb(f"NIq{i}", [128, 4, 256], I32) for i in range(NTB)]
        RRq = [pb(f"RRq{i}", [128, 4, 256], F32) for i in range(NTB)]
        ABq = [pb(f"ABq{i}", [128, 4, 256], F32) for i in range(NTB)]
        CSq = [pb(f"CSq{i}", [128, 2, 4, 256], F32) for i in range(NTB)]
        R_TQ = [Res(f"pl_tq{i}") for i in range(NTB)]
        R_CSQ = [Res(f"pl_csq{i}") for i in range(NTB)]
        R_BB = Res("pl_bb")

        R_PC = Res("pl_const")
        R_IN = Res("pl_in")
        R_A = Res("pl_a")
        R_E = Res("pl_E")
        R_X = Res("pl_X")
        R_T = Res("pl_T")
        R_W5 = Res("pl_W5")
        R_TMPT = [Res("pl_tmpT0"), Res("pl_tmpT1")]
        R_ANG = Res("pl_ang")
        R_TAB = Res("pl_tab")
        R_PW = Res("pl_pw")

        P.op("pool", lambda e: e.memset(mask[:], 1.0), writes=[R_PC])
        P.op("pool", lambda e: e.affine_select(out=mask[:], in_=mask[:], pattern=[[16, 8], [0, 16]],
                                               compare_op=ALU.is_ge, fill=0.0, base=15, channel_multiplier=-1),
             reads=[R_PC], writes=[R_PC])
        P.op("pool", lambda e: e.iota(iotaKi[:], pattern=[[1, 256]], base=0, channel_multiplier=0), writes=[R_PC])
        P.op("pool", lambda e: e.tensor_copy(out=iotaK[:], in_=iotaKi[:]), reads=[R_PC], writes=[R_PC])
        P.op("pool", lambda e: e.iota(JTi[:], pattern=[[-1, 7], [0, 32]], base=7, channel_multiplier=0), writes=[R_PC])
        P.op("pool", lambda e: e.tensor_copy(out=JT[:], in_=JTi[:]), reads=[R_PC], writes=[R_PC])
        P.op("pool", lambda e: e.tensor_copy(out=Smat[:, 0:64], in_=identF[:, 64:128]), reads=[R_CONST], writes=[R_PC])
        P.op("pool", lambda e: e.tensor_scalar(out=Smat[:, 64:128], in0=identF[:, 0:64], scalar1=-1.0, scalar2=None, op0=ALU.mult),
             reads=[R_CONST, R_PC], writes=[R_PC])
        P.op("pool", lambda e: e.memset(hpiT[:], HALF_PI), reads=[R_PC], writes=[R_PC])

        def bc3(ap32):
            return ap32.unsqueeze(1).to_broadcast([128, 7, 32])

        def bcc(ap32):
            return ap32.unsqueeze(2).to_broadcast([128, 32, 16])

        def bE(apE, lo, hi):
            return apE[lo:hi].unsqueeze(3).to_broadcast([hi - lo, 32, 8, 16])

        def bB(apB, lo, hi):
            return apB[lo:hi].unsqueeze(2).to_broadcast([hi - lo, 32, 8, 16])

        def bR(apR, lo, hi):
            return apR[lo:hi].rearrange("p (a g) -> p a g", g=4).unsqueeze(3).to_broadcast([hi - lo, 8, 4, 128])

        wbf_in = dram_scr("wbf_in", [DEPTH, D, 2 * D], BF16)
        wbf_out = dram_scr("wbf_out", [DEPTH, D, D], BF16)
        wbf_glu = dram_scr("wbf_glu", [DEPTH, 512, 512], BF16)
        R_WBF = [Res(f"wbf{l}") for l in range(DEPTH)]
        for l in range(n_layers):
            for a in range(8):
                P.op("pool", lambda e, l=l, a=a: e.dma_start(out=wbf_in[l, a * 128:(a + 1) * 128, :], in_=w_in_d[l, a * 128:(a + 1) * 128, :]),
                     writes=[R_WBF[l]], dma=f"d_wbf{l}")
            for a in range(4):
                P.op("pool", lambda e, l=l, a=a: e.dma_start(out=wbf_out[l, a * 256:(a + 1) * 256, :], in_=w_out_d[l, a * 256:(a + 1) * 256, :]),
                     writes=[R_WBF[l]], dma=f"d_wbf{l}")
            P.op("pool", lambda e, l=l: e.dma_start(out=wbf_glu[l], in_=glu_w_d[l]), writes=[R_WBF[l]], dma=f"d_wbf{l}")


        for l in range(n_layers):
            P.op("sp", lambda e, l=l: e.dma_start(out=sm[:], in_=ssm_small_d[l]), writes=[R_IN], dma="d_plin")
            P.op("sp", lambda e, l=l: e.dma_start(out=bc[:], in_=ssm_bc_d[l]), writes=[R_IN], dma="d_plin")
            P.op("sp", lambda e, l=l: e.dma_start(out=pw[:], in_=pool_w_d[l]), writes=[R_PW], dma="d_plpw")
            are, aim, ldt, dvec = sm[:, 0, :], sm[:, 1, :], sm[:, 2, :], sm[:, 3, :]
            bre = bc[:, 0, :].rearrange("p (g c) -> p g c", c=16)
            bim = bc[:, 1, :].rearrange("p (g c) -> p g c", c=16)
            cre = bc[:, 2, :].rearrange("p (g c) -> p g c", c=16)
            cim = bc[:, 3, :].rearrange("p (g c) -> p g c", c=16)

            for g4 in range(4):
                P.op("pool", lambda e, g4=g4: e.tensor_scalar(out=PWo[:, g4, 0, :], in0=pw[:, g4, :],
                                                            scalar1=1.0 / POOL_WINDOWS[g4], scalar2=None, op0=ALU.mult),
                     reads=[R_PW], writes=[R_PW])
            P.op("pool", lambda e: e.tensor_scalar(out=PWo[:, :, 1, :], in0=pw[:, :, :], scalar1=-1.0, scalar2=None, op0=ALU.mult),
                 reads=[R_PW], writes=[R_PW])
            P.op("sp", lambda e, l=l: e.dma_start(out=poolW_d[l], in_=PWo[:].rearrange("p a b c -> p (a b c)")),
                 reads=[R_PW], writes=[R_POOLW[l]], dma=f"d_plpo{l}")

            P.op("act", lambda e: e.activation(out=dtt[:], in_=ldt, func=AF.Exp), reads=[R_IN], writes=[R_A])
            P.op("dve", lambda e: e.tensor_tensor(out=ard[:], in0=are, in1=dtt[:], op=ALU.mult), reads=[R_IN, R_A], writes=[R_A])
            P.op("dve", lambda e: e.tensor_tensor(out=ang[:], in0=aim, in1=dtt[:], op=ALU.mult), reads=[R_IN, R_A], writes=[R_A])
            P.op("dve", lambda e: e.tensor_tensor(out=ARJ[:], in0=JT[:], in1=bc3(ard[:]), op=ALU.mult), reads=[R_PC, R_A], writes=[R_A])
            P.op("dve", lambda e: e.tensor_tensor(out=ANJ[:], in0=JT[:], in1=bc3(ang[:]), op=ALU.mult), reads=[R_PC, R_A], writes=[R_A])
            P.op("act", lambda e: e.activation(out=MAGP[:], in_=ARJ[:], func=AF.Exp), reads=[R_A], writes=[R_A])
            P.op("act", lambda e: e.activation(out=MAGN[:], in_=ARJ[:], func=AF.Exp, scale=-1.0), reads=[R_A], writes=[R_A])
            P.op("act", lambda e: e.activation(out=NIj[:], in_=ANJ[:], func=AF.Copy, scale=INV_2PI), reads=[R_A], writes=[R_A])
            P.op("dve", lambda e: e.scalar_tensor_tensor(out=RRj[:], in0=NIj[:], scalar=-TWO_PI, in1=ANJ[:], op0=ALU.mult, op1=ALU.add),
                 reads=[R_A], writes=[R_A])
            P.op("act", lambda e: e.activation(out=SN[:], in_=RRj[:], func=AF.Sin, scale=SIN_SCALE), reads=[R_A], writes=[R_A])
            P.op("dve", lambda e: e.tensor_scalar(out=RRj[:], in0=ANJ[:], scalar1=HALF_PI, scalar2=None, op0=ALU.add), reads=[R_A], writes=[R_A])
            P.op("act", lambda e: e.activation(out=NIj[:], in_=RRj[:], func=AF.Copy, scale=INV_2PI), reads=[R_A], writes=[R_A])
            P.op("dve", lambda e: e.scalar_tensor_tensor(out=RRj[:], in0=NIj[:], scalar=-TWO_PI, in1=RRj[:], op0=ALU.mult, op1=ALU.add),
                 reads=[R_A], writes=[R_A])
            P.op("act", lambda e: e.activation(out=CS[:], in_=RRj[:], func=AF.Sin, scale=SIN_SCALE), reads=[R_A], writes=[R_A])
            def Ev(t):
                return t[:].rearrange("p g t -> p t g")[:, 0:7, :]
            P.op("dve", lambda e: e.tensor_tensor(out=Ev(EXre), in0=MAGP[:], in1=CS[:], op=ALU.mult), reads=[R_A], writes=[R_E])
            P.op("dve", lambda e: e.tensor_tensor(out=Ev(EXim), in0=MAGP[:], in1=SN[:], op=ALU.mult), reads=[R_A], writes=[R_E])
            P.op("dve", lambda e: e.tensor_tensor(out=Ev(EYre), in0=MAGN[:], in1=CS[:], op=ALU.mult), reads=[R_A], writes=[R_E])
            P.op("dve", lambda e: e.scalar_tensor_tensor(out=Ev(EYim), in0=MAGN[:], scalar=-1.0, in1=SN[:], op0=ALU.mult, op1=ALU.mult),
                 reads=[R_A], writes=[R_E])
            P.op("dve", lambda e: e.memset(EXre[:, :, 7:8], 1.0), writes=[R_E])
            P.op("dve", lambda e: e.memset(EXim[:, :, 7:8], 0.0), writes=[R_E])
            P.op("dve", lambda e: e.memset(EYre[:, :, 7:8], 1.0), writes=[R_E])
            P.op("dve", lambda e: e.memset(EYim[:, :, 7:8], 0.0), writes=[R_E])
            lre, lim = EXre[:, :, 6], EXim[:, :, 6]
            P.op("dve", lambda e: e.tensor_scalar(out=s1[:], in0=lre, scalar1=-1.0, scalar2=None, op0=ALU.add), reads=[R_E], writes=[R_A])
            P.op("dve", lambda e: e.tensor_tensor(out=s2[:], in0=are, in1=are, op=ALU.mult), reads=[R_IN], writes=[R_A])
            P.op("dve", lambda e: e.tensor_tensor(out=s3[:], in0=aim, in1=aim, op=ALU.mult), reads=[R_IN], writes=[R_A])
            P.op("dve", lambda e: e.tensor_tensor(out=s2[:], in0=s2[:], in1=s3[:], op=ALU.add), reads=[R_A], writes=[R_A])
            P.op("dve", lambda e: e.reciprocal(out=s2[:], in_=s2[:]), reads=[R_A], writes=[R_A])
            P.op("dve", lambda e: e.tensor_tensor(out=s3[:], in0=s1[:], in1=are, op=ALU.mult), reads=[R_A, R_IN], writes=[R_A])
            P.op("dve", lambda e: e.tensor_tensor(out=s4[:], in0=lim, in1=aim, op=ALU.mult), reads=[R_E, R_IN], writes=[R_A])
            P.op("dve", lambda e: e.tensor_tensor(out=s3[:], in0=s3[:], in1=s4[:], op=ALU.add), reads=[R_A], writes=[R_A])
            P.op("dve", lambda e: e.tensor_tensor(out=fre[:], in0=s3[:], in1=s2[:], op=ALU.mult), reads=[R_A], writes=[R_A])
            P.op("dve", lambda e: e.tensor_tensor(out=s3[:], in0=lim, in1=are, op=ALU.mult), reads=[R_E, R_IN], writes=[R_A])
            P.op("dve", lambda e: e.tensor_tensor(out=s4[:], in0=s1[:], in1=aim, op=ALU.mult), reads=[R_A, R_IN], writes=[R_A])
            P.op("dve", lambda e: e.tensor_tensor(out=s3[:], in0=s3[:], in1=s4[:], op=ALU.subtract), reads=[R_A], writes=[R_A])
            P.op("dve", lambda e: e.tensor_tensor(out=fim[:], in0=s3[:], in1=s2[:], op=ALU.mult), reads=[R_A], writes=[R_A])
            P.op("dve", lambda e: e.tensor_tensor(out=ta[:], in0=bre, in1=bcc(fre[:]), op=ALU.mult), reads=[R_A, R_IN], writes=[R_BB])
            P.op("dve", lambda e: e.tensor_tensor(out=tb_[:], in0=bim, in1=bcc(fim[:]), op=ALU.mult), reads=[R_A, R_IN], acc=[R_BB])
            P.op("dve", lambda e: e.tensor_tensor(out=tc_[:], in0=bim, in1=bcc(fre[:]), op=ALU.mult), reads=[R_A, R_IN], acc=[R_BB])
            P.op("dve", lambda e: e.tensor_tensor(out=td[:], in0=bre, in1=bcc(fim[:]), op=ALU.mult), reads=[R_A, R_IN], acc=[R_BB])
            P.op("dve", lambda e: e.tensor_tensor(out=bbA[0:64], in0=ta[0:64], in1=tb_[0:64], op=ALU.subtract), reads=[R_BB], acc=[R_BB])
            P.op("dve", lambda e: e.tensor_tensor(out=bbA[64:128], in0=tc_[64:128], in1=td[64:128], op=ALU.add), reads=[R_BB], acc=[R_BB])
            P.op("dve", lambda e: e.scalar_tensor_tensor(out=bbB[0:64], in0=tc_[0:64], scalar=-1.0, in1=td[0:64], op0=ALU.mult, op1=ALU.subtract),
                 reads=[R_BB], acc=[R_BB])
            P.op("dve", lambda e: e.tensor_tensor(out=bbB[64:128], in0=ta[64:128], in1=tb_[64:128], op=ALU.subtract), reads=[R_BB], acc=[R_BB])
            P.op("act", lambda e: e.activation(out=cA[0:64], in_=cre[0:64], func=AF.Copy), reads=[R_IN], acc=[R_BB])
            P.op("act", lambda e: e.activation(out=cA[64:128], in_=cim[64:128], func=AF.Copy, scale=-1.0), reads=[R_IN], acc=[R_BB])
            P.op("act", lambda e: e.activation(out=cB[0:64], in_=cim[0:64], func=AF.Copy, scale=-1.0), reads=[R_IN], acc=[R_BB])
            P.op("act", lambda e: e.activation(out=cB[64:128], in_=cre[64:128], func=AF.Copy, scale=-1.0), reads=[R_IN], acc=[R_BB])
            P.op("act", lambda e: e.activation(out=Rt[:], in_=ard[:], func=AF.Exp, scale=8.0), reads=[R_A], writes=[R_A])
            P.op("act", lambda e, l=l: e.activation(out=Rb_all[:, l, :], in_=Rt[:], func=AF.Copy), reads=[R_A], writes=[R_RB])
            P.op("dve", lambda e: e.tensor_scalar(out=s1[:], in0=ang[:], scalar1=8.0, scalar2=None, op0=ALU.mult), reads=[R_A], writes=[R_A])
            P.op("act", lambda e: e.activation(out=NI8[:], in_=s1[:], func=AF.Copy, scale=INV_2PI), reads=[R_A], writes=[R_A])
            P.op("dve", lambda e: e.scalar_tensor_tensor(out=TH[:], in0=NI8[:], scalar=-TWO_PI, in1=s1[:], op0=ALU.mult, op1=ALU.add),
                 reads=[R_A], writes=[R_A])

            tv = tabs_d[l].rearrange("a p two g k -> p a two g k")
            tq_rr = [0]

            def table_piece(gb, l=l, tv=tv):
                i = tq_rr[0]
                tq_rr[0] = (i + 1) % NTB
                gs = slice(gb * 4, gb * 4 + 4)
                P.op("dve", lambda e: e.tensor_tensor(out=ANGq[i][:], in0=TH[:, gs].unsqueeze(2).to_broadcast([128, 4, 256]),
                                                     in1=iotaK[:].unsqueeze(1).to_broadcast([128, 4, 256]), op=ALU.mult),
                     reads=[R_A, R_PC], writes=[R_TQ[i]])
                P.op("act", lambda e: e.activation(out=NIq[i][:], in_=ANGq[i][:], func=AF.Copy, scale=INV_2PI), reads=[R_TQ[i]], acc=[R_TQ[i]])
                P.op("dve", lambda e: e.scalar_tensor_tensor(out=RRq[i][:], in0=NIq[i][:], scalar=-TWO_PI, in1=ANGq[i][:], op0=ALU.mult, op1=ALU.add),
                     reads=[R_TQ[i]], acc=[R_TQ[i]])
                P.op("act", lambda e: e.activation(out=ABq[i][:], in_=RRq[i][:], func=AF.Abs), reads=[R_TQ[i]], acc=[R_TQ[i]])
                P.op("act", lambda e: e.activation(out=CSq[i][:, 1], in_=RRq[i][:], func=AF.Sin, scale=SIN_SCALE), reads=[R_TQ[i]], writes=[R_CSQ[i]])
                P.op("act", lambda e: e.activation(out=CSq[i][:, 0], in_=ABq[i][:], func=AF.Sin, scale=-1.0, bias=hpiT[:]),
                     reads=[R_TQ[i], R_PC], acc=[R_CSQ[i]])
                P.op("sp", lambda e: e.dma_start(out=tv[:, gb], in_=CSq[i][:]), reads=[R_CSQ[i]], acc=[R_TABS[l]], dma=f"d_csq{i}")

            bigops = [
                lambda: P.op("dve", lambda e: e.tensor_tensor(out=T1[:], in0=bE(EXre, 0, 128), in1=bB(bbA, 0, 128), op=ALU.mult), reads=[R_E, R_BB], writes=[R_T]),
                lambda: P.op("dve", lambda e: e.tensor_tensor(out=T2[:], in0=bE(EXim, 0, 128), in1=bB(bbB, 0, 128), op=ALU.mult), reads=[R_E, R_BB], acc=[R_T]),
                lambda: P.op("dve", lambda e: e.tensor_tensor(out=XBm[:], in0=T1[:], in1=T2[:], op=ALU.add), reads=[R_T], writes=[R_X]),
                lambda: P.op("dve", lambda e: e.tensor_tensor(out=T1[:], in0=bE(EYre, 0, 128), in1=bB(cA, 0, 128), op=ALU.mult), reads=[R_E, R_BB], writes=[R_T]),
                lambda: P.op("dve", lambda e: e.tensor_tensor(out=T2[:], in0=bE(EYim, 0, 128), in1=bB(cB, 0, 128), op=ALU.mult), reads=[R_E, R_BB], acc=[R_T]),
                lambda: P.op("dve", lambda e: e.tensor_tensor(out=Gm[:], in0=T1[:], in1=T2[:], op=ALU.add), reads=[R_T], acc=[R_X]),
            ]
            for i_, bo in enumerate(bigops):
                table_piece(i_)
                bo()

            for gb in range(8):
                if gb in (2, 5):
                    table_piece(6 + (gb == 5))
                ti = gb % 2
                pi = next_ps()

                def f_toep(e, gb=gb, pi=pi):
                    ins = None
                    for g4 in range(4):
                        g = gb * 4 + g4
                        ins = e.matmul(psum[:, pi, g4 * 128:(g4 + 1) * 128],
                                       lhsT=XBm[:, g].rearrange("p t c -> p (t c)"),
                                       rhs=Gm[:, g].rearrange("p t c -> p (t c)"), start=True, stop=True)
                    return ins
                P.op("pe", f_toep, reads=[R_X], writes=[R_PS[pi]])
                P.op("dve", lambda e, pi=pi, ti=ti: e.tensor_tensor(out=tmpT[ti][:], in0=psum[:, pi, :].rearrange("p (g n) -> p g n", g=4),
                                                                   in1=mask[:].unsqueeze(1).to_broadcast([128, 4, 128]), op=ALU.mult),
                     reads=[R_PS[pi], R_PC], writes=[R_TMPT[ti]])
                for g4 in range(4):
                    P.op("dve", lambda e, gb=gb, g4=g4, ti=ti: e.scalar_tensor_tensor(
                        out=W5[:, gb, 0, g4, :], in0=identF[:], scalar=sm[:, 3, gb * 4 + g4:gb * 4 + g4 + 1], in1=tmpT[ti][:, g4, :],
                        op0=ALU.mult, op1=ALU.add), reads=[R_TMPT[ti], R_CONST, R_IN], acc=[R_W5])
                pi2 = next_ps()

                def f_tr(e, gb=gb, pi2=pi2):
                    ins = None
                    for g4 in range(4):
                        g = gb * 4 + g4
                        ins = e.transpose(out=psum[:, pi2, g4 * 128:(g4 + 1) * 128],
                                          in_=XBm[:, g].rearrange("p t c -> p (t c)"), identity=identF[:])
                    return ins
                P.op("pe", f_tr, reads=[R_X, R_CONST], writes=[R_PS[pi2]])
                pv = psum[:, pi2, :].rearrange("p (g n) -> p g n", g=4)
                P.op("act", lambda e, gb=gb, pv=pv: e.activation(out=W5[:, gb, 1, :, :], in_=pv, func=AF.Copy),
                     reads=[R_PS[pi2]], acc=[R_W5])
                P.op("act", lambda e, gb=gb, pv=pv: e.activation(out=W5[:, gb, 2, :, 0:64], in_=pv[:, :, 64:128], func=AF.Copy),
                     reads=[R_PS[pi2]], acc=[R_W5])
                P.op("act", lambda e, gb=gb, pv=pv: e.activation(out=W5[:, gb, 2, :, 64:128], in_=pv[:, :, 0:64], func=AF.Copy, scale=-1.0),
                     reads=[R_PS[pi2]], acc=[R_W5])
                pi3 = next_ps()

                def f_sw(e, gb=gb, pi3=pi3):
                    ins = None
                    for g4 in range(4):
                        g = gb * 4 + g4
                        ins = e.matmul(psum[:, pi3, g4 * 128:(g4 + 1) * 128], lhsT=Smat[:],
                                       rhs=Gm[:, g].rearrange("p t c -> p (t c)"), start=True, stop=True)
                    return ins
                P.op("pe", f_sw, reads=[R_X, R_PC], writes=[R_PS[pi3]])
                for g4 in range(4):
                    g = gb * 4 + g4
                    P.op("act", lambda e, gb=gb, g4=g4, g=g: e.activation(out=W5[:, gb, 3, g4, :], in_=Gm[:, g].rearrange("p t c -> p (t c)"),
                                                                         func=AF.Copy, scale=Rt[:, g:g + 1]),
                         reads=[R_X, R_A], acc=[R_W5])
                    P.op("act", lambda e, gb=gb, g4=g4, g=g, pi3=pi3: e.activation(out=W5[:, gb, 4, g4, :], in_=psum[:, pi3, g4 * 128:(g4 + 1) * 128],
                                                                                  func=AF.Copy, scale=Rt[:, g:g + 1]),
                         reads=[R_PS[pi3], R_A], acc=[R_W5])
            P.op("sp", lambda e, l=l: e.dma_start(out=ssmW_d[l].rearrange("a p n -> p a n"),
                                                  in_=W5[:].rearrange("p a k g n -> p a (k g n)")),
                 reads=[R_W5], writes=[R_SSMW[l]], dma=f"d_plssm{l}")

    if debug == "prologue":
        rb_d = dram_out("rb_dbg", [128, DEPTH, 32])
        P.op("sp", lambda e: e.dma_start(out=rb_d[:], in_=Rb_all[:]), reads=[R_RB], writes=[R_OUT], dma="d_out")
        fin = [R_OUT] + R_SSMW[:n_layers] + R_TABS[:n_layers] + R_POOLW[:n_layers] + R_WBF[:n_layers]
        P.op("sp", None, reads=fin, sig=False)
        sems = {k: es.enter_context(nc.semaphore(k)) for k in P.cnt}
        with nc.Block() as block:
            P.emit(nc, block, sems)
        es.close()
        return nc

    P.fence()
    xres = sb("xres", [128, 8, ST], F32)
    h = sb("h", [128, 8, ST], BF16)
    gate = sb("gate", [128, 8, ST], BF16)
    ycat = sb("ycat", [128, 8, ST], BF16)
    upool = sb("upool", [128, 4, 16 + ST], BF16)
    spool = sb("spool", [128, 4, ST], BF16)
    ysT = sb("ysT", [128, 4, ST], BF16)
    xsq0 = spool[:].rearrange("p a n -> p (a n)").rearrange("p (a n) -> p a n", n=512)
    xsq1 = upool[:].rearrange("p a n -> p (a n)")[:, 0:4096].rearrange("p (a n) -> p a n", n=512)
    xsqs = [xsq0, xsq1]
    ZYf = sb("ZY", [128, 4096], BF16)
    ZY = ZYf[:].rearrange("p (t f) -> p t f", t=8)
    Zs = ZYf[:].rearrange("p (g t c) -> p g t c", g=32, t=8)
    U = sb("U", [128, 32, 128], BF16)
    rs = sb("rs", [128, 2, 512], F32)
    t1 = sb("t1", [128, 4, 128], F32)
    t2 = sb("t2", [128, 4, 128], F32)
    NQ = 2
    Q = [sb(f"Q{i}", [128, 4, 129], F32) for i in range(NQ)]
    Pc = [sb(f"Pc{i}", [128, 4, 128], BF16) for i in range(NQ)]
    Ps = [sb(f"Ps{i}", [128, 4, 128], BF16) for i in range(NQ)]
    sig = sb("sig", [128, 2, 512], BF16)
    xs = sb("xs", [128, 2, D], F32)
    pwk = xs[:].rearrange("p a n -> p (a n)").bitcast(BF16)[:, 0:3 * (16 + ST)].rearrange("p (a n) -> p a n", a=3)
    carry = sb("carry", [128, DEPTH, 32], F32)
    halo = sb("halo", [128, DEPTH, 4, 16], BF16)
    cfix = sb("cfix", [128, 4, 16], F32)
    epsT = sb("epsT", [128, 1], F32)
    NWS, NSW, NSTB = 4, 3, 2
    WS = [sb(f"WS{i}", [128, 8 * 512], BF16) for i in range(NWS)]
    SSw = [sb(f"SSw{i}", [128, 5, 4, 128], BF16) for i in range(NSW)]
    SSt = [sb(f"SSt{i}", [128, 2, 4, 128], F32) for i in range(NSTB)]
    R_WS = [Res(f"WS{i}") for i in range(NWS)]
    R_SW = [Res(f"SW{i}") for i in range(NSW)]
    R_STB = [Res(f"STB{i}") for i in range(NSTB)]
    R_ZY, R_U = Res("ZY"), Res("U")
    R_XRES = [Res("xres0"), Res("xres1")]
    R_RS = [Res("rs0"), Res("rs1")]
    R_H = [Res("h0"), Res("h1")]
    R_GATE = [Res("g0"), Res("g1")]
    R_YCAT = [Res("yc0"), Res("yc1")]
    R_UPOOL, R_SPOOL, R_YST = Res("upool"), Res("spool"), Res("ysT")
    R_T1, R_T2 = Res("t1"), Res("t2")
    R_Q = [Res(f"Q{i}") for i in range(NQ)]
    R_PCS = [Res(f"PcPs{i}") for i in range(NQ)]
    R_SIG = [Res("sig0"), Res("sig1")]
    R_XS = [Res("xs0"), Res("xs1")]
    R_CARRY = [[Res(f"carry{l}_{gb}") for gb in range(8)] for l in range(DEPTH)]
    R_HALO = [Res(f"halo{l}") for l in range(DEPTH)]
    R_C2 = Res("const2")
    ws_rr, sw_rr, stb_rr, xs_rr, ev_rr = [0], [0], [0], [0], [0]

    def psB(pi):
        return psum[:, pi, :].bitcast(BF16)

    def evac_eng():
        ev_rr[0] ^= 1
        return "act" if ev_rr[0] else "dve"

    def copy_op(eng, out, in_, reads, acc):
        if eng == "act":
            P.op("act", lambda e: e.activation(out=out, in_=in_, func=AF.Copy), reads=reads, acc=acc)
        else:
            P.op(eng, lambda e: e.tensor_copy(out=out, in_=in_), reads=reads, acc=acc)

    P.op("pool", lambda e: e.memset(epsT[:], EPS), writes=[R_C2])
    P.op("pool", lambda e: e.memset(cfix[:], 1.0), writes=[R_C2])
    for f in range(4):
        w = POOL_WINDOWS[f]
        for t in range(w - 1):
            P.op("pool", lambda e, f=f, t=t, w=w: e.memset(cfix[:, f, t:t + 1], float(w) / float(t + 1)), writes=[R_C2])

    def load_ws(src_ap, nparts, reads):
        i = ws_rr[0]
        ws_rr[0] = (i + 1) % NWS
        dst = WS[i][:, 0:nparts * 512].rearrange("p (a n) -> p a n", n=512)
        P.op("sp", lambda e: e.dma_start(out=dst, in_=src_ap), reads=reads, writes=[R_WS[i]], dma=f"d_ws{i}")
        return i

    def wsv(i):
        return WS[i][:].rearrange("p (a n) -> p a n", n=512)

    def mm_group(pi, lhs_fn, rhs_fn, nk, reads):
        def f(e):
            ins = None
            for kc in range(nk):
                ins = e.matmul(psum[:, pi, :], lhsT=lhs_fn(kc), rhs=rhs_fn(kc), start=(kc == 0), stop=(kc == nk - 1))
            return ins
        P.op("pe", f, reads=reads, writes=[R_PS[pi]])

    def R_XSQ(n):
        return R_SPOOL if n == 0 else R_UPOOL

    def norm_square(n, f=None):
        nr = slice(n * 512, (n + 1) * 512)
        if f is None:
            P.op("act", lambda e: e.activation(out=xsqs[n], in_=xres[:, :, nr], func=AF.Square), reads=[R_XRES[n]], writes=[R_XSQ(n)])
        elif f == 0:
            P.op("act", lambda e: e.activation(out=xsqs[n][:, 0, :], in_=xres[:, 0, nr], func=AF.Square), reads=[R_XRES[n]], writes=[R_XSQ(n)])
        else:
            P.op("act", lambda e: e.activation(out=xsqs[n][:, f, :], in_=xres[:, f, nr], func=AF.Square), reads=[R_XRES[n]], acc=[R_XSQ(n)])

    def norm_rstd(n):
        pi = next_ps()
        mm_group(pi, lambda kc: onesB[:], lambda kc: xsqs[n][:, kc, :], 8, [R_XSQ(n), R_CONST])
        P.op("act", lambda e, pi=pi: e.activation(out=rs[:, n, :], in_=psum[:, pi, :], func=AF.Ln, bias=epsT[:], scale=1.0),
             reads=[R_PS[pi], R_C2], writes=[R_RS[n]])
        P.op("act", lambda e: e.activation(out=rs[:, n, :], in_=rs[:, n, :], func=AF.Exp, scale=-0.5), reads=[R_RS[n]], writes=[R_RS[n]])

    def norm_apply(n, lnext, fs=range(8)):
        nr = slice(n * 512, (n + 1) * 512)
        for f in fs:
            if lnext is None:
                P.op("dve", lambda e, f=f: e.scalar_tensor_tensor(
                    out=xres[:, f, nr], in0=xres[:, f, nr], scalar=finalg[:, f:f + 1], in1=rs[:, n, :], op0=ALU.mult, op1=ALU.mult),
                    reads=[R_RS[n], R_SMALLP, R_XRES[n]], acc=[R_XRES[n]])
            else:
                P.op("dve", lambda e, f=f: e.scalar_tensor_tensor(
                    out=h[:, f, nr], in0=xres[:, f, nr], scalar=smallp[:, lnext, f:f + 1], in1=rs[:, n, :], op0=ALU.mult, op1=ALU.mult),
                    reads=[R_XRES[n], R_RS[n], R_SMALLP], acc=[R_H[n]])

    for st in range(n_sub):
        seq, half = st // 2, st % 2
        t0 = half * ST
        if half == 0:
            P.op("pool", lambda e: e.memset(carry[:], 0.0), writes=[r for rl in R_CARRY for r in rl])
            P.op("pool", lambda e: e.memset(halo[:], 0.0), writes=R_HALO)
        for tt in range(8):
            si = xs_rr[0]
            xs_rr[0] ^= 1
            P.op("sp", lambda e, si=si, tt=tt, seq=seq, t0=t0: e.dma_start(out=xs[:, si, :], in_=x_d[seq, t0 + tt * 128:t0 + (tt + 1) * 128, :]),
                 writes=[R_XS[si]], dma=f"d_xs{si}")
            for fh in range(2):
                pi = next_ps()

                def f_xt(e, si=si, fh=fh, pi=pi):
                    ins = None
                    for f4 in range(4):
                        f = fh * 4 + f4
                        ins = e.transpose(out=psum[:, pi, f4 * 128:(f4 + 1) * 128], in_=xs[:, si, f * 128:(f + 1) * 128], identity=identF[:])
                    return ins
                P.op("pe", f_xt, reads=[R_XS[si], R_CONST], writes=[R_PS[pi]])
                copy_op(evac_eng(), xres[:, fh * 4:(fh + 1) * 4, tt * 128:(tt + 1) * 128],
                        psum[:, pi, :].rearrange("p (f n) -> p f n", f=4), [R_PS[pi]], [R_XRES[tt // 4]])

        for l in range(n_layers):
            k0 = half * NK
            win = wbf_in[l].rearrange("(a p) n -> p a n", p=128)
            if l == 0:
                for n in range(2):
                    norm_square(n)
                    norm_rstd(n)
                    norm_apply(n, 0)
            wi_g = [load_ws(win[:, :, 1024 + sg * 512:1024 + (sg + 1) * 512], 8, [R_WBF[l]]) for sg in range(2)]

            def gate_group(fo, n, wi_g=wi_g):
                nr = slice(n * 512, (n + 1) * 512)
                wi, f4 = wi_g[fo // 4], fo % 4
                pi = next_ps()
                mm_group(pi, lambda kc: wsv(wi)[:, kc, f4 * 128:(f4 + 1) * 128], lambda kc: h[:, kc, nr], 8, [R_WS[wi], R_H[n]])
                P.op("act", lambda e: e.activation(out=gate[:, fo, nr], in_=psum[:, pi, :], func=AF.Silu), reads=[R_PS[pi]], acc=[R_GATE[n]])
            if l > 0:
                for fo in range(4):
                    gate_group(fo, 0)
            wi_s = load_ws(win[:, :, 512:1024], 8, [R_WBF[l]])
            for tau in range(8):
                pi = next_ps()
                mm_group(pi, lambda kc, tau=tau: h[:, kc, :].rearrange("p (k t) -> p t k", t=8)[:, tau, :],
                         lambda kc, wi_s=wi_s: wsv(wi_s)[:, kc, :], 8, [R_WS[wi_s], R_H[0], R_H[1]])
                copy_op(evac_eng(), Zs[:, :, tau, :], psum[:, pi, :].rearrange("p (g c) -> p g c", c=16), [R_PS[pi]], [R_ZY])
            for gq in range(4):
                pi = next_ps()

                def f_tr(e, gq=gq, pi=pi):
                    ins = None
                    for g8 in range(8):
                        g = gq * 8 + g8
                        ins = e.transpose(out=psB(pi)[:, g8 * 128:(g8 + 1) * 128], in_=Zs[:, g].rearrange("p t c -> p (t c)"), identity=identB[:])
                    return ins
                P.op("pe", f_tr, reads=[R_ZY, R_CONST], writes=[R_PS[pi]])
                copy_op(evac_eng(), U[:, gq * 8:(gq + 1) * 8, :], psB(pi).rearrange("p (g k) -> p g k", g=8), [R_PS[pi]], [R_U])

            wi_p = load_ws(win[:, :, 0:512], 8, [R_WBF[l]])
            fillers = []

            def pool_in_group(f, n, wi_p=wi_p, l=l):
                nr = slice(n * 512, (n + 1) * 512)
                pi = next_ps()
                mm_group(pi, lambda kc: wsv(wi_p)[:, kc, f * 128:(f + 1) * 128], lambda kc: h[:, kc, nr], 8, [R_WS[wi_p], R_H[n]])
                P.op("act", lambda e: e.activation(out=upool[:, f, 16 + n * 512:16 + (n + 1) * 512], in_=psum[:, pi, :], func=AF.Copy),
                     reads=[R_PS[pi]], acc=[R_UPOOL])

            def pool_sums(l=l, half=half):
                LT = 16 + ST
                R_PWK = R_XS
                P.op("dve", lambda e: e.tensor_copy(out=halo[:, l, :, :], in_=upool[:, :, ST:ST + 16]), reads=[R_UPOOL], writes=[R_HALO[l]])
                P.op("dve", lambda e: e.tensor_tensor(out=spool[:, 0, :], in0=upool[:, 0, 16:LT], in1=upool[:, 0, 15:LT - 1], op=ALU.add),
                     reads=[R_UPOOL], writes=[R_SPOOL])
                for f in (1, 2, 3):
                    P.op("dve", lambda e, f=f: e.tensor_tensor(out=pwk[:, 0, 1:LT], in0=upool[:, f, 1:LT], in1=upool[:, f, 0:LT - 1], op=ALU.add),
                         reads=[R_UPOOL], writes=R_PWK)
                    if f == 1:
                        P.op("dve", lambda e: e.tensor_tensor(out=spool[:, 1, :], in0=pwk[:, 0, 16:LT], in1=pwk[:, 0, 14:LT - 2], op=ALU.add),
                             reads=R_PWK, acc=[R_SPOOL])
                        continue
                    P.op("dve", lambda e: e.tensor_tensor(out=pwk[:, 1, 3:LT], in0=pwk[:, 0, 3:LT], in1=pwk[:, 0, 1:LT - 2], op=ALU.add),
                         reads=R_PWK, writes=R_PWK)
                    if f == 2:
                        P.op("dve", lambda e: e.tensor_tensor(out=spool[:, 2, :], in0=pwk[:, 1, 16:LT], in1=pwk[:, 1, 12:LT - 4], op=ALU.add),
                             reads=R_PWK, acc=[R_SPOOL])
                        continue
                    P.op("dve", lambda e: e.tensor_tensor(out=pwk[:, 2, 7:LT], in0=pwk[:, 1, 7:LT], in1=pwk[:, 1, 3:LT - 4], op=ALU.add),
                         reads=R_PWK, writes=R_PWK)
                    P.op("dve", lambda e: e.tensor_tensor(out=spool[:, 3, :], in0=pwk[:, 2, 16:LT], in1=pwk[:, 2, 8:LT - 8], op=ALU.add),
                         reads=R_PWK, acc=[R_SPOOL])
                if half == 0:
                    P.op("dve", lambda e: e.tensor_tensor(out=spool[:, :, 0:16], in0=spool[:, :, 0:16], in1=cfix[:], op=ALU.mult),
                         reads=[R_C2], writes=[R_SPOOL])

            P.op("dve", lambda e, l=l: e.tensor_copy(out=upool[:, :, 0:16], in_=halo[:, l, :, :]), reads=[R_HALO[l]], writes=[R_UPOOL])
            for fo in range(4):
                for n in range(2):
                    if l > 0 and n == 0:
                        continue
                    fillers.append(lambda fo=fo, n=n: gate_group(fo, n))
            for f in range(4):
                for n in range(2):
                    fillers.append(lambda f=f, n=n: pool_in_group(f, n))
            fillers.append(pool_sums)
            for fo in range(4, 8):
                for n in range(2):
                    fillers.append(lambda fo=fo, n=n: gate_group(fo, n))
            fillers.reverse()

            def run_fillers(k):
                for _ in range(k):
                    if fillers:
                        fillers.pop()()

            slots = {}

            def ssm_front(gb, l=l, k0=k0):
                wi = sw_rr[0]
                sw_rr[0] = (wi + 1) % NSW
                ti = stb_rr[0]
                stb_rr[0] = (ti + 1) % NSTB
                qi = gb % NQ
                slots[gb] = (wi, qi)
                P.op("sp", lambda e: e.dma_start(out=SSw[wi][:].rearrange("p a g n -> p (a g n)"), in_=ssmW_d[l, gb]),
                     reads=[R_SSMW[l]], writes=[R_SW[wi]], dma=f"d_sw{wi}")
                P.op("sp", lambda e: e.dma_start(out=SSt[ti][:], in_=tabs_d[l, gb][:, :, :, k0:k0 + NK]),
                     reads=[R_TABS[l]], writes=[R_STB[ti]], dma=f"d_stb{ti}")
                pa, pb_ = next_ps(), next_ps()

                def f_v(e):
                    ins = None
                    for kind, pi in ((1, pa), (2, pb_)):
                        for g4 in range(4):
                            ins = e.matmul(psum[:, pi, g4 * 128:(g4 + 1) * 128], lhsT=SSw[wi][:, kind, g4, :], rhs=U[:, gb * 4 + g4, :],
                                           start=True, stop=True)
                    return ins
                P.op("pe", f_v, reads=[R_SW[wi], R_U], writes=[R_PS[pa], R_PS[pb_]])

                def pv(pi):
                    return psum[:, pi, :].rearrange("p (g k) -> p g k", g=4)
                P.op("dve", lambda e: e.tensor_tensor(out=t1[:], in0=pv(pa), in1=SSt[ti][:, 0], op=ALU.mult),
                     reads=[R_PS[pa], R_STB[ti]], writes=[R_T1])
                P.op("dve", lambda e: e.tensor_tensor(out=t2[:], in0=pv(pb_), in1=SSt[ti][:, 1], op=ALU.mult),
                     reads=[R_PS[pb_], R_STB[ti]], writes=[R_T2])
                P.op("dve", lambda e: e.tensor_tensor(out=t1[:], in0=t1[:], in1=t2[:], op=ALU.add), reads=[R_T1, R_T2], writes=[R_T1])
                P.op("dve", lambda e: e.tensor_copy(out=Q[qi][:, :, 0], in_=carry[:, l, gb * 4:(gb + 1) * 4]),
                     reads=[R_CARRY[l][gb]], writes=[R_Q[qi]])
                for g4 in range(4):
                    g = gb * 4 + g4
                    P.op("dve", lambda e, g4=g4, g=g: e.tensor_tensor_scan(
                        out=Q[qi][:, g4, 1:129], data0=Rb_all[:, l, g:g + 1].to_broadcast([128, 128]), data1=t1[:, g4, :],
                        initial=carry[:, l, g:g + 1], op0=ALU.mult, op1=ALU.add),
                        reads=[R_T1, R_RB, R_CARRY[l][gb]], acc=[R_Q[qi]])
                P.op("dve", lambda e: e.tensor_copy(out=carry[:, l, gb * 4:(gb + 1) * 4], in_=Q[qi][:, :, 128]),
                     reads=[R_Q[qi]], writes=[R_CARRY[l][gb]])
                P.op("dve", lambda e: e.tensor_tensor(out=Pc[qi][:], in0=Q[qi][:, :, 0:128], in1=SSt[ti][:, 0], op=ALU.mult),
                     reads=[R_Q[qi], R_STB[ti]], writes=[R_PCS[qi]])
                P.op("dve", lambda e: e.tensor_tensor(out=Ps[qi][:], in0=Q[qi][:, :, 0:128], in1=SSt[ti][:, 1], op=ALU.mult),
                     reads=[R_Q[qi], R_STB[ti]], acc=[R_PCS[qi]])

            def ssm_back(gb):
                wi, qi = slots[gb]
                py = next_ps()

                def f_y(e):
                    ins = None
                    for g4 in range(4):
                        o = psum[:, py, g4 * 128:(g4 + 1) * 128]
                        e.matmul(o, lhsT=U[:, gb * 4 + g4, :], rhs=SSw[wi][:, 0, g4, :], start=True, stop=False)
                        e.matmul(o, lhsT=Pc[qi][:, g4, :], rhs=SSw[wi][:, 3, g4, :], start=False, stop=False)
                        ins = e.matmul(o, lhsT=Ps[qi][:, g4, :], rhs=SSw[wi][:, 4, g4, :], start=False, stop=True)
                    return ins
                P.op("pe", f_y, reads=[R_SW[wi], R_U, R_PCS[qi]], writes=[R_PS[py]])
                P.op("act", lambda e: e.activation(
                    out=ZY[:, :, gb * 64:(gb + 1) * 64].rearrange("p t (g c) -> p g t c", g=4),
                    in_=psum[:, py, :].rearrange("p (g t c) -> p g t c", g=4, t=8), func=AF.Gelu_apprx_tanh),
                    reads=[R_PS[py]], acc=[R_ZY])

            def b3_tile(f):
                pi = next_ps()

                def f_tr(e, f=f, pi=pi):
                    ins = None
                    for tau in range(8):
                        ins = e.transpose(out=psB(pi)[:, tau * 128:(tau + 1) * 128], in_=ZY[:, tau, f * 128:(f + 1) * 128], identity=identB[:])
                    return ins
                P.op("pe", f_tr, reads=[R_ZY, R_CONST], writes=[R_PS[pi]])
                copy_op(evac_eng(), ysT[:, f, :].rearrange("p (k t) -> p t k", t=8), psB(pi).rearrange("p (t k) -> p t k", t=8),
                        [R_PS[pi]], [R_YST])

            wg = None
            LAG = 1
            for s_ in range(8 + LAG):
                if s_ < 8:
                    ssm_front(s_)
                run_fillers(2)
                if s_ == 3:
                    wg = load_ws(wbf_glu[l].rearrange("(a p) n -> p a n", p=128), 4, [R_WBF[l]])
                    P.op("sp", lambda e, wg=wg, l=l: e.dma_start(out=WS[wg][:, 2048:3072], in_=poolW_d[l]),
                         reads=[R_POOLW[l]], acc=[R_WS[wg]], dma=f"d_ws{wg}")
                if s_ >= LAG:
                    ssm_back(s_ - LAG)
                    if (s_ - LAG) % 2 == 1:
                        b3_tile((s_ - LAG) // 2)
            run_fillers(len(fillers))

            pwv = WS[wg][:, 2048:3072].rearrange("p (g k n) -> p g k n", g=4, k=2)
            for n in range(2):
                nr = slice(n * 512, (n + 1) * 512)
                for f in range(4):
                    pi = next_ps()

                    def f_pm(e, f=f, n=n, nr=nr, pi=pi, pwv=pwv):
                        e.matmul(psum[:, pi, :], lhsT=pwv[:, f, 0, :], rhs=spool[:, f, nr], start=True, stop=False)
                        return e.matmul(psum[:, pi, :], lhsT=pwv[:, f, 1, :], rhs=upool[:, f, 16 + n * 512:16 + (n + 1) * 512], start=False, stop=True)
                    P.op("pe", f_pm, reads=[R_WS[wg], R_SPOOL, R_UPOOL], writes=[R_PS[pi]])
                    P.op("dve", lambda e, f=f, nr=nr, pi=pi, l=l: e.scalar_tensor_tensor(
                        out=ycat[:, f, nr], in0=psum[:, pi, :], scalar=smallp[:, l, 8 + f:9 + f], in1=gate[:, f, nr], op0=ALU.mult, op1=ALU.mult),
                        reads=[R_PS[pi], R_GATE[n], R_SMALLP], acc=[R_YCAT[n]])
            gluv = WS[wg][:, 0:2048].rearrange("p (a n) -> p a n", n=512)
            wov = wbf_out[l].rearrange("(a p) n -> p a n", p=128)
            wi_o = [load_ws(wov[:, :, so * 512:(so + 1) * 512], 8, [R_WBF[l]]) for so in range(2)]

            def c1_half(n, l=l, gluv=gluv, wg=wg):
                nr = slice(n * 512, (n + 1) * 512)
                for fo in range(4):
                    pi = next_ps()
                    sgi = fo % 2
                    mm_group(pi, lambda kc, fo=fo: gluv[:, kc, fo * 128:(fo + 1) * 128], lambda kc: ysT[:, kc, nr], 4, [R_WS[wg], R_YST])
                    P.op("act", lambda e, fo=fo, pi=pi, sgi=sgi: e.activation(out=sig[:, sgi, :], in_=psum[:, pi, :], func=AF.Sigmoid,
                                                                             bias=smallp[:, l, 12 + fo:13 + fo], scale=1.0),
                         reads=[R_PS[pi], R_SMALLP], writes=[R_SIG[sgi]])
                    P.op("dve", lambda e, fo=fo, sgi=sgi: e.tensor_tensor(out=sig[:, sgi, :], in0=sig[:, sgi, :], in1=ysT[:, fo, nr], op=ALU.mult),
                         reads=[R_SIG[sgi], R_YST], writes=[R_SIG[sgi]])
                    P.op("dve", lambda e, fo=fo, sgi=sgi: e.tensor_tensor(out=ycat[:, 4 + fo, nr], in0=sig[:, sgi, :], in1=gate[:, 4 + fo, nr], op=ALU.mult),
                         reads=[R_SIG[sgi], R_GATE[n]], acc=[R_YCAT[n]])

            lnext = l + 1 if l + 1 < n_layers else None

            def c2_half(n, wi_o=wi_o, lnext=lnext):
                nr = slice(n * 512, (n + 1) * 512)
                for fo in range(8):
                    if n == 1 and fo == 4:
                        norm_rstd(0)
                    wi, f4 = wi_o[fo // 4], fo % 4
                    pi = next_ps()
                    mm_group(pi, lambda kc, wi=wi, f4=f4: wsv(wi)[:, kc, f4 * 128:(f4 + 1) * 128], lambda kc: ycat[:, kc, nr], 8, [R_WS[wi], R_YCAT[n]])
                    P.op("dve", lambda e, fo=fo, pi=pi: e.tensor_tensor(out=xres[:, fo, nr], in0=xres[:, fo, nr], in1=psum[:, pi, :], op=ALU.add),
                         reads=[R_PS[pi], R_XRES[n]], acc=[R_XRES[n]])
                    norm_square(n, fo)
                    if n == 1 and fo >= 4:
                        norm_apply(0, lnext, [2 * (fo - 4), 2 * (fo - 4) + 1])
            c1_half(0)
            c1_half(1)
            c2_half(0)
            c2_half(1)
            norm_rstd(1)
            norm_apply(1, lnext)

        for tt in range(8):
            si = xs_rr[0]
            xs_rr[0] ^= 1
            for fh in range(2):
                pi = next_ps()

                def f_ot(e, tt=tt, fh=fh, pi=pi):
                    ins = None
                    for f4 in range(4):
                        f = fh * 4 + f4
                        ins = e.transpose(out=psum[:, pi, f4 * 128:(f4 + 1) * 128], in_=xres[:, f, tt * 128:(tt + 1) * 128], identity=identF[:])
                    return ins
                P.op("pe", f_ot, reads=[R_XRES[tt // 4], R_CONST], writes=[R_PS[pi]])
                copy_op(evac_eng(), xs[:, si, fh * 512:(fh + 1) * 512], psum[:, pi, :], [R_PS[pi]], [R_XS[si]])
            P.op("sp", lambda e, si=si, tt=tt, seq=seq, t0=t0: e.dma_start(out=out_d[seq, t0 + tt * 128:t0 + (tt + 1) * 128, :], in_=xs[:, si, :]),
                 reads=[R_XS[si]], acc=[R_OUT], dma=f"d_out{si}")

    P.op("sp", None, reads=[R_OUT], sig=False)
    sems = {k: es.enter_context(nc.semaphore(k)) for k in P.cnt}
    with nc.Block() as block:
        P.emit(nc, block, sems)
    es.close()
    return nc


def prep_inputs(inp):
    f = np.float32
    shared = {}
    shared["w_in"] = np.ascontiguousarray(inp["w_in"], dtype=f)
    shared["w_out"] = np.ascontiguousarray(inp["w_out"], dtype=f)
    shared["glu_w"] = np.ascontiguousarray(inp["glu_w"], dtype=f)
    shared["pool_w"] = np.ascontiguousarray(np.transpose(inp["pool_w"], (0, 2, 1, 3)), dtype=f)
    ng = np.transpose(np.asarray(inp["norm_g"], f).reshape(DEPTH, 8, 128), (2, 0, 1))
    psc = np.transpose(np.asarray(inp["pool_scale"], f).reshape(DEPTH, 4, 128), (2, 0, 1))
    gb = np.transpose(np.asarray(inp["glu_b"], f).reshape(DEPTH, 4, 128), (2, 0, 1))
    shared["smallp"] = np.ascontiguousarray(np.concatenate([ng, psc, gb], axis=2), dtype=f)
    shared["finalg"] = np.ascontiguousarray(np.asarray(inp["final_g"], f).reshape(8, 128).T)

    def dup(a):
        t = np.transpose(np.asarray(a, f), (0, 2, 1))
        return np.concatenate([t, t], axis=1)
    are = dup(inp["a_re"])
    aim = dup(inp["a_im"])
    ldt = np.broadcast_to(np.asarray(inp["log_dt"], f)[:, None, :], (DEPTH, 128, G))
    dv = np.asarray(inp["d_skip"], f).reshape(DEPTH, G, 16)
    dvec = np.broadcast_to(np.transpose(dv, (0, 2, 1))[:, None, :, :], (DEPTH, 8, 16, G)).reshape(DEPTH, 128, G)
    shared["ssm_small"] = np.ascontiguousarray(np.stack([are, aim, ldt, dvec], axis=2), dtype=f)

    def dupb(a):
        t = np.transpose(np.asarray(a, f), (0, 2, 1, 3)).reshape(DEPTH, 64, G * 16)
        return np.concatenate([t, t], axis=1)

    def dupc(a):
        t = np.transpose(np.asarray(a, f), (0, 3, 1, 2)).reshape(DEPTH, 64, G * 16)
        return np.concatenate([t, t], axis=1)
    shared["ssm_bc"] = np.ascontiguousarray(
        np.stack([dupb(inp["b_re"]), dupb(inp["b_im"]), dupc(inp["c_re"]), dupc(inp["c_im"])], axis=2), dtype=f)
    x = np.ascontiguousarray(inp["x"], dtype=f)
    maps = []
    for c in range(NCORES):
        m = dict(shared)
        m["x"] = x[c * NSEQ:(c + 1) * NSEQ]
        maps.append(m)
    return maps


def kernel(**inputs):
    maps = prep_inputs(inputs)
    nc = build_program()
    res = run_bass_kernel_spmd(nc, maps, core_ids=list(range(NCORES)))
    out = np.concatenate([np.asarray(r["out"], dtype=np.float32) for r in res.results], axis=0)
    return out
```

```python
import math
from contextlib import ExitStack

import numpy as np
import concourse.bass as bass
import concourse.mybir as mybir
from concourse.bass_utils import run_bass_kernel_spmd

F32 = mybir.dt.float32
BF16 = mybir.dt.bfloat16
I32 = mybir.dt.int32
AF = mybir.ActivationFunctionType
ALU = mybir.AluOpType

NCORES = 8
DEPTH = 4
D = 1024
SEQ = 2048
NSEQ = 2
ST = 1024
NK = ST // 8
G = 32
TWO_PI = float(2.0 * math.pi)
INV_2PI = float(1.0 / (2.0 * math.pi))
HALF_PI = float(math.pi / 2.0)
SIN_SCALE = 1.0 - 4e-5
EPS = 1e-5
POOL_WINDOWS = (2, 4, 8, 16)


class Res:
    __slots__ = ("name", "w", "r")

    def __init__(self, name):
        self.name = name
        self.w = {}
        self.r = {}


class Prog:
    ENG = ("pe", "act", "dve", "pool", "sp")

    def __init__(self):
        self.ops = {e: [] for e in self.ENG}
        self.cnt = {}
        self.seen = {e: {} for e in self.ENG}
        self.pending = {e: {} for e in self.ENG}

    def fence(self):
        for e in self.ENG:
            self.pending[e] = dict(self.cnt)

    def op(self, eng, fn, reads=(), writes=(), dma=None, sig=True, acc=()):
        need = self.pending[eng]
        self.pending[eng] = {}
        for r in acc:
            for k, v in r.r.items():
                if need.get(k, 0) < v:
                    need[k] = v
        for r in reads:
            for k, v in r.w.items():
                if need.get(k, 0) < v:
                    need[k] = v
        for r in writes:
            for k, v in r.w.items():
                if need.get(k, 0) < v:
                    need[k] = v
            for k, v in r.r.items():
                if need.get(k, 0) < v:
                    need[k] = v
        waits = []
        seen = self.seen[eng]
        for k, v in need.items():
            if eng == "pe" and k == "pe":
                continue
            if seen.get(k, 0) >= v:
                continue
            seen[k] = v
            waits.append((k, v))
        tok = None
        inc = 1
        if sig:
            key = dma if dma is not None else eng
            inc = 16 if dma is not None else 1
            self.cnt[key] = self.cnt.get(key, 0) + inc
            tok = (key, self.cnt[key])
            for r in reads:
                if r.r.get(key, 0) < tok[1]:
                    r.r[key] = tok[1]
            for r in writes:
                r.w = {key: tok[1]}
            for r in acc:
                if r.w.get(key, 0) < tok[1]:
                    r.w[key] = tok[1]
        self.ops[eng].append((waits, fn, tok, inc))
        return tok

    def emit(self, nc, block, sems):
        def mk(name):
            def body(e):
                for waits, fn, tok, inc in self.ops[name]:
                    for k, v in waits:
                        e.wait_ge(sems[k], v)
                    if fn is None:
                        continue
                    ins = fn(e)
                    if tok is not None:
                        ins.then_inc(sems[tok[0]], inc)
            return body

        block.tensor(mk("pe"))
        block.scalar(mk("act"))
        block.vector(mk("dve"))
        block.gpsimd(mk("pool"))
        block.sync(mk("sp"))


def build_program(debug=False, n_layers=DEPTH, n_sub=4):
    nc = bass.Bass("TRN2", target_bir_lowering=False)
    P = Prog()
    es = ExitStack()

    def dram_in(name, shape, dt=F32):
        return nc.dram_tensor(name, list(shape), dt, kind="ExternalInput").ap()

    def dram_out(name, shape, dt=F32):
        return nc.dram_tensor(name, list(shape), dt, kind="ExternalOutput").ap()

    def dram_scr(name, shape, dt):
        kind = "ExternalOutput" if debug else "Internal"
        return nc.dram_tensor(name, list(shape), dt, kind=kind).ap()

    def sb(name, shape, dt, stack=None):
        return (stack or es).enter_context(nc.sbuf_tensor(name, list(shape), dt))

    x_d = dram_in("x", [NSEQ, SEQ, D])
    w_in_d = dram_in("w_in", [DEPTH, D, 2 * D])
    w_out_d = dram_in("w_out", [DEPTH, D, D])
    glu_w_d = dram_in("glu_w", [DEPTH, 512, 512])
    pool_w_d = dram_in("pool_w", [DEPTH, 128, 4, 128])
    smallp_d = dram_in("smallp", [128, DEPTH, 16])
    finalg_d = dram_in("finalg", [128, 8])
    ssm_small_d = dram_in("ssm_small", [DEPTH, 128, 4, 32])
    ssm_bc_d = dram_in("ssm_bc", [DEPTH, 128, 4, 512])
    out_d = dram_out("out", [NSEQ, SEQ, D])

    ssmW_d = dram_scr("ssmW", [DEPTH, 8, 128, 5 * 4 * 128], BF16)
    tabs_d = dram_scr("tabs", [DEPTH, 8, 128, 2, 4, 256], F32)
    poolW_d = dram_scr("poolW", [DEPTH, 128, 4 * 2 * 128], BF16)
    R_SSMW = [Res(f"ssmW{l}") for l in range(DEPTH)]
    R_TABS = [Res(f"tabs{l}") for l in range(DEPTH)]
    R_POOLW = [Res(f"poolW{l}") for l in range(DEPTH)]
    R_OUT = Res("out")

    identF = sb("identF", [128, 128], F32)
    identB = sb("identB", [128, 128], BF16)
    onesB = sb("onesB", [128, 128], BF16)
    Rb_all = sb("Rb_all", [128, DEPTH, 32], F32)
    smallp = sb("smallp_sb", [128, DEPTH, 16], F32)
    finalg = sb("finalg_sb", [128, 8], F32)
    psum = es.enter_context(nc.psum_tensor("psum", [128, 8, 512], F32))
    R_CONST = Res("const")
    R_RB = Res("Rb")
    R_SMALLP = Res("smallp")
    R_PS = [Res(f"ps{i}") for i in range(8)]
    ps_rr = [0]

    def next_ps():
        i = ps_rr[0]
        ps_rr[0] = (i + 1) % 8
        return i

    P.op("pool", lambda e: e.memset(identF[:], 1.0), writes=[R_CONST])
    P.op("pool", lambda e: e.affine_select(out=identF[:], in_=identF[:], pattern=[[-1, 128]],
                                           compare_op=ALU.is_equal, fill=0.0, base=0, channel_multiplier=1),
         reads=[R_CONST], writes=[R_CONST])
    P.op("pool", lambda e: e.tensor_copy(out=identB[:], in_=identF[:]), reads=[R_CONST], writes=[R_CONST])
    P.op("pool", lambda e: e.memset(onesB[:], 1.0 / 1024.0), writes=[R_CONST])
    P.op("sp", lambda e: e.dma_start(out=smallp[:], in_=smallp_d[:]), writes=[R_SMALLP], dma="d_small")
    P.op("sp", lambda e: e.dma_start(out=finalg[:], in_=finalg_d[:]), writes=[R_SMALLP], dma="d_small")

    with ExitStack() as ps_:
        def pb(name, shape, dt):
            return sb("pl_" + name, shape, dt, ps_)
        mask = pb("mask", [128, 128], F32)
        iotaKi = pb("iotaKi", [128, 256], I32)
        iotaK = pb("iotaK", [128, 256], F32)
        JTi = pb("JTi", [128, 7, 32], I32)
        JT = pb("JT", [128, 7, 32], F32)
        sm = pb("sm", [128, 4, 32], F32)
        bc = pb("bc", [128, 4, 512], F32)
        pw = pb("pw", [128, 4, 128], F32)
        PWo = pb("PWo", [128, 4, 2, 128], BF16)
        dtt = pb("dtt", [128, 32], F32)
        ard = pb("ard", [128, 32], F32)
        ang = pb("ang", [128, 32], F32)
        ARJ = pb("ARJ", [128, 7, 32], F32)
        ANJ = pb("ANJ", [128, 7, 32], F32)
        MAGP = pb("MAGP", [128, 7, 32], F32)
        MAGN = pb("MAGN", [128, 7, 32], F32)
        NIj = pb("NIj", [128, 7, 32], I32)
        RRj = pb("RRj", [128, 7, 32], F32)
        SN = pb("SN", [128, 7, 32], F32)
        CS = pb("CS", [128, 7, 32], F32)
        EXre = pb("EXre", [128, 32, 8], F32)
        EXim = pb("EXim", [128, 32, 8], F32)
        EYre = pb("EYre", [128, 32, 8], F32)
        EYim = pb("EYim", [128, 32, 8], F32)
        s1 = pb("s1", [128, 32], F32)
        s2 = pb("s2", [128, 32], F32)
        s3 = pb("s3", [128, 32], F32)
        s4 = pb("s4", [128, 32], F32)
        fre = pb("fre", [128, 32], F32)
        fim = pb("fim", [128, 32], F32)
        Rt = pb("Rt", [128, 32], F32)
        nRt = pb("nRt", [128, 32], F32)
        TH = pb("TH", [128, 32], F32)
        NI8 = pb("NI8", [128, 32], I32)
        ta = pb("ta", [128, 32, 16], F32)
        tb_ = pb("tb", [128, 32, 16], F32)
        tc_ = pb("tc", [128, 32, 16], F32)
        td = pb("td", [128, 32, 16], F32)
        bbA = pb("bbA", [128, 32, 16], F32)
        bbB = pb("bbB", [128, 32, 16], F32)
        cA = pb("cA", [128, 32, 16], F32)
        cB = pb("cB", [128, 32, 16], F32)
        Smat = pb("Smat", [128, 128], F32)
        hpiT = pb("hpiT", [128, 1], F32)
        T1 = pb("T1", [128, 32, 8, 16], F32)
        T2 = pb("T2", [128, 32, 8, 16], F32)
        XBm = pb("XBm", [128, 32, 8, 16], F32)
        Gm = pb("Gm", [128, 32, 8, 16], F32)
        W5 = pb("W5", [128, 8, 5, 4, 128], BF16)
        tmpT = [pb(f"tmpT{i}", [128, 4, 128], F32) for i in range(2)]
        NTB = 2
        ANGq = [pb(f"ANGq{i}", [128, 4, 256], F32) for i in range(NTB)]
        NIq = [pb(f"NIq{i}", [128, 4, 256], I32) for i in range(NTB)]
        RRq = [pb(f"RRq{i}", [128, 4, 256], F32) for i in range(NTB)]
        ABq = [pb(f"ABq{i}", [128, 4, 256], F32) for i in range(NTB)]
        CSq = [pb(f"CSq{i}", [128, 2, 4, 256], F32) for i in range(NTB)]
        R_TQ = [Res(f"pl_tq{i}") for i in range(NTB)]
        R_CSQ = [Res(f"pl_csq{i}") for i in range(NTB)]
        R_BB = Res("pl_bb")

        R_PC = Res("pl_const")
        R_IN = Res("pl_in")
        R_A = Res("pl_a")
        R_E = Res("pl_E")
        R_X = Res("pl_X")
        R_T = Res("pl_T")
        R_W5 = Res("pl_W5")
        R_TMPT = [Res("pl_tmpT0"), Res("pl_tmpT1")]
        R_ANG = Res("pl_ang")
        R_TAB = Res("pl_tab")
        R_PW = Res("pl_pw")

        P.op("pool", lambda e: e.memset(mask[:], 1.0), writes=[R_PC])
        P.op("pool", lambda e: e.affine_select(out=mask[:], in_=mask[:], pattern=[[16, 8], [0, 16]],
                                               compare_op=ALU.is_ge, fill=0.0, base=15, channel_multiplier=-1),
             reads=[R_PC], writes=[R_PC])
        P.op("pool", lambda e: e.iota(iotaKi[:], pattern=[[1, 256]], base=0, channel_multiplier=0), writes=[R_PC])
        P.op("pool", lambda e: e.tensor_copy(out=iotaK[:], in_=iotaKi[:]), reads=[R_PC], writes=[R_PC])
        P.op("pool", lambda e: e.iota(JTi[:], pattern=[[-1, 7], [0, 32]], base=7, channel_multiplier=0), writes=[R_PC])
        P.op("pool", lambda e: e.tensor_copy(out=JT[:], in_=JTi[:]), reads=[R_PC], writes=[R_PC])
        P.op("pool", lambda e: e.tensor_copy(out=Smat[:, 0:64], in_=identF[:, 64:128]), reads=[R_CONST], writes=[R_PC])
        P.op("pool", lambda e: e.tensor_scalar(out=Smat[:, 64:128], in0=identF[:, 0:64], scalar1=-1.0, scalar2=None, op0=ALU.mult),
             reads=[R_CONST, R_PC], writes=[R_PC])
        P.op("pool", lambda e: e.memset(hpiT[:], HALF_PI), reads=[R_PC], writes=[R_PC])

        def bc3(ap32):
            return ap32.unsqueeze(1).to_broadcast([128, 7, 32])

        def bcc(ap32):
            return ap32.unsqueeze(2).to_broadcast([128, 32, 16])

        def bE(apE, lo, hi):
            return apE[lo:hi].unsqueeze(3).to_broadcast([hi - lo, 32, 8, 16])

        def bB(apB, lo, hi):
            return apB[lo:hi].unsqueeze(2).to_broadcast([hi - lo, 32, 8, 16])

        def bR(apR, lo, hi):
            return apR[lo:hi].rearrange("p (a g) -> p a g", g=4).unsqueeze(3).to_broadcast([hi - lo, 8, 4, 128])

        wbf_in = dram_scr("wbf_in", [DEPTH, D, 2 * D], BF16)
        wbf_out = dram_scr("wbf_out", [DEPTH, D, D], BF16)
        wbf_glu = dram_scr("wbf_glu", [DEPTH, 512, 512], BF16)
        R_WBF = [Res(f"wbf{l}") for l in range(DEPTH)]
        for l in range(n_layers):
            for a in range(8):
                P.op("pool", lambda e, l=l, a=a: e.dma_start(out=wbf_in[l, a * 128:(a + 1) * 128, :], in_=w_in_d[l, a * 128:(a + 1) * 128, :]),
                     writes=[R_WBF[l]], dma=f"d_wbf{l}")
            for a in range(4):
                P.op("pool", lambda e, l=l, a=a: e.dma_start(out=wbf_out[l, a * 256:(a + 1) * 256, :], in_=w_out_d[l, a * 256:(a + 1) * 256, :]),
                     writes=[R_WBF[l]], dma=f"d_wbf{l}")
            P.op("pool", lambda e, l=l: e.dma_start(out=wbf_glu[l], in_=glu_w_d[l]), writes=[R_WBF[l]], dma=f"d_wbf{l}")


        for l in range(n_layers):
            P.op("sp", lambda e, l=l: e.dma_start(out=sm[:], in_=ssm_small_d[l]), writes=[R_IN], dma="d_plin")
            P.op("sp", lambda e, l=l: e.dma_start(out=bc[:], in_=ssm_bc_d[l]), writes=[R_IN], dma="d_plin")
            P.op("sp", lambda e, l=l: e.dma_start(out=pw[:], in_=pool_w_d[l]), writes=[R_PW], dma="d_plpw")
            are, aim, ldt, dvec = sm[:, 0, :], sm[:, 1, :], sm[:, 2, :], sm[:, 3, :]
            bre = bc[:, 0, :].rearrange("p (g c) -> p g c", c=16)
            bim = bc[:, 1, :].rearrange("p (g c) -> p g c", c=16)
            cre = bc[:, 2, :].rearrange("p (g c) -> p g c", c=16)
            cim = bc[:, 3, :].rearrange("p (g c) -> p g c", c=16)

            for g4 in range(4):
                P.op("act", lambda e, g4=g4: e.activation(out=PWo[:, g4, 0, :], in_=pw[:, g4, :], func=AF.Copy, scale=1.0 / POOL_WINDOWS[g4]),
                     reads=[R_PW], acc=[R_PW])
            P.op("act", lambda e: e.activation(out=PWo[:, :, 1, :], in_=pw[:, :, :], func=AF.Copy, scale=-1.0), reads=[R_PW], acc=[R_PW])
            P.op("sp", lambda e, l=l: e.dma_start(out=poolW_d[l], in_=PWo[:].rearrange("p a b c -> p (a b c)")),
                 reads=[R_PW], writes=[R_POOLW[l]], dma=f"d_plpo{l}")

            P.op("act", lambda e: e.activation(out=dtt[:], in_=ldt, func=AF.Exp), reads=[R_IN], writes=[R_A])
            P.op("dve", lambda e: e.tensor_tensor(out=ard[:], in0=are, in1=dtt[:], op=ALU.mult), reads=[R_IN, R_A], writes=[R_A])
            P.op("dve", lambda e: e.tensor_tensor(out=ang[:], in0=aim, in1=dtt[:], op=ALU.mult), reads=[R_IN, R_A], writes=[R_A])
            P.op("dve", lambda e: e.tensor_tensor(out=ARJ[:], in0=JT[:], in1=bc3(ard[:]), op=ALU.mult), reads=[R_PC, R_A], writes=[R_A])
            P.op("dve", lambda e: e.tensor_tensor(out=ANJ[:], in0=JT[:], in1=bc3(ang[:]), op=ALU.mult), reads=[R_PC, R_A], writes=[R_A])
            P.op("act", lambda e: e.activation(out=MAGP[:], in_=ARJ[:], func=AF.Exp), reads=[R_A], writes=[R_A])
            P.op("act", lambda e: e.activation(out=MAGN[:], in_=ARJ[:], func=AF.Exp, scale=-1.0), reads=[R_A], writes=[R_A])
            P.op("act", lambda e: e.activation(out=NIj[:], in_=ANJ[:], func=AF.Copy, scale=INV_2PI), reads=[R_A], writes=[R_A])
            P.op("dve", lambda e: e.scalar_tensor_tensor(out=RRj[:], in0=NIj[:], scalar=-TWO_PI, in1=ANJ[:], op0=ALU.mult, op1=ALU.add),
                 reads=[R_A], writes=[R_A])
            P.op("act", lambda e: e.activation(out=SN[:], in_=RRj[:], func=AF.Sin, scale=SIN_SCALE), reads=[R_A], writes=[R_A])
            P.op("dve", lambda e: e.tensor_scalar(out=RRj[:], in0=ANJ[:], scalar1=HALF_PI, scalar2=None, op0=ALU.add), reads=[R_A], writes=[R_A])
            P.op("act", lambda e: e.activation(out=NIj[:], in_=RRj[:], func=AF.Copy, scale=INV_2PI), reads=[R_A], writes=[R_A])
            P.op("dve", lambda e: e.scalar_tensor_tensor(out=RRj[:], in0=NIj[:], scalar=-TWO_PI, in1=RRj[:], op0=ALU.mult, op1=ALU.add),
                 reads=[R_A], writes=[R_A])
            P.op("act", lambda e: e.activation(out=CS[:], in_=RRj[:], func=AF.Sin, scale=SIN_SCALE), reads=[R_A], writes=[R_A])
            def Ev(t):
                return t[:].rearrange("p g t -> p t g")[:, 0:7, :]
            P.op("dve", lambda e: e.tensor_tensor(out=Ev(EXre), in0=MAGP[:], in1=CS[:], op=ALU.mult), reads=[R_A], writes=[R_E])
            P.op("dve", lambda e: e.tensor_tensor(out=Ev(EXim), in0=MAGP[:], in1=SN[:], op=ALU.mult), reads=[R_A], writes=[R_E])
            P.op("dve", lambda e: e.tensor_tensor(out=Ev(EYre), in0=MAGN[:], in1=CS[:], op=ALU.mult), reads=[R_A], writes=[R_E])
            P.op("dve", lambda e: e.scalar_tensor_tensor(out=Ev(EYim), in0=MAGN[:], scalar=-1.0, in1=SN[:], op0=ALU.mult, op1=ALU.mult),
                 reads=[R_A], writes=[R_E])
            P.op("dve", lambda e: e.memset(EXre[:, :, 7:8], 1.0), writes=[R_E])
            P.op("dve", lambda e: e.memset(EXim[:, :, 7:8], 0.0), writes=[R_E])
            P.op("dve", lambda e: e.memset(EYre[:, :, 7:8], 1.0), writes=[R_E])
            P.op("dve", lambda e: e.memset(EYim[:, :, 7:8], 0.0), writes=[R_E])
            lre, lim = EXre[:, :, 6], EXim[:, :, 6]
            P.op("dve", lambda e: e.tensor_scalar(out=s1[:], in0=lre, scalar1=-1.0, scalar2=None, op0=ALU.add), reads=[R_E], writes=[R_A])
            P.op("dve", lambda e: e.tensor_tensor(out=s2[:], in0=are, in1=are, op=ALU.mult), reads=[R_IN], writes=[R_A])
            P.op("dve", lambda e: e.tensor_tensor(out=s3[:], in0=aim, in1=aim, op=ALU.mult), reads=[R_IN], writes=[R_A])
            P.op("dve", lambda e: e.tensor_tensor(out=s2[:], in0=s2[:], in1=s3[:], op=ALU.add), reads=[R_A], writes=[R_A])
            P.op("dve", lambda e: e.reciprocal(out=s2[:], in_=s2[:]), reads=[R_A], writes=[R_A])
            P.op("dve", lambda e: e.tensor_tensor(out=s3[:], in0=s1[:], in1=are, op=ALU.mult), reads=[R_A, R_IN], writes=[R_A])
            P.op("dve", lambda e: e.tensor_tensor(out=s4[:], in0=lim, in1=aim, op=ALU.mult), reads=[R_E, R_IN], writes=[R_A])
            P.op("dve", lambda e: e.tensor_tensor(out=s3[:], in0=s3[:], in1=s4[:], op=ALU.add), reads=[R_A], writes=[R_A])
            P.op("dve", lambda e: e.tensor_tensor(out=fre[:], in0=s3[:], in1=s2[:], op=ALU.mult), reads=[R_A], writes=[R_A])
            P.op("dve", lambda e: e.tensor_tensor(out=s3[:], in0=lim, in1=are, op=ALU.mult), reads=[R_E, R_IN], writes=[R_A])
            P.op("dve", lambda e: e.tensor_tensor(out=s4[:], in0=s1[:], in1=aim, op=ALU.mult), reads=[R_A, R_IN], writes=[R_A])
            P.op("dve", lambda e: e.tensor_tensor(out=s3[:], in0=s3[:], in1=s4[:], op=ALU.subtract), reads=[R_A], writes=[R_A])
            P.op("dve", lambda e: e.tensor_tensor(out=fim[:], in0=s3[:], in1=s2[:], op=ALU.mult), reads=[R_A], writes=[R_A])
            P.op("dve", lambda e: e.tensor_tensor(out=ta[:], in0=bre, in1=bcc(fre[:]), op=ALU.mult), reads=[R_A, R_IN], writes=[R_BB])
            P.op("dve", lambda e: e.tensor_tensor(out=tb_[:], in0=bim, in1=bcc(fim[:]), op=ALU.mult), reads=[R_A, R_IN], acc=[R_BB])
            P.op("dve", lambda e: e.tensor_tensor(out=tc_[:], in0=bim, in1=bcc(fre[:]), op=ALU.mult), reads=[R_A, R_IN], acc=[R_BB])
            P.op("dve", lambda e: e.tensor_tensor(out=td[:], in0=bre, in1=bcc(fim[:]), op=ALU.mult), reads=[R_A, R_IN], acc=[R_BB])
            P.op("dve", lambda e: e.tensor_tensor(out=bbA[0:64], in0=ta[0:64], in1=tb_[0:64], op=ALU.subtract), reads=[R_BB], acc=[R_BB])
            P.op("dve", lambda e: e.tensor_tensor(out=bbA[64:128], in0=tc_[64:128], in1=td[64:128], op=ALU.add), reads=[R_BB], acc=[R_BB])
            P.op("dve", lambda e: e.scalar_tensor_tensor(out=bbB[0:64], in0=tc_[0:64], scalar=-1.0, in1=td[0:64], op0=ALU.mult, op1=ALU.subtract),
                 reads=[R_BB], acc=[R_BB])
            P.op("dve", lambda e: e.tensor_tensor(out=bbB[64:128], in0=ta[64:128], in1=tb_[64:128], op=ALU.subtract), reads=[R_BB], acc=[R_BB])
            P.op("act", lambda e: e.activation(out=cA[0:64], in_=cre[0:64], func=AF.Copy), reads=[R_IN], acc=[R_BB])
            P.op("act", lambda e: e.activation(out=cA[64:128], in_=cim[64:128], func=AF.Copy, scale=-1.0), reads=[R_IN], acc=[R_BB])
            P.op("act", lambda e: e.activation(out=cB[0:64], in_=cim[0:64], func=AF.Copy, scale=-1.0), reads=[R_IN], acc=[R_BB])
            P.op("act", lambda e: e.activation(out=cB[64:128], in_=cre[64:128], func=AF.Copy, scale=-1.0), reads=[R_IN], acc=[R_BB])
            P.op("act", lambda e: e.activation(out=Rt[:], in_=ard[:], func=AF.Exp, scale=8.0), reads=[R_A], writes=[R_A])
            P.op("act", lambda e, l=l: e.activation(out=Rb_all[:, l, :], in_=Rt[:], func=AF.Copy), reads=[R_A], writes=[R_RB])
            P.op("dve", lambda e: e.tensor_scalar(out=s1[:], in0=ang[:], scalar1=8.0, scalar2=None, op0=ALU.mult), reads=[R_A], writes=[R_A])
            P.op("act", lambda e: e.activation(out=NI8[:], in_=s1[:], func=AF.Copy, scale=INV_2PI), reads=[R_A], writes=[R_A])
            P.op("dve", lambda e: e.scalar_tensor_tensor(out=TH[:], in0=NI8[:], scalar=-TWO_PI, in1=s1[:], op0=ALU.mult, op1=ALU.add),
                 reads=[R_A], writes=[R_A])

            tv = tabs_d[l].rearrange("a p two g k -> p a two g k")
            tq_rr = [0]

            def table_piece(gb, l=l, tv=tv):
                i = tq_rr[0]
                tq_rr[0] = (i + 1) % NTB
                gs = slice(gb * 4, gb * 4 + 4)
                P.op("dve", lambda e: e.tensor_tensor(out=ANGq[i][:], in0=TH[:, gs].unsqueeze(2).to_broadcast([128, 4, 256]),
                                                     in1=iotaK[:].unsqueeze(1).to_broadcast([128, 4, 256]), op=ALU.mult),
                     reads=[R_A, R_PC], writes=[R_TQ[i]])
                P.op("act", lambda e: e.activation(out=NIq[i][:], in_=ANGq[i][:], func=AF.Copy, scale=INV_2PI), reads=[R_TQ[i]], acc=[R_TQ[i]])
                P.op("dve", lambda e: e.scalar_tensor_tensor(out=RRq[i][:], in0=NIq[i][:], scalar=-TWO_PI, in1=ANGq[i][:], op0=ALU.mult, op1=ALU.add),
                     reads=[R_TQ[i]], acc=[R_TQ[i]])
                P.op("act", lambda e: e.activation(out=ABq[i][:], in_=RRq[i][:], func=AF.Abs), reads=[R_TQ[i]], acc=[R_TQ[i]])
                P.op("act", lambda e: e.activation(out=CSq[i][:, 1], in_=RRq[i][:], func=AF.Sin, scale=SIN_SCALE), reads=[R_TQ[i]], writes=[R_CSQ[i]])
                P.op("act", lambda e: e.activation(out=CSq[i][:, 0], in_=ABq[i][:], func=AF.Sin, scale=-1.0, bias=hpiT[:]),
                     reads=[R_TQ[i], R_PC], acc=[R_CSQ[i]])
                P.op("sp", lambda e: e.dma_start(out=tv[:, gb], in_=CSq[i][:]), reads=[R_CSQ[i]], acc=[R_TABS[l]], dma=f"d_csq{i}")

            bigops = [
                lambda: P.op("dve", lambda e: e.tensor_tensor(out=T1[:], in0=bE(EXre, 0, 128), in1=bB(bbA, 0, 128), op=ALU.mult), reads=[R_E, R_BB], writes=[R_T]),
                lambda: P.op("dve", lambda e: e.tensor_tensor(out=T2[:], in0=bE(EXim, 0, 128), in1=bB(bbB, 0, 128), op=ALU.mult), reads=[R_E, R_BB], acc=[R_T]),
                lambda: P.op("dve", lambda e: e.tensor_tensor(out=XBm[:], in0=T1[:], in1=T2[:], op=ALU.add), reads=[R_T], writes=[R_X]),
                lambda: P.op("dve", lambda e: e.tensor_tensor(out=T1[:], in0=bE(EYre, 0, 128), in1=bB(cA, 0, 128), op=ALU.mult), reads=[R_E, R_BB], writes=[R_T]),
                lambda: P.op("dve", lambda e: e.tensor_tensor(out=T2[:], in0=bE(EYim, 0, 128), in1=bB(cB, 0, 128), op=ALU.mult), reads=[R_E, R_BB], acc=[R_T]),
                lambda: P.op("dve", lambda e: e.tensor_tensor(out=Gm[:], in0=T1[:], in1=T2[:], op=ALU.add), reads=[R_T], acc=[R_X]),
            ]
            for i_, bo in enumerate(bigops):
                table_piece(i_)
                bo()

            for gb in range(8):
                if gb in (2, 5):
                    table_piece(6 + (gb == 5))
                ti = gb % 2
                pi = next_ps()

                def f_toep(e, gb=gb, pi=pi):
                    ins = None
                    for g4 in range(4):
                        g = gb * 4 + g4
                        ins = e.matmul(psum[:, pi, g4 * 128:(g4 + 1) * 128],
                                       lhsT=XBm[:, g].rearrange("p t c -> p (t c)"),
                                       rhs=Gm[:, g].rearrange("p t c -> p (t c)"), start=True, stop=True)
                    return ins
                P.op("pe", f_toep, reads=[R_X], writes=[R_PS[pi]])
                P.op("dve", lambda e, pi=pi, ti=ti: e.tensor_tensor(out=tmpT[ti][:], in0=psum[:, pi, :].rearrange("p (g n) -> p g n", g=4),
                                                                   in1=mask[:].unsqueeze(1).to_broadcast([128, 4, 128]), op=ALU.mult),
                     reads=[R_PS[pi], R_PC], writes=[R_TMPT[ti]])
                for g4 in range(4):
                    P.op("dve", lambda e, gb=gb, g4=g4, ti=ti: e.scalar_tensor_tensor(
                        out=W5[:, gb, 0, g4, :], in0=identF[:], scalar=sm[:, 3, gb * 4 + g4:gb * 4 + g4 + 1], in1=tmpT[ti][:, g4, :],
                        op0=ALU.mult, op1=ALU.add), reads=[R_TMPT[ti], R_CONST, R_IN], acc=[R_W5])
                pi2 = next_ps()

                def f_tr(e, gb=gb, pi2=pi2):
                    ins = None
                    for g4 in range(4):
                        g = gb * 4 + g4
                        ins = e.transpose(out=psum[:, pi2, g4 * 128:(g4 + 1) * 128],
                                          in_=XBm[:, g].rearrange("p t c -> p (t c)"), identity=identF[:])
                    return ins
                P.op("pe", f_tr, reads=[R_X, R_CONST], writes=[R_PS[pi2]])
                pv = psum[:, pi2, :].rearrange("p (g n) -> p g n", g=4)
                P.op("act", lambda e, gb=gb, pv=pv: e.activation(out=W5[:, gb, 1, :, :], in_=pv, func=AF.Copy),
                     reads=[R_PS[pi2]], acc=[R_W5])
                P.op("act", lambda e, gb=gb, pv=pv: e.activation(out=W5[:, gb, 2, :, 0:64], in_=pv[:, :, 64:128], func=AF.Copy),
                     reads=[R_PS[pi2]], acc=[R_W5])
                P.op("act", lambda e, gb=gb, pv=pv: e.activation(out=W5[:, gb, 2, :, 64:128], in_=pv[:, :, 0:64], func=AF.Copy, scale=-1.0),
                     reads=[R_PS[pi2]], acc=[R_W5])
                pi3 = next_ps()

                def f_sw(e, gb=gb, pi3=pi3):
                    ins = None
                    for g4 in range(4):
                        g = gb * 4 + g4
                        ins = e.matmul(psum[:, pi3, g4 * 128:(g4 + 1) * 128], lhsT=Smat[:],
                                       rhs=Gm[:, g].rearrange("p t c -> p (t c)"), start=True, stop=True)
                    return ins
                P.op("pe", f_sw, reads=[R_X, R_PC], writes=[R_PS[pi3]])
                for g4 in range(4):
                    g = gb * 4 + g4
                    P.op("act", lambda e, gb=gb, g4=g4, g=g: e.activation(out=W5[:, gb, 3, g4, :], in_=Gm[:, g].rearrange("p t c -> p (t c)"),
                                                                         func=AF.Copy, scale=Rt[:, g:g + 1]),
                         reads=[R_X, R_A], acc=[R_W5])
                    P.op("act", lambda e, gb=gb, g4=g4, g=g, pi3=pi3: e.activation(out=W5[:, gb, 4, g4, :], in_=psum[:, pi3, g4 * 128:(g4 + 1) * 128],
                                                                                  func=AF.Copy, scale=Rt[:, g:g + 1]),
                         reads=[R_PS[pi3], R_A], acc=[R_W5])
            P.op("sp", lambda e, l=l: e.dma_start(out=ssmW_d[l].rearrange("a p n -> p a n"),
                                                  in_=W5[:].rearrange("p a k g n -> p a (k g n)")),
                 reads=[R_W5], writes=[R_SSMW[l]], dma=f"d_plssm{l}")

    if debug == "prologue":
        rb_d = dram_out("rb_dbg", [128, DEPTH, 32])
        P.op("sp", lambda e: e.dma_start(out=rb_d[:], in_=Rb_all[:]), reads=[R_RB], writes=[R_OUT], dma="d_out")
        fin = [R_OUT] + R_SSMW[:n_layers] + R_TABS[:n_layers] + R_POOLW[:n_layers] + R_WBF[:n_layers]
        P.op("sp", None, reads=fin, sig=False)
        sems = {k: es.enter_context(nc.semaphore(k)) for k in P.cnt}
        with nc.Block() as block:
            P.emit(nc, block, sems)
        es.close()
        return nc

    P.fence()
    xres = sb("xres", [128, 8, ST], F32)
    h = sb("h", [128, 8, ST], BF16)
    gate = sb("gate", [128, 8, ST], BF16)
    ycat = sb("ycat", [128, 8, ST], BF16)
    upool = sb("upool", [128, 4, 16 + ST], BF16)
    spool = sb("spool", [128, 4, ST], BF16)
    ysT = sb("ysT", [128, 4, ST], BF16)
    xsq0 = spool[:].rearrange("p a n -> p (a n)").rearrange("p (a n) -> p a n", n=512)
    xsq1 = upool[:].rearrange("p a n -> p (a n)")[:, 0:4096].rearrange("p (a n) -> p a n", n=512)
    xsqs = [xsq0, xsq1]
    ZYf = sb("ZY", [128, 4096], BF16)
    ZY = ZYf[:].rearrange("p (t f) -> p t f", t=8)
    Zs = ZYf[:].rearrange("p (g t c) -> p g t c", g=32, t=8)
    U = sb("U", [128, 32, 128], BF16)
    rs = sb("rs", [128, 2, 512], F32)
    t1 = sb("t1", [128, 4, 128], F32)
    t2 = sb("t2", [128, 4, 128], F32)
    NQ = 2
    Q = [sb(f"Q{i}", [128, 4, 129], F32) for i in range(NQ)]
    Pc = [sb(f"Pc{i}", [128, 4, 128], BF16) for i in range(NQ)]
    Ps = [sb(f"Ps{i}", [128, 4, 128], BF16) for i in range(NQ)]
    sig = sb("sig", [128, 2, 512], BF16)
    xs = sb("xs", [128, 2, D], F32)
    pwk = xs[:].rearrange("p a n -> p (a n)").bitcast(BF16)[:, 0:3 * (16 + ST)].rearrange("p (a n) -> p a n", a=3)
    carry = sb("carry", [128, DEPTH, 32], F32)
    halo = sb("halo", [128, DEPTH, 4, 16], BF16)
    cfix = sb("cfix", [128, 4, 16], F32)
    epsT = sb("epsT", [128, 1], F32)
    NWS, NSW, NSTB = 4, 3, 2
    WS = [sb(f"WS{i}", [128, 8 * 512], BF16) for i in range(NWS)]
    SSw = [sb(f"SSw{i}", [128, 5, 4, 128], BF16) for i in range(NSW)]
    SSt = [sb(f"SSt{i}", [128, 2, 4, 128], F32) for i in range(NSTB)]
    R_WS = [Res(f"WS{i}") for i in range(NWS)]
    R_SW = [Res(f"SW{i}") for i in range(NSW)]
    R_STB = [Res(f"STB{i}") for i in range(NSTB)]
    R_ZY, R_U = Res("ZY"), Res("U")
    R_XRES = [Res("xres0"), Res("xres1")]
    R_RS = [Res("rs0"), Res("rs1")]
    R_H = [Res("h0"), Res("h1")]
    R_GATE = [Res("g0"), Res("g1")]
    R_YCAT = [Res("yc0"), Res("yc1")]
    R_UPOOL, R_SPOOL, R_YST = Res("upool"), Res("spool"), Res("ysT")
    R_T1, R_T2 = Res("t1"), Res("t2")
    R_Q = [Res(f"Q{i}") for i in range(NQ)]
    R_PCS = [Res(f"PcPs{i}") for i in range(NQ)]
    R_SIG = [Res("sig0"), Res("sig1")]
    R_XS = [Res("xs0"), Res("xs1")]
    R_CARRY = [[Res(f"carry{l}_{gb}") for gb in range(8)] for l in range(DEPTH)]
    R_HALO = [Res(f"halo{l}") for l in range(DEPTH)]
    R_C2 = Res("const2")
    ws_rr, sw_rr, stb_rr, xs_rr, ev_rr = [0], [0], [0], [0], [0]

    def psB(pi):
        return psum[:, pi, :].bitcast(BF16)

    def evac_eng():
        ev_rr[0] ^= 1
        return "act" if ev_rr[0] else "dve"

    def copy_op(eng, out, in_, reads, acc):
        if eng == "act":
            P.op("act", lambda e: e.activation(out=out, in_=in_, func=AF.Copy), reads=reads, acc=acc)
        else:
            P.op(eng, lambda e: e.tensor_copy(out=out, in_=in_), reads=reads, acc=acc)

    P.op("pool", lambda e: e.memset(epsT[:], EPS), writes=[R_C2])
    P.op("pool", lambda e: e.memset(cfix[:], 1.0), writes=[R_C2])
    for f in range(4):
        w = POOL_WINDOWS[f]
        for t in range(w - 1):
            P.op("pool", lambda e, f=f, t=t, w=w: e.memset(cfix[:, f, t:t + 1], float(w) / float(t + 1)), writes=[R_C2])

    def load_ws(src_ap, nparts, reads):
        i = ws_rr[0]
        ws_rr[0] = (i + 1) % NWS
        dst = WS[i][:, 0:nparts * 512].rearrange("p (a n) -> p a n", n=512)
        P.op("sp", lambda e: e.dma_start(out=dst, in_=src_ap), reads=reads, writes=[R_WS[i]], dma=f"d_ws{i}")
        return i

    def wsv(i):
        return WS[i][:].rearrange("p (a n) -> p a n", n=512)

    def mm_group(pi, lhs_fn, rhs_fn, nk, reads):
        def f(e):
            ins = None
            for kc in range(nk):
                ins = e.matmul(psum[:, pi, :], lhsT=lhs_fn(kc), rhs=rhs_fn(kc), start=(kc == 0), stop=(kc == nk - 1))
            return ins
        P.op("pe", f, reads=reads, writes=[R_PS[pi]])

    def R_XSQ(n):
        return R_SPOOL if n == 0 else R_UPOOL

    def norm_square(n, f=None):
        nr = slice(n * 512, (n + 1) * 512)
        if f is None:
            P.op("act", lambda e: e.activation(out=xsqs[n], in_=xres[:, :, nr], func=AF.Square), reads=[R_XRES[n]], writes=[R_XSQ(n)])
        elif f == 0:
            P.op("act", lambda e: e.activation(out=xsqs[n][:, 0, :], in_=xres[:, 0, nr], func=AF.Square), reads=[R_XRES[n]], writes=[R_XSQ(n)])
        else:
            P.op("act", lambda e: e.activation(out=xsqs[n][:, f, :], in_=xres[:, f, nr], func=AF.Square), reads=[R_XRES[n]], acc=[R_XSQ(n)])

    def norm_rstd(n):
        pi = next_ps()
        mm_group(pi, lambda kc: onesB[:], lambda kc: xsqs[n][:, kc, :], 8, [R_XSQ(n), R_CONST])
        P.op("act", lambda e, pi=pi: e.activation(out=rs[:, n, :], in_=psum[:, pi, :], func=AF.Ln, bias=epsT[:], scale=1.0),
             reads=[R_PS[pi], R_C2], writes=[R_RS[n]])
        P.op("act", lambda e: e.activation(out=rs[:, n, :], in_=rs[:, n, :], func=AF.Exp, scale=-0.5), reads=[R_RS[n]], writes=[R_RS[n]])

    def norm_apply(n, lnext, fs=range(8)):
        nr = slice(n * 512, (n + 1) * 512)
        for f in fs:
            if lnext is None:
                P.op("dve", lambda e, f=f: e.scalar_tensor_tensor(
                    out=xres[:, f, nr], in0=xres[:, f, nr], scalar=finalg[:, f:f + 1], in1=rs[:, n, :], op0=ALU.mult, op1=ALU.mult),
                    reads=[R_RS[n], R_SMALLP, R_XRES[n]], acc=[R_XRES[n]])
            else:
                P.op("dve", lambda e, f=f: e.scalar_tensor_tensor(
                    out=h[:, f, nr], in0=xres[:, f, nr], scalar=smallp[:, lnext, f:f + 1], in1=rs[:, n, :], op0=ALU.mult, op1=ALU.mult),
                    reads=[R_XRES[n], R_RS[n], R_SMALLP], acc=[R_H[n]])

    for st in range(n_sub):
        seq, half = st // 2, st % 2
        t0 = half * ST
        if half == 0:
            P.op("pool", lambda e: e.memset(carry[:], 0.0), writes=[r for rl in R_CARRY for r in rl])
            P.op("pool", lambda e: e.memset(halo[:], 0.0), writes=R_HALO)
        for tt in range(8):
            si = xs_rr[0]
            xs_rr[0] ^= 1
            P.op("sp", lambda e, si=si, tt=tt, seq=seq, t0=t0: e.dma_start(out=xs[:, si, :], in_=x_d[seq, t0 + tt * 128:t0 + (tt + 1) * 128, :]),
                 writes=[R_XS[si]], dma=f"d_xs{si}")
            for fh in range(2):
                pi = next_ps()

                def f_xt(e, si=si, fh=fh, pi=pi):
                    ins = None
                    for f4 in range(4):
                        f = fh * 4 + f4
                        ins = e.transpose(out=psum[:, pi, f4 * 128:(f4 + 1) * 128], in_=xs[:, si, f * 128:(f + 1) * 128], identity=identF[:])
                    return ins
                P.op("pe", f_xt, reads=[R_XS[si], R_CONST], writes=[R_PS[pi]])
                copy_op(evac_eng(), xres[:, fh * 4:(fh + 1) * 4, tt * 128:(tt + 1) * 128],
                        psum[:, pi, :].rearrange("p (f n) -> p f n", f=4), [R_PS[pi]], [R_XRES[tt // 4]])

        for l in range(n_layers):
            k0 = half * NK
            win = wbf_in[l].rearrange("(a p) n -> p a n", p=128)
            if l == 0:
                for n in range(2):
                    norm_square(n)
                    norm_rstd(n)
                    norm_apply(n, 0)
            wi_g = [load_ws(win[:, :, 1024 + sg * 512:1024 + (sg + 1) * 512], 8, [R_WBF[l]]) for sg in range(2)]

            def gate_group(fo, n, wi_g=wi_g):
                nr = slice(n * 512, (n + 1) * 512)
                wi, f4 = wi_g[fo // 4], fo % 4
                pi = next_ps()
                mm_group(pi, lambda kc: wsv(wi)[:, kc, f4 * 128:(f4 + 1) * 128], lambda kc: h[:, kc, nr], 8, [R_WS[wi], R_H[n]])
                P.op("act", lambda e: e.activation(out=gate[:, fo, nr], in_=psum[:, pi, :], func=AF.Silu), reads=[R_PS[pi]], acc=[R_GATE[n]])
            if l > 0:
                for fo in range(4):
                    gate_group(fo, 0)
            wi_s = load_ws(win[:, :, 512:1024], 8, [R_WBF[l]])
            for tau in range(8):
                pi = next_ps()
                mm_group(pi, lambda kc, tau=tau: h[:, kc, :].rearrange("p (k t) -> p t k", t=8)[:, tau, :],
                         lambda kc, wi_s=wi_s: wsv(wi_s)[:, kc, :], 8, [R_WS[wi_s], R_H[0], R_H[1]])
                copy_op(evac_eng(), Zs[:, :, tau, :], psum[:, pi, :].rearrange("p (g c) -> p g c", c=16), [R_PS[pi]], [R_ZY])
            for gq in range(4):
                pi = next_ps()

                def f_tr(e, gq=gq, pi=pi):
                    ins = None
                    for g8 in range(8):
                        g = gq * 8 + g8
                        ins = e.transpose(out=psB(pi)[:, g8 * 128:(g8 + 1) * 128], in_=Zs[:, g].rearrange("p t c -> p (t c)"), identity=identB[:])
                    return ins
                P.op("pe", f_tr, reads=[R_ZY, R_CONST], writes=[R_PS[pi]])
                copy_op(evac_eng(), U[:, gq * 8:(gq + 1) * 8, :], psB(pi).rearrange("p (g k) -> p g k", g=8), [R_PS[pi]], [R_U])

            wi_p = load_ws(win[:, :, 0:512], 8, [R_WBF[l]])
            fillers = []

            def pool_in_group(f, n, wi_p=wi_p, l=l):
                nr = slice(n * 512, (n + 1) * 512)
                pi = next_ps()
                mm_group(pi, lambda kc: wsv(wi_p)[:, kc, f * 128:(f + 1) * 128], lambda kc: h[:, kc, nr], 8, [R_WS[wi_p], R_H[n]])
                P.op("act", lambda e: e.activation(out=upool[:, f, 16 + n * 512:16 + (n + 1) * 512], in_=psum[:, pi, :], func=AF.Copy),
                     reads=[R_PS[pi]], acc=[R_UPOOL])

            def pool_sums(l=l, half=half):
                LT = 16 + ST
                R_PWK = R_XS
                P.op("dve", lambda e: e.tensor_copy(out=halo[:, l, :, :], in_=upool[:, :, ST:ST + 16]), reads=[R_UPOOL], writes=[R_HALO[l]])
                P.op("dve", lambda e: e.tensor_tensor(out=spool[:, 0, :], in0=upool[:, 0, 16:LT], in1=upool[:, 0, 15:LT - 1], op=ALU.add),
                     reads=[R_UPOOL], writes=[R_SPOOL])
                for f in (1, 2, 3):
                    P.op("dve", lambda e, f=f: e.tensor_tensor(out=pwk[:, 0, 1:LT], in0=upool[:, f, 1:LT], in1=upool[:, f, 0:LT - 1], op=ALU.add),
                         reads=[R_UPOOL], writes=R_PWK)
                    if f == 1:
                        P.op("dve", lambda e: e.tensor_tensor(out=spool[:, 1, :], in0=pwk[:, 0, 16:LT], in1=pwk[:, 0, 14:LT - 2], op=ALU.add),
                             reads=R_PWK, acc=[R_SPOOL])
                        continue
                    P.op("dve", lambda e: e.tensor_tensor(out=pwk[:, 1, 3:LT], in0=pwk[:, 0, 3:LT], in1=pwk[:, 0, 1:LT - 2], op=ALU.add),
                         reads=R_PWK, writes=R_PWK)
                    if f == 2:
                        P.op("dve", lambda e: e.tensor_tensor(out=spool[:, 2, :], in0=pwk[:, 1, 16:LT], in1=pwk[:, 1, 12:LT - 4], op=ALU.add),
                             reads=R_PWK, acc=[R_SPOOL])
                        continue
                    P.op("dve", lambda e: e.tensor_tensor(out=pwk[:, 2, 7:LT], in0=pwk[:, 1, 7:LT], in1=pwk[:, 1, 3:LT - 4], op=ALU.add),
                         reads=R_PWK, writes=R_PWK)
                    P.op("dve", lambda e: e.tensor_tensor(out=spool[:, 3, :], in0=pwk[:, 2, 16:LT], in1=pwk[:, 2, 8:LT - 8], op=ALU.add),
                         reads=R_PWK, acc=[R_SPOOL])
                if half == 0:
                    P.op("dve", lambda e: e.tensor_tensor(out=spool[:, :, 0:16], in0=spool[:, :, 0:16], in1=cfix[:], op=ALU.mult),
                         reads=[R_C2], writes=[R_SPOOL])

            P.op("dve", lambda e, l=l: e.tensor_copy(out=upool[:, :, 0:16], in_=halo[:, l, :, :]), reads=[R_HALO[l]], writes=[R_UPOOL])
            for fo in range(4):
                for n in range(2):
                    if l > 0 and n == 0:
                        continue
                    fillers.append(lambda fo=fo, n=n: gate_group(fo, n))
            for f in range(4):
                for n in range(2):
                    fillers.append(lambda f=f, n=n: pool_in_group(f, n))
            fillers.append(pool_sums)
            for fo in range(4, 8):
                for n in range(2):
                    fillers.append(lambda fo=fo, n=n: gate_group(fo, n))
            fillers.reverse()

            def run_fillers(k):
                for _ in range(k):
                    if fillers:
                        fillers.pop()()

            slots = {}

            def ssm_front(gb, l=l, k0=k0):
                wi = sw_rr[0]
                sw_rr[0] = (wi + 1) % NSW
                ti = stb_rr[0]
                stb_rr[0] = (ti + 1) % NSTB
                qi = gb % NQ
                slots[gb] = (wi, qi)
                P.op("sp", lambda e: e.dma_start(out=SSw[wi][:].rearrange("p a g n -> p (a g n)"), in_=ssmW_d[l, gb]),
                     reads=[R_SSMW[l]], writes=[R_SW[wi]], dma=f"d_sw{wi}")
                P.op("sp", lambda e: e.dma_start(out=SSt[ti][:], in_=tabs_d[l, gb][:, :, :, k0:k0 + NK]),
                     reads=[R_TABS[l]], writes=[R_STB[ti]], dma=f"d_stb{ti}")
                pa, pb_ = next_ps(), next_ps()

                def f_v(e):
                    ins = None
                    for kind, pi in ((1, pa), (2, pb_)):
                        for g4 in range(4):
                            ins = e.matmul(psum[:, pi, g4 * 128:(g4 + 1) * 128], lhsT=SSw[wi][:, kind, g4, :], rhs=U[:, gb * 4 + g4, :],
                                           start=True, stop=True)
                    return ins
                P.op("pe", f_v, reads=[R_SW[wi], R_U], writes=[R_PS[pa], R_PS[pb_]])

                def pv(pi):
                    return psum[:, pi, :].rearrange("p (g k) -> p g k", g=4)
                P.op("dve", lambda e: e.tensor_tensor(out=t1[:], in0=pv(pa), in1=SSt[ti][:, 0], op=ALU.mult),
                     reads=[R_PS[pa], R_STB[ti]], writes=[R_T1])
                P.op("dve", lambda e: e.tensor_tensor(out=t2[:], in0=pv(pb_), in1=SSt[ti][:, 1], op=ALU.mult),
                     reads=[R_PS[pb_], R_STB[ti]], writes=[R_T2])
                P.op("dve", lambda e: e.tensor_tensor(out=t1[:], in0=t1[:], in1=t2[:], op=ALU.add), reads=[R_T1, R_T2], writes=[R_T1])
                P.op("dve", lambda e: e.tensor_copy(out=Q[qi][:, :, 0], in_=carry[:, l, gb * 4:(gb + 1) * 4]),
                     reads=[R_CARRY[l][gb]], writes=[R_Q[qi]])
                for g4 in range(4):
                    g = gb * 4 + g4
                    P.op("dve", lambda e, g4=g4, g=g: e.tensor_tensor_scan(
                        out=Q[qi][:, g4, 1:129], data0=Rb_all[:, l, g:g + 1].to_broadcast([128, 128]), data1=t1[:, g4, :],
                        initial=carry[:, l, g:g + 1], op0=ALU.mult, op1=ALU.add),
                        reads=[R_T1, R_RB, R_CARRY[l][gb]], acc=[R_Q[qi]])
                P.op("dve", lambda e: e.tensor_copy(out=carry[:, l, gb * 4:(gb + 1) * 4], in_=Q[qi][:, :, 128]),
                     reads=[R_Q[qi]], writes=[R_CARRY[l][gb]])
                P.op("dve", lambda e: e.tensor_tensor(out=Pc[qi][:], in0=Q[qi][:, :, 0:128], in1=SSt[ti][:, 0], op=ALU.mult),
                     reads=[R_Q[qi], R_STB[ti]], writes=[R_PCS[qi]])
                P.op("dve", lambda e: e.tensor_tensor(out=Ps[qi][:], in0=Q[qi][:, :, 0:128], in1=SSt[ti][:, 1], op=ALU.mult),
                     reads=[R_Q[qi], R_STB[ti]], acc=[R_PCS[qi]])

            def ssm_back(gb):
                wi, qi = slots[gb]
                py = next_ps()

                def f_y(e):
                    ins = None
                    for g4 in range(4):
                        o = psum[:, py, g4 * 128:(g4 + 1) * 128]
                        e.matmul(o, lhsT=U[:, gb * 4 + g4, :], rhs=SSw[wi][:, 0, g4, :], start=True, stop=False)
                        e.matmul(o, lhsT=Pc[qi][:, g4, :], rhs=SSw[wi][:, 3, g4, :], start=False, stop=False)
                        ins = e.matmul(o, lhsT=Ps[qi][:, g4, :], rhs=SSw[wi][:, 4, g4, :], start=False, stop=True)
                    return ins
                P.op("pe", f_y, reads=[R_SW[wi], R_U, R_PCS[qi]], writes=[R_PS[py]])
                P.op("act", lambda e: e.activation(
                    out=ZY[:, :, gb * 64:(gb + 1) * 64].rearrange("p t (g c) -> p g t c", g=4),
                    in_=psum[:, py, :].rearrange("p (g t c) -> p g t c", g=4, t=8), func=AF.Gelu_apprx_tanh),
                    reads=[R_PS[py]], acc=[R_ZY])

            def b3_tile(f):
                pi = next_ps()

                def f_tr(e, f=f, pi=pi):
                    ins = None
                    for tau in range(8):
                        ins = e.transpose(out=psB(pi)[:, tau * 128:(tau + 1) * 128], in_=ZY[:, tau, f * 128:(f + 1) * 128], identity=identB[:])
                    return ins
                P.op("pe", f_tr, reads=[R_ZY, R_CONST], writes=[R_PS[pi]])
                copy_op(evac_eng(), ysT[:, f, :].rearrange("p (k t) -> p t k", t=8), psB(pi).rearrange("p (t k) -> p t k", t=8),
                        [R_PS[pi]], [R_YST])

            wg = None
            LAG = 1
            for s_ in range(8 + LAG):
                if s_ < 8:
                    ssm_front(s_)
                run_fillers(2)
                if s_ == 3:
                    wg = load_ws(wbf_glu[l].rearrange("(a p) n -> p a n", p=128), 4, [R_WBF[l]])
                    P.op("sp", lambda e, wg=wg, l=l: e.dma_start(out=WS[wg][:, 2048:3072], in_=poolW_d[l]),
                         reads=[R_POOLW[l]], acc=[R_WS[wg]], dma=f"d_ws{wg}")
                if s_ >= LAG:
                    ssm_back(s_ - LAG)
                    if (s_ - LAG) % 2 == 1:
                        b3_tile((s_ - LAG) // 2)
            run_fillers(len(fillers))

            pwv = WS[wg][:, 2048:3072].rearrange("p (g k n) -> p g k n", g=4, k=2)
            for n in range(2):
                nr = slice(n * 512, (n + 1) * 512)
                for f in range(4):
                    pi = next_ps()

                    def f_pm(e, f=f, n=n, nr=nr, pi=pi, pwv=pwv):
                        e.matmul(psum[:, pi, :], lhsT=pwv[:, f, 0, :], rhs=spool[:, f, nr], start=True, stop=False)
                        return e.matmul(psum[:, pi, :], lhsT=pwv[:, f, 1, :], rhs=upool[:, f, 16 + n * 512:16 + (n + 1) * 512], start=False, stop=True)
                    P.op("pe", f_pm, reads=[R_WS[wg], R_SPOOL, R_UPOOL], writes=[R_PS[pi]])
                    P.op("dve", lambda e, f=f, nr=nr, pi=pi, l=l: e.scalar_tensor_tensor(
                        out=ycat[:, f, nr], in0=psum[:, pi, :], scalar=smallp[:, l, 8 + f:9 + f], in1=gate[:, f, nr], op0=ALU.mult, op1=ALU.mult),
                        reads=[R_PS[pi], R_GATE[n], R_SMALLP], acc=[R_YCAT[n]])
            gluv = WS[wg][:, 0:2048].rearrange("p (a n) -> p a n", n=512)
            wov = wbf_out[l].rearrange("(a p) n -> p a n", p=128)
            wi_o = [load_ws(wov[:, :, so * 512:(so + 1) * 512], 8, [R_WBF[l]]) for so in range(2)]

            def c1_half(n, l=l, gluv=gluv, wg=wg):
                nr = slice(n * 512, (n + 1) * 512)
                for fo in range(4):
                    pi = next_ps()
                    sgi = fo % 2
                    mm_group(pi, lambda kc, fo=fo: gluv[:, kc, fo * 128:(fo + 1) * 128], lambda kc: ysT[:, kc, nr], 4, [R_WS[wg], R_YST])
                    P.op("act", lambda e, fo=fo, pi=pi, sgi=sgi: e.activation(out=sig[:, sgi, :], in_=psum[:, pi, :], func=AF.Sigmoid,
                                                                             bias=smallp[:, l, 12 + fo:13 + fo], scale=1.0),
                         reads=[R_PS[pi], R_SMALLP], writes=[R_SIG[sgi]])
                    P.op("dve", lambda e, fo=fo, sgi=sgi: e.tensor_tensor(out=sig[:, sgi, :], in0=sig[:, sgi, :], in1=ysT[:, fo, nr], op=ALU.mult),
                         reads=[R_SIG[sgi], R_YST], writes=[R_SIG[sgi]])
                    P.op("dve", lambda e, fo=fo, sgi=sgi: e.tensor_tensor(out=ycat[:, 4 + fo, nr], in0=sig[:, sgi, :], in1=gate[:, 4 + fo, nr], op=ALU.mult),
                         reads=[R_SIG[sgi], R_GATE[n]], acc=[R_YCAT[n]])

            lnext = l + 1 if l + 1 < n_layers else None

            def c2_half(n, wi_o=wi_o, lnext=lnext):
                nr = slice(n * 512, (n + 1) * 512)
                for fo in range(8):
                    if n == 1 and fo == 4:
                        norm_rstd(0)
                    wi, f4 = wi_o[fo // 4], fo % 4
                    pi = next_ps()
                    mm_group(pi, lambda kc, wi=wi, f4=f4: wsv(wi)[:, kc, f4 * 128:(f4 + 1) * 128], lambda kc: ycat[:, kc, nr], 8, [R_WS[wi], R_YCAT[n]])
                    P.op("dve", lambda e, fo=fo, pi=pi: e.tensor_tensor(out=xres[:, fo, nr], in0=xres[:, fo, nr], in1=psum[:, pi, :], op=ALU.add),
                         reads=[R_PS[pi], R_XRES[n]], acc=[R_XRES[n]])
                    norm_square(n, fo)
                    if n == 1 and fo >= 4:
                        norm_apply(0, lnext, [2 * (fo - 4), 2 * (fo - 4) + 1])
            c1_half(0)
            c1_half(1)
            c2_half(0)
            c2_half(1)
            norm_rstd(1)
            norm_apply(1, lnext)

        for tt in range(8):
            si = xs_rr[0]
            xs_rr[0] ^= 1
            for fh in range(2):
                pi = next_ps()

                def f_ot(e, tt=tt, fh=fh, pi=pi):
                    ins = None
                    for f4 in range(4):
                        f = fh * 4 + f4
                        ins = e.transpose(out=psum[:, pi, f4 * 128:(f4 + 1) * 128], in_=xres[:, f, tt * 128:(tt + 1) * 128], identity=identF[:])
                    return ins
                P.op("pe", f_ot, reads=[R_XRES[tt // 4], R_CONST], writes=[R_PS[pi]])
                copy_op(evac_eng(), xs[:, si, fh * 512:(fh + 1) * 512], psum[:, pi, :], [R_PS[pi]], [R_XS[si]])
            P.op("sp", lambda e, si=si, tt=tt, seq=seq, t0=t0: e.dma_start(out=out_d[seq, t0 + tt * 128:t0 + (tt + 1) * 128, :], in_=xs[:, si, :]),
                 reads=[R_XS[si]], acc=[R_OUT], dma=f"d_out{si}")

    P.op("sp", None, reads=[R_OUT], sig=False)
    sems = {k: es.enter_context(nc.semaphore(k)) for k in P.cnt}
    with nc.Block() as block:
        P.emit(nc, block, sems)
    es.close()
    return nc


def prep_inputs(inp):
    f = np.float32
    shared = {}
    shared["w_in"] = np.ascontiguousarray(inp["w_in"], dtype=f)
    shared["w_out"] = np.ascontiguousarray(inp["w_out"], dtype=f)
    shared["glu_w"] = np.ascontiguousarray(inp["glu_w"], dtype=f)
    shared["pool_w"] = np.ascontiguousarray(np.transpose(inp["pool_w"], (0, 2, 1, 3)), dtype=f)
    ng = np.transpose(np.asarray(inp["norm_g"], f).reshape(DEPTH, 8, 128), (2, 0, 1))
    psc = np.transpose(np.asarray(inp["pool_scale"], f).reshape(DEPTH, 4, 128), (2, 0, 1))
    gb = np.transpose(np.asarray(inp["glu_b"], f).reshape(DEPTH, 4, 128), (2, 0, 1))
    shared["smallp"] = np.ascontiguousarray(np.concatenate([ng, psc, gb], axis=2), dtype=f)
    shared["finalg"] = np.ascontiguousarray(np.asarray(inp["final_g"], f).reshape(8, 128).T)

    def dup(a):
        t = np.transpose(np.asarray(a, f), (0, 2, 1))
        return np.concatenate([t, t], axis=1)
    are = dup(inp["a_re"])
    aim = dup(inp["a_im"])
    ldt = np.broadcast_to(np.asarray(inp["log_dt"], f)[:, None, :], (DEPTH, 128, G))
    dv = np.asarray(inp["d_skip"], f).reshape(DEPTH, G, 16)
    dvec = np.broadcast_to(np.transpose(dv, (0, 2, 1))[:, None, :, :], (DEPTH, 8, 16, G)).reshape(DEPTH, 128, G)
    shared["ssm_small"] = np.ascontiguousarray(np.stack([are, aim, ldt, dvec], axis=2), dtype=f)

    def dupb(a):
        t = np.transpose(np.asarray(a, f), (0, 2, 1, 3)).reshape(DEPTH, 64, G * 16)
        return np.concatenate([t, t], axis=1)

    def dupc(a):
        t = np.transpose(np.asarray(a, f), (0, 3, 1, 2)).reshape(DEPTH, 64, G * 16)
        return np.concatenate([t, t], axis=1)
    shared["ssm_bc"] = np.ascontiguousarray(
        np.stack([dupb(inp["b_re"]), dupb(inp["b_im"]), dupc(inp["c_re"]), dupc(inp["c_im"])], axis=2), dtype=f)
    x = np.ascontiguousarray(inp["x"], dtype=f)
    maps = []
    for c in range(NCORES):
        m = dict(shared)
        m["x"] = x[c * NSEQ:(c + 1) * NSEQ]
        maps.append(m)
    return maps


def kernel(**inputs):
    maps = prep_inputs(inputs)
    nc = build_program()
    res = run_bass_kernel_spmd(nc, maps, core_ids=list(range(NCORES)))
    out = np.concatenate([np.asarray(r["out"], dtype=np.float32) for r in res.results], axis=0)
    return out
```

```python
import math
from contextlib import ExitStack

import numpy as np
import concourse.bass as bass
import concourse.mybir as mybir
from concourse.bass_utils import run_bass_kernel_spmd

F32 = mybir.dt.float32
BF16 = mybir.dt.bfloat16
I32 = mybir.dt.int32
AF = mybir.ActivationFunctionType
ALU = mybir.AluOpType

NCORES = 8
DEPTH = 4
D = 1024
SEQ = 2048
NSEQ = 2
ST = 1024
NK = ST // 8
G = 32
LG = DEPTH * G
TWO_PI = float(2.0 * math.pi)
INV_2PI = float(1.0 / (2.0 * math.pi))
HALF_PI = float(math.pi / 2.0)
SIN_SCALE = 1.0 - 4e-5
EPS = 1e-5
POOL_WINDOWS = (2, 4, 8, 16)


class Res:
    __slots__ = ("name", "w", "r")

    def __init__(self, name):
        self.name = name
        self.w = {}
        self.r = {}


class Prog:
    ENG = ("pe", "act", "dve", "pool", "sp")

    def __init__(self):
        self.ops = {e: [] for e in self.ENG}
        self.cnt = {}
        self.seen = {e: {} for e in self.ENG}
        self.pending = {e: {} for e in self.ENG}

    def fence(self):
        for e in self.ENG:
            self.pending[e] = dict(self.cnt)

    def op(self, eng, fn, reads=(), writes=(), dma=None, sig=True, acc=()):
        need = self.pending[eng]
        self.pending[eng] = {}
        for r in acc:
            for k, v in r.r.items():
                if need.get(k, 0) < v:
                    need[k] = v
        for r in reads:
            for k, v in r.w.items():
                if need.get(k, 0) < v:
                    need[k] = v
        for r in writes:
            for k, v in r.w.items():
                if need.get(k, 0) < v:
                    need[k] = v
            for k, v in r.r.items():
                if need.get(k, 0) < v:
                    need[k] = v
        waits = []
        seen = self.seen[eng]
        for k, v in need.items():
            if eng == "pe" and k == "pe":
                continue
            if seen.get(k, 0) >= v:
                continue
            seen[k] = v
            waits.append((k, v))
        tok = None
        inc = 1
        if sig:
            key = dma if dma is not None else eng
            inc = 16 if dma is not None else 1
            self.cnt[key] = self.cnt.get(key, 0) + inc
            tok = (key, self.cnt[key])
            for r in reads:
                if r.r.get(key, 0) < tok[1]:
                    r.r[key] = tok[1]
            for r in writes:
                r.w = {key: tok[1]}
            for r in acc:
                if r.w.get(key, 0) < tok[1]:
                    r.w[key] = tok[1]
        self.ops[eng].append((waits, fn, tok, inc))
        return tok

    def emit(self, nc, block, sems):
        def mk(name):
            def body(e):
                for waits, fn, tok, inc in self.ops[name]:
                    for k, v in waits:
                        e.wait_ge(sems[k], v)
                    if fn is None:
                        continue
                    ins = fn(e)
                    if tok is not None:
                        ins.then_inc(sems[tok[0]], inc)
            return body

        block.tensor(mk("pe"))
        block.scalar(mk("act"))
        block.vector(mk("dve"))
        block.gpsimd(mk("pool"))
        block.sync(mk("sp"))


def build_program(debug=False, n_layers=DEPTH, n_sub=4):
    nc = bass.Bass("TRN2", target_bir_lowering=False)
    P = Prog()
    es = ExitStack()

    def dram_in(name, shape, dt=F32):
        return nc.dram_tensor(name, list(shape), dt, kind="ExternalInput").ap()

    def dram_out(name, shape, dt=F32):
        return nc.dram_tensor(name, list(shape), dt, kind="ExternalOutput").ap()

    def dram_scr(name, shape, dt):
        kind = "ExternalOutput" if debug else "Internal"
        return nc.dram_tensor(name, list(shape), dt, kind=kind).ap()

    def sb(name, shape, dt, stack=None):
        return (stack or es).enter_context(nc.sbuf_tensor(name, list(shape), dt))

    x_d = dram_in("x", [NSEQ, SEQ, D])
    w_in_d = dram_in("w_in", [DEPTH, D, 2 * D])
    w_out_d = dram_in("w_out", [DEPTH, D, D])
    glu_w_d = dram_in("glu_w", [DEPTH, 512, 512])
    pool_w_d = dram_in("pool_w", [DEPTH, 128, 4, 128])
    smallp_d = dram_in("smallp", [128, DEPTH, 16])
    finalg_d = dram_in("finalg", [128, 8])
    ssm_small_d = dram_in("ssm_small", [128, 4, LG])
    ssm_bc_d = dram_in("ssm_bc", [DEPTH, 128, 4, 512])
    out_d = dram_out("out", [NSEQ, SEQ, D])

    ssmW_d = dram_scr("ssmW", [DEPTH, 8, 128, 5 * 4 * 128], BF16)
    tabs_d = dram_scr("tabs", [DEPTH, 8, 128, 2, 4, 256], F32)
    poolW_d = dram_scr("poolW", [DEPTH, 128, 4 * 2 * 128], BF16)
    R_SSMW = [Res(f"ssmW{l}") for l in range(DEPTH)]
    R_TABS = [Res(f"tabs{l}") for l in range(DEPTH)]
    R_POOLW = [Res(f"poolW{l}") for l in range(DEPTH)]
    R_OUT = Res("out")

    identF = sb("identF", [128, 128], F32)
    identB = sb("identB", [128, 128], BF16)
    onesB = sb("onesB", [128, 128], BF16)
    Rb_all = sb("Rb_all", [128, DEPTH, 32], F32)
    smallp = sb("smallp_sb", [128, DEPTH, 16], F32)
    finalg = sb("finalg_sb", [128, 8], F32)
    psum = es.enter_context(nc.psum_tensor("psum", [128, 8, 512], F32))
    R_CONST = Res("const")
    R_RB = Res("Rb")
    R_SMALLP = Res("smallp")
    R_PS = [Res(f"ps{i}") for i in range(8)]
    ps_rr = [0]

    def next_ps():
        i = ps_rr[0]
        ps_rr[0] = (i + 1) % 8
        return i

    P.op("pool", lambda e: e.memset(identF[:], 1.0), writes=[R_CONST])
    P.op("pool", lambda e: e.affine_select(out=identF[:], in_=identF[:], pattern=[[-1, 128]],
                                           compare_op=ALU.is_equal, fill=0.0, base=0, channel_multiplier=1),
         reads=[R_CONST], writes=[R_CONST])
    P.op("pool", lambda e: e.tensor_copy(out=identB[:], in_=identF[:]), reads=[R_CONST], writes=[R_CONST])
    P.op("pool", lambda e: e.memset(onesB[:], 1.0 / 1024.0), writes=[R_CONST])
    P.op("sp", lambda e: e.dma_start(out=smallp[:], in_=smallp_d[:]), writes=[R_SMALLP], dma="d_small")
    P.op("sp", lambda e: e.dma_start(out=finalg[:], in_=finalg_d[:]), writes=[R_SMALLP], dma="d_small")

    with ExitStack() as ps_:
        def pb(name, shape, dt):
            return sb("pl_" + name, shape, dt, ps_)
        mask = pb("mask", [128, 128], F32)
        iotaKi = pb("iotaKi", [128, 256], I32)
        iotaK = pb("iotaK", [128, 256], F32)
        JTi = pb("JTi", [128, 7, LG], I32)
        JT = pb("JT", [128, 7, LG], F32)
        sm = pb("sm", [128, 4, LG], F32)
        bc = pb("bc", [128, 4, 512], F32)
        pw = pb("pw", [128, 4, 128], F32)
        PWo = pb("PWo", [128, 4, 2, 128], BF16)
        dtt = pb("dtt", [128, LG], F32)
        ard = pb("ard", [128, LG], F32)
        ang = pb("ang", [128, LG], F32)
        EXre = pb("EXre", [128, LG, 8], F32)
        EXim = pb("EXim", [128, LG, 8], F32)
        EYre = pb("EYre", [128, LG, 8], F32)
        EYim = pb("EYim", [128, LG, 8], F32)
        s1 = pb("s1", [128, LG], F32)
        s2 = pb("s2", [128, LG], F32)
        s3 = pb("s3", [128, LG], F32)
        s4 = pb("s4", [128, LG], F32)
        fre = pb("fre", [128, LG], F32)
        fim = pb("fim", [128, LG], F32)
        Rt = pb("Rt", [128, LG], F32)
        TH = pb("TH", [128, LG], F32)
        NI8 = pb("NI8", [128, LG], I32)
        bbA = pb("bbA", [128, 32, 16], F32)
        bbB = pb("bbB", [128, 32, 16], F32)
        cA = pb("cA", [128, 32, 16], F32)
        cB = pb("cB", [128, 32, 16], F32)
        Smat = pb("Smat", [128, 128], F32)
        hpiT = pb("hpiT", [128, 1], F32)
        T1 = pb("T1", [128, 32, 8, 16], F32)
        T2 = pb("T2", [128, 32, 8, 16], F32)

        def jview(t, k):
            return t[:].rearrange("p g t c -> p (g t c)")[:, k * 7 * LG:(k + 1) * 7 * LG].rearrange("p (j n) -> p j n", j=7)
        def bview(t, k):
            return t[:].rearrange("p g t c -> p (g t c)")[:, k * 512:(k + 1) * 512].rearrange("p (g c) -> p g c", c=16)
        ta, tb_ = bview(T1, 0), bview(T1, 1)
        tc_, td = bview(T2, 0), bview(T2, 1)
        ARJ, ANJ, MAGP, MAGN = (jview(T1, k) for k in range(4))
        NIj = jview(T2, 0).bitcast(I32)
        RRj, SN, CS = (jview(T2, k) for k in range(1, 4))
        XBm = pb("XBm", [128, 32, 8, 16], F32)
        Gm = pb("Gm", [128, 32, 8, 16], F32)
        W5 = pb("W5", [128, 8, 5, 4, 128], BF16)
        tmpT = [pb(f"tmpT{i}", [128, 4, 128], F32) for i in range(2)]
        NTB = 2
        ANGq = [pb(f"ANGq{i}", [128, 4, 256], F32) for i in range(NTB)]
        NIq = [pb(f"NIq{i}", [128, 4, 256], I32) for i in range(NTB)]
        RRq = [pb(f"RRq{i}", [128, 4, 256], F32) for i in range(NTB)]
        ABq = [NIq[i][:].bitcast(F32) for i in range(NTB)]
        CSq = [pb(f"CSq{i}", [128, 2, 4, 256], F32) for i in range(NTB)]
        R_TQ = [Res(f"pl_tq{i}") for i in range(NTB)]
        R_CSQ = [Res(f"pl_csq{i}") for i in range(NTB)]
        R_BB = Res("pl_bb")

        R_PC = Res("pl_const")
        R_IN = Res("pl_in")
        R_A = Res("pl_a")
        R_E = Res("pl_E")
        R_X = Res("pl_X")
        R_T = Res("pl_T")
        R_W5 = Res("pl_W5")
        R_TMPT = [Res("pl_tmpT0"), Res("pl_tmpT1")]
        R_ANG = Res("pl_ang")
        R_TAB = Res("pl_tab")
        R_PW = Res("pl_pw")

        P.op("pool", lambda e: e.memset(mask[:], 1.0), writes=[R_PC])
        P.op("pool", lambda e: e.affine_select(out=mask[:], in_=mask[:], pattern=[[16, 8], [0, 16]],
                                               compare_op=ALU.is_ge, fill=0.0, base=15, channel_multiplier=-1),
             reads=[R_PC], writes=[R_PC])
        P.op("pool", lambda e: e.iota(iotaKi[:], pattern=[[1, 256]], base=0, channel_multiplier=0), writes=[R_PC])
        P.op("pool", lambda e: e.tensor_copy(out=iotaK[:], in_=iotaKi[:]), reads=[R_PC], writes=[R_PC])
        P.op("pool", lambda e: e.iota(JTi[:], pattern=[[-1, 7], [0, LG]], base=7, channel_multiplier=0), writes=[R_PC])
        P.op("pool", lambda e: e.tensor_copy(out=JT[:], in_=JTi[:]), reads=[R_PC], writes=[R_PC])
        P.op("pool", lambda e: e.tensor_copy(out=Smat[:, 0:64], in_=identF[:, 64:128]), reads=[R_CONST], writes=[R_PC])
        P.op("pool", lambda e: e.tensor_scalar(out=Smat[:, 64:128], in0=identF[:, 0:64], scalar1=-1.0, scalar2=None, op0=ALU.mult),
             reads=[R_CONST, R_PC], writes=[R_PC])
        P.op("pool", lambda e: e.memset(hpiT[:], HALF_PI), reads=[R_PC], writes=[R_PC])

        def bc3(ap32):
            return ap32.unsqueeze(1).to_broadcast([128, 7, LG])

        def bcc(ap32):
            return ap32.unsqueeze(2).to_broadcast([128, 32, 16])

        def bE(apE, lo, hi):
            return apE[lo:hi].unsqueeze(3).to_broadcast([hi - lo, 32, 8, 16])

        def bB(apB, lo, hi):
            return apB[lo:hi].unsqueeze(2).to_broadcast([hi - lo, 32, 8, 16])

        def bR(apR, lo, hi):
            return apR[lo:hi].rearrange("p (a g) -> p a g", g=4).unsqueeze(3).to_broadcast([hi - lo, 8, 4, 128])

        wbf_in = dram_scr("wbf_in", [DEPTH, D, 2 * D], BF16)
        wbf_out = dram_scr("wbf_out", [DEPTH, D, D], BF16)
        wbf_glu = dram_scr("wbf_glu", [DEPTH, 512, 512], BF16)
        R_WBF = [Res(f"wbf{l}") for l in range(DEPTH)]
        NCQ = 4
        R_CASTQ = [Res(f"castq{i}") for i in range(NCQ)]
        cq = [0]

        def cast_dma(dst, src, l):
            j = cq[0] % NCQ
            cq[0] += 1
            P.op("pool", lambda e: e.dma_start(out=dst, in_=src), writes=[R_CASTQ[j]], acc=[R_WBF[l]], dma=f"d_cast{j}")
        for l in range(n_layers):
            for a in range(8):
                cast_dma(wbf_in[l, a * 128:(a + 1) * 128, :], w_in_d[l, a * 128:(a + 1) * 128, :], l)
            for a in range(4):
                cast_dma(wbf_out[l, a * 256:(a + 1) * 256, :], w_out_d[l, a * 256:(a + 1) * 256, :], l)
            cast_dma(wbf_glu[l], glu_w_d[l], l)

        R_SM = Res("pl_sm")
        P.op("sp", lambda e: e.dma_start(out=sm[:], in_=ssm_small_d[:]), writes=[R_SM], dma="d_plsm")
        are, aim, ldt = sm[:, 0, :], sm[:, 1, :], sm[:, 2, :]
        P.op("act", lambda e: e.activation(out=dtt[:], in_=ldt, func=AF.Exp), reads=[R_SM], writes=[R_A])
        P.op("dve", lambda e: e.tensor_tensor(out=ard[:], in0=are, in1=dtt[:], op=ALU.mult), reads=[R_SM, R_A], writes=[R_A])
        P.op("dve", lambda e: e.tensor_tensor(out=ang[:], in0=aim, in1=dtt[:], op=ALU.mult), reads=[R_SM, R_A], writes=[R_A])
        P.op("dve", lambda e: e.tensor_tensor(out=ARJ, in0=JT[:], in1=bc3(ard[:]), op=ALU.mult), reads=[R_PC, R_A], writes=[R_A])
        P.op("dve", lambda e: e.tensor_tensor(out=ANJ, in0=JT[:], in1=bc3(ang[:]), op=ALU.mult), reads=[R_PC, R_A], writes=[R_A])
        P.op("act", lambda e: e.activation(out=MAGP, in_=ARJ, func=AF.Exp), reads=[R_A], writes=[R_A])
        P.op("act", lambda e: e.activation(out=MAGN, in_=ARJ, func=AF.Exp, scale=-1.0), reads=[R_A], writes=[R_A])
        P.op("act", lambda e: e.activation(out=NIj, in_=ANJ, func=AF.Copy, scale=INV_2PI), reads=[R_A], writes=[R_A])
        P.op("dve", lambda e: e.scalar_tensor_tensor(out=RRj, in0=NIj, scalar=-TWO_PI, in1=ANJ, op0=ALU.mult, op1=ALU.add),
             reads=[R_A], writes=[R_A])
        P.op("act", lambda e: e.activation(out=SN, in_=RRj, func=AF.Sin, scale=SIN_SCALE), reads=[R_A], writes=[R_A])
        P.op("dve", lambda e: e.tensor_scalar(out=RRj, in0=ANJ, scalar1=HALF_PI, scalar2=None, op0=ALU.add), reads=[R_A], writes=[R_A])
        P.op("act", lambda e: e.activation(out=NIj, in_=RRj, func=AF.Copy, scale=INV_2PI), reads=[R_A], writes=[R_A])
        P.op("dve", lambda e: e.scalar_tensor_tensor(out=RRj, in0=NIj, scalar=-TWO_PI, in1=RRj, op0=ALU.mult, op1=ALU.add),
             reads=[R_A], writes=[R_A])
        P.op("act", lambda e: e.activation(out=CS, in_=RRj, func=AF.Sin, scale=SIN_SCALE), reads=[R_A], writes=[R_A])
        def Ev(t):
            return t[:].rearrange("p g t -> p t g")[:, 0:7, :]
        P.op("dve", lambda e: e.tensor_tensor(out=Ev(EXre), in0=MAGP, in1=CS, op=ALU.mult), reads=[R_A], writes=[R_E])
        P.op("dve", lambda e: e.tensor_tensor(out=Ev(EXim), in0=MAGP, in1=SN, op=ALU.mult), reads=[R_A], writes=[R_E])
        P.op("dve", lambda e: e.tensor_tensor(out=Ev(EYre), in0=MAGN, in1=CS, op=ALU.mult), reads=[R_A], writes=[R_E])
        P.op("dve", lambda e: e.scalar_tensor_tensor(out=Ev(EYim), in0=MAGN, scalar=-1.0, in1=SN, op0=ALU.mult, op1=ALU.mult),
             reads=[R_A], writes=[R_E])
        P.op("dve", lambda e: e.memset(EXre[:, :, 7:8], 1.0), writes=[R_E])
        P.op("dve", lambda e: e.memset(EXim[:, :, 7:8], 0.0), writes=[R_E])
        P.op("dve", lambda e: e.memset(EYre[:, :, 7:8], 1.0), writes=[R_E])
        P.op("dve", lambda e: e.memset(EYim[:, :, 7:8], 0.0), writes=[R_E])
        lre, lim = EXre[:, :, 6], EXim[:, :, 6]
        P.op("dve", lambda e: e.tensor_scalar(out=s1[:], in0=lre, scalar1=-1.0, scalar2=None, op0=ALU.add), reads=[R_E], writes=[R_A])
        P.op("dve", lambda e: e.tensor_tensor(out=s2[:], in0=are, in1=are, op=ALU.mult), reads=[R_SM], writes=[R_A])
        P.op("dve", lambda e: e.tensor_tensor(out=s3[:], in0=aim, in1=aim, op=ALU.mult), reads=[R_SM], writes=[R_A])
        P.op("dve", lambda e: e.tensor_tensor(out=s2[:], in0=s2[:], in1=s3[:], op=ALU.add), reads=[R_A], writes=[R_A])
        P.op("dve", lambda e: e.reciprocal(out=s2[:], in_=s2[:]), reads=[R_A], writes=[R_A])
        P.op("dve", lambda e: e.tensor_tensor(out=s3[:], in0=s1[:], in1=are, op=ALU.mult), reads=[R_A, R_SM], writes=[R_A])
        P.op("dve", lambda e: e.tensor_tensor(out=s4[:], in0=lim, in1=aim, op=ALU.mult), reads=[R_E, R_SM], writes=[R_A])
        P.op("dve", lambda e: e.tensor_tensor(out=s3[:], in0=s3[:], in1=s4[:], op=ALU.add), reads=[R_A], writes=[R_A])
        P.op("dve", lambda e: e.tensor_tensor(out=fre[:], in0=s3[:], in1=s2[:], op=ALU.mult), reads=[R_A], writes=[R_A])
        P.op("dve", lambda e: e.tensor_tensor(out=s3[:], in0=lim, in1=are, op=ALU.mult), reads=[R_E, R_SM], writes=[R_A])
        P.op("dve", lambda e: e.tensor_tensor(out=s4[:], in0=s1[:], in1=aim, op=ALU.mult), reads=[R_A, R_SM], writes=[R_A])
        P.op("dve", lambda e: e.tensor_tensor(out=s3[:], in0=s3[:], in1=s4[:], op=ALU.subtract), reads=[R_A], writes=[R_A])
        P.op("dve", lambda e: e.tensor_tensor(out=fim[:], in0=s3[:], in1=s2[:], op=ALU.mult), reads=[R_A], writes=[R_A])
        P.op("act", lambda e: e.activation(out=Rt[:], in_=ard[:], func=AF.Exp, scale=8.0), reads=[R_A], writes=[R_A])
        P.op("act", lambda e: e.activation(out=Rb_all[:].rearrange("p l g -> p (l g)"), in_=Rt[:], func=AF.Copy), reads=[R_A], writes=[R_RB])
        P.op("dve", lambda e: e.tensor_scalar(out=s1[:], in0=ang[:], scalar1=8.0, scalar2=None, op0=ALU.mult), reads=[R_A], writes=[R_A])
        P.op("act", lambda e: e.activation(out=NI8[:], in_=s1[:], func=AF.Copy, scale=INV_2PI), reads=[R_A], writes=[R_A])
        P.op("dve", lambda e: e.scalar_tensor_tensor(out=TH[:], in0=NI8[:], scalar=-TWO_PI, in1=s1[:], op0=ALU.mult, op1=ALU.add),
             reads=[R_A], writes=[R_A])


        for l in range(n_layers):
            P.op("sp", lambda e, l=l: e.dma_start(out=bc[:], in_=ssm_bc_d[l]), writes=[R_IN], dma="d_plin")
            ls = slice(l * 32, (l + 1) * 32)
            fre_l, fim_l = fre[:, ls], fim[:, ls]
            EXre_l, EXim_l, EYre_l, EYim_l = EXre[:, ls, :], EXim[:, ls, :], EYre[:, ls, :], EYim[:, ls, :]
            P.op("sp", lambda e, l=l: e.dma_start(out=pw[:], in_=pool_w_d[l]), writes=[R_PW], dma="d_plpw")
            bre = bc[:, 0, :].rearrange("p (g c) -> p g c", c=16)
            bim = bc[:, 1, :].rearrange("p (g c) -> p g c", c=16)
            cre = bc[:, 2, :].rearrange("p (g c) -> p g c", c=16)
            cim = bc[:, 3, :].rearrange("p (g c) -> p g c", c=16)

            for g4 in range(4):
                P.op("act", lambda e, g4=g4: e.activation(out=PWo[:, g4, 0, :], in_=pw[:, g4, :], func=AF.Copy, scale=1.0 / POOL_WINDOWS[g4]),
                     reads=[R_PW], acc=[R_PW])
            P.op("act", lambda e: e.activation(out=PWo[:, :, 1, :], in_=pw[:, :, :], func=AF.Copy, scale=-1.0), reads=[R_PW], acc=[R_PW])
            P.op("sp", lambda e, l=l: e.dma_start(out=poolW_d[l], in_=PWo[:].rearrange("p a b c -> p (a b c)")),
                 reads=[R_PW], writes=[R_POOLW[l]], dma=f"d_plpo{l}")

            P.op("dve", lambda e, fre_l=fre_l, fim_l=fim_l: e.tensor_tensor(out=ta, in0=bre, in1=bcc(fre_l), op=ALU.mult), reads=[R_A, R_IN], writes=[R_BB], acc=[R_T])
            P.op("dve", lambda e, fre_l=fre_l, fim_l=fim_l: e.tensor_tensor(out=tb_, in0=bim, in1=bcc(fim_l), op=ALU.mult), reads=[R_A, R_IN], acc=[R_BB])
            P.op("dve", lambda e, fre_l=fre_l, fim_l=fim_l: e.tensor_tensor(out=tc_, in0=bim, in1=bcc(fre_l), op=ALU.mult), reads=[R_A, R_IN], acc=[R_BB])
            P.op("dve", lambda e, fre_l=fre_l, fim_l=fim_l: e.tensor_tensor(out=td, in0=bre, in1=bcc(fim_l), op=ALU.mult), reads=[R_A, R_IN], acc=[R_BB])
            P.op("dve", lambda e: e.tensor_tensor(out=bbA[0:64], in0=ta[0:64], in1=tb_[0:64], op=ALU.subtract), reads=[R_BB], acc=[R_BB])
            P.op("dve", lambda e: e.tensor_tensor(out=bbA[64:128], in0=tc_[64:128], in1=td[64:128], op=ALU.add), reads=[R_BB], acc=[R_BB])
            P.op("dve", lambda e: e.scalar_tensor_tensor(out=bbB[0:64], in0=tc_[0:64], scalar=-1.0, in1=td[0:64], op0=ALU.mult, op1=ALU.subtract),
                 reads=[R_BB], acc=[R_BB])
            P.op("dve", lambda e: e.tensor_tensor(out=bbB[64:128], in0=ta[64:128], in1=tb_[64:128], op=ALU.subtract), reads=[R_BB], acc=[R_BB])
            P.op("act", lambda e: e.activation(out=cA[0:64], in_=cre[0:64], func=AF.Copy), reads=[R_IN], acc=[R_BB])
            P.op("act", lambda e: e.activation(out=cA[64:128], in_=cim[64:128], func=AF.Copy, scale=-1.0), reads=[R_IN], acc=[R_BB])
            P.op("act", lambda e: e.activation(out=cB[0:64], in_=cim[0:64], func=AF.Copy, scale=-1.0), reads=[R_IN], acc=[R_BB])
            P.op("act", lambda e: e.activation(out=cB[64:128], in_=cre[64:128], func=AF.Copy, scale=-1.0), reads=[R_IN], acc=[R_BB])
            tv = tabs_d[l].rearrange("a p two g k -> p a two g k")
            tq_rr = [0]

            def table_front(gb, l=l):
                i = tq_rr[0]
                tq_rr[0] = (i + 1) % NTB
                gs = slice(gb * 4, gb * 4 + 4)
                P.op("dve", lambda e: e.tensor_tensor(out=ANGq[i][:], in0=TH[:, l * 32 + gb * 4:l * 32 + gb * 4 + 4].unsqueeze(2).to_broadcast([128, 4, 256]),
                                                     in1=iotaK[:].unsqueeze(1).to_broadcast([128, 4, 256]), op=ALU.mult),
                     reads=[R_A, R_PC], writes=[R_TQ[i]])
                P.op("act", lambda e: e.activation(out=NIq[i][:], in_=ANGq[i][:], func=AF.Copy, scale=INV_2PI), reads=[R_TQ[i]], acc=[R_TQ[i]])
                return i

            def table_rest(gb, i, l=l, tv=tv):
                P.op("dve", lambda e: e.scalar_tensor_tensor(out=RRq[i][:], in0=NIq[i][:], scalar=-TWO_PI, in1=ANGq[i][:], op0=ALU.mult, op1=ALU.add),
                     reads=[R_TQ[i]], acc=[R_TQ[i]])
                P.op("act", lambda e: e.activation(out=ABq[i], in_=RRq[i][:], func=AF.Abs), reads=[R_TQ[i]], acc=[R_TQ[i]])
                P.op("act", lambda e: e.activation(out=CSq[i][:, 1], in_=RRq[i][:], func=AF.Sin, scale=SIN_SCALE), reads=[R_TQ[i]], writes=[R_CSQ[i]])
                P.op("act", lambda e: e.activation(out=CSq[i][:, 0], in_=ABq[i], func=AF.Sin, scale=-1.0, bias=hpiT[:]),
                     reads=[R_TQ[i], R_PC], acc=[R_CSQ[i]])
                P.op("sp", lambda e: e.dma_start(out=tv[:, gb], in_=CSq[i][:]), reads=[R_CSQ[i]], acc=[R_TABS[l]], dma=f"d_csq{i}")

            bigops = [
                lambda: P.op("dve", lambda e, EXre_l=EXre_l, EXim_l=EXim_l, EYre_l=EYre_l, EYim_l=EYim_l: e.tensor_tensor(out=T1[:], in0=bE(EXre_l, 0, 128), in1=bB(bbA, 0, 128), op=ALU.mult), reads=[R_E, R_BB], writes=[R_T]),
                lambda: P.op("dve", lambda e, EXre_l=EXre_l, EXim_l=EXim_l, EYre_l=EYre_l, EYim_l=EYim_l: e.tensor_tensor(out=T2[:], in0=bE(EXim_l, 0, 128), in1=bB(bbB, 0, 128), op=ALU.mult), reads=[R_E, R_BB], acc=[R_T]),
                lambda: P.op("dve", lambda e: e.tensor_tensor(out=XBm[:], in0=T1[:], in1=T2[:], op=ALU.add), reads=[R_T], writes=[R_X]),
                lambda: P.op("dve", lambda e, EXre_l=EXre_l, EXim_l=EXim_l, EYre_l=EYre_l, EYim_l=EYim_l: e.tensor_tensor(out=T1[:], in0=bE(EYre_l, 0, 128), in1=bB(cA, 0, 128), op=ALU.mult), reads=[R_E, R_BB], writes=[R_T]),
                lambda: P.op("dve", lambda e, EXre_l=EXre_l, EXim_l=EXim_l, EYre_l=EYre_l, EYim_l=EYim_l: e.tensor_tensor(out=T2[:], in0=bE(EYim_l, 0, 128), in1=bB(cB, 0, 128), op=ALU.mult), reads=[R_E, R_BB], acc=[R_T]),
                lambda: P.op("dve", lambda e: e.tensor_tensor(out=Gm[:], in0=T1[:], in1=T2[:], op=ALU.add), reads=[R_T], acc=[R_X]),
            ]
            for i_, bo in enumerate(bigops):
                ti_ = table_front(i_)
                bo()
                table_rest(i_, ti_)

            for gb in range(8):
                if gb in (2, 5):
                    tp_gb = 6 + (gb == 5)
                    tp_i = table_front(tp_gb)
                ti = gb % 2
                pi = next_ps()

                def f_toep(e, gb=gb, pi=pi):
                    ins = None
                    for g4 in range(4):
                        g = gb * 4 + g4
                        ins = e.matmul(psum[:, pi, g4 * 128:(g4 + 1) * 128],
                                       lhsT=XBm[:, g].rearrange("p t c -> p (t c)"),
                                       rhs=Gm[:, g].rearrange("p t c -> p (t c)"), start=True, stop=True)
                    return ins
                P.op("pe", f_toep, reads=[R_X], writes=[R_PS[pi]])
                P.op("dve", lambda e, pi=pi, ti=ti: e.tensor_tensor(out=tmpT[ti][:], in0=psum[:, pi, :].rearrange("p (g n) -> p g n", g=4),
                                                                   in1=mask[:].unsqueeze(1).to_broadcast([128, 4, 128]), op=ALU.mult),
                     reads=[R_PS[pi], R_PC], writes=[R_TMPT[ti]])
                for g4 in range(4):
                    P.op("dve", lambda e, gb=gb, g4=g4, ti=ti, l=l: e.scalar_tensor_tensor(
                        out=W5[:, gb, 0, g4, :], in0=identF[:], scalar=sm[:, 3, l * 32 + gb * 4 + g4:l * 32 + gb * 4 + g4 + 1], in1=tmpT[ti][:, g4, :],
                        op0=ALU.mult, op1=ALU.add), reads=[R_TMPT[ti], R_CONST, R_SM], acc=[R_W5])
                if gb in (2, 5):
                    table_rest(tp_gb, tp_i)
                pi2 = next_ps()

                def f_tr(e, gb=gb, pi2=pi2):
                    ins = None
                    for g4 in range(4):
                        g = gb * 4 + g4
                        ins = e.transpose(out=psum[:, pi2, g4 * 128:(g4 + 1) * 128],
                                          in_=XBm[:, g].rearrange("p t c -> p (t c)"), identity=identF[:])
                    return ins
                P.op("pe", f_tr, reads=[R_X, R_CONST], writes=[R_PS[pi2]])
                pv = psum[:, pi2, :].rearrange("p (g n) -> p g n", g=4)
                P.op("act", lambda e, gb=gb, pv=pv: e.activation(out=W5[:, gb, 1, :, :], in_=pv, func=AF.Copy),
                     reads=[R_PS[pi2]], acc=[R_W5])
                P.op("act", lambda e, gb=gb, pv=pv: e.activation(out=W5[:, gb, 2, :, 0:64], in_=pv[:, :, 64:128], func=AF.Copy),
                     reads=[R_PS[pi2]], acc=[R_W5])
                P.op("act", lambda e, gb=gb, pv=pv: e.activation(out=W5[:, gb, 2, :, 64:128], in_=pv[:, :, 0:64], func=AF.Copy, scale=-1.0),
                     reads=[R_PS[pi2]], acc=[R_W5])
                pi3 = next_ps()

                def f_sw(e, gb=gb, pi3=pi3):
                    ins = None
                    for g4 in range(4):
                        g = gb * 4 + g4
                        ins = e.matmul(psum[:, pi3, g4 * 128:(g4 + 1) * 128], lhsT=Smat[:],
                                       rhs=Gm[:, g].rearrange("p t c -> p (t c)"), start=True, stop=True)
                    return ins
                P.op("pe", f_sw, reads=[R_X, R_PC], writes=[R_PS[pi3]])
                for g4 in range(4):
                    g = gb * 4 + g4
                    P.op("dve", lambda e, gb=gb, g4=g4, g=g, l=l: e.tensor_scalar(out=W5[:, gb, 3, g4, :], in0=Gm[:, g].rearrange("p t c -> p (t c)"),
                                                                            scalar1=Rt[:, l * 32 + g:l * 32 + g + 1], scalar2=None, op0=ALU.mult),
                         reads=[R_X, R_A], acc=[R_W5])
                    P.op("act", lambda e, gb=gb, g4=g4, g=g, pi3=pi3, l=l: e.activation(out=W5[:, gb, 4, g4, :], in_=psum[:, pi3, g4 * 128:(g4 + 1) * 128],
                                                                                  func=AF.Copy, scale=Rt[:, l * 32 + g:l * 32 + g + 1]),
                         reads=[R_PS[pi3], R_A], acc=[R_W5])
            P.op("sp", lambda e, l=l: e.dma_start(out=ssmW_d[l].rearrange("a p n -> p a n"),
                                                  in_=W5[:].rearrange("p a k g n -> p a (k g n)")),
                 reads=[R_W5], writes=[R_SSMW[l]], dma=f"d_plssm{l}")

    if debug == "prologue":
        rb_d = dram_out("rb_dbg", [128, DEPTH, 32])
        P.op("sp", lambda e: e.dma_start(out=rb_d[:], in_=Rb_all[:]), reads=[R_RB], writes=[R_OUT], dma="d_out")
        fin = [R_OUT] + R_SSMW[:n_layers] + R_TABS[:n_layers] + R_POOLW[:n_layers] + R_WBF[:n_layers]
        P.op("sp", None, reads=fin, sig=False)
        sems = {k: es.enter_context(nc.semaphore(k)) for k in P.cnt}
        with nc.Block() as block:
            P.emit(nc, block, sems)
        es.close()
        return nc

    P.fence()
    xres = sb("xres", [128, 8, ST], F32)
    h = sb("h", [128, 8, ST], BF16)
    gate = sb("gate", [128, 8, ST], BF16)
    ycat = sb("ycat", [128, 8, ST], BF16)
    upool = sb("upool", [128, 4, 16 + ST], BF16)
    spool = sb("spool", [128, 4, ST], BF16)
    ysT = sb("ysT", [128, 4, ST], BF16)
    xsq0 = spool[:].rearrange("p a n -> p (a n)").rearrange("p (a n) -> p a n", n=512)
    xsq1 = upool[:].rearrange("p a n -> p (a n)")[:, 0:4096].rearrange("p (a n) -> p a n", n=512)
    xsqs = [xsq0, xsq1]
    ZYf = sb("ZY", [128, 4096], BF16)
    ZY = ZYf[:].rearrange("p (t f) -> p t f", t=8)
    Zs = ZYf[:].rearrange("p (g t c) -> p g t c", g=32, t=8)
    U = sb("U", [128, 32, 128], BF16)
    rs = sb("rs", [128, 2, 512], F32)
    t1 = sb("t1", [128, 4, 128], F32)
    t2 = sb("t2", [128, 4, 128], F32)
    NQ = 2
    Q = [sb(f"Q{i}", [128, 4, 129], F32) for i in range(NQ)]
    Pc = [sb(f"Pc{i}", [128, 4, 128], BF16) for i in range(NQ)]
    Ps = [sb(f"Ps{i}", [128, 4, 128], BF16) for i in range(NQ)]
    sig = sb("sig", [128, 2, 512], BF16)
    xs = sb("xs", [128, 2, D], F32)
    pwk = xs[:].rearrange("p a n -> p (a n)").bitcast(BF16)[:, 0:3 * (16 + ST)].rearrange("p (a n) -> p a n", a=3)
    carry = sb("carry", [128, DEPTH, 32], F32)
    halo = sb("halo", [128, DEPTH, 4, 16], BF16)
    cfix = sb("cfix", [128, 4, 16], F32)
    epsT = sb("epsT", [128, 1], F32)
    NWS, NSW, NSTB = 4, 3, 2
    WS = [sb(f"WS{i}", [128, 8 * 512], BF16) for i in range(NWS)]
    SSw = [sb(f"SSw{i}", [128, 5, 4, 128], BF16) for i in range(NSW)]
    SSt = [sb(f"SSt{i}", [128, 2, 4, 128], F32) for i in range(NSTB)]
    R_WS = [Res(f"WS{i}") for i in range(NWS)]
    R_SW = [Res(f"SW{i}") for i in range(NSW)]
    R_STB = [Res(f"STB{i}") for i in range(NSTB)]
    R_ZY, R_U = Res("ZY"), Res("U")
    R_XRES = [Res("xres0"), Res("xres1")]
    R_RS = [Res("rs0"), Res("rs1")]
    R_H = [Res("h0"), Res("h1")]
    R_GATE = [Res("g0"), Res("g1")]
    R_YCAT = [Res("yc0"), Res("yc1")]
    R_UPOOL, R_SPOOL, R_YST = Res("upool"), Res("spool"), Res("ysT")
    R_T1, R_T2 = Res("t1"), Res("t2")
    R_Q = [Res(f"Q{i}") for i in range(NQ)]
    R_PCS = [Res(f"PcPs{i}") for i in range(NQ)]
    R_SIG = [Res("sig0"), Res("sig1")]
    R_XS = [Res("xs0"), Res("xs1")]
    R_CARRY = [[Res(f"carry{l}_{gb}") for gb in range(8)] for l in range(DEPTH)]
    R_HALO = [Res(f"halo{l}") for l in range(DEPTH)]
    R_C2 = Res("const2")
    ws_rr, sw_rr, stb_rr, xs_rr, ev_rr = [0], [0], [0], [0], [0]

    def psB(pi):
        return psum[:, pi, :].bitcast(BF16)

    def evac_eng():
        ev_rr[0] ^= 1
        return "act" if ev_rr[0] else "dve"

    def copy_op(eng, out, in_, reads, acc):
        if eng == "act":
            P.op("act", lambda e: e.activation(out=out, in_=in_, func=AF.Copy), reads=reads, acc=acc)
        else:
            P.op(eng, lambda e: e.tensor_copy(out=out, in_=in_), reads=reads, acc=acc)

    P.op("pool", lambda e: e.memset(epsT[:], EPS), writes=[R_C2])
    P.op("pool", lambda e: e.memset(cfix[:], 1.0), writes=[R_C2])
    for f in range(4):
        w = POOL_WINDOWS[f]
        for t in range(w - 1):
            P.op("pool", lambda e, f=f, t=t, w=w: e.memset(cfix[:, f, t:t + 1], float(w) / float(t + 1)), writes=[R_C2])

    def load_ws(src_ap, nparts, reads):
        i = ws_rr[0]
        ws_rr[0] = (i + 1) % NWS
        dst = WS[i][:, 0:nparts * 512].rearrange("p (a n) -> p a n", n=512)
        P.op("sp", lambda e: e.dma_start(out=dst, in_=src_ap), reads=reads, writes=[R_WS[i]], dma=f"d_ws{i}")
        return i

    def wsv(i):
        return WS[i][:].rearrange("p (a n) -> p a n", n=512)

    def mm_group(pi, lhs_fn, rhs_fn, nk, reads):
        def f(e):
            ins = None
            for kc in range(nk):
                ins = e.matmul(psum[:, pi, :], lhsT=lhs_fn(kc), rhs=rhs_fn(kc), start=(kc == 0), stop=(kc == nk - 1))
            return ins
        P.op("pe", f, reads=reads, writes=[R_PS[pi]])

    def R_XSQ(n):
        return R_SPOOL if n == 0 else R_UPOOL

    def norm_square(n, f=None):
        nr = slice(n * 512, (n + 1) * 512)
        if f is None:
            P.op("act", lambda e: e.activation(out=xsqs[n], in_=xres[:, :, nr], func=AF.Square), reads=[R_XRES[n]], writes=[R_XSQ(n)])
        elif f == 0:
            P.op("act", lambda e: e.activation(out=xsqs[n][:, 0, :], in_=xres[:, 0, nr], func=AF.Square), reads=[R_XRES[n]], writes=[R_XSQ(n)])
        else:
            P.op("act", lambda e: e.activation(out=xsqs[n][:, f, :], in_=xres[:, f, nr], func=AF.Square), reads=[R_XRES[n]], acc=[R_XSQ(n)])

    def norm_rstd(n):
        pi = next_ps()
        mm_group(pi, lambda kc: onesB[:], lambda kc: xsqs[n][:, kc, :], 8, [R_XSQ(n), R_CONST])
        P.op("act", lambda e, pi=pi: e.activation(out=rs[:, n, :], in_=psum[:, pi, :], func=AF.Ln, bias=epsT[:], scale=1.0),
             reads=[R_PS[pi], R_C2], writes=[R_RS[n]])
        P.op("act", lambda e: e.activation(out=rs[:, n, :], in_=rs[:, n, :], func=AF.Exp, scale=-0.5), reads=[R_RS[n]], writes=[R_RS[n]])

    def norm_apply(n, lnext, fs=range(8)):
        nr = slice(n * 512, (n + 1) * 512)
        for f in fs:
            if lnext is None:
                P.op("dve", lambda e, f=f: e.scalar_tensor_tensor(
                    out=xres[:, f, nr], in0=xres[:, f, nr], scalar=finalg[:, f:f + 1], in1=rs[:, n, :], op0=ALU.mult, op1=ALU.mult),
                    reads=[R_RS[n], R_SMALLP, R_XRES[n]], acc=[R_XRES[n]])
            else:
                P.op("dve", lambda e, f=f: e.scalar_tensor_tensor(
                    out=h[:, f, nr], in0=xres[:, f, nr], scalar=smallp[:, lnext, f:f + 1], in1=rs[:, n, :], op0=ALU.mult, op1=ALU.mult),
                    reads=[R_XRES[n], R_RS[n], R_SMALLP], acc=[R_H[n]])

    for st in range(n_sub):
        seq, half = st // 2, st % 2
        t0 = half * ST
        if half == 0:
            P.op("pool", lambda e: e.memset(carry[:], 0.0), writes=[r for rl in R_CARRY for r in rl])
            P.op("pool", lambda e: e.memset(halo[:], 0.0), writes=R_HALO)
        for tt in range(8):
            si = xs_rr[0]
            xs_rr[0] ^= 1
            P.op("sp", lambda e, si=si, tt=tt, seq=seq, t0=t0: e.dma_start(out=xs[:, si, :], in_=x_d[seq, t0 + tt * 128:t0 + (tt + 1) * 128, :]),
                 writes=[R_XS[si]], dma=f"d_xs{si}")
            for fh in range(2):
                pi = next_ps()

                def f_xt(e, si=si, fh=fh, pi=pi):
                    ins = None
                    for f4 in range(4):
                        f = fh * 4 + f4
                        ins = e.transpose(out=psum[:, pi, f4 * 128:(f4 + 1) * 128], in_=xs[:, si, f * 128:(f + 1) * 128], identity=identF[:])
                    return ins
                P.op("pe", f_xt, reads=[R_XS[si], R_CONST], writes=[R_PS[pi]])
                copy_op(evac_eng(), xres[:, fh * 4:(fh + 1) * 4, tt * 128:(tt + 1) * 128],
                        psum[:, pi, :].rearrange("p (f n) -> p f n", f=4), [R_PS[pi]], [R_XRES[tt // 4]])

        for l in range(n_layers):
            k0 = half * NK
            win = wbf_in[l].rearrange("(a p) n -> p a n", p=128)
            if l == 0:
                for n in range(2):
                    norm_square(n)
                    norm_rstd(n)
                    norm_apply(n, 0)
            wi_g = [load_ws(win[:, :, 1024 + sg * 512:1024 + (sg + 1) * 512], 8, [R_WBF[l]]) for sg in range(2)]

            def gate_group(fo, n, wi_g=wi_g):
                nr = slice(n * 512, (n + 1) * 512)
                wi, f4 = wi_g[fo // 4], fo % 4
                pi = next_ps()
                mm_group(pi, lambda kc: wsv(wi)[:, kc, f4 * 128:(f4 + 1) * 128], lambda kc: h[:, kc, nr], 8, [R_WS[wi], R_H[n]])
                P.op("act", lambda e: e.activation(out=gate[:, fo, nr], in_=psum[:, pi, :], func=AF.Silu), reads=[R_PS[pi]], acc=[R_GATE[n]])
            if l > 0:
                for fo in range(4):
                    gate_group(fo, 0)
            wi_s = load_ws(win[:, :, 512:1024], 8, [R_WBF[l]])
            for tau in range(8):
                pi = next_ps()
                mm_group(pi, lambda kc, tau=tau: h[:, kc, :].rearrange("p (k t) -> p t k", t=8)[:, tau, :],
                         lambda kc, wi_s=wi_s: wsv(wi_s)[:, kc, :], 8, [R_WS[wi_s], R_H[0], R_H[1]])
                copy_op(evac_eng(), Zs[:, :, tau, :], psum[:, pi, :].rearrange("p (g c) -> p g c", c=16), [R_PS[pi]], [R_ZY])
            for gq in range(4):
                pi = next_ps()

                def f_tr(e, gq=gq, pi=pi):
                    ins = None
                    for g8 in range(8):
                        g = gq * 8 + g8
                        ins = e.transpose(out=psB(pi)[:, g8 * 128:(g8 + 1) * 128], in_=Zs[:, g].rearrange("p t c -> p (t c)"), identity=identB[:])
                    return ins
                P.op("pe", f_tr, reads=[R_ZY, R_CONST], writes=[R_PS[pi]])
                copy_op(evac_eng(), U[:, gq * 8:(gq + 1) * 8, :], psB(pi).rearrange("p (g k) -> p g k", g=8), [R_PS[pi]], [R_U])

            wi_p = load_ws(win[:, :, 0:512], 8, [R_WBF[l]])
            fillers = []

            def pool_in_group(f, n, wi_p=wi_p, l=l):
                nr = slice(n * 512, (n + 1) * 512)
                pi = next_ps()
                mm_group(pi, lambda kc: wsv(wi_p)[:, kc, f * 128:(f + 1) * 128], lambda kc: h[:, kc, nr], 8, [R_WS[wi_p], R_H[n]])
                P.op("act", lambda e: e.activation(out=upool[:, f, 16 + n * 512:16 + (n + 1) * 512], in_=psum[:, pi, :], func=AF.Copy),
                     reads=[R_PS[pi]], acc=[R_UPOOL])

            def pool_sums(l=l, half=half):
                LT = 16 + ST
                R_PWK = R_XS
                P.op("dve", lambda e: e.tensor_copy(out=halo[:, l, :, :], in_=upool[:, :, ST:ST + 16]), reads=[R_UPOOL], writes=[R_HALO[l]])
                P.op("dve", lambda e: e.tensor_tensor(out=spool[:, 0, :], in0=upool[:, 0, 16:LT], in1=upool[:, 0, 15:LT - 1], op=ALU.add),
                     reads=[R_UPOOL], writes=[R_SPOOL])
                for f in (1, 2, 3):
                    P.op("dve", lambda e, f=f: e.tensor_tensor(out=pwk[:, 0, 1:LT], in0=upool[:, f, 1:LT], in1=upool[:, f, 0:LT - 1], op=ALU.add),
                         reads=[R_UPOOL], writes=R_PWK)
                    if f == 1:
                        P.op("dve", lambda e: e.tensor_tensor(out=spool[:, 1, :], in0=pwk[:, 0, 16:LT], in1=pwk[:, 0, 14:LT - 2], op=ALU.add),
                             reads=R_PWK, acc=[R_SPOOL])
                        continue
                    P.op("dve", lambda e: e.tensor_tensor(out=pwk[:, 1, 3:LT], in0=pwk[:, 0, 3:LT], in1=pwk[:, 0, 1:LT - 2], op=ALU.add),
                         reads=R_PWK, writes=R_PWK)
                    if f == 2:
                        P.op("dve", lambda e: e.tensor_tensor(out=spool[:, 2, :], in0=pwk[:, 1, 16:LT], in1=pwk[:, 1, 12:LT - 4], op=ALU.add),
                             reads=R_PWK, acc=[R_SPOOL])
                        continue
                    P.op("dve", lambda e: e.tensor_tensor(out=pwk[:, 2, 7:LT], in0=pwk[:, 1, 7:LT], in1=pwk[:, 1, 3:LT - 4], op=ALU.add),
                         reads=R_PWK, writes=R_PWK)
                    P.op("dve", lambda e: e.tensor_tensor(out=spool[:, 3, :], in0=pwk[:, 2, 16:LT], in1=pwk[:, 2, 8:LT - 8], op=ALU.add),
                         reads=R_PWK, acc=[R_SPOOL])
                if half == 0:
                    P.op("dve", lambda e: e.tensor_tensor(out=spool[:, :, 0:16], in0=spool[:, :, 0:16], in1=cfix[:], op=ALU.mult),
                         reads=[R_C2], writes=[R_SPOOL])

            P.op("dve", lambda e, l=l: e.tensor_copy(out=upool[:, :, 0:16], in_=halo[:, l, :, :]), reads=[R_HALO[l]], writes=[R_UPOOL])
            for fo in range(4):
                for n in range(2):
                    if l > 0 and n == 0:
                        continue
                    fillers.append(lambda fo=fo, n=n: gate_group(fo, n))
            for f in range(4):
                for n in range(2):
                    fillers.append(lambda f=f, n=n: pool_in_group(f, n))
            fillers.append(pool_sums)
            for fo in range(4, 8):
                for n in range(2):
                    fillers.append(lambda fo=fo, n=n: gate_group(fo, n))
            fillers.reverse()

            def run_fillers(k):
                for _ in range(k):
                    if fillers:
                        fillers.pop()()

            slots = {}

            def ssm_front(gb, l=l, k0=k0):
                wi = sw_rr[0]
                sw_rr[0] = (wi + 1) % NSW
                ti = stb_rr[0]
                stb_rr[0] = (ti + 1) % NSTB
                qi = gb % NQ
                slots[gb] = (wi, qi)
                P.op("sp", lambda e: e.dma_start(out=SSw[wi][:].rearrange("p a g n -> p (a g n)"), in_=ssmW_d[l, gb]),
                     reads=[R_SSMW[l]], writes=[R_SW[wi]], dma=f"d_sw{wi}")
                P.op("sp", lambda e: e.dma_start(out=SSt[ti][:], in_=tabs_d[l, gb][:, :, :, k0:k0 + NK]),
                     reads=[R_TABS[l]], writes=[R_STB[ti]], dma=f"d_stb{ti}")
                pa, pb_ = next_ps(), next_ps()

                def f_v(e):
                    ins = None
                    for kind, pi in ((1, pa), (2, pb_)):
                        for g4 in range(4):
                            ins = e.matmul(psum[:, pi, g4 * 128:(g4 + 1) * 128], lhsT=SSw[wi][:, kind, g4, :], rhs=U[:, gb * 4 + g4, :],
                                           start=True, stop=True)
                    return ins
                P.op("pe", f_v, reads=[R_SW[wi], R_U], writes=[R_PS[pa], R_PS[pb_]])

                def pv(pi):
                    return psum[:, pi, :].rearrange("p (g k) -> p g k", g=4)
                P.op("dve", lambda e: e.tensor_tensor(out=t1[:], in0=pv(pa), in1=SSt[ti][:, 0], op=ALU.mult),
                     reads=[R_PS[pa], R_STB[ti]], writes=[R_T1])
                P.op("dve", lambda e: e.tensor_tensor(out=t2[:], in0=pv(pb_), in1=SSt[ti][:, 1], op=ALU.mult),
                     reads=[R_PS[pb_], R_STB[ti]], writes=[R_T2])
                P.op("dve", lambda e: e.tensor_tensor(out=t1[:], in0=t1[:], in1=t2[:], op=ALU.add), reads=[R_T1, R_T2], writes=[R_T1])
                P.op("dve", lambda e: e.tensor_copy(out=Q[qi][:, :, 0], in_=carry[:, l, gb * 4:(gb + 1) * 4]),
                     reads=[R_CARRY[l][gb]], writes=[R_Q[qi]])
                for g4 in range(4):
                    g = gb * 4 + g4
                    P.op("dve", lambda e, g4=g4, g=g: e.tensor_tensor_scan(
                        out=Q[qi][:, g4, 1:129], data0=Rb_all[:, l, g:g + 1].to_broadcast([128, 128]), data1=t1[:, g4, :],
                        initial=carry[:, l, g:g + 1], op0=ALU.mult, op1=ALU.add),
                        reads=[R_T1, R_RB, R_CARRY[l][gb]], acc=[R_Q[qi]])
                P.op("dve", lambda e: e.tensor_copy(out=carry[:, l, gb * 4:(gb + 1) * 4], in_=Q[qi][:, :, 128]),
                     reads=[R_Q[qi]], writes=[R_CARRY[l][gb]])
                P.op("dve", lambda e: e.tensor_tensor(out=Pc[qi][:], in0=Q[qi][:, :, 0:128], in1=SSt[ti][:, 0], op=ALU.mult),
                     reads=[R_Q[qi], R_STB[ti]], writes=[R_PCS[qi]])
                P.op("dve", lambda e: e.tensor_tensor(out=Ps[qi][:], in0=Q[qi][:, :, 0:128], in1=SSt[ti][:, 1], op=ALU.mult),
                     reads=[R_Q[qi], R_STB[ti]], acc=[R_PCS[qi]])

            def ssm_back(gb):
                wi, qi = slots[gb]
                py = next_ps()

                def f_y(e):
                    ins = None
                    for g4 in range(4):
                        o = psum[:, py, g4 * 128:(g4 + 1) * 128]
                        e.matmul(o, lhsT=U[:, gb * 4 + g4, :], rhs=SSw[wi][:, 0, g4, :], start=True, stop=False)
                        e.matmul(o, lhsT=Pc[qi][:, g4, :], rhs=SSw[wi][:, 3, g4, :], start=False, stop=False)
                        ins = e.matmul(o, lhsT=Ps[qi][:, g4, :], rhs=SSw[wi][:, 4, g4, :], start=False, stop=True)
                    return ins
                P.op("pe", f_y, reads=[R_SW[wi], R_U, R_PCS[qi]], writes=[R_PS[py]])
                P.op("act", lambda e: e.activation(
                    out=ZY[:, :, gb * 64:(gb + 1) * 64].rearrange("p t (g c) -> p g t c", g=4),
                    in_=psum[:, py, :].rearrange("p (g t c) -> p g t c", g=4, t=8), func=AF.Gelu_apprx_tanh),
                    reads=[R_PS[py]], acc=[R_ZY])

            def b3_tile(f):
                pi = next_ps()

                def f_tr(e, f=f, pi=pi):
                    ins = None
                    for tau in range(8):
                        ins = e.transpose(out=psB(pi)[:, tau * 128:(tau + 1) * 128], in_=ZY[:, tau, f * 128:(f + 1) * 128], identity=identB[:])
                    return ins
                P.op("pe", f_tr, reads=[R_ZY, R_CONST], writes=[R_PS[pi]])
                copy_op(evac_eng(), ysT[:, f, :].rearrange("p (k t) -> p t k", t=8), psB(pi).rearrange("p (t k) -> p t k", t=8),
                        [R_PS[pi]], [R_YST])

            wg = None
            LAG = 1
            for s_ in range(8 + LAG):
                if s_ < 8:
                    ssm_front(s_)
                run_fillers(2)
                if s_ == 3:
                    wg = load_ws(wbf_glu[l].rearrange("(a p) n -> p a n", p=128), 4, [R_WBF[l]])
                    P.op("sp", lambda e, wg=wg, l=l: e.dma_start(out=WS[wg][:, 2048:3072], in_=poolW_d[l]),
                         reads=[R_POOLW[l]], acc=[R_WS[wg]], dma=f"d_ws{wg}")
                if s_ >= LAG:
                    ssm_back(s_ - LAG)
                    if (s_ - LAG) % 2 == 1:
                        b3_tile((s_ - LAG) // 2)
            run_fillers(len(fillers))

            pwv = WS[wg][:, 2048:3072].rearrange("p (g k n) -> p g k n", g=4, k=2)
            for n in range(2):
                nr = slice(n * 512, (n + 1) * 512)
                for f in range(4):
                    pi = next_ps()

                    def f_pm(e, f=f, n=n, nr=nr, pi=pi, pwv=pwv):
                        e.matmul(psum[:, pi, :], lhsT=pwv[:, f, 0, :], rhs=spool[:, f, nr], start=True, stop=False)
                        return e.matmul(psum[:, pi, :], lhsT=pwv[:, f, 1, :], rhs=upool[:, f, 16 + n * 512:16 + (n + 1) * 512], start=False, stop=True)
                    P.op("pe", f_pm, reads=[R_WS[wg], R_SPOOL, R_UPOOL], writes=[R_PS[pi]])
                    P.op("dve", lambda e, f=f, nr=nr, pi=pi, l=l: e.scalar_tensor_tensor(
                        out=ycat[:, f, nr], in0=psum[:, pi, :], scalar=smallp[:, l, 8 + f:9 + f], in1=gate[:, f, nr], op0=ALU.mult, op1=ALU.mult),
                        reads=[R_PS[pi], R_GATE[n], R_SMALLP], acc=[R_YCAT[n]])
            gluv = WS[wg][:, 0:2048].rearrange("p (a n) -> p a n", n=512)
            wov = wbf_out[l].rearrange("(a p) n -> p a n", p=128)
            wi_o = [load_ws(wov[:, :, so * 512:(so + 1) * 512], 8, [R_WBF[l]]) for so in range(2)]

            def c1_half(n, l=l, gluv=gluv, wg=wg):
                nr = slice(n * 512, (n + 1) * 512)
                for fo in range(4):
                    pi = next_ps()
                    sgi = fo % 2
                    mm_group(pi, lambda kc, fo=fo: gluv[:, kc, fo * 128:(fo + 1) * 128], lambda kc: ysT[:, kc, nr], 4, [R_WS[wg], R_YST])
                    P.op("act", lambda e, fo=fo, pi=pi, sgi=sgi: e.activation(out=sig[:, sgi, :], in_=psum[:, pi, :], func=AF.Sigmoid,
                                                                             bias=smallp[:, l, 12 + fo:13 + fo], scale=1.0),
                         reads=[R_PS[pi], R_SMALLP], writes=[R_SIG[sgi]])
                    P.op("dve", lambda e, fo=fo, sgi=sgi: e.tensor_tensor(out=sig[:, sgi, :], in0=sig[:, sgi, :], in1=ysT[:, fo, nr], op=ALU.mult),
                         reads=[R_SIG[sgi], R_YST], writes=[R_SIG[sgi]])
                    P.op("dve", lambda e, fo=fo, sgi=sgi: e.tensor_tensor(out=ycat[:, 4 + fo, nr], in0=sig[:, sgi, :], in1=gate[:, 4 + fo, nr], op=ALU.mult),
                         reads=[R_SIG[sgi], R_GATE[n]], acc=[R_YCAT[n]])

            lnext = l + 1 if l + 1 < n_layers else None

            def c2_half(n, wi_o=wi_o, lnext=lnext):
                nr = slice(n * 512, (n + 1) * 512)
                for fo in range(8):
                    if n == 1 and fo == 4:
                        norm_rstd(0)
                    wi, f4 = wi_o[fo // 4], fo % 4
                    pi = next_ps()
                    mm_group(pi, lambda kc, wi=wi, f4=f4: wsv(wi)[:, kc, f4 * 128:(f4 + 1) * 128], lambda kc: ycat[:, kc, nr], 8, [R_WS[wi], R_YCAT[n]])
                    P.op("dve", lambda e, fo=fo, pi=pi: e.tensor_tensor(out=xres[:, fo, nr], in0=xres[:, fo, nr], in1=psum[:, pi, :], op=ALU.add),
                         reads=[R_PS[pi], R_XRES[n]], acc=[R_XRES[n]])
                    norm_square(n, fo)
                    if n == 1 and fo >= 4:
                        norm_apply(0, lnext, [2 * (fo - 4), 2 * (fo - 4) + 1])
            c1_half(0)
            c1_half(1)
            c2_half(0)
            c2_half(1)
            norm_rstd(1)
            norm_apply(1, lnext)

        for tt in range(8):
            si = xs_rr[0]
            xs_rr[0] ^= 1
            for fh in range(2):
                pi = next_ps()

                def f_ot(e, tt=tt, fh=fh, pi=pi):
                    ins = None
                    for f4 in range(4):
                        f = fh * 4 + f4
                        ins = e.transpose(out=psum[:, pi, f4 * 128:(f4 + 1) * 128], in_=xres[:, f, tt * 128:(tt + 1) * 128], identity=identF[:])
                    return ins
                P.op("pe", f_ot, reads=[R_XRES[tt // 4], R_CONST], writes=[R_PS[pi]])
                copy_op(evac_eng(), xs[:, si, fh * 512:(fh + 1) * 512], psum[:, pi, :], [R_PS[pi]], [R_XS[si]])
            P.op("sp", lambda e, si=si, tt=tt, seq=seq, t0=t0: e.dma_start(out=out_d[seq, t0 + tt * 128:t0 + (tt + 1) * 128, :], in_=xs[:, si, :]),
                 reads=[R_XS[si]], acc=[R_OUT], dma=f"d_out{si}")

    P.op("sp", None, reads=[R_OUT], sig=False)
    sems = {k: es.enter_context(nc.semaphore(k)) for k in P.cnt}
    with nc.Block() as block:
        P.emit(nc, block, sems)
    es.close()
    return nc


def prep_inputs(inp):
    f = np.float32
    shared = {}
    shared["w_in"] = np.ascontiguousarray(inp["w_in"], dtype=f)
    shared["w_out"] = np.ascontiguousarray(inp["w_out"], dtype=f)
    shared["glu_w"] = np.ascontiguousarray(inp["glu_w"], dtype=f)
    shared["pool_w"] = np.ascontiguousarray(np.transpose(inp["pool_w"], (0, 2, 1, 3)), dtype=f)
    ng = np.transpose(np.asarray(inp["norm_g"], f).reshape(DEPTH, 8, 128), (2, 0, 1))
    psc = np.transpose(np.asarray(inp["pool_scale"], f).reshape(DEPTH, 4, 128), (2, 0, 1))
    gb = np.transpose(np.asarray(inp["glu_b"], f).reshape(DEPTH, 4, 128), (2, 0, 1))
    shared["smallp"] = np.ascontiguousarray(np.concatenate([ng, psc, gb], axis=2), dtype=f)
    shared["finalg"] = np.ascontiguousarray(np.asarray(inp["final_g"], f).reshape(8, 128).T)

    def dup(a):
        t = np.transpose(np.asarray(a, f), (0, 2, 1))
        return np.concatenate([t, t], axis=1)
    are = dup(inp["a_re"])
    aim = dup(inp["a_im"])
    ldt = np.broadcast_to(np.asarray(inp["log_dt"], f)[:, None, :], (DEPTH, 128, G))
    dv = np.asarray(inp["d_skip"], f).reshape(DEPTH, G, 16)
    dvec = np.broadcast_to(np.transpose(dv, (0, 2, 1))[:, None, :, :], (DEPTH, 8, 16, G)).reshape(DEPTH, 128, G)
    shared["ssm_small"] = np.ascontiguousarray(
        np.transpose(np.stack([are, aim, ldt, dvec], axis=2), (1, 2, 0, 3)).reshape(128, 4, LG), dtype=f)

    def dupb(a):
        t = np.transpose(np.asarray(a, f), (0, 2, 1, 3)).reshape(DEPTH, 64, G * 16)
        return np.concatenate([t, t], axis=1)

    def dupc(a):
        t = np.transpose(np.asarray(a, f), (0, 3, 1, 2)).reshape(DEPTH, 64, G * 16)
        return np.concatenate([t, t], axis=1)
    shared["ssm_bc"] = np.ascontiguousarray(
        np.stack([dupb(inp["b_re"]), dupb(inp["b_im"]), dupc(inp["c_re"]), dupc(inp["c_im"])], axis=2), dtype=f)
    x = np.ascontiguousarray(inp["x"], dtype=f)
    maps = []
    for c in range(NCORES):
        m = dict(shared)
        m["x"] = x[c * NSEQ:(c + 1) * NSEQ]
        maps.append(m)
    return maps


def kernel(**inputs):
    maps = prep_inputs(inputs)
    nc = build_program()
    res = run_bass_kernel_spmd(nc, maps, core_ids=list(range(NCORES)))
    out = np.concatenate([np.asarray(r["out"], dtype=np.float32) for r in res.results], axis=0)
    return out
```

```python
import math
from contextlib import ExitStack

import numpy as np
import concourse.bass as bass
import concourse.mybir as mybir
from concourse.bass_utils import run_bass_kernel_spmd

F32 = mybir.dt.float32
BF16 = mybir.dt.bfloat16
I32 = mybir.dt.int32
AF = mybir.ActivationFunctionType
ALU = mybir.AluOpType

NCORES = 8
DEPTH = 4
D = 1024
SEQ = 2048
NSEQ = 2
ST = 1024
NK = ST // 8
G = 32
LG = DEPTH * G
TWO_PI = float(2.0 * math.pi)
INV_2PI = float(1.0 / (2.0 * math.pi))
HALF_PI = float(math.pi / 2.0)
SIN_SCALE = 1.0 - 4e-5
EPS = 1e-5
POOL_WINDOWS = (2, 4, 8, 16)


class Res:
    __slots__ = ("name", "w", "r")

    def __init__(self, name):
        self.name = name
        self.w = {}
        self.r = {}


class Prog:
    ENG = ("pe", "act", "dve", "pool", "sp")

    def __init__(self):
        self.ops = {e: [] for e in self.ENG}
        self.cnt = {}
        self.seen = {e: {} for e in self.ENG}
        self.pending = {e: {} for e in self.ENG}

    def fence(self):
        for e in self.ENG:
            self.pending[e] = dict(self.cnt)

    def op(self, eng, fn, reads=(), writes=(), dma=None, sig=True, acc=()):
        need = self.pending[eng]
        self.pending[eng] = {}
        for r in acc:
            for k, v in r.r.items():
                if need.get(k, 0) < v:
                    need[k] = v
        for r in reads:
            for k, v in r.w.items():
                if need.get(k, 0) < v:
                    need[k] = v
        for r in writes:
            for k, v in r.w.items():
                if need.get(k, 0) < v:
                    need[k] = v
            for k, v in r.r.items():
                if need.get(k, 0) < v:
                    need[k] = v
        waits = []
        seen = self.seen[eng]
        for k, v in need.items():
            if eng == "pe" and k == "pe":
                continue
            if seen.get(k, 0) >= v:
                continue
            seen[k] = v
            waits.append((k, v))
        tok = None
        inc = 1
        if sig:
            key = dma if dma is not None else eng
            inc = 16 if dma is not None else 1
            self.cnt[key] = self.cnt.get(key, 0) + inc
            tok = (key, self.cnt[key])
            for r in reads:
                if r.r.get(key, 0) < tok[1]:
                    r.r[key] = tok[1]
            for r in writes:
                r.w = {key: tok[1]}
            for r in acc:
                if r.w.get(key, 0) < tok[1]:
                    r.w[key] = tok[1]
        self.ops[eng].append((waits, fn, tok, inc))
        return tok

    def emit(self, nc, block, sems):
        def mk(name):
            def body(e):
                for waits, fn, tok, inc in self.ops[name]:
                    for k, v in waits:
                        e.wait_ge(sems[k], v)
                    if fn is None:
                        continue
                    ins = fn(e)
                    if tok is not None:
                        ins.then_inc(sems[tok[0]], inc)
            return body

        block.tensor(mk("pe"))
        block.scalar(mk("act"))
        block.vector(mk("dve"))
        block.gpsimd(mk("pool"))
        block.sync(mk("sp"))


def build_program(debug=False, n_layers=DEPTH, n_sub=4):
    nc = bass.Bass("TRN2", target_bir_lowering=False)
    P = Prog()
    es = ExitStack()

    def dram_in(name, shape, dt=F32):
        return nc.dram_tensor(name, list(shape), dt, kind="ExternalInput").ap()

    def dram_out(name, shape, dt=F32):
        return nc.dram_tensor(name, list(shape), dt, kind="ExternalOutput").ap()

    def dram_scr(name, shape, dt):
        kind = "ExternalOutput" if debug else "Internal"
        return nc.dram_tensor(name, list(shape), dt, kind=kind).ap()

    def sb(name, shape, dt, stack=None):
        return (stack or es).enter_context(nc.sbuf_tensor(name, list(shape), dt))

    x_d = dram_in("x", [NSEQ, SEQ, D])
    w_in_d = dram_in("w_in", [DEPTH, D, 2 * D])
    w_out_d = dram_in("w_out", [DEPTH, D, D])
    glu_w_d = dram_in("glu_w", [DEPTH, 512, 512])
    pool_w_d = dram_in("pool_w", [DEPTH, 128, 4, 128])
    smallp_d = dram_in("smallp", [128, DEPTH, 16])
    finalg_d = dram_in("finalg", [128, 8])
    ssm_small_d = dram_in("ssm_small", [128, 4, LG])
    ssm_bc_d = dram_in("ssm_bc", [DEPTH, 128, 4, 512])
    out_d = dram_out("out", [NSEQ, SEQ, D])

    ssmW_d = dram_scr("ssmW", [DEPTH, 8, 128, 5 * 4 * 128], BF16)
    tabs_d = dram_scr("tabs", [DEPTH, 8, 128, 2, 4, 256], F32)
    poolW_d = dram_scr("poolW", [DEPTH, 128, 4 * 2 * 128], BF16)
    R_SSMW = [Res(f"ssmW{l}") for l in range(DEPTH)]
    R_TABS = [Res(f"tabs{l}") for l in range(DEPTH)]
    R_POOLW = [Res(f"poolW{l}") for l in range(DEPTH)]
    R_OUT = Res("out")

    identF = sb("identF", [128, 128], F32)
    identB = sb("identB", [128, 128], BF16)
    onesB = sb("onesB", [128, 128], BF16)
    Rb_all = sb("Rb_all", [128, DEPTH, 32], F32)
    smallp = sb("smallp_sb", [128, DEPTH, 16], F32)
    finalg = sb("finalg_sb", [128, 8], F32)
    psum = es.enter_context(nc.psum_tensor("psum", [128, 8, 512], F32))
    R_CONST = Res("const")
    R_RB = Res("Rb")
    R_SMALLP = Res("smallp")
    R_PS = [Res(f"ps{i}") for i in range(8)]
    ps_rr = [0]

    def next_ps():
        i = ps_rr[0]
        ps_rr[0] = (i + 1) % 8
        return i

    P.op("pool", lambda e: e.memset(identF[:], 1.0), writes=[R_CONST])
    P.op("pool", lambda e: e.affine_select(out=identF[:], in_=identF[:], pattern=[[-1, 128]],
                                           compare_op=ALU.is_equal, fill=0.0, base=0, channel_multiplier=1),
         reads=[R_CONST], writes=[R_CONST])
    P.op("pool", lambda e: e.tensor_copy(out=identB[:], in_=identF[:]), reads=[R_CONST], writes=[R_CONST])
    P.op("pool", lambda e: e.memset(onesB[:], 1.0 / 1024.0), writes=[R_CONST])
    P.op("sp", lambda e: e.dma_start(out=smallp[:], in_=smallp_d[:]), writes=[R_SMALLP], dma="d_small")
    P.op("sp", lambda e: e.dma_start(out=finalg[:], in_=finalg_d[:]), writes=[R_SMALLP], dma="d_small")

    with ExitStack() as ps_:
        def pb(name, shape, dt):
            return sb("pl_" + name, shape, dt, ps_)
        mask = pb("mask", [128, 128], F32)
        iotaKi = pb("iotaKi", [128, 256], I32)
        iotaK = pb("iotaK", [128, 256], F32)
        JTi = pb("JTi", [128, 7, LG], I32)
        JT = pb("JT", [128, 7, LG], F32)
        sm = pb("sm", [128, 4, LG], F32)
        bcs = [pb(f"bc{i}", [128, 4, 512], F32) for i in range(2)]
        pws = [pb(f"pw{i}", [128, 4, 128], F32) for i in range(2)]
        PWo = pb("PWo", [128, 4, 2, 128], BF16)
        dtt = pb("dtt", [128, LG], F32)
        ard = pb("ard", [128, LG], F32)
        ang = pb("ang", [128, LG], F32)
        EXre = pb("EXre", [128, LG, 8], F32)
        EXim = pb("EXim", [128, LG, 8], F32)
        EYre = pb("EYre", [128, LG, 8], F32)
        EYim = pb("EYim", [128, LG, 8], F32)
        s1 = pb("s1", [128, LG], F32)
        s2 = pb("s2", [128, LG], F32)
        s3 = pb("s3", [128, LG], F32)
        s4 = pb("s4", [128, LG], F32)
        fre = pb("fre", [128, LG], F32)
        fim = pb("fim", [128, LG], F32)
        Rt = pb("Rt", [128, LG], F32)
        TH = pb("TH", [128, LG], F32)
        NI8 = pb("NI8", [128, LG], I32)
        bbA = pb("bbA", [128, 32, 16], F32)
        bbB = pb("bbB", [128, 32, 16], F32)
        cA = pb("cA", [128, 32, 16], F32)
        cB = pb("cB", [128, 32, 16], F32)
        Smat = pb("Smat", [128, 128], F32)
        hpiT = pb("hpiT", [128, 1], F32)
        T1 = pb("T1", [128, 32, 8, 16], F32)
        T2 = pb("T2", [128, 32, 8, 16], F32)

        def jview(t, k):
            return t[:].rearrange("p g t c -> p (g t c)")[:, k * 7 * LG:(k + 1) * 7 * LG].rearrange("p (j n) -> p j n", j=7)
        def bview(t, k):
            return t[:].rearrange("p g t c -> p (g t c)")[:, k * 512:(k + 1) * 512].rearrange("p (g c) -> p g c", c=16)
        ta, tb_ = bview(T1, 0), bview(T1, 1)
        tc_, td = bview(T2, 0), bview(T2, 1)
        ARJ, ANJ, MAGP, MAGN = (jview(T1, k) for k in range(4))
        NIj = JTi[:]
        RRj, SN, CS = (jview(T2, k) for k in range(1, 4))
        XBm = pb("XBm", [128, 32, 8, 16], F32)
        Gm = pb("Gm", [128, 32, 8, 16], F32)
        W5 = pb("W5", [128, 4, 5, 4, 128], BF16)
        tmpT = [pb(f"tmpT{i}", [128, 4, 128], F32) for i in range(2)]
        NTB = 2
        ANGq = [pb(f"ANGq{i}", [128, 4, 256], F32) for i in range(NTB)]
        NIq = [pb(f"NIq{i}", [128, 4, 256], I32) for i in range(NTB)]
        RRq = [pb(f"RRq{i}", [128, 4, 256], F32) for i in range(NTB)]
        ABq = [ANGq[i][:] for i in range(NTB)]
        CSq = [pb(f"CSq{i}", [128, 2, 4, 256], F32) for i in range(NTB)]
        R_TQ = [Res(f"pl_tq{i}") for i in range(NTB)]
        R_CSQ = [Res(f"pl_csq{i}") for i in range(NTB)]
        R_BB = Res("pl_bb")

        R_PC = Res("pl_const")
        R_INS = [Res("pl_in0"), Res("pl_in1")]
        R_PWS = [Res("pl_pw0"), Res("pl_pw1")]
        R_A = Res("pl_a")
        R_E = Res("pl_E")
        R_X = Res("pl_X")
        R_T = Res("pl_T")
        R_W5 = Res("pl_W5")
        R_TMPT = [Res("pl_tmpT0"), Res("pl_tmpT1")]
        R_ANG = Res("pl_ang")
        R_TAB = Res("pl_tab")
        R_PWO = Res("pl_pwo")

        P.op("pool", lambda e: e.memset(mask[:], 1.0), writes=[R_PC])
        P.op("pool", lambda e: e.affine_select(out=mask[:], in_=mask[:], pattern=[[16, 8], [0, 16]],
                                               compare_op=ALU.is_ge, fill=0.0, base=15, channel_multiplier=-1),
             reads=[R_PC], writes=[R_PC])
        P.op("pool", lambda e: e.iota(iotaKi[:], pattern=[[1, 256]], base=0, channel_multiplier=0), writes=[R_PC])
        P.op("pool", lambda e: e.tensor_copy(out=iotaK[:], in_=iotaKi[:]), reads=[R_PC], writes=[R_PC])
        P.op("pool", lambda e: e.iota(JTi[:], pattern=[[-1, 7], [0, LG]], base=7, channel_multiplier=0), writes=[R_PC])
        P.op("pool", lambda e: e.tensor_copy(out=JT[:], in_=JTi[:]), reads=[R_PC], writes=[R_PC])
        P.op("pool", lambda e: e.tensor_copy(out=Smat[:, 0:64], in_=identF[:, 64:128]), reads=[R_CONST], writes=[R_PC])
        P.op("pool", lambda e: e.tensor_scalar(out=Smat[:, 64:128], in0=identF[:, 0:64], scalar1=-1.0, scalar2=None, op0=ALU.mult),
             reads=[R_CONST, R_PC], writes=[R_PC])
        P.op("pool", lambda e: e.memset(hpiT[:], HALF_PI), reads=[R_PC], writes=[R_PC])

        def bc3(ap32):
            return ap32.unsqueeze(1).to_broadcast([128, 7, LG])

        def bcc(ap32):
            return ap32.unsqueeze(2).to_broadcast([128, 32, 16])

        def bE(apE, lo, hi):
            return apE[lo:hi].unsqueeze(3).to_broadcast([hi - lo, 32, 8, 16])

        def bB(apB, lo, hi):
            return apB[lo:hi].unsqueeze(2).to_broadcast([hi - lo, 32, 8, 16])

        def bR(apR, lo, hi):
            return apR[lo:hi].rearrange("p (a g) -> p a g", g=4).unsqueeze(3).to_broadcast([hi - lo, 8, 4, 128])

        wbf_in = dram_scr("wbf_in", [DEPTH, D, 2 * D], BF16)
        wbf_out = dram_scr("wbf_out", [DEPTH, D, D], BF16)
        wbf_glu = dram_scr("wbf_glu", [DEPTH, 512, 512], BF16)
        R_WBF = [Res(f"wbf{l}") for l in range(DEPTH)]
        NCQ = 4
        R_CASTQ = [Res(f"castq{i}") for i in range(NCQ)]
        cq = [0]

        def cast_dma(dst, src, l):
            j = cq[0] % NCQ
            cq[0] += 1
            P.op("pool", lambda e: e.dma_start(out=dst, in_=src), writes=[R_CASTQ[j]], acc=[R_WBF[l]], dma=f"d_cast{j}")
        for l in range(n_layers):
            for a in range(8):
                cast_dma(wbf_in[l, a * 128:(a + 1) * 128, :], w_in_d[l, a * 128:(a + 1) * 128, :], l)
            for a in range(4):
                cast_dma(wbf_out[l, a * 256:(a + 1) * 256, :], w_out_d[l, a * 256:(a + 1) * 256, :], l)
            cast_dma(wbf_glu[l], glu_w_d[l], l)

        R_SM = Res("pl_sm")
        P.op("sp", lambda e: e.dma_start(out=sm[:], in_=ssm_small_d[:]), writes=[R_SM], dma="d_plsm")
        are, aim, ldt = sm[:, 0, :], sm[:, 1, :], sm[:, 2, :]
        P.op("act", lambda e: e.activation(out=dtt[:], in_=ldt, func=AF.Exp), reads=[R_SM], writes=[R_A])
        P.op("dve", lambda e: e.tensor_tensor(out=ard[:], in0=are, in1=dtt[:], op=ALU.mult), reads=[R_SM, R_A], writes=[R_A])
        P.op("dve", lambda e: e.tensor_tensor(out=ang[:], in0=aim, in1=dtt[:], op=ALU.mult), reads=[R_SM, R_A], writes=[R_A])
        P.op("dve", lambda e: e.tensor_tensor(out=ARJ, in0=JT[:], in1=bc3(ard[:]), op=ALU.mult), reads=[R_PC, R_A], writes=[R_A])
        P.op("dve", lambda e: e.tensor_tensor(out=ANJ, in0=JT[:], in1=bc3(ang[:]), op=ALU.mult), reads=[R_PC, R_A], writes=[R_A])
        P.op("act", lambda e: e.activation(out=MAGP, in_=ARJ, func=AF.Exp), reads=[R_A], writes=[R_A])
        P.op("act", lambda e: e.activation(out=MAGN, in_=ARJ, func=AF.Exp, scale=-1.0), reads=[R_A], writes=[R_A])
        P.op("act", lambda e: e.activation(out=NIj, in_=ANJ, func=AF.Copy, scale=INV_2PI), reads=[R_A], writes=[R_A])
        P.op("dve", lambda e: e.scalar_tensor_tensor(out=RRj, in0=NIj, scalar=-TWO_PI, in1=ANJ, op0=ALU.mult, op1=ALU.add),
             reads=[R_A], writes=[R_A])
        P.op("act", lambda e: e.activation(out=SN, in_=RRj, func=AF.Sin, scale=SIN_SCALE), reads=[R_A], writes=[R_A])
        P.op("dve", lambda e: e.tensor_scalar(out=RRj, in0=ANJ, scalar1=HALF_PI, scalar2=None, op0=ALU.add), reads=[R_A], writes=[R_A])
        P.op("act", lambda e: e.activation(out=NIj, in_=RRj, func=AF.Copy, scale=INV_2PI), reads=[R_A], writes=[R_A])
        P.op("dve", lambda e: e.scalar_tensor_tensor(out=RRj, in0=NIj, scalar=-TWO_PI, in1=RRj, op0=ALU.mult, op1=ALU.add),
             reads=[R_A], writes=[R_A])
        P.op("act", lambda e: e.activation(out=CS, in_=RRj, func=AF.Sin, scale=SIN_SCALE), reads=[R_A], writes=[R_A])
        def Ev(t):
            return t[:].rearrange("p g t -> p t g")[:, 0:7, :]
        P.op("dve", lambda e: e.tensor_tensor(out=Ev(EXre), in0=MAGP, in1=CS, op=ALU.mult), reads=[R_A], writes=[R_E])
        P.op("dve", lambda e: e.tensor_tensor(out=Ev(EXim), in0=MAGP, in1=SN, op=ALU.mult), reads=[R_A], writes=[R_E])
        P.op("dve", lambda e: e.tensor_tensor(out=Ev(EYre), in0=MAGN, in1=CS, op=ALU.mult), reads=[R_A], writes=[R_E])
        P.op("dve", lambda e: e.scalar_tensor_tensor(out=Ev(EYim), in0=MAGN, scalar=-1.0, in1=SN, op0=ALU.mult, op1=ALU.mult),
             reads=[R_A], writes=[R_E])
        P.op("dve", lambda e: e.memset(EXre[:, :, 7:8], 1.0), writes=[R_E])
        P.op("dve", lambda e: e.memset(EXim[:, :, 7:8], 0.0), writes=[R_E])
        P.op("dve", lambda e: e.memset(EYre[:, :, 7:8], 1.0), writes=[R_E])
        P.op("dve", lambda e: e.memset(EYim[:, :, 7:8], 0.0), writes=[R_E])
        lre, lim = EXre[:, :, 6], EXim[:, :, 6]
        P.op("dve", lambda e: e.tensor_scalar(out=s1[:], in0=lre, scalar1=-1.0, scalar2=None, op0=ALU.add), reads=[R_E], writes=[R_A])
        P.op("dve", lambda e: e.tensor_tensor(out=s2[:], in0=are, in1=are, op=ALU.mult), reads=[R_SM], writes=[R_A])
        P.op("dve", lambda e: e.tensor_tensor(out=s3[:], in0=aim, in1=aim, op=ALU.mult), reads=[R_SM], writes=[R_A])
        P.op("dve", lambda e: e.tensor_tensor(out=s2[:], in0=s2[:], in1=s3[:], op=ALU.add), reads=[R_A], writes=[R_A])
        P.op("dve", lambda e: e.reciprocal(out=s2[:], in_=s2[:]), reads=[R_A], writes=[R_A])
        P.op("dve", lambda e: e.tensor_tensor(out=s3[:], in0=s1[:], in1=are, op=ALU.mult), reads=[R_A, R_SM], writes=[R_A])
        P.op("dve", lambda e: e.tensor_tensor(out=s4[:], in0=lim, in1=aim, op=ALU.mult), reads=[R_E, R_SM], writes=[R_A])
        P.op("dve", lambda e: e.tensor_tensor(out=s3[:], in0=s3[:], in1=s4[:], op=ALU.add), reads=[R_A], writes=[R_A])
        P.op("dve", lambda e: e.tensor_tensor(out=fre[:], in0=s3[:], in1=s2[:], op=ALU.mult), reads=[R_A], writes=[R_A])
        P.op("dve", lambda e: e.tensor_tensor(out=s3[:], in0=lim, in1=are, op=ALU.mult), reads=[R_E, R_SM], writes=[R_A])
        P.op("dve", lambda e: e.tensor_tensor(out=s4[:], in0=s1[:], in1=aim, op=ALU.mult), reads=[R_A, R_SM], writes=[R_A])
        P.op("dve", lambda e: e.tensor_tensor(out=s3[:], in0=s3[:], in1=s4[:], op=ALU.subtract), reads=[R_A], writes=[R_A])
        P.op("dve", lambda e: e.tensor_tensor(out=fim[:], in0=s3[:], in1=s2[:], op=ALU.mult), reads=[R_A], writes=[R_A])
        P.op("act", lambda e: e.activation(out=Rt[:], in_=ard[:], func=AF.Exp, scale=8.0), reads=[R_A], writes=[R_A])
        P.op("act", lambda e: e.activation(out=Rb_all[:].rearrange("p l g -> p (l g)"), in_=Rt[:], func=AF.Copy), reads=[R_A], writes=[R_RB])
        P.op("dve", lambda e: e.tensor_scalar(out=s1[:], in0=ang[:], scalar1=8.0, scalar2=None, op0=ALU.mult), reads=[R_A], writes=[R_A])
        P.op("act", lambda e: e.activation(out=NI8[:], in_=s1[:], func=AF.Copy, scale=INV_2PI), reads=[R_A], writes=[R_A])
        P.op("dve", lambda e: e.scalar_tensor_tensor(out=TH[:], in0=NI8[:], scalar=-TWO_PI, in1=s1[:], op0=ALU.mult, op1=ALU.add),
             reads=[R_A], writes=[R_A])


        for l in range(n_layers):
            def pl_loads(ll):
                P.op("sp", lambda e: e.dma_start(out=bcs[ll % 2][:], in_=ssm_bc_d[ll]), writes=[R_INS[ll % 2]], dma=f"d_plin{ll % 2}")
                P.op("sp", lambda e: e.dma_start(out=pws[ll % 2][:], in_=pool_w_d[ll]), writes=[R_PWS[ll % 2]], dma=f"d_plpw{ll % 2}")
            if l == 0:
                pl_loads(0)
            if l + 1 < n_layers:
                pl_loads(l + 1)
            bc, pw, R_IN, R_PW = bcs[l % 2], pws[l % 2], R_INS[l % 2], R_PWS[l % 2]
            ls = slice(l * 32, (l + 1) * 32)
            fre_l, fim_l = fre[:, ls], fim[:, ls]
            EXre_l, EXim_l, EYre_l, EYim_l = EXre[:, ls, :], EXim[:, ls, :], EYre[:, ls, :], EYim[:, ls, :]
            bre = bc[:, 0, :].rearrange("p (g c) -> p g c", c=16)
            bim = bc[:, 1, :].rearrange("p (g c) -> p g c", c=16)
            cre = bc[:, 2, :].rearrange("p (g c) -> p g c", c=16)
            cim = bc[:, 3, :].rearrange("p (g c) -> p g c", c=16)

            for g4 in range(4):
                P.op("act", lambda e, g4=g4, pw=pw: e.activation(out=PWo[:, g4, 0, :], in_=pw[:, g4, :], func=AF.Copy, scale=1.0 / POOL_WINDOWS[g4]),
                     reads=[R_PW], acc=[R_PWO])
            P.op("act", lambda e, pw=pw: e.activation(out=PWo[:, :, 1, :], in_=pw[:, :, :], func=AF.Copy, scale=-1.0), reads=[R_PW], acc=[R_PWO])
            P.op("sp", lambda e, l=l: e.dma_start(out=poolW_d[l], in_=PWo[:].rearrange("p a b c -> p (a b c)")),
                 reads=[R_PWO], writes=[R_POOLW[l]], dma=f"d_plpo{l}")

            P.op("dve", lambda e, fre_l=fre_l, fim_l=fim_l, bre=bre, bim=bim: e.tensor_tensor(out=ta, in0=bre, in1=bcc(fre_l), op=ALU.mult), reads=[R_A, R_IN], writes=[R_BB], acc=[R_T])
            P.op("dve", lambda e, fre_l=fre_l, fim_l=fim_l, bre=bre, bim=bim: e.tensor_tensor(out=tb_, in0=bim, in1=bcc(fim_l), op=ALU.mult), reads=[R_A, R_IN], acc=[R_BB])
            P.op("dve", lambda e, fre_l=fre_l, fim_l=fim_l, bre=bre, bim=bim: e.tensor_tensor(out=tc_, in0=bim, in1=bcc(fre_l), op=ALU.mult), reads=[R_A, R_IN], acc=[R_BB])
            P.op("dve", lambda e, fre_l=fre_l, fim_l=fim_l, bre=bre, bim=bim: e.tensor_tensor(out=td, in0=bre, in1=bcc(fim_l), op=ALU.mult), reads=[R_A, R_IN], acc=[R_BB])
            P.op("dve", lambda e: e.tensor_tensor(out=bbA[0:64], in0=ta[0:64], in1=tb_[0:64], op=ALU.subtract), reads=[R_BB], acc=[R_BB])
            P.op("dve", lambda e: e.tensor_tensor(out=bbA[64:128], in0=tc_[64:128], in1=td[64:128], op=ALU.add), reads=[R_BB], acc=[R_BB])
            P.op("dve", lambda e: e.scalar_tensor_tensor(out=bbB[0:64], in0=tc_[0:64], scalar=-1.0, in1=td[0:64], op0=ALU.mult, op1=ALU.subtract),
                 reads=[R_BB], acc=[R_BB])
            P.op("dve", lambda e: e.tensor_tensor(out=bbB[64:128], in0=ta[64:128], in1=tb_[64:128], op=ALU.subtract), reads=[R_BB], acc=[R_BB])
            P.op("act", lambda e, cre=cre, cim=cim: e.activation(out=cA[0:64], in_=cre[0:64], func=AF.Copy), reads=[R_IN], acc=[R_BB])
            P.op("act", lambda e, cre=cre, cim=cim: e.activation(out=cA[64:128], in_=cim[64:128], func=AF.Copy, scale=-1.0), reads=[R_IN], acc=[R_BB])
            P.op("act", lambda e, cre=cre, cim=cim: e.activation(out=cB[0:64], in_=cim[0:64], func=AF.Copy, scale=-1.0), reads=[R_IN], acc=[R_BB])
            P.op("act", lambda e, cre=cre, cim=cim: e.activation(out=cB[64:128], in_=cre[64:128], func=AF.Copy, scale=-1.0), reads=[R_IN], acc=[R_BB])
            tv = tabs_d[l].rearrange("a p two g k -> p a two g k")
            tq_rr = [0]

            def table_front(gb, l=l):
                i = tq_rr[0]
                tq_rr[0] = (i + 1) % NTB
                gs = slice(gb * 4, gb * 4 + 4)
                P.op("dve", lambda e: e.tensor_tensor(out=ANGq[i][:], in0=TH[:, l * 32 + gb * 4:l * 32 + gb * 4 + 4].unsqueeze(2).to_broadcast([128, 4, 256]),
                                                     in1=iotaK[:].unsqueeze(1).to_broadcast([128, 4, 256]), op=ALU.mult),
                     reads=[R_A, R_PC], writes=[R_TQ[i]])
                P.op("act", lambda e: e.activation(out=NIq[i][:], in_=ANGq[i][:], func=AF.Copy, scale=INV_2PI), reads=[R_TQ[i]], acc=[R_TQ[i]])
                return i

            def table_rest(gb, i, l=l, tv=tv):
                P.op("dve", lambda e: e.scalar_tensor_tensor(out=RRq[i][:], in0=NIq[i][:], scalar=-TWO_PI, in1=ANGq[i][:], op0=ALU.mult, op1=ALU.add),
                     reads=[R_TQ[i]], acc=[R_TQ[i]])
                P.op("act", lambda e: e.activation(out=ABq[i], in_=RRq[i][:], func=AF.Abs), reads=[R_TQ[i]], acc=[R_TQ[i]])
                P.op("act", lambda e: e.activation(out=CSq[i][:, 1], in_=RRq[i][:], func=AF.Sin, scale=SIN_SCALE), reads=[R_TQ[i]], writes=[R_CSQ[i]])
                P.op("act", lambda e: e.activation(out=CSq[i][:, 0], in_=ABq[i], func=AF.Sin, scale=-1.0, bias=hpiT[:]),
                     reads=[R_TQ[i], R_PC], acc=[R_CSQ[i]])
                P.op("sp", lambda e: e.dma_start(out=tv[:, gb], in_=CSq[i][:]), reads=[R_CSQ[i]], acc=[R_TABS[l]], dma=f"d_csq{i}")

            bigops = [
                lambda: P.op("dve", lambda e, EXre_l=EXre_l, EXim_l=EXim_l, EYre_l=EYre_l, EYim_l=EYim_l: e.tensor_tensor(out=T1[:], in0=bE(EXre_l, 0, 128), in1=bB(bbA, 0, 128), op=ALU.mult), reads=[R_E, R_BB], writes=[R_T]),
                lambda: P.op("dve", lambda e, EXre_l=EXre_l, EXim_l=EXim_l, EYre_l=EYre_l, EYim_l=EYim_l: e.tensor_tensor(out=T2[:], in0=bE(EXim_l, 0, 128), in1=bB(bbB, 0, 128), op=ALU.mult), reads=[R_E, R_BB], acc=[R_T]),
                lambda: P.op("dve", lambda e: e.tensor_tensor(out=XBm[:], in0=T1[:], in1=T2[:], op=ALU.add), reads=[R_T], writes=[R_X]),
                lambda: P.op("dve", lambda e, EXre_l=EXre_l, EXim_l=EXim_l, EYre_l=EYre_l, EYim_l=EYim_l: e.tensor_tensor(out=T1[:], in0=bE(EYre_l, 0, 128), in1=bB(cA, 0, 128), op=ALU.mult), reads=[R_E, R_BB], writes=[R_T]),
                lambda: P.op("dve", lambda e, EXre_l=EXre_l, EXim_l=EXim_l, EYre_l=EYre_l, EYim_l=EYim_l: e.tensor_tensor(out=T2[:], in0=bE(EYim_l, 0, 128), in1=bB(cB, 0, 128), op=ALU.mult), reads=[R_E, R_BB], acc=[R_T]),
                lambda: P.op("dve", lambda e: e.tensor_tensor(out=Gm[:], in0=T1[:], in1=T2[:], op=ALU.add), reads=[R_T], acc=[R_X]),
            ]
            for i_, bo in enumerate(bigops):
                ti_ = table_front(i_)
                bo()
                table_rest(i_, ti_)

            for gb in range(8):
                if gb in (2, 5):
                    tp_gb = 6 + (gb == 5)
                    tp_i = table_front(tp_gb)
                ti = gb % 2
                pi = next_ps()

                def f_toep(e, gb=gb, pi=pi):
                    ins = None
                    for g4 in range(4):
                        g = gb * 4 + g4
                        ins = e.matmul(psum[:, pi, g4 * 128:(g4 + 1) * 128],
                                       lhsT=XBm[:, g].rearrange("p t c -> p (t c)"),
                                       rhs=Gm[:, g].rearrange("p t c -> p (t c)"), start=True, stop=True)
                    return ins
                P.op("pe", f_toep, reads=[R_X], writes=[R_PS[pi]])
                P.op("dve", lambda e, pi=pi, ti=ti: e.tensor_tensor(out=tmpT[ti][:], in0=psum[:, pi, :].rearrange("p (g n) -> p g n", g=4),
                                                                   in1=mask[:].unsqueeze(1).to_broadcast([128, 4, 128]), op=ALU.mult),
                     reads=[R_PS[pi], R_PC], writes=[R_TMPT[ti]])
                for g4 in range(4):
                    P.op("dve", lambda e, gb=gb, g4=g4, ti=ti, l=l: e.scalar_tensor_tensor(
                        out=W5[:, gb % 4, 0, g4, :], in0=identF[:], scalar=sm[:, 3, l * 32 + gb * 4 + g4:l * 32 + gb * 4 + g4 + 1], in1=tmpT[ti][:, g4, :],
                        op0=ALU.mult, op1=ALU.add), reads=[R_TMPT[ti], R_CONST, R_SM], acc=[R_W5])
                if gb in (2, 5):
                    table_rest(tp_gb, tp_i)
                pi2 = next_ps()

                def f_tr(e, gb=gb, pi2=pi2):
                    ins = None
                    for g4 in range(4):
                        g = gb * 4 + g4
                        ins = e.transpose(out=psum[:, pi2, g4 * 128:(g4 + 1) * 128],
                                          in_=XBm[:, g].rearrange("p t c -> p (t c)"), identity=identF[:])
                    return ins
                P.op("pe", f_tr, reads=[R_X, R_CONST], writes=[R_PS[pi2]])
                pv = psum[:, pi2, :].rearrange("p (g n) -> p g n", g=4)
                P.op("act", lambda e, gb=gb, pv=pv: e.activation(out=W5[:, gb % 4, 1, :, :], in_=pv, func=AF.Copy),
                     reads=[R_PS[pi2]], acc=[R_W5])
                P.op("act", lambda e, gb=gb, pv=pv: e.activation(out=W5[:, gb % 4, 2, :, 0:64], in_=pv[:, :, 64:128], func=AF.Copy),
                     reads=[R_PS[pi2]], acc=[R_W5])
                P.op("act", lambda e, gb=gb, pv=pv: e.activation(out=W5[:, gb % 4, 2, :, 64:128], in_=pv[:, :, 0:64], func=AF.Copy, scale=-1.0),
                     reads=[R_PS[pi2]], acc=[R_W5])
                pi3 = next_ps()

                def f_sw(e, gb=gb, pi3=pi3):
                    ins = None
                    for g4 in range(4):
                        g = gb * 4 + g4
                        ins = e.matmul(psum[:, pi3, g4 * 128:(g4 + 1) * 128], lhsT=Smat[:],
                                       rhs=Gm[:, g].rearrange("p t c -> p (t c)"), start=True, stop=True)
                    return ins
                P.op("pe", f_sw, reads=[R_X, R_PC], writes=[R_PS[pi3]])
                for g4 in range(4):
                    g = gb * 4 + g4
                    P.op("dve", lambda e, gb=gb, g4=g4, g=g, l=l: e.tensor_scalar(out=W5[:, gb % 4, 3, g4, :], in0=Gm[:, g].rearrange("p t c -> p (t c)"),
                                                                            scalar1=Rt[:, l * 32 + g:l * 32 + g + 1], scalar2=None, op0=ALU.mult),
                         reads=[R_X, R_A], acc=[R_W5])
                    P.op("act", lambda e, gb=gb, g4=g4, g=g, pi3=pi3, l=l: e.activation(out=W5[:, gb % 4, 4, g4, :], in_=psum[:, pi3, g4 * 128:(g4 + 1) * 128],
                                                                                  func=AF.Copy, scale=Rt[:, l * 32 + g:l * 32 + g + 1]),
                         reads=[R_PS[pi3], R_A], acc=[R_W5])
                if gb % 4 == 3:
                    hb = gb // 4
                    P.op("sp", lambda e, l=l, hb=hb: e.dma_start(out=ssmW_d[l, hb * 4:(hb + 1) * 4].rearrange("a p n -> p a n"),
                                                             in_=W5[:].rearrange("p a k g n -> p a (k g n)")),
                         reads=[R_W5], acc=[R_SSMW[l]], dma=f"d_plssm{l}_{hb}")


    if debug == "prologue":
        rb_d = dram_out("rb_dbg", [128, DEPTH, 32])
        P.op("sp", lambda e: e.dma_start(out=rb_d[:], in_=Rb_all[:]), reads=[R_RB], writes=[R_OUT], dma="d_out")
        fin = [R_OUT] + R_SSMW[:n_layers] + R_TABS[:n_layers] + R_POOLW[:n_layers] + R_WBF[:n_layers]
        P.op("sp", None, reads=fin, sig=False)
        sems = {k: es.enter_context(nc.semaphore(k)) for k in P.cnt}
        with nc.Block() as block:
            P.emit(nc, block, sems)
        es.close()
        return nc

    P.fence()
    xres = sb("xres", [128, 8, ST], F32)
    h = sb("h", [128, 8, ST], BF16)
    gate = sb("gate", [128, 8, ST], BF16)
    ycat = sb("ycat", [128, 8, ST], BF16)
    upool = sb("upool", [128, 4, 16 + ST], BF16)
    spool = sb("spool", [128, 4, ST], BF16)
    ysT = sb("ysT", [128, 4, ST], BF16)
    xsq0 = spool[:].rearrange("p a n -> p (a n)").rearrange("p (a n) -> p a n", n=512)
    xsq1 = upool[:].rearrange("p a n -> p (a n)")[:, 0:4096].rearrange("p (a n) -> p a n", n=512)
    xsqs = [xsq0, xsq1]
    ZYf = sb("ZY", [128, 4096], BF16)
    ZY = ZYf[:].rearrange("p (t f) -> p t f", t=8)
    Zs = ZYf[:].rearrange("p (g t c) -> p g t c", g=32, t=8)
    U = sb("U", [128, 32, 128], BF16)
    rs = sb("rs", [128, 2, 512], F32)
    t1 = sb("t1", [128, 4, 128], F32)
    t2 = sb("t2", [128, 4, 128], F32)
    NQ = 2
    Q = [sb(f"Q{i}", [128, 4, 129], F32) for i in range(NQ)]
    Pc = [sb(f"Pc{i}", [128, 4, 128], BF16) for i in range(NQ)]
    Ps = [sb(f"Ps{i}", [128, 4, 128], BF16) for i in range(NQ)]
    sig = sb("sig", [128, 2, 512], BF16)
    xs = sb("xs", [128, 2, D], F32)
    pwk = xs[:].rearrange("p a n -> p (a n)").bitcast(BF16)[:, 0:3 * (16 + ST)].rearrange("p (a n) -> p a n", a=3)
    carry = sb("carry", [128, DEPTH, 32], F32)
    halo = sb("halo", [128, DEPTH, 4, 16], BF16)
    cfix = sb("cfix", [128, 4, 16], F32)
    epsT = sb("epsT", [128, 1], F32)
    NWS, NSW, NSTB = 4, 3, 2
    WS = [sb(f"WS{i}", [128, 8 * 512], BF16) for i in range(NWS)]
    SSw = [sb(f"SSw{i}", [128, 5, 4, 128], BF16) for i in range(NSW)]
    SSt = [sb(f"SSt{i}", [128, 2, 4, 128], F32) for i in range(NSTB)]
    R_WS = [Res(f"WS{i}") for i in range(NWS)]
    R_SW = [Res(f"SW{i}") for i in range(NSW)]
    R_STB = [Res(f"STB{i}") for i in range(NSTB)]
    R_ZY, R_U = Res("ZY"), Res("U")
    R_XRES = [Res("xres0"), Res("xres1")]
    R_RS = [Res("rs0"), Res("rs1")]
    R_H = [Res("h0"), Res("h1")]
    R_GATE = [Res("g0"), Res("g1")]
    R_YCAT = [Res("yc0"), Res("yc1")]
    R_UPOOL, R_SPOOL, R_YST = Res("upool"), Res("spool"), Res("ysT")
    R_T1, R_T2 = Res("t1"), Res("t2")
    R_Q = [Res(f"Q{i}") for i in range(NQ)]
    R_PCS = [Res(f"PcPs{i}") for i in range(NQ)]
    R_SIG = [Res("sig0"), Res("sig1")]
    R_XS = [Res("xs0"), Res("xs1")]
    R_CARRY = [[Res(f"carry{l}_{gb}") for gb in range(8)] for l in range(DEPTH)]
    R_HALO = [Res(f"halo{l}") for l in range(DEPTH)]
    R_C2 = Res("const2")
    ws_rr, sw_rr, stb_rr, xs_rr, ev_rr = [0], [0], [0], [0], [0]

    def psB(pi):
        return psum[:, pi, :].bitcast(BF16)

    def evac_eng():
        ev_rr[0] ^= 1
        return "act" if ev_rr[0] else "dve"

    def copy_op(eng, out, in_, reads, acc):
        if eng == "act":
            P.op("act", lambda e: e.activation(out=out, in_=in_, func=AF.Copy), reads=reads, acc=acc)
        else:
            P.op(eng, lambda e: e.tensor_copy(out=out, in_=in_), reads=reads, acc=acc)

    P.op("pool", lambda e: e.memset(epsT[:], EPS), writes=[R_C2])
    P.op("pool", lambda e: e.memset(cfix[:], 1.0), writes=[R_C2])
    for f in range(4):
        w = POOL_WINDOWS[f]
        for t in range(w - 1):
            P.op("pool", lambda e, f=f, t=t, w=w: e.memset(cfix[:, f, t:t + 1], float(w) / float(t + 1)), writes=[R_C2])

    def load_ws(src_ap, nparts, reads):
        i = ws_rr[0]
        ws_rr[0] = (i + 1) % NWS
        dst = WS[i][:, 0:nparts * 512].rearrange("p (a n) -> p a n", n=512)
        P.op("sp", lambda e: e.dma_start(out=dst, in_=src_ap), reads=reads, writes=[R_WS[i]], dma=f"d_ws{i}")
        return i

    def wsv(i):
        return WS[i][:].rearrange("p (a n) -> p a n", n=512)

    def mm_group(pi, lhs_fn, rhs_fn, nk, reads):
        def f(e):
            ins = None
            for kc in range(nk):
                ins = e.matmul(psum[:, pi, :], lhsT=lhs_fn(kc), rhs=rhs_fn(kc), start=(kc == 0), stop=(kc == nk - 1))
            return ins
        P.op("pe", f, reads=reads, writes=[R_PS[pi]])

    def R_XSQ(n):
        return R_SPOOL if n == 0 else R_UPOOL

    def norm_square(n, f=None):
        nr = slice(n * 512, (n + 1) * 512)
        if f is None:
            P.op("act", lambda e: e.activation(out=xsqs[n], in_=xres[:, :, nr], func=AF.Square), reads=[R_XRES[n]], writes=[R_XSQ(n)])
        elif f == 0:
            P.op("act", lambda e: e.activation(out=xsqs[n][:, 0, :], in_=xres[:, 0, nr], func=AF.Square), reads=[R_XRES[n]], writes=[R_XSQ(n)])
        else:
            P.op("act", lambda e: e.activation(out=xsqs[n][:, f, :], in_=xres[:, f, nr], func=AF.Square), reads=[R_XRES[n]], acc=[R_XSQ(n)])

    def norm_rstd(n):
        pi = next_ps()
        mm_group(pi, lambda kc: onesB[:], lambda kc: xsqs[n][:, kc, :], 8, [R_XSQ(n), R_CONST])
        P.op("act", lambda e, pi=pi: e.activation(out=rs[:, n, :], in_=psum[:, pi, :], func=AF.Ln, bias=epsT[:], scale=1.0),
             reads=[R_PS[pi], R_C2], writes=[R_RS[n]])
        P.op("act", lambda e: e.activation(out=rs[:, n, :], in_=rs[:, n, :], func=AF.Exp, scale=-0.5), reads=[R_RS[n]], writes=[R_RS[n]])

    def norm_apply(n, lnext, fs=range(8)):
        nr = slice(n * 512, (n + 1) * 512)
        for f in fs:
            if lnext is None:
                P.op("dve", lambda e, f=f: e.scalar_tensor_tensor(
                    out=xres[:, f, nr], in0=xres[:, f, nr], scalar=finalg[:, f:f + 1], in1=rs[:, n, :], op0=ALU.mult, op1=ALU.mult),
                    reads=[R_RS[n], R_SMALLP, R_XRES[n]], acc=[R_XRES[n]])
            else:
                P.op("dve", lambda e, f=f: e.scalar_tensor_tensor(
                    out=h[:, f, nr], in0=xres[:, f, nr], scalar=smallp[:, lnext, f:f + 1], in1=rs[:, n, :], op0=ALU.mult, op1=ALU.mult),
                    reads=[R_XRES[n], R_RS[n], R_SMALLP], acc=[R_H[n]])

    for st in range(n_sub):
        seq, half = st // 2, st % 2
        t0 = half * ST
        if half == 0:
            P.op("pool", lambda e: e.memset(carry[:], 0.0), writes=[r for rl in R_CARRY for r in rl])
            P.op("pool", lambda e: e.memset(halo[:], 0.0), writes=R_HALO)
        for tt in range(8):
            si = xs_rr[0]
            xs_rr[0] ^= 1
            P.op("sp", lambda e, si=si, tt=tt, seq=seq, t0=t0: e.dma_start(out=xs[:, si, :], in_=x_d[seq, t0 + tt * 128:t0 + (tt + 1) * 128, :]),
                 writes=[R_XS[si]], dma=f"d_xs{si}")
            for fh in range(2):
                pi = next_ps()

                def f_xt(e, si=si, fh=fh, pi=pi):
                    ins = None
                    for f4 in range(4):
                        f = fh * 4 + f4
                        ins = e.transpose(out=psum[:, pi, f4 * 128:(f4 + 1) * 128], in_=xs[:, si, f * 128:(f + 1) * 128], identity=identF[:])
                    return ins
                P.op("pe", f_xt, reads=[R_XS[si], R_CONST], writes=[R_PS[pi]])
                copy_op(evac_eng(), xres[:, fh * 4:(fh + 1) * 4, tt * 128:(tt + 1) * 128],
                        psum[:, pi, :].rearrange("p (f n) -> p f n", f=4), [R_PS[pi]], [R_XRES[tt // 4]])

        for l in range(n_layers):
            k0 = half * NK
            win = wbf_in[l].rearrange("(a p) n -> p a n", p=128)
            if l == 0:
                for n in range(2):
                    norm_square(n)
                    norm_rstd(n)
                    norm_apply(n, 0)
            wi_g = [load_ws(win[:, :, 1024 + sg * 512:1024 + (sg + 1) * 512], 8, [R_WBF[l]]) for sg in range(2)]

            def gate_group(fo, n, wi_g=wi_g):
                nr = slice(n * 512, (n + 1) * 512)
                wi, f4 = wi_g[fo // 4], fo % 4
                pi = next_ps()
                mm_group(pi, lambda kc: wsv(wi)[:, kc, f4 * 128:(f4 + 1) * 128], lambda kc: h[:, kc, nr], 8, [R_WS[wi], R_H[n]])
                P.op("act", lambda e: e.activation(out=gate[:, fo, nr], in_=psum[:, pi, :], func=AF.Silu), reads=[R_PS[pi]], acc=[R_GATE[n]])
            if l > 0:
                for fo in range(4):
                    gate_group(fo, 0)
            wi_s = load_ws(win[:, :, 512:1024], 8, [R_WBF[l]])
            for tau in range(8):
                pi = next_ps()
                mm_group(pi, lambda kc, tau=tau: h[:, kc, :].rearrange("p (k t) -> p t k", t=8)[:, tau, :],
                         lambda kc, wi_s=wi_s: wsv(wi_s)[:, kc, :], 8, [R_WS[wi_s], R_H[0], R_H[1]])
                copy_op(evac_eng(), Zs[:, :, tau, :], psum[:, pi, :].rearrange("p (g c) -> p g c", c=16), [R_PS[pi]], [R_ZY])
            for gq in range(4):
                pi = next_ps()

                def f_tr(e, gq=gq, pi=pi):
                    ins = None
                    for g8 in range(8):
                        g = gq * 8 + g8
                        ins = e.transpose(out=psB(pi)[:, g8 * 128:(g8 + 1) * 128], in_=Zs[:, g].rearrange("p t c -> p (t c)"), identity=identB[:])
                    return ins
                P.op("pe", f_tr, reads=[R_ZY, R_CONST], writes=[R_PS[pi]])
                copy_op(evac_eng(), U[:, gq * 8:(gq + 1) * 8, :], psB(pi).rearrange("p (g k) -> p g k", g=8), [R_PS[pi]], [R_U])

            wi_p = load_ws(win[:, :, 0:512], 8, [R_WBF[l]])
            fillers = []

            def pool_in_group(f, n, wi_p=wi_p, l=l):
                nr = slice(n * 512, (n + 1) * 512)
                pi = next_ps()
                mm_group(pi, lambda kc: wsv(wi_p)[:, kc, f * 128:(f + 1) * 128], lambda kc: h[:, kc, nr], 8, [R_WS[wi_p], R_H[n]])
                P.op("act", lambda e: e.activation(out=upool[:, f, 16 + n * 512:16 + (n + 1) * 512], in_=psum[:, pi, :], func=AF.Copy),
                     reads=[R_PS[pi]], acc=[R_UPOOL])

            def pool_sums(l=l, half=half):
                LT = 16 + ST
                R_PWK = R_XS
                P.op("dve", lambda e: e.tensor_copy(out=halo[:, l, :, :], in_=upool[:, :, ST:ST + 16]), reads=[R_UPOOL], writes=[R_HALO[l]])
                P.op("dve", lambda e: e.tensor_tensor(out=spool[:, 0, :], in0=upool[:, 0, 16:LT], in1=upool[:, 0, 15:LT - 1], op=ALU.add),
                     reads=[R_UPOOL], writes=[R_SPOOL])
                for f in (1, 2, 3):
                    P.op("dve", lambda e, f=f: e.tensor_tensor(out=pwk[:, 0, 1:LT], in0=upool[:, f, 1:LT], in1=upool[:, f, 0:LT - 1], op=ALU.add),
                         reads=[R_UPOOL], writes=R_PWK)
                    if f == 1:
                        P.op("dve", lambda e: e.tensor_tensor(out=spool[:, 1, :], in0=pwk[:, 0, 16:LT], in1=pwk[:, 0, 14:LT - 2], op=ALU.add),
                             reads=R_PWK, acc=[R_SPOOL])
                        continue
                    P.op("dve", lambda e: e.tensor_tensor(out=pwk[:, 1, 3:LT], in0=pwk[:, 0, 3:LT], in1=pwk[:, 0, 1:LT - 2], op=ALU.add),
                         reads=R_PWK, writes=R_PWK)
                    if f == 2:
                        P.op("dve", lambda e: e.tensor_tensor(out=spool[:, 2, :], in0=pwk[:, 1, 16:LT], in1=pwk[:, 1, 12:LT - 4], op=ALU.add),
                             reads=R_PWK, acc=[R_SPOOL])
                        continue
                    P.op("dve", lambda e: e.tensor_tensor(out=pwk[:, 2, 7:LT], in0=pwk[:, 1, 7:LT], in1=pwk[:, 1, 3:LT - 4], op=ALU.add),
                         reads=R_PWK, writes=R_PWK)
                    P.op("dve", lambda e: e.tensor_tensor(out=spool[:, 3, :], in0=pwk[:, 2, 16:LT], in1=pwk[:, 2, 8:LT - 8], op=ALU.add),
                         reads=R_PWK, acc=[R_SPOOL])
                if half == 0:
                    P.op("dve", lambda e: e.tensor_tensor(out=spool[:, :, 0:16], in0=spool[:, :, 0:16], in1=cfix[:], op=ALU.mult),
                         reads=[R_C2], writes=[R_SPOOL])

            P.op("dve", lambda e, l=l: e.tensor_copy(out=upool[:, :, 0:16], in_=halo[:, l, :, :]), reads=[R_HALO[l]], writes=[R_UPOOL])
            for fo in range(4):
                for n in range(2):
                    if l > 0 and n == 0:
                        continue
                    fillers.append(lambda fo=fo, n=n: gate_group(fo, n))
            for f in range(4):
                for n in range(2):
                    fillers.append(lambda f=f, n=n: pool_in_group(f, n))
            fillers.append(pool_sums)
            for fo in range(4, 8):
                for n in range(2):
                    fillers.append(lambda fo=fo, n=n: gate_group(fo, n))
            fillers.reverse()

            def run_fillers(k):
                for _ in range(k):
                    if fillers:
                        fillers.pop()()

            slots = {}

            def ssm_front(gb, l=l, k0=k0):
                wi = sw_rr[0]
                sw_rr[0] = (wi + 1) % NSW
                ti = stb_rr[0]
                stb_rr[0] = (ti + 1) % NSTB
                qi = gb % NQ
                slots[gb] = (wi, qi)
                P.op("sp", lambda e: e.dma_start(out=SSw[wi][:].rearrange("p a g n -> p (a g n)"), in_=ssmW_d[l, gb]),
                     reads=[R_SSMW[l]], writes=[R_SW[wi]], dma=f"d_sw{wi}")
                P.op("sp", lambda e: e.dma_start(out=SSt[ti][:], in_=tabs_d[l, gb][:, :, :, k0:k0 + NK]),
                     reads=[R_TABS[l]], writes=[R_STB[ti]], dma=f"d_stb{ti}")
                pa, pb_ = next_ps(), next_ps()

                def f_v(e):
                    ins = None
                    for kind, pi in ((1, pa), (2, pb_)):
                        for g4 in range(4):
                            ins = e.matmul(psum[:, pi, g4 * 128:(g4 + 1) * 128], lhsT=SSw[wi][:, kind, g4, :], rhs=U[:, gb * 4 + g4, :],
                                           start=True, stop=True)
                    return ins
                P.op("pe", f_v, reads=[R_SW[wi], R_U], writes=[R_PS[pa], R_PS[pb_]])

                def pv(pi):
                    return psum[:, pi, :].rearrange("p (g k) -> p g k", g=4)
                P.op("dve", lambda e: e.tensor_tensor(out=t1[:], in0=pv(pa), in1=SSt[ti][:, 0], op=ALU.mult),
                     reads=[R_PS[pa], R_STB[ti]], writes=[R_T1])
                P.op("dve", lambda e: e.tensor_tensor(out=t2[:], in0=pv(pb_), in1=SSt[ti][:, 1], op=ALU.mult),
                     reads=[R_PS[pb_], R_STB[ti]], writes=[R_T2])
                P.op("dve", lambda e: e.tensor_tensor(out=t1[:], in0=t1[:], in1=t2[:], op=ALU.add), reads=[R_T1, R_T2], writes=[R_T1])
                P.op("dve", lambda e: e.tensor_copy(out=Q[qi][:, :, 0], in_=carry[:, l, gb * 4:(gb + 1) * 4]),
                     reads=[R_CARRY[l][gb]], writes=[R_Q[qi]])
                for g4 in range(4):
                    g = gb * 4 + g4
                    P.op("dve", lambda e, g4=g4, g=g: e.tensor_tensor_scan(
                        out=Q[qi][:, g4, 1:129], data0=Rb_all[:, l, g:g + 1].to_broadcast([128, 128]), data1=t1[:, g4, :],
                        initial=carry[:, l, g:g + 1], op0=ALU.mult, op1=ALU.add),
                        reads=[R_T1, R_RB, R_CARRY[l][gb]], acc=[R_Q[qi]])
                P.op("dve", lambda e: e.tensor_copy(out=carry[:, l, gb * 4:(gb + 1) * 4], in_=Q[qi][:, :, 128]),
                     reads=[R_Q[qi]], writes=[R_CARRY[l][gb]])
                P.op("dve", lambda e: e.tensor_tensor(out=Pc[qi][:], in0=Q[qi][:, :, 0:128], in1=SSt[ti][:, 0], op=ALU.mult),
                     reads=[R_Q[qi], R_STB[ti]], writes=[R_PCS[qi]])
                P.op("dve", lambda e: e.tensor_tensor(out=Ps[qi][:], in0=Q[qi][:, :, 0:128], in1=SSt[ti][:, 1], op=ALU.mult),
                     reads=[R_Q[qi], R_STB[ti]], acc=[R_PCS[qi]])

            def ssm_back(gb):
                wi, qi = slots[gb]
                py = next_ps()

                def f_y(e):
                    ins = None
                    for g4 in range(4):
                        o = psum[:, py, g4 * 128:(g4 + 1) * 128]
                        e.matmul(o, lhsT=U[:, gb * 4 + g4, :], rhs=SSw[wi][:, 0, g4, :], start=True, stop=False)
                        e.matmul(o, lhsT=Pc[qi][:, g4, :], rhs=SSw[wi][:, 3, g4, :], start=False, stop=False)
                        ins = e.matmul(o, lhsT=Ps[qi][:, g4, :], rhs=SSw[wi][:, 4, g4, :], start=False, stop=True)
                    return ins
                P.op("pe", f_y, reads=[R_SW[wi], R_U, R_PCS[qi]], writes=[R_PS[py]])
                P.op("act", lambda e: e.activation(
                    out=ZY[:, :, gb * 64:(gb + 1) * 64].rearrange("p t (g c) -> p g t c", g=4),
                    in_=psum[:, py, :].rearrange("p (g t c) -> p g t c", g=4, t=8), func=AF.Gelu_apprx_tanh),
                    reads=[R_PS[py]], acc=[R_ZY])

            def b3_tile(f):
                pi = next_ps()

                def f_tr(e, f=f, pi=pi):
                    ins = None
                    for tau in range(8):
                        ins = e.transpose(out=psB(pi)[:, tau * 128:(tau + 1) * 128], in_=ZY[:, tau, f * 128:(f + 1) * 128], identity=identB[:])
                    return ins
                P.op("pe", f_tr, reads=[R_ZY, R_CONST], writes=[R_PS[pi]])
                copy_op(evac_eng(), ysT[:, f, :].rearrange("p (k t) -> p t k", t=8), psB(pi).rearrange("p (t k) -> p t k", t=8),
                        [R_PS[pi]], [R_YST])

            wg = None
            LAG = 1
            for s_ in range(8 + LAG):
                if s_ < 8:
                    ssm_front(s_)
                run_fillers(2)
                if s_ == 3:
                    wg = load_ws(wbf_glu[l].rearrange("(a p) n -> p a n", p=128), 4, [R_WBF[l]])
                    P.op("sp", lambda e, wg=wg, l=l: e.dma_start(out=WS[wg][:, 2048:3072], in_=poolW_d[l]),
                         reads=[R_POOLW[l]], acc=[R_WS[wg]], dma=f"d_ws{wg}")
                if s_ >= LAG:
                    ssm_back(s_ - LAG)
                    if (s_ - LAG) % 2 == 1:
                        b3_tile((s_ - LAG) // 2)
            run_fillers(len(fillers))

            pwv = WS[wg][:, 2048:3072].rearrange("p (g k n) -> p g k n", g=4, k=2)
            for n in range(2):
                nr = slice(n * 512, (n + 1) * 512)
                for f in range(4):
                    pi = next_ps()

                    def f_pm(e, f=f, n=n, nr=nr, pi=pi, pwv=pwv):
                        e.matmul(psum[:, pi, :], lhsT=pwv[:, f, 0, :], rhs=spool[:, f, nr], start=True, stop=False)
                        return e.matmul(psum[:, pi, :], lhsT=pwv[:, f, 1, :], rhs=upool[:, f, 16 + n * 512:16 + (n + 1) * 512], start=False, stop=True)
                    P.op("pe", f_pm, reads=[R_WS[wg], R_SPOOL, R_UPOOL], writes=[R_PS[pi]])
                    P.op("dve", lambda e, f=f, nr=nr, pi=pi, l=l: e.scalar_tensor_tensor(
                        out=ycat[:, f, nr], in0=psum[:, pi, :], scalar=smallp[:, l, 8 + f:9 + f], in1=gate[:, f, nr], op0=ALU.mult, op1=ALU.mult),
                        reads=[R_PS[pi], R_GATE[n], R_SMALLP], acc=[R_YCAT[n]])
            gluv = WS[wg][:, 0:2048].rearrange("p (a n) -> p a n", n=512)
            wov = wbf_out[l].rearrange("(a p) n -> p a n", p=128)
            wi_o = [load_ws(wov[:, :, so * 512:(so + 1) * 512], 8, [R_WBF[l]]) for so in range(2)]

            def c1_half(n, l=l, gluv=gluv, wg=wg):
                nr = slice(n * 512, (n + 1) * 512)
                for fo in range(4):
                    pi = next_ps()
                    sgi = fo % 2
                    mm_group(pi, lambda kc, fo=fo: gluv[:, kc, fo * 128:(fo + 1) * 128], lambda kc: ysT[:, kc, nr], 4, [R_WS[wg], R_YST])
                    P.op("act", lambda e, fo=fo, pi=pi, sgi=sgi: e.activation(out=sig[:, sgi, :], in_=psum[:, pi, :], func=AF.Sigmoid,
                                                                             bias=smallp[:, l, 12 + fo:13 + fo], scale=1.0),
                         reads=[R_PS[pi], R_SMALLP], writes=[R_SIG[sgi]])
                    P.op("dve", lambda e, fo=fo, sgi=sgi: e.tensor_tensor(out=sig[:, sgi, :], in0=sig[:, sgi, :], in1=ysT[:, fo, nr], op=ALU.mult),
                         reads=[R_SIG[sgi], R_YST], writes=[R_SIG[sgi]])
                    P.op("dve", lambda e, fo=fo, sgi=sgi: e.tensor_tensor(out=ycat[:, 4 + fo, nr], in0=sig[:, sgi, :], in1=gate[:, 4 + fo, nr], op=ALU.mult),
                         reads=[R_SIG[sgi], R_GATE[n]], acc=[R_YCAT[n]])

            lnext = l + 1 if l + 1 < n_layers else None

            def c2_half(n, wi_o=wi_o, lnext=lnext):
                nr = slice(n * 512, (n + 1) * 512)
                for fo in range(8):
                    if n == 1 and fo == 4:
                        norm_rstd(0)
                    wi, f4 = wi_o[fo // 4], fo % 4
                    pi = next_ps()
                    mm_group(pi, lambda kc, wi=wi, f4=f4: wsv(wi)[:, kc, f4 * 128:(f4 + 1) * 128], lambda kc: ycat[:, kc, nr], 8, [R_WS[wi], R_YCAT[n]])
                    P.op("dve", lambda e, fo=fo, pi=pi: e.tensor_tensor(out=xres[:, fo, nr], in0=xres[:, fo, nr], in1=psum[:, pi, :], op=ALU.add),
                         reads=[R_PS[pi], R_XRES[n]], acc=[R_XRES[n]])
                    norm_square(n, fo)
                    if n == 1 and fo >= 4:
                        norm_apply(0, lnext, [2 * (fo - 4), 2 * (fo - 4) + 1])
            c1_half(0)
            c1_half(1)
            c2_half(0)
            c2_half(1)
            norm_rstd(1)
            norm_apply(1, lnext)

        for tt in range(8):
            si = xs_rr[0]
            xs_rr[0] ^= 1
            for fh in range(2):
                pi = next_ps()

                def f_ot(e, tt=tt, fh=fh, pi=pi):
                    ins = None
                    for f4 in range(4):
                        f = fh * 4 + f4
                        ins = e.transpose(out=psum[:, pi, f4 * 128:(f4 + 1) * 128], in_=xres[:, f, tt * 128:(tt + 1) * 128], identity=identF[:])
                    return ins
                P.op("pe", f_ot, reads=[R_XRES[tt // 4], R_CONST], writes=[R_PS[pi]])
                copy_op(evac_eng(), xs[:, si, fh * 512:(fh + 1) * 512], psum[:, pi, :], [R_PS[pi]], [R_XS[si]])
            P.op("sp", lambda e, si=si, tt=tt, seq=seq, t0=t0: e.dma_start(out=out_d[seq, t0 + tt * 128:t0 + (tt + 1) * 128, :], in_=xs[:, si, :]),
                 reads=[R_XS[si]], acc=[R_OUT], dma=f"d_out{si}")

    P.op("sp", None, reads=[R_OUT], sig=False)
    sems = {k: es.enter_context(nc.semaphore(k)) for k in P.cnt}
    with nc.Block() as block:
        P.emit(nc, block, sems)
    es.close()
    return nc


def prep_inputs(inp):
    f = np.float32
    shared = {}
    shared["w_in"] = np.ascontiguousarray(inp["w_in"], dtype=f)
    shared["w_out"] = np.ascontiguousarray(inp["w_out"], dtype=f)
    shared["glu_w"] = np.ascontiguousarray(inp["glu_w"], dtype=f)
    shared["pool_w"] = np.ascontiguousarray(np.transpose(inp["pool_w"], (0, 2, 1, 3)), dtype=f)
    ng = np.transpose(np.asarray(inp["norm_g"], f).reshape(DEPTH, 8, 128), (2, 0, 1))
    psc = np.transpose(np.asarray(inp["pool_scale"], f).reshape(DEPTH, 4, 128), (2, 0, 1))
    gb = np.transpose(np.asarray(inp["glu_b"], f).reshape(DEPTH, 4, 128), (2, 0, 1))
    shared["smallp"] = np.ascontiguousarray(np.concatenate([ng, psc, gb], axis=2), dtype=f)
    shared["finalg"] = np.ascontiguousarray(np.asarray(inp["final_g"], f).reshape(8, 128).T)

    def dup(a):
        t = np.transpose(np.asarray(a, f), (0, 2, 1))
        return np.concatenate([t, t], axis=1)
    are = dup(inp["a_re"])
    aim = dup(inp["a_im"])
    ldt = np.broadcast_to(np.asarray(inp["log_dt"], f)[:, None, :], (DEPTH, 128, G))
    dv = np.asarray(inp["d_skip"], f).reshape(DEPTH, G, 16)
    dvec = np.broadcast_to(np.transpose(dv, (0, 2, 1))[:, None, :, :], (DEPTH, 8, 16, G)).reshape(DEPTH, 128, G)
    shared["ssm_small"] = np.ascontiguousarray(
        np.transpose(np.stack([are, aim, ldt, dvec], axis=2), (1, 2, 0, 3)).reshape(128, 4, LG), dtype=f)

    def dupb(a):
        t = np.transpose(np.asarray(a, f), (0, 2, 1, 3)).reshape(DEPTH, 64, G * 16)
        return np.concatenate([t, t], axis=1)

    def dupc(a):
        t = np.transpose(np.asarray(a, f), (0, 3, 1, 2)).reshape(DEPTH, 64, G * 16)
        return np.concatenate([t, t], axis=1)
    shared["ssm_bc"] = np.ascontiguousarray(
        np.stack([dupb(inp["b_re"]), dupb(inp["b_im"]), dupc(inp["c_re"]), dupc(inp["c_im"])], axis=2), dtype=f)
    x = np.ascontiguousarray(inp["x"], dtype=f)
    maps = []
    for c in range(NCORES):
        m = dict(shared)
        m["x"] = x[c * NSEQ:(c + 1) * NSEQ]
        maps.append(m)
    return maps


def kernel(**inputs):
    maps = prep_inputs(inputs)
    nc = build_program()
    res = run_bass_kernel_spmd(nc, maps, core_ids=list(range(NCORES)))
    out = np.concatenate([np.asarray(r["out"], dtype=np.float32) for r in res.results], axis=0)
    return out
```

```python
import math
from contextlib import ExitStack

import numpy as np
import concourse.bass as bass
import concourse.mybir as mybir
from concourse.bass_utils import run_bass_kernel_spmd

F32 = mybir.dt.float32
BF16 = mybir.dt.bfloat16
I32 = mybir.dt.int32
AF = mybir.ActivationFunctionType
ALU = mybir.AluOpType

NCORES = 8
DEPTH = 4
D = 1024
SEQ = 2048
NSEQ = 2
ST = 1024
NK = ST // 8
G = 32
LG = DEPTH * G
TWO_PI = float(2.0 * math.pi)
INV_2PI = float(1.0 / (2.0 * math.pi))
HALF_PI = float(math.pi / 2.0)
SIN_SCALE = 1.0 - 4e-5
EPS = 1e-5
POOL_WINDOWS = (2, 4, 8, 16)


class Res:
    __slots__ = ("name", "w", "r")

    def __init__(self, name):
        self.name = name
        self.w = {}
        self.r = {}


class Prog:
    ENG = ("pe", "act", "dve", "pool", "sp")

    def __init__(self):
        self.ops = {e: [] for e in self.ENG}
        self.cnt = {}
        self.seen = {e: {} for e in self.ENG}
        self.pending = {e: {} for e in self.ENG}

    def fence(self):
        for e in self.ENG:
            self.pending[e] = dict(self.cnt)

    def op(self, eng, fn, reads=(), writes=(), dma=None, sig=True, acc=()):
        need = self.pending[eng]
        self.pending[eng] = {}
        for r in acc:
            for k, v in r.r.items():
                if need.get(k, 0) < v:
                    need[k] = v
        for r in reads:
            for k, v in r.w.items():
                if need.get(k, 0) < v:
                    need[k] = v
        for r in writes:
            for k, v in r.w.items():
                if need.get(k, 0) < v:
                    need[k] = v
            for k, v in r.r.items():
                if need.get(k, 0) < v:
                    need[k] = v
        waits = []
        seen = self.seen[eng]
        for k, v in need.items():
            if eng == "pe" and k == "pe":
                continue
            if seen.get(k, 0) >= v:
                continue
            seen[k] = v
            waits.append((k, v))
        tok = None
        inc = 1
        if sig:
            key = dma if dma is not None else eng
            inc = 16 if dma is not None else 1
            self.cnt[key] = self.cnt.get(key, 0) + inc
            tok = (key, self.cnt[key])
            for r in reads:
                if r.r.get(key, 0) < tok[1]:
                    r.r[key] = tok[1]
            for r in writes:
                r.w = {key: tok[1]}
            for r in acc:
                if r.w.get(key, 0) < tok[1]:
                    r.w[key] = tok[1]
        self.ops[eng].append((waits, fn, tok, inc))
        return tok

    def emit(self, nc, block, sems):
        def mk(name):
            def body(e):
                for waits, fn, tok, inc in self.ops[name]:
                    for k, v in waits:
                        e.wait_ge(sems[k], v)
                    if fn is None:
                        continue
                    ins = fn(e)
                    if tok is not None:
                        ins.then_inc(sems[tok[0]], inc)
            return body

        block.tensor(mk("pe"))
        block.scalar(mk("act"))
        block.vector(mk("dve"))
        block.gpsimd(mk("pool"))
        block.sync(mk("sp"))


def build_program(debug=False, n_layers=DEPTH, n_sub=4):
    nc = bass.Bass("TRN2", target_bir_lowering=False)
    P = Prog()
    es = ExitStack()

    def dram_in(name, shape, dt=F32):
        return nc.dram_tensor(name, list(shape), dt, kind="ExternalInput").ap()

    def dram_out(name, shape, dt=F32):
        return nc.dram_tensor(name, list(shape), dt, kind="ExternalOutput").ap()

    def dram_scr(name, shape, dt):
        kind = "ExternalOutput" if debug else "Internal"
        return nc.dram_tensor(name, list(shape), dt, kind=kind).ap()

    def sb(name, shape, dt, stack=None):
        return (stack or es).enter_context(nc.sbuf_tensor(name, list(shape), dt))

    x_d = dram_in("x", [NSEQ, SEQ, D])
    w_in_d = dram_in("w_in", [DEPTH, D, 2 * D])
    w_out_d = dram_in("w_out", [DEPTH, D, D])
    glu_w_d = dram_in("glu_w", [DEPTH, 512, 512])
    pool_w_d = dram_in("pool_w", [DEPTH, 128, 4, 128])
    smallp_d = dram_in("smallp", [128, DEPTH, 16])
    finalg_d = dram_in("finalg", [128, 8])
    ssm_small_d = dram_in("ssm_small", [128, 4, LG])
    ssm_bc_d = dram_in("ssm_bc", [DEPTH, 128, 4, 512])
    out_d = dram_out("out", [NSEQ, SEQ, D])

    ssmW_d = dram_scr("ssmW", [DEPTH, 8, 128, 5 * 4 * 128], BF16)
    tabs_d = dram_scr("tabs", [DEPTH, 8, 128, 2, 4, 256], F32)
    poolW_d = dram_scr("poolW", [DEPTH, 128, 4 * 2 * 128], BF16)
    R_SSMW = [Res(f"ssmW{l}") for l in range(DEPTH)]
    R_TABS = [Res(f"tabs{l}") for l in range(DEPTH)]
    R_POOLW = [Res(f"poolW{l}") for l in range(DEPTH)]
    R_OUT = Res("out")

    identF = sb("identF", [128, 128], F32)
    identB = sb("identB", [128, 128], BF16)
    onesB = sb("onesB", [128, 128], BF16)
    Rb_all = sb("Rb_all", [128, DEPTH, 32], F32)
    smallp = sb("smallp_sb", [128, DEPTH, 16], F32)
    finalg = sb("finalg_sb", [128, 8], F32)
    psum = es.enter_context(nc.psum_tensor("psum", [128, 8, 512], F32))
    R_CONST = Res("const")
    R_RB = Res("Rb")
    R_SMALLP = Res("smallp")
    R_PS = [Res(f"ps{i}") for i in range(8)]
    ps_rr = [0]

    def next_ps():
        i = ps_rr[0]
        ps_rr[0] = (i + 1) % 8
        return i

    P.op("pool", lambda e: e.memset(identF[:], 1.0), writes=[R_CONST])
    P.op("pool", lambda e: e.affine_select(out=identF[:], in_=identF[:], pattern=[[-1, 128]],
                                           compare_op=ALU.is_equal, fill=0.0, base=0, channel_multiplier=1),
         reads=[R_CONST], writes=[R_CONST])
    P.op("pool", lambda e: e.tensor_copy(out=identB[:], in_=identF[:]), reads=[R_CONST], writes=[R_CONST])
    P.op("pool", lambda e: e.memset(onesB[:], 1.0 / 1024.0), writes=[R_CONST])
    P.op("sp", lambda e: e.dma_start(out=smallp[:], in_=smallp_d[:]), writes=[R_SMALLP], dma="d_small")
    P.op("sp", lambda e: e.dma_start(out=finalg[:], in_=finalg_d[:]), writes=[R_SMALLP], dma="d_small")

    with ExitStack() as ps_:
        def pb(name, shape, dt):
            return sb("pl_" + name, shape, dt, ps_)
        mask = pb("mask", [128, 128], F32)
        iotaKi = pb("iotaKi", [128, 256], I32)
        iotaK = pb("iotaK", [128, 256], F32)
        JTi = pb("JTi", [128, 7, LG], I32)
        JT = pb("JT", [128, 7, LG], F32)
        sm = pb("sm", [128, 4, LG], F32)
        bcs = [pb(f"bc{i}", [128, 4, 512], F32) for i in range(2)]
        pws = [pb(f"pw{i}", [128, 4, 128], F32) for i in range(2)]
        PWo = pb("PWo", [128, 4, 2, 128], BF16)
        dtt = pb("dtt", [128, LG], F32)
        ard = pb("ard", [128, LG], F32)
        ang = pb("ang", [128, LG], F32)
        EXre = pb("EXre", [128, LG, 8], F32)
        EXim = pb("EXim", [128, LG, 8], F32)
        EYre = pb("EYre", [128, LG, 8], F32)
        EYim = pb("EYim", [128, LG, 8], F32)
        s1 = pb("s1", [128, LG], F32)
        s2 = pb("s2", [128, LG], F32)
        s3 = pb("s3", [128, LG], F32)
        s4 = pb("s4", [128, LG], F32)
        fre = pb("fre", [128, LG], F32)
        fim = pb("fim", [128, LG], F32)
        Rt = pb("Rt", [128, LG], F32)
        TH = pb("TH", [128, LG], F32)
        NI8 = pb("NI8", [128, LG], I32)
        bbA = pb("bbA", [128, 32, 16], F32)
        bbB = pb("bbB", [128, 32, 16], F32)
        cA = pb("cA", [128, 32, 16], F32)
        cB = pb("cB", [128, 32, 16], F32)
        Smat = pb("Smat", [128, 128], F32)
        hpiT = pb("hpiT", [128, 1], F32)
        T1 = pb("T1", [128, 32, 8, 16], F32)
        T2 = pb("T2", [128, 32, 8, 16], F32)

        def jview(t, k):
            return t[:].rearrange("p g t c -> p (g t c)")[:, k * 7 * LG:(k + 1) * 7 * LG].rearrange("p (j n) -> p j n", j=7)
        def bview(t, k):
            return t[:].rearrange("p g t c -> p (g t c)")[:, k * 512:(k + 1) * 512].rearrange("p (g c) -> p g c", c=16)
        ta, tb_ = bview(T1, 0), bview(T1, 1)
        tc_, td = bview(T2, 0), bview(T2, 1)
        ARJ, ANJ, MAGP, MAGN = (jview(T1, k) for k in range(4))
        NIj = JTi[:]
        RRj, SN, CS = (jview(T2, k) for k in range(1, 4))
        XBm = pb("XBm", [128, 32, 8, 16], F32)
        Gm = pb("Gm", [128, 32, 8, 16], F32)
        W5 = pb("W5", [128, 4, 5, 4, 128], BF16)
        tmpT = [pb(f"tmpT{i}", [128, 4, 128], F32) for i in range(2)]
        NTB = 2
        ANGq = [pb(f"ANGq{i}", [128, 4, 256], F32) for i in range(NTB)]
        NIq = [pb(f"NIq{i}", [128, 4, 256], I32) for i in range(NTB)]
        RRq = [pb(f"RRq{i}", [128, 4, 256], F32) for i in range(NTB)]
        ABq = [ANGq[i][:] for i in range(NTB)]
        CSq = [pb(f"CSq{i}", [128, 2, 4, 256], F32) for i in range(NTB)]
        R_TQ = [Res(f"pl_tq{i}") for i in range(NTB)]
        R_CSQ = [Res(f"pl_csq{i}") for i in range(NTB)]
        R_BB = Res("pl_bb")

        R_PC = Res("pl_const")
        R_INS = [Res("pl_in0"), Res("pl_in1")]
        R_PWS = [Res("pl_pw0"), Res("pl_pw1")]
        R_A = Res("pl_a")
        R_E = Res("pl_E")
        R_X = Res("pl_X")
        R_T = Res("pl_T")
        R_W5 = Res("pl_W5")
        R_TMPT = [Res("pl_tmpT0"), Res("pl_tmpT1")]
        R_ANG = Res("pl_ang")
        R_TAB = Res("pl_tab")
        R_PWO = Res("pl_pwo")

        P.op("pool", lambda e: e.memset(mask[:], 1.0), writes=[R_PC])
        P.op("pool", lambda e: e.affine_select(out=mask[:], in_=mask[:], pattern=[[16, 8], [0, 16]],
                                               compare_op=ALU.is_ge, fill=0.0, base=15, channel_multiplier=-1),
             reads=[R_PC], writes=[R_PC])
        P.op("pool", lambda e: e.iota(iotaKi[:], pattern=[[1, 256]], base=0, channel_multiplier=0), writes=[R_PC])
        P.op("pool", lambda e: e.tensor_copy(out=iotaK[:], in_=iotaKi[:]), reads=[R_PC], writes=[R_PC])
        P.op("pool", lambda e: e.iota(JTi[:], pattern=[[-1, 7], [0, LG]], base=7, channel_multiplier=0), writes=[R_PC])
        P.op("pool", lambda e: e.tensor_copy(out=JT[:], in_=JTi[:]), reads=[R_PC], writes=[R_PC])
        P.op("pool", lambda e: e.tensor_copy(out=Smat[:, 0:64], in_=identF[:, 64:128]), reads=[R_CONST], writes=[R_PC])
        P.op("pool", lambda e: e.tensor_scalar(out=Smat[:, 64:128], in0=identF[:, 0:64], scalar1=-1.0, scalar2=None, op0=ALU.mult),
             reads=[R_CONST, R_PC], writes=[R_PC])
        P.op("pool", lambda e: e.memset(hpiT[:], HALF_PI), reads=[R_PC], writes=[R_PC])

        def bc3(ap32):
            return ap32.unsqueeze(1).to_broadcast([128, 7, LG])

        def bcc(ap32):
            return ap32.unsqueeze(2).to_broadcast([128, 32, 16])

        def bE(apE, lo, hi):
            return apE[lo:hi].unsqueeze(3).to_broadcast([hi - lo, 32, 8, 16])

        def bB(apB, lo, hi):
            return apB[lo:hi].unsqueeze(2).to_broadcast([hi - lo, 32, 8, 16])

        def bR(apR, lo, hi):
            return apR[lo:hi].rearrange("p (a g) -> p a g", g=4).unsqueeze(3).to_broadcast([hi - lo, 8, 4, 128])

        wbf_in = dram_scr("wbf_in", [DEPTH, D, 2 * D], BF16)
        wbf_out = dram_scr("wbf_out", [DEPTH, D, D], BF16)
        wbf_glu = dram_scr("wbf_glu", [DEPTH, 512, 512], BF16)
        R_WBF = [Res(f"wbf{l}") for l in range(DEPTH)]
        NCQ = 4
        R_CASTQ = [Res(f"castq{i}") for i in range(NCQ)]
        cq = [0]

        def cast_dma(dst, src, l):
            j = cq[0] % NCQ
            cq[0] += 1
            P.op("pool", lambda e: e.dma_start(out=dst, in_=src), writes=[R_CASTQ[j]], acc=[R_WBF[l]], dma=f"d_cast{j}")
        def cast_layer(l):
            for a in range(8):
                cast_dma(wbf_in[l, a * 128:(a + 1) * 128, :], w_in_d[l, a * 128:(a + 1) * 128, :], l)
            for a in range(4):
                cast_dma(wbf_out[l, a * 256:(a + 1) * 256, :], w_out_d[l, a * 256:(a + 1) * 256, :], l)
            cast_dma(wbf_glu[l], glu_w_d[l], l)
        cast_layer(0)

        R_SM = Res("pl_sm")
        P.op("sp", lambda e: e.dma_start(out=sm[:], in_=ssm_small_d[:]), writes=[R_SM], dma="d_plsm")
        are, aim, ldt = sm[:, 0, :], sm[:, 1, :], sm[:, 2, :]
        P.op("act", lambda e: e.activation(out=dtt[:], in_=ldt, func=AF.Exp), reads=[R_SM], writes=[R_A])
        P.op("dve", lambda e: e.tensor_tensor(out=ard[:], in0=are, in1=dtt[:], op=ALU.mult), reads=[R_SM, R_A], writes=[R_A])
        P.op("dve", lambda e: e.tensor_tensor(out=ang[:], in0=aim, in1=dtt[:], op=ALU.mult), reads=[R_SM, R_A], writes=[R_A])
        P.op("dve", lambda e: e.tensor_tensor(out=ARJ, in0=JT[:], in1=bc3(ard[:]), op=ALU.mult), reads=[R_PC, R_A], writes=[R_A])
        P.op("dve", lambda e: e.tensor_tensor(out=ANJ, in0=JT[:], in1=bc3(ang[:]), op=ALU.mult), reads=[R_PC, R_A], writes=[R_A])
        P.op("act", lambda e: e.activation(out=MAGP, in_=ARJ, func=AF.Exp), reads=[R_A], writes=[R_A])
        P.op("act", lambda e: e.activation(out=MAGN, in_=ARJ, func=AF.Exp, scale=-1.0), reads=[R_A], writes=[R_A])
        P.op("act", lambda e: e.activation(out=NIj, in_=ANJ, func=AF.Copy, scale=INV_2PI), reads=[R_A], writes=[R_A])
        P.op("dve", lambda e: e.scalar_tensor_tensor(out=RRj, in0=NIj, scalar=-TWO_PI, in1=ANJ, op0=ALU.mult, op1=ALU.add),
             reads=[R_A], writes=[R_A])
        P.op("act", lambda e: e.activation(out=SN, in_=RRj, func=AF.Sin, scale=SIN_SCALE), reads=[R_A], writes=[R_A])
        P.op("dve", lambda e: e.tensor_scalar(out=RRj, in0=ANJ, scalar1=HALF_PI, scalar2=None, op0=ALU.add), reads=[R_A], writes=[R_A])
        P.op("act", lambda e: e.activation(out=NIj, in_=RRj, func=AF.Copy, scale=INV_2PI), reads=[R_A], writes=[R_A])
        P.op("dve", lambda e: e.scalar_tensor_tensor(out=RRj, in0=NIj, scalar=-TWO_PI, in1=RRj, op0=ALU.mult, op1=ALU.add),
             reads=[R_A], writes=[R_A])
        P.op("act", lambda e: e.activation(out=CS, in_=RRj, func=AF.Sin, scale=SIN_SCALE), reads=[R_A], writes=[R_A])
        def Ev(t):
            return t[:].rearrange("p g t -> p t g")[:, 0:7, :]
        P.op("dve", lambda e: e.tensor_tensor(out=Ev(EXre), in0=MAGP, in1=CS, op=ALU.mult), reads=[R_A], writes=[R_E])
        P.op("dve", lambda e: e.tensor_tensor(out=Ev(EXim), in0=MAGP, in1=SN, op=ALU.mult), reads=[R_A], writes=[R_E])
        P.op("dve", lambda e: e.tensor_tensor(out=Ev(EYre), in0=MAGN, in1=CS, op=ALU.mult), reads=[R_A], writes=[R_E])
        P.op("dve", lambda e: e.scalar_tensor_tensor(out=Ev(EYim), in0=MAGN, scalar=-1.0, in1=SN, op0=ALU.mult, op1=ALU.mult),
             reads=[R_A], writes=[R_E])
        P.op("dve", lambda e: e.memset(EXre[:, :, 7:8], 1.0), writes=[R_E])
        P.op("dve", lambda e: e.memset(EXim[:, :, 7:8], 0.0), writes=[R_E])
        P.op("dve", lambda e: e.memset(EYre[:, :, 7:8], 1.0), writes=[R_E])
        P.op("dve", lambda e: e.memset(EYim[:, :, 7:8], 0.0), writes=[R_E])
        lre, lim = EXre[:, :, 6], EXim[:, :, 6]
        P.op("dve", lambda e: e.tensor_scalar(out=s1[:], in0=lre, scalar1=-1.0, scalar2=None, op0=ALU.add), reads=[R_E], writes=[R_A])
        P.op("dve", lambda e: e.tensor_tensor(out=s2[:], in0=are, in1=are, op=ALU.mult), reads=[R_SM], writes=[R_A])
        P.op("dve", lambda e: e.tensor_tensor(out=s3[:], in0=aim, in1=aim, op=ALU.mult), reads=[R_SM], writes=[R_A])
        P.op("dve", lambda e: e.tensor_tensor(out=s2[:], in0=s2[:], in1=s3[:], op=ALU.add), reads=[R_A], writes=[R_A])
        P.op("dve", lambda e: e.reciprocal(out=s2[:], in_=s2[:]), reads=[R_A], writes=[R_A])
        P.op("dve", lambda e: e.tensor_tensor(out=s3[:], in0=s1[:], in1=are, op=ALU.mult), reads=[R_A, R_SM], writes=[R_A])
        P.op("dve", lambda e: e.tensor_tensor(out=s4[:], in0=lim, in1=aim, op=ALU.mult), reads=[R_E, R_SM], writes=[R_A])
        P.op("dve", lambda e: e.tensor_tensor(out=s3[:], in0=s3[:], in1=s4[:], op=ALU.add), reads=[R_A], writes=[R_A])
        P.op("dve", lambda e: e.tensor_tensor(out=fre[:], in0=s3[:], in1=s2[:], op=ALU.mult), reads=[R_A], writes=[R_A])
        P.op("dve", lambda e: e.tensor_tensor(out=s3[:], in0=lim, in1=are, op=ALU.mult), reads=[R_E, R_SM], writes=[R_A])
        P.op("dve", lambda e: e.tensor_tensor(out=s4[:], in0=s1[:], in1=aim, op=ALU.mult), reads=[R_A, R_SM], writes=[R_A])
        P.op("dve", lambda e: e.tensor_tensor(out=s3[:], in0=s3[:], in1=s4[:], op=ALU.subtract), reads=[R_A], writes=[R_A])
        P.op("dve", lambda e: e.tensor_tensor(out=fim[:], in0=s3[:], in1=s2[:], op=ALU.mult), reads=[R_A], writes=[R_A])
        P.op("act", lambda e: e.activation(out=Rt[:], in_=ard[:], func=AF.Exp, scale=8.0), reads=[R_A], writes=[R_A])
        P.op("act", lambda e: e.activation(out=Rb_all[:].rearrange("p l g -> p (l g)"), in_=Rt[:], func=AF.Copy), reads=[R_A], writes=[R_RB])
        P.op("dve", lambda e: e.tensor_scalar(out=s1[:], in0=ang[:], scalar1=8.0, scalar2=None, op0=ALU.mult), reads=[R_A], writes=[R_A])
        P.op("act", lambda e: e.activation(out=NI8[:], in_=s1[:], func=AF.Copy, scale=INV_2PI), reads=[R_A], writes=[R_A])
        P.op("dve", lambda e: e.scalar_tensor_tensor(out=TH[:], in0=NI8[:], scalar=-TWO_PI, in1=s1[:], op0=ALU.mult, op1=ALU.add),
             reads=[R_A], writes=[R_A])


        for l in range(n_layers):
            def pl_loads(ll):
                P.op("sp", lambda e: e.dma_start(out=bcs[ll % 2][:], in_=ssm_bc_d[ll]), writes=[R_INS[ll % 2]], dma=f"d_plin{ll % 2}")
                P.op("sp", lambda e: e.dma_start(out=pws[ll % 2][:], in_=pool_w_d[ll]), writes=[R_PWS[ll % 2]], dma=f"d_plpw{ll % 2}")
            if l == 0:
                pl_loads(0)
            if l + 1 < n_layers:
                pl_loads(l + 1)
            bc, pw, R_IN, R_PW = bcs[l % 2], pws[l % 2], R_INS[l % 2], R_PWS[l % 2]
            ls = slice(l * 32, (l + 1) * 32)
            fre_l, fim_l = fre[:, ls], fim[:, ls]
            EXre_l, EXim_l, EYre_l, EYim_l = EXre[:, ls, :], EXim[:, ls, :], EYre[:, ls, :], EYim[:, ls, :]
            bre = bc[:, 0, :].rearrange("p (g c) -> p g c", c=16)
            bim = bc[:, 1, :].rearrange("p (g c) -> p g c", c=16)
            cre = bc[:, 2, :].rearrange("p (g c) -> p g c", c=16)
            cim = bc[:, 3, :].rearrange("p (g c) -> p g c", c=16)

            for g4 in range(4):
                P.op("act", lambda e, g4=g4, pw=pw: e.activation(out=PWo[:, g4, 0, :], in_=pw[:, g4, :], func=AF.Copy, scale=1.0 / POOL_WINDOWS[g4]),
                     reads=[R_PW], acc=[R_PWO])
            P.op("act", lambda e, pw=pw: e.activation(out=PWo[:, :, 1, :], in_=pw[:, :, :], func=AF.Copy, scale=-1.0), reads=[R_PW], acc=[R_PWO])
            P.op("sp", lambda e, l=l: e.dma_start(out=poolW_d[l], in_=PWo[:].rearrange("p a b c -> p (a b c)")),
                 reads=[R_PWO], writes=[R_POOLW[l]], dma=f"d_plpo{l}")

            P.op("dve", lambda e, fre_l=fre_l, fim_l=fim_l, bre=bre, bim=bim: e.tensor_tensor(out=ta, in0=bre, in1=bcc(fre_l), op=ALU.mult), reads=[R_A, R_IN], writes=[R_BB], acc=[R_T])
            P.op("dve", lambda e, fre_l=fre_l, fim_l=fim_l, bre=bre, bim=bim: e.tensor_tensor(out=tb_, in0=bim, in1=bcc(fim_l), op=ALU.mult), reads=[R_A, R_IN], acc=[R_BB])
            P.op("dve", lambda e, fre_l=fre_l, fim_l=fim_l, bre=bre, bim=bim: e.tensor_tensor(out=tc_, in0=bim, in1=bcc(fre_l), op=ALU.mult), reads=[R_A, R_IN], acc=[R_BB])
            P.op("dve", lambda e, fre_l=fre_l, fim_l=fim_l, bre=bre, bim=bim: e.tensor_tensor(out=td, in0=bre, in1=bcc(fim_l), op=ALU.mult), reads=[R_A, R_IN], acc=[R_BB])
            P.op("dve", lambda e: e.tensor_tensor(out=bbA[0:64], in0=ta[0:64], in1=tb_[0:64], op=ALU.subtract), reads=[R_BB], acc=[R_BB])
            P.op("dve", lambda e: e.tensor_tensor(out=bbA[64:128], in0=tc_[64:128], in1=td[64:128], op=ALU.add), reads=[R_BB], acc=[R_BB])
            P.op("dve", lambda e: e.scalar_tensor_tensor(out=bbB[0:64], in0=tc_[0:64], scalar=-1.0, in1=td[0:64], op0=ALU.mult, op1=ALU.subtract),
                 reads=[R_BB], acc=[R_BB])
            P.op("dve", lambda e: e.tensor_tensor(out=bbB[64:128], in0=ta[64:128], in1=tb_[64:128], op=ALU.subtract), reads=[R_BB], acc=[R_BB])
            P.op("act", lambda e, cre=cre, cim=cim: e.activation(out=cA[0:64], in_=cre[0:64], func=AF.Copy), reads=[R_IN], acc=[R_BB])
            P.op("act", lambda e, cre=cre, cim=cim: e.activation(out=cA[64:128], in_=cim[64:128], func=AF.Copy, scale=-1.0), reads=[R_IN], acc=[R_BB])
            P.op("act", lambda e, cre=cre, cim=cim: e.activation(out=cB[0:64], in_=cim[0:64], func=AF.Copy, scale=-1.0), reads=[R_IN], acc=[R_BB])
            P.op("act", lambda e, cre=cre, cim=cim: e.activation(out=cB[64:128], in_=cre[64:128], func=AF.Copy, scale=-1.0), reads=[R_IN], acc=[R_BB])
            tv = tabs_d[l].rearrange("a p two g k -> p a two g k")
            tq_rr = [0]

            def table_front(gb, l=l):
                i = tq_rr[0]
                tq_rr[0] = (i + 1) % NTB
                gs = slice(gb * 4, gb * 4 + 4)
                P.op("dve", lambda e: e.tensor_tensor(out=ANGq[i][:], in0=TH[:, l * 32 + gb * 4:l * 32 + gb * 4 + 4].unsqueeze(2).to_broadcast([128, 4, 256]),
                                                     in1=iotaK[:].unsqueeze(1).to_broadcast([128, 4, 256]), op=ALU.mult),
                     reads=[R_A, R_PC], writes=[R_TQ[i]])
                P.op("act", lambda e: e.activation(out=NIq[i][:], in_=ANGq[i][:], func=AF.Copy, scale=INV_2PI), reads=[R_TQ[i]], acc=[R_TQ[i]])
                return i

            def table_rest(gb, i, l=l, tv=tv):
                P.op("dve", lambda e: e.scalar_tensor_tensor(out=RRq[i][:], in0=NIq[i][:], scalar=-TWO_PI, in1=ANGq[i][:], op0=ALU.mult, op1=ALU.add),
                     reads=[R_TQ[i]], acc=[R_TQ[i]])
                P.op("act", lambda e: e.activation(out=ABq[i], in_=RRq[i][:], func=AF.Abs), reads=[R_TQ[i]], acc=[R_TQ[i]])
                P.op("act", lambda e: e.activation(out=CSq[i][:, 1], in_=RRq[i][:], func=AF.Sin, scale=SIN_SCALE), reads=[R_TQ[i]], writes=[R_CSQ[i]])
                P.op("act", lambda e: e.activation(out=CSq[i][:, 0], in_=ABq[i], func=AF.Sin, scale=-1.0, bias=hpiT[:]),
                     reads=[R_TQ[i], R_PC], acc=[R_CSQ[i]])
                P.op("sp", lambda e: e.dma_start(out=tv[:, gb], in_=CSq[i][:]), reads=[R_CSQ[i]], acc=[R_TABS[l]], dma=f"d_csq{i}")

            bigops = [
                lambda: P.op("dve", lambda e, EXre_l=EXre_l, EXim_l=EXim_l, EYre_l=EYre_l, EYim_l=EYim_l: e.tensor_tensor(out=T1[:], in0=bE(EXre_l, 0, 128), in1=bB(bbA, 0, 128), op=ALU.mult), reads=[R_E, R_BB], writes=[R_T]),
                lambda: P.op("dve", lambda e, EXre_l=EXre_l, EXim_l=EXim_l, EYre_l=EYre_l, EYim_l=EYim_l: e.tensor_tensor(out=T2[:], in0=bE(EXim_l, 0, 128), in1=bB(bbB, 0, 128), op=ALU.mult), reads=[R_E, R_BB], acc=[R_T]),
                lambda: P.op("dve", lambda e: e.tensor_tensor(out=XBm[:], in0=T1[:], in1=T2[:], op=ALU.add), reads=[R_T], writes=[R_X]),
                lambda: P.op("dve", lambda e, EXre_l=EXre_l, EXim_l=EXim_l, EYre_l=EYre_l, EYim_l=EYim_l: e.tensor_tensor(out=T1[:], in0=bE(EYre_l, 0, 128), in1=bB(cA, 0, 128), op=ALU.mult), reads=[R_E, R_BB], writes=[R_T]),
                lambda: P.op("dve", lambda e, EXre_l=EXre_l, EXim_l=EXim_l, EYre_l=EYre_l, EYim_l=EYim_l: e.tensor_tensor(out=T2[:], in0=bE(EYim_l, 0, 128), in1=bB(cB, 0, 128), op=ALU.mult), reads=[R_E, R_BB], acc=[R_T]),
                lambda: P.op("dve", lambda e: e.tensor_tensor(out=Gm[:], in0=T1[:], in1=T2[:], op=ALU.add), reads=[R_T], acc=[R_X]),
            ]
            for i_, bo in enumerate(bigops):
                ti_ = table_front(i_)
                bo()
                table_rest(i_, ti_)

            for gb in range(8):
                if gb in (2, 5):
                    tp_gb = 6 + (gb == 5)
                    tp_i = table_front(tp_gb)
                ti = gb % 2
                pi = next_ps()

                def f_toep(e, gb=gb, pi=pi):
                    ins = None
                    for g4 in range(4):
                        g = gb * 4 + g4
                        ins = e.matmul(psum[:, pi, g4 * 128:(g4 + 1) * 128],
                                       lhsT=XBm[:, g].rearrange("p t c -> p (t c)"),
                                       rhs=Gm[:, g].rearrange("p t c -> p (t c)"), start=True, stop=True)
                    return ins
                P.op("pe", f_toep, reads=[R_X], writes=[R_PS[pi]])
                P.op("dve", lambda e, pi=pi, ti=ti: e.tensor_tensor(out=tmpT[ti][:], in0=psum[:, pi, :].rearrange("p (g n) -> p g n", g=4),
                                                                   in1=mask[:].unsqueeze(1).to_broadcast([128, 4, 128]), op=ALU.mult),
                     reads=[R_PS[pi], R_PC], writes=[R_TMPT[ti]])
                for g4 in range(4):
                    P.op("dve", lambda e, gb=gb, g4=g4, ti=ti, l=l: e.scalar_tensor_tensor(
                        out=W5[:, gb % 4, 0, g4, :], in0=identF[:], scalar=sm[:, 3, l * 32 + gb * 4 + g4:l * 32 + gb * 4 + g4 + 1], in1=tmpT[ti][:, g4, :],
                        op0=ALU.mult, op1=ALU.add), reads=[R_TMPT[ti], R_CONST, R_SM], acc=[R_W5])
                if gb in (2, 5):
                    table_rest(tp_gb, tp_i)
                pi2 = next_ps()

                def f_tr(e, gb=gb, pi2=pi2):
                    ins = None
                    for g4 in range(4):
                        g = gb * 4 + g4
                        ins = e.transpose(out=psum[:, pi2, g4 * 128:(g4 + 1) * 128],
                                          in_=XBm[:, g].rearrange("p t c -> p (t c)"), identity=identF[:])
                    return ins
                P.op("pe", f_tr, reads=[R_X, R_CONST], writes=[R_PS[pi2]])
                pv = psum[:, pi2, :].rearrange("p (g n) -> p g n", g=4)
                P.op("act", lambda e, gb=gb, pv=pv: e.activation(out=W5[:, gb % 4, 1, :, :], in_=pv, func=AF.Copy),
                     reads=[R_PS[pi2]], acc=[R_W5])
                P.op("act", lambda e, gb=gb, pv=pv: e.activation(out=W5[:, gb % 4, 2, :, 0:64], in_=pv[:, :, 64:128], func=AF.Copy),
                     reads=[R_PS[pi2]], acc=[R_W5])
                P.op("act", lambda e, gb=gb, pv=pv: e.activation(out=W5[:, gb % 4, 2, :, 64:128], in_=pv[:, :, 0:64], func=AF.Copy, scale=-1.0),
                     reads=[R_PS[pi2]], acc=[R_W5])
                pi3 = next_ps()

                def f_sw(e, gb=gb, pi3=pi3):
                    ins = None
                    for g4 in range(4):
                        g = gb * 4 + g4
                        ins = e.matmul(psum[:, pi3, g4 * 128:(g4 + 1) * 128], lhsT=Smat[:],
                                       rhs=Gm[:, g].rearrange("p t c -> p (t c)"), start=True, stop=True)
                    return ins
                P.op("pe", f_sw, reads=[R_X, R_PC], writes=[R_PS[pi3]])
                for g4 in range(4):
                    g = gb * 4 + g4
                    P.op("dve", lambda e, gb=gb, g4=g4, g=g, l=l: e.tensor_scalar(out=W5[:, gb % 4, 3, g4, :], in0=Gm[:, g].rearrange("p t c -> p (t c)"),
                                                                            scalar1=Rt[:, l * 32 + g:l * 32 + g + 1], scalar2=None, op0=ALU.mult),
                         reads=[R_X, R_A], acc=[R_W5])
                    P.op("act", lambda e, gb=gb, g4=g4, g=g, pi3=pi3, l=l: e.activation(out=W5[:, gb % 4, 4, g4, :], in_=psum[:, pi3, g4 * 128:(g4 + 1) * 128],
                                                                                  func=AF.Copy, scale=Rt[:, l * 32 + g:l * 32 + g + 1]),
                         reads=[R_PS[pi3], R_A], acc=[R_W5])
                if gb % 4 == 3:
                    hb = gb // 4
                    P.op("sp", lambda e, l=l, hb=hb: e.dma_start(out=ssmW_d[l, hb * 4:(hb + 1) * 4].rearrange("a p n -> p a n"),
                                                             in_=W5[:].rearrange("p a k g n -> p a (k g n)")),
                         reads=[R_W5], acc=[R_SSMW[l]], dma=f"d_plssm{l}_{hb}")


    if debug == "prologue":
        rb_d = dram_out("rb_dbg", [128, DEPTH, 32])
        P.op("sp", lambda e: e.dma_start(out=rb_d[:], in_=Rb_all[:]), reads=[R_RB], writes=[R_OUT], dma="d_out")
        fin = [R_OUT] + R_SSMW[:n_layers] + R_TABS[:n_layers] + R_POOLW[:n_layers] + R_WBF[:n_layers]
        P.op("sp", None, reads=fin, sig=False)
        sems = {k: es.enter_context(nc.semaphore(k)) for k in P.cnt}
        with nc.Block() as block:
            P.emit(nc, block, sems)
        es.close()
        return nc

    P.fence()
    xres = sb("xres", [128, 8, ST], F32)
    h = sb("h", [128, 8, ST], BF16)
    gate = sb("gate", [128, 8, ST], BF16)
    ycat = sb("ycat", [128, 8, ST], BF16)
    upool = sb("upool", [128, 4, 16 + ST], BF16)
    spool = sb("spool", [128, 4, ST], BF16)
    ysT = sb("ysT", [128, 4, ST], BF16)
    xsq0 = spool[:].rearrange("p a n -> p (a n)").rearrange("p (a n) -> p a n", n=512)
    xsq1 = upool[:].rearrange("p a n -> p (a n)")[:, 0:4096].rearrange("p (a n) -> p a n", n=512)
    xsqs = [xsq0, xsq1]
    ZYf = sb("ZY", [128, 4096], BF16)
    ZY = ZYf[:].rearrange("p (t f) -> p t f", t=8)
    Zs = ZYf[:].rearrange("p (g t c) -> p g t c", g=32, t=8)
    U = sb("U", [128, 32, 128], BF16)
    rs = sb("rs", [128, 2, 512], F32)
    t1 = sb("t1", [128, 4, 128], F32)
    t2 = sb("t2", [128, 4, 128], F32)
    NQ = 2
    Q = [sb(f"Q{i}", [128, 4, 129], F32) for i in range(NQ)]
    Pc = [sb(f"Pc{i}", [128, 4, 128], BF16) for i in range(NQ)]
    Ps = [sb(f"Ps{i}", [128, 4, 128], BF16) for i in range(NQ)]
    sig = sb("sig", [128, 2, 512], BF16)
    xs = sb("xs", [128, 2, D], F32)
    pwk = xs[:].rearrange("p a n -> p (a n)").bitcast(BF16)[:, 0:3 * (16 + ST)].rearrange("p (a n) -> p a n", a=3)
    carry = sb("carry", [128, DEPTH, 32], F32)
    halo = sb("halo", [128, DEPTH, 4, 16], BF16)
    cfix = sb("cfix", [128, 4, 16], F32)
    epsT = sb("epsT", [128, 1], F32)
    NWS, NSW, NSTB = 4, 3, 2
    WS = [sb(f"WS{i}", [128, 8 * 512], BF16) for i in range(NWS)]
    SSw = [sb(f"SSw{i}", [128, 5, 4, 128], BF16) for i in range(NSW)]
    SSt = [sb(f"SSt{i}", [128, 2, 4, 128], F32) for i in range(NSTB)]
    R_WS = [Res(f"WS{i}") for i in range(NWS)]
    R_SW = [Res(f"SW{i}") for i in range(NSW)]
    R_STB = [Res(f"STB{i}") for i in range(NSTB)]
    R_ZY, R_U = Res("ZY"), Res("U")
    R_XRES = [Res("xres0"), Res("xres1")]
    R_RS = [Res("rs0"), Res("rs1")]
    R_H = [Res("h0"), Res("h1")]
    R_GATE = [Res("g0"), Res("g1")]
    R_YCAT = [Res("yc0"), Res("yc1")]
    R_UPOOL, R_SPOOL, R_YST = Res("upool"), Res("spool"), Res("ysT")
    R_T1, R_T2 = Res("t1"), Res("t2")
    R_Q = [Res(f"Q{i}") for i in range(NQ)]
    R_PCS = [Res(f"PcPs{i}") for i in range(NQ)]
    R_SIG = [Res("sig0"), Res("sig1")]
    R_XS = [Res("xs0"), Res("xs1")]
    R_CARRY = [[Res(f"carry{l}_{gb}") for gb in range(8)] for l in range(DEPTH)]
    R_HALO = [Res(f"halo{l}") for l in range(DEPTH)]
    R_C2 = Res("const2")
    ws_rr, sw_rr, stb_rr, xs_rr, ev_rr = [0], [0], [0], [0], [0]

    def psB(pi):
        return psum[:, pi, :].bitcast(BF16)

    def evac_eng():
        ev_rr[0] ^= 1
        return "act" if ev_rr[0] else "dve"

    def copy_op(eng, out, in_, reads, acc):
        if eng == "act":
            P.op("act", lambda e: e.activation(out=out, in_=in_, func=AF.Copy), reads=reads, acc=acc)
        else:
            P.op(eng, lambda e: e.tensor_copy(out=out, in_=in_), reads=reads, acc=acc)

    P.op("pool", lambda e: e.memset(epsT[:], EPS), writes=[R_C2])
    P.op("pool", lambda e: e.memset(cfix[:], 1.0), writes=[R_C2])
    for f in range(4):
        w = POOL_WINDOWS[f]
        for t in range(w - 1):
            P.op("pool", lambda e, f=f, t=t, w=w: e.memset(cfix[:, f, t:t + 1], float(w) / float(t + 1)), writes=[R_C2])

    def load_ws(src_ap, nparts, reads):
        i = ws_rr[0]
        ws_rr[0] = (i + 1) % NWS
        dst = WS[i][:, 0:nparts * 512].rearrange("p (a n) -> p a n", n=512)
        P.op("sp", lambda e: e.dma_start(out=dst, in_=src_ap), reads=reads, writes=[R_WS[i]], dma=f"d_ws{i}")
        return i

    def wsv(i):
        return WS[i][:].rearrange("p (a n) -> p a n", n=512)

    def mm_group(pi, lhs_fn, rhs_fn, nk, reads):
        def f(e):
            ins = None
            for kc in range(nk):
                ins = e.matmul(psum[:, pi, :], lhsT=lhs_fn(kc), rhs=rhs_fn(kc), start=(kc == 0), stop=(kc == nk - 1))
            return ins
        P.op("pe", f, reads=reads, writes=[R_PS[pi]])

    def R_XSQ(n):
        return R_SPOOL if n == 0 else R_UPOOL

    def norm_square(n, f=None):
        nr = slice(n * 512, (n + 1) * 512)
        if f is None:
            P.op("act", lambda e: e.activation(out=xsqs[n], in_=xres[:, :, nr], func=AF.Square), reads=[R_XRES[n]], writes=[R_XSQ(n)])
        elif f == 0:
            P.op("act", lambda e: e.activation(out=xsqs[n][:, 0, :], in_=xres[:, 0, nr], func=AF.Square), reads=[R_XRES[n]], writes=[R_XSQ(n)])
        else:
            P.op("act", lambda e: e.activation(out=xsqs[n][:, f, :], in_=xres[:, f, nr], func=AF.Square), reads=[R_XRES[n]], acc=[R_XSQ(n)])

    def norm_rstd(n):
        pi = next_ps()
        mm_group(pi, lambda kc: onesB[:], lambda kc: xsqs[n][:, kc, :], 8, [R_XSQ(n), R_CONST])
        P.op("act", lambda e, pi=pi: e.activation(out=rs[:, n, :], in_=psum[:, pi, :], func=AF.Ln, bias=epsT[:], scale=1.0),
             reads=[R_PS[pi], R_C2], writes=[R_RS[n]])
        P.op("act", lambda e: e.activation(out=rs[:, n, :], in_=rs[:, n, :], func=AF.Exp, scale=-0.5), reads=[R_RS[n]], writes=[R_RS[n]])

    def norm_apply(n, lnext, fs=range(8)):
        nr = slice(n * 512, (n + 1) * 512)
        for f in fs:
            if lnext is None:
                P.op("dve", lambda e, f=f: e.scalar_tensor_tensor(
                    out=xres[:, f, nr], in0=xres[:, f, nr], scalar=finalg[:, f:f + 1], in1=rs[:, n, :], op0=ALU.mult, op1=ALU.mult),
                    reads=[R_RS[n], R_SMALLP, R_XRES[n]], acc=[R_XRES[n]])
            else:
                P.op("dve", lambda e, f=f: e.scalar_tensor_tensor(
                    out=h[:, f, nr], in0=xres[:, f, nr], scalar=smallp[:, lnext, f:f + 1], in1=rs[:, n, :], op0=ALU.mult, op1=ALU.mult),
                    reads=[R_XRES[n], R_RS[n], R_SMALLP], acc=[R_H[n]])

    for st in range(n_sub):
        seq, half = st // 2, st % 2
        t0 = half * ST
        if half == 0:
            P.op("pool", lambda e: e.memset(carry[:], 0.0), writes=[r for rl in R_CARRY for r in rl])
            P.op("pool", lambda e: e.memset(halo[:], 0.0), writes=R_HALO)
        for tt in range(8):
            si = xs_rr[0]
            xs_rr[0] ^= 1
            P.op("sp", lambda e, si=si, tt=tt, seq=seq, t0=t0: e.dma_start(out=xs[:, si, :], in_=x_d[seq, t0 + tt * 128:t0 + (tt + 1) * 128, :]),
                 writes=[R_XS[si]], dma=f"d_xs{si}")
            for fh in range(2):
                pi = next_ps()

                def f_xt(e, si=si, fh=fh, pi=pi):
                    ins = None
                    for f4 in range(4):
                        f = fh * 4 + f4
                        ins = e.transpose(out=psum[:, pi, f4 * 128:(f4 + 1) * 128], in_=xs[:, si, f * 128:(f + 1) * 128], identity=identF[:])
                    return ins
                P.op("pe", f_xt, reads=[R_XS[si], R_CONST], writes=[R_PS[pi]])
                copy_op(evac_eng(), xres[:, fh * 4:(fh + 1) * 4, tt * 128:(tt + 1) * 128],
                        psum[:, pi, :].rearrange("p (f n) -> p f n", f=4), [R_PS[pi]], [R_XRES[tt // 4]])

        for l in range(n_layers):
            k0 = half * NK
            if st == 0 and l + 1 < n_layers:
                cast_layer(l + 1)
            win = wbf_in[l].rearrange("(a p) n -> p a n", p=128)
            if l == 0:
                for n in range(2):
                    norm_square(n)
                    norm_rstd(n)
                    norm_apply(n, 0)
            wi_g = [load_ws(win[:, :, 1024 + sg * 512:1024 + (sg + 1) * 512], 8, [R_WBF[l]]) for sg in range(2)]

            def gate_group(fo, n, wi_g=wi_g):
                nr = slice(n * 512, (n + 1) * 512)
                wi, f4 = wi_g[fo // 4], fo % 4
                pi = next_ps()
                mm_group(pi, lambda kc: wsv(wi)[:, kc, f4 * 128:(f4 + 1) * 128], lambda kc: h[:, kc, nr], 8, [R_WS[wi], R_H[n]])
                P.op("act", lambda e: e.activation(out=gate[:, fo, nr], in_=psum[:, pi, :], func=AF.Silu), reads=[R_PS[pi]], acc=[R_GATE[n]])
            if l > 0:
                for fo in range(4):
                    gate_group(fo, 0)
            wi_s = load_ws(win[:, :, 512:1024], 8, [R_WBF[l]])
            for tau in range(8):
                pi = next_ps()
                mm_group(pi, lambda kc, tau=tau: h[:, kc, :].rearrange("p (k t) -> p t k", t=8)[:, tau, :],
                         lambda kc, wi_s=wi_s: wsv(wi_s)[:, kc, :], 8, [R_WS[wi_s], R_H[0], R_H[1]])
                copy_op(evac_eng(), Zs[:, :, tau, :], psum[:, pi, :].rearrange("p (g c) -> p g c", c=16), [R_PS[pi]], [R_ZY])
            for gq in range(4):
                pi = next_ps()

                def f_tr(e, gq=gq, pi=pi):
                    ins = None
                    for g8 in range(8):
                        g = gq * 8 + g8
                        ins = e.transpose(out=psB(pi)[:, g8 * 128:(g8 + 1) * 128], in_=Zs[:, g].rearrange("p t c -> p (t c)"), identity=identB[:])
                    return ins
                P.op("pe", f_tr, reads=[R_ZY, R_CONST], writes=[R_PS[pi]])
                copy_op(evac_eng(), U[:, gq * 8:(gq + 1) * 8, :], psB(pi).rearrange("p (g k) -> p g k", g=8), [R_PS[pi]], [R_U])

            wi_p = load_ws(win[:, :, 0:512], 8, [R_WBF[l]])
            fillers = []

            def pool_in_group(f, n, wi_p=wi_p, l=l):
                nr = slice(n * 512, (n + 1) * 512)
                pi = next_ps()
                mm_group(pi, lambda kc: wsv(wi_p)[:, kc, f * 128:(f + 1) * 128], lambda kc: h[:, kc, nr], 8, [R_WS[wi_p], R_H[n]])
                P.op("act", lambda e: e.activation(out=upool[:, f, 16 + n * 512:16 + (n + 1) * 512], in_=psum[:, pi, :], func=AF.Copy),
                     reads=[R_PS[pi]], acc=[R_UPOOL])

            def pool_sums(l=l, half=half):
                LT = 16 + ST
                R_PWK = R_XS
                P.op("dve", lambda e: e.tensor_copy(out=halo[:, l, :, :], in_=upool[:, :, ST:ST + 16]), reads=[R_UPOOL], writes=[R_HALO[l]])
                P.op("dve", lambda e: e.tensor_tensor(out=spool[:, 0, :], in0=upool[:, 0, 16:LT], in1=upool[:, 0, 15:LT - 1], op=ALU.add),
                     reads=[R_UPOOL], writes=[R_SPOOL])
                for f in (1, 2, 3):
                    P.op("dve", lambda e, f=f: e.tensor_tensor(out=pwk[:, 0, 1:LT], in0=upool[:, f, 1:LT], in1=upool[:, f, 0:LT - 1], op=ALU.add),
                         reads=[R_UPOOL], writes=R_PWK)
                    if f == 1:
                        P.op("dve", lambda e: e.tensor_tensor(out=spool[:, 1, :], in0=pwk[:, 0, 16:LT], in1=pwk[:, 0, 14:LT - 2], op=ALU.add),
                             reads=R_PWK, acc=[R_SPOOL])
                        continue
                    P.op("dve", lambda e: e.tensor_tensor(out=pwk[:, 1, 3:LT], in0=pwk[:, 0, 3:LT], in1=pwk[:, 0, 1:LT - 2], op=ALU.add),
                         reads=R_PWK, writes=R_PWK)
                    if f == 2:
                        P.op("dve", lambda e: e.tensor_tensor(out=spool[:, 2, :], in0=pwk[:, 1, 16:LT], in1=pwk[:, 1, 12:LT - 4], op=ALU.add),
                             reads=R_PWK, acc=[R_SPOOL])
                        continue
                    P.op("dve", lambda e: e.tensor_tensor(out=pwk[:, 2, 7:LT], in0=pwk[:, 1, 7:LT], in1=pwk[:, 1, 3:LT - 4], op=ALU.add),
                         reads=R_PWK, writes=R_PWK)
                    P.op("dve", lambda e: e.tensor_tensor(out=spool[:, 3, :], in0=pwk[:, 2, 16:LT], in1=pwk[:, 2, 8:LT - 8], op=ALU.add),
                         reads=R_PWK, acc=[R_SPOOL])
                if half == 0:
                    P.op("dve", lambda e: e.tensor_tensor(out=spool[:, :, 0:16], in0=spool[:, :, 0:16], in1=cfix[:], op=ALU.mult),
                         reads=[R_C2], writes=[R_SPOOL])

            P.op("dve", lambda e, l=l: e.tensor_copy(out=upool[:, :, 0:16], in_=halo[:, l, :, :]), reads=[R_HALO[l]], writes=[R_UPOOL])
            for fo in range(4):
                for n in range(2):
                    if l > 0 and n == 0:
                        continue
                    fillers.append(lambda fo=fo, n=n: gate_group(fo, n))
            for f in range(4):
                for n in range(2):
                    fillers.append(lambda f=f, n=n: pool_in_group(f, n))
            fillers.append(pool_sums)
            for fo in range(4, 8):
                for n in range(2):
                    fillers.append(lambda fo=fo, n=n: gate_group(fo, n))
            fillers.reverse()

            def run_fillers(k):
                for _ in range(k):
                    if fillers:
                        fillers.pop()()

            slots = {}

            def ssm_front(gb, l=l, k0=k0):
                wi = sw_rr[0]
                sw_rr[0] = (wi + 1) % NSW
                ti = stb_rr[0]
                stb_rr[0] = (ti + 1) % NSTB
                qi = gb % NQ
                slots[gb] = (wi, qi)
                P.op("sp", lambda e: e.dma_start(out=SSw[wi][:].rearrange("p a g n -> p (a g n)"), in_=ssmW_d[l, gb]),
                     reads=[R_SSMW[l]], writes=[R_SW[wi]], dma=f"d_sw{wi}")
                P.op("sp", lambda e: e.dma_start(out=SSt[ti][:], in_=tabs_d[l, gb][:, :, :, k0:k0 + NK]),
                     reads=[R_TABS[l]], writes=[R_STB[ti]], dma=f"d_stb{ti}")
                pa, pb_ = next_ps(), next_ps()

                def f_v(e):
                    ins = None
                    for kind, pi in ((1, pa), (2, pb_)):
                        for g4 in range(4):
                            ins = e.matmul(psum[:, pi, g4 * 128:(g4 + 1) * 128], lhsT=SSw[wi][:, kind, g4, :], rhs=U[:, gb * 4 + g4, :],
                                           start=True, stop=True)
                    return ins
                P.op("pe", f_v, reads=[R_SW[wi], R_U], writes=[R_PS[pa], R_PS[pb_]])

                def pv(pi):
                    return psum[:, pi, :].rearrange("p (g k) -> p g k", g=4)
                P.op("dve", lambda e: e.tensor_tensor(out=t1[:], in0=pv(pa), in1=SSt[ti][:, 0], op=ALU.mult),
                     reads=[R_PS[pa], R_STB[ti]], writes=[R_T1])
                P.op("dve", lambda e: e.tensor_tensor(out=t2[:], in0=pv(pb_), in1=SSt[ti][:, 1], op=ALU.mult),
                     reads=[R_PS[pb_], R_STB[ti]], writes=[R_T2])
                P.op("dve", lambda e: e.tensor_tensor(out=t1[:], in0=t1[:], in1=t2[:], op=ALU.add), reads=[R_T1, R_T2], writes=[R_T1])
                P.op("dve", lambda e: e.tensor_copy(out=Q[qi][:, :, 0], in_=carry[:, l, gb * 4:(gb + 1) * 4]),
                     reads=[R_CARRY[l][gb]], writes=[R_Q[qi]])
                for g4 in range(4):
                    g = gb * 4 + g4
                    P.op("dve", lambda e, g4=g4, g=g: e.tensor_tensor_scan(
                        out=Q[qi][:, g4, 1:129], data0=Rb_all[:, l, g:g + 1].to_broadcast([128, 128]), data1=t1[:, g4, :],
                        initial=carry[:, l, g:g + 1], op0=ALU.mult, op1=ALU.add),
                        reads=[R_T1, R_RB, R_CARRY[l][gb]], acc=[R_Q[qi]])
                P.op("dve", lambda e: e.tensor_copy(out=carry[:, l, gb * 4:(gb + 1) * 4], in_=Q[qi][:, :, 128]),
                     reads=[R_Q[qi]], writes=[R_CARRY[l][gb]])
                P.op("dve", lambda e: e.tensor_tensor(out=Pc[qi][:], in0=Q[qi][:, :, 0:128], in1=SSt[ti][:, 0], op=ALU.mult),
                     reads=[R_Q[qi], R_STB[ti]], writes=[R_PCS[qi]])
                P.op("dve", lambda e: e.tensor_tensor(out=Ps[qi][:], in0=Q[qi][:, :, 0:128], in1=SSt[ti][:, 1], op=ALU.mult),
                     reads=[R_Q[qi], R_STB[ti]], acc=[R_PCS[qi]])

            def ssm_back(gb):
                wi, qi = slots[gb]
                py = next_ps()

                def f_y(e):
                    ins = None
                    for g4 in range(4):
                        o = psum[:, py, g4 * 128:(g4 + 1) * 128]
                        e.matmul(o, lhsT=U[:, gb * 4 + g4, :], rhs=SSw[wi][:, 0, g4, :], start=True, stop=False)
                        e.matmul(o, lhsT=Pc[qi][:, g4, :], rhs=SSw[wi][:, 3, g4, :], start=False, stop=False)
                        ins = e.matmul(o, lhsT=Ps[qi][:, g4, :], rhs=SSw[wi][:, 4, g4, :], start=False, stop=True)
                    return ins
                P.op("pe", f_y, reads=[R_SW[wi], R_U, R_PCS[qi]], writes=[R_PS[py]])
                P.op("act", lambda e: e.activation(
                    out=ZY[:, :, gb * 64:(gb + 1) * 64].rearrange("p t (g c) -> p g t c", g=4),
                    in_=psum[:, py, :].rearrange("p (g t c) -> p g t c", g=4, t=8), func=AF.Gelu_apprx_tanh),
                    reads=[R_PS[py]], acc=[R_ZY])

            def b3_tile(f):
                pi = next_ps()

                def f_tr(e, f=f, pi=pi):
                    ins = None
                    for tau in range(8):
                        ins = e.transpose(out=psB(pi)[:, tau * 128:(tau + 1) * 128], in_=ZY[:, tau, f * 128:(f + 1) * 128], identity=identB[:])
                    return ins
                P.op("pe", f_tr, reads=[R_ZY, R_CONST], writes=[R_PS[pi]])
                copy_op(evac_eng(), ysT[:, f, :].rearrange("p (k t) -> p t k", t=8), psB(pi).rearrange("p (t k) -> p t k", t=8),
                        [R_PS[pi]], [R_YST])

            wg = None
            LAG = 1
            for s_ in range(8 + LAG):
                if s_ < 8:
                    ssm_front(s_)
                run_fillers(2)
                if s_ == 3:
                    wg = load_ws(wbf_glu[l].rearrange("(a p) n -> p a n", p=128), 4, [R_WBF[l]])
                    P.op("sp", lambda e, wg=wg, l=l: e.dma_start(out=WS[wg][:, 2048:3072], in_=poolW_d[l]),
                         reads=[R_POOLW[l]], acc=[R_WS[wg]], dma=f"d_ws{wg}")
                if s_ >= LAG:
                    ssm_back(s_ - LAG)
                    if (s_ - LAG) % 2 == 1:
                        b3_tile((s_ - LAG) // 2)
            run_fillers(len(fillers))

            pwv = WS[wg][:, 2048:3072].rearrange("p (g k n) -> p g k n", g=4, k=2)
            for n in range(2):
                nr = slice(n * 512, (n + 1) * 512)
                for f in range(4):
                    pi = next_ps()

                    def f_pm(e, f=f, n=n, nr=nr, pi=pi, pwv=pwv):
                        e.matmul(psum[:, pi, :], lhsT=pwv[:, f, 0, :], rhs=spool[:, f, nr], start=True, stop=False)
                        return e.matmul(psum[:, pi, :], lhsT=pwv[:, f, 1, :], rhs=upool[:, f, 16 + n * 512:16 + (n + 1) * 512], start=False, stop=True)
                    P.op("pe", f_pm, reads=[R_WS[wg], R_SPOOL, R_UPOOL], writes=[R_PS[pi]])
                    P.op("dve", lambda e, f=f, nr=nr, pi=pi, l=l: e.scalar_tensor_tensor(
                        out=ycat[:, f, nr], in0=psum[:, pi, :], scalar=smallp[:, l, 8 + f:9 + f], in1=gate[:, f, nr], op0=ALU.mult, op1=ALU.mult),
                        reads=[R_PS[pi], R_GATE[n], R_SMALLP], acc=[R_YCAT[n]])
            gluv = WS[wg][:, 0:2048].rearrange("p (a n) -> p a n", n=512)
            wov = wbf_out[l].rearrange("(a p) n -> p a n", p=128)
            wi_o = [load_ws(wov[:, :, so * 512:(so + 1) * 512], 8, [R_WBF[l]]) for so in range(2)]

            def c1_half(n, l=l, gluv=gluv, wg=wg):
                nr = slice(n * 512, (n + 1) * 512)
                for fo in range(4):
                    pi = next_ps()
                    sgi = fo % 2
                    mm_group(pi, lambda kc, fo=fo: gluv[:, kc, fo * 128:(fo + 1) * 128], lambda kc: ysT[:, kc, nr], 4, [R_WS[wg], R_YST])
                    P.op("act", lambda e, fo=fo, pi=pi, sgi=sgi: e.activation(out=sig[:, sgi, :], in_=psum[:, pi, :], func=AF.Sigmoid,
                                                                             bias=smallp[:, l, 12 + fo:13 + fo], scale=1.0),
                         reads=[R_PS[pi], R_SMALLP], writes=[R_SIG[sgi]])
                    P.op("dve", lambda e, fo=fo, sgi=sgi: e.tensor_tensor(out=sig[:, sgi, :], in0=sig[:, sgi, :], in1=ysT[:, fo, nr], op=ALU.mult),
                         reads=[R_SIG[sgi], R_YST], writes=[R_SIG[sgi]])
                    P.op("dve", lambda e, fo=fo, sgi=sgi: e.tensor_tensor(out=ycat[:, 4 + fo, nr], in0=sig[:, sgi, :], in1=gate[:, 4 + fo, nr], op=ALU.mult),
                         reads=[R_SIG[sgi], R_GATE[n]], acc=[R_YCAT[n]])

            lnext = l + 1 if l + 1 < n_layers else None

            def c2_half(n, wi_o=wi_o, lnext=lnext):
                nr = slice(n * 512, (n + 1) * 512)
                for fo in range(8):
                    if n == 1 and fo == 4:
                        norm_rstd(0)
                    wi, f4 = wi_o[fo // 4], fo % 4
                    pi = next_ps()
                    mm_group(pi, lambda kc, wi=wi, f4=f4: wsv(wi)[:, kc, f4 * 128:(f4 + 1) * 128], lambda kc: ycat[:, kc, nr], 8, [R_WS[wi], R_YCAT[n]])
                    P.op("dve", lambda e, fo=fo, pi=pi: e.tensor_tensor(out=xres[:, fo, nr], in0=xres[:, fo, nr], in1=psum[:, pi, :], op=ALU.add),
                         reads=[R_PS[pi], R_XRES[n]], acc=[R_XRES[n]])
                    norm_square(n, fo)
                    if n == 1 and fo >= 4:
                        norm_apply(0, lnext, [2 * (fo - 4), 2 * (fo - 4) + 1])
            c1_half(0)
            c1_half(1)
            c2_half(0)
            c2_half(1)
            norm_rstd(1)
            norm_apply(1, lnext)

        for tt in range(8):
            si = xs_rr[0]
            xs_rr[0] ^= 1
            for fh in range(2):
                pi = next_ps()

                def f_ot(e, tt=tt, fh=fh, pi=pi):
                    ins = None
                    for f4 in range(4):
                        f = fh * 4 + f4
                        ins = e.transpose(out=psum[:, pi, f4 * 128:(f4 + 1) * 128], in_=xres[:, f, tt * 128:(tt + 1) * 128], identity=identF[:])
                    return ins
                P.op("pe", f_ot, reads=[R_XRES[tt // 4], R_CONST], writes=[R_PS[pi]])
                copy_op(evac_eng(), xs[:, si, fh * 512:(fh + 1) * 512], psum[:, pi, :], [R_PS[pi]], [R_XS[si]])
            P.op("sp", lambda e, si=si, tt=tt, seq=seq, t0=t0: e.dma_start(out=out_d[seq, t0 + tt * 128:t0 + (tt + 1) * 128, :], in_=xs[:, si, :]),
                 reads=[R_XS[si]], acc=[R_OUT], dma=f"d_out{si}")

    P.op("sp", None, reads=[R_OUT], sig=False)
    sems = {k: es.enter_context(nc.semaphore(k)) for k in P.cnt}
    with nc.Block() as block:
        P.emit(nc, block, sems)
    es.close()
    return nc


def prep_inputs(inp):
    f = np.float32
    shared = {}
    shared["w_in"] = np.ascontiguousarray(inp["w_in"], dtype=f)
    shared["w_out"] = np.ascontiguousarray(inp["w_out"], dtype=f)
    shared["glu_w"] = np.ascontiguousarray(inp["glu_w"], dtype=f)
    shared["pool_w"] = np.ascontiguousarray(np.transpose(inp["pool_w"], (0, 2, 1, 3)), dtype=f)
    ng = np.transpose(np.asarray(inp["norm_g"], f).reshape(DEPTH, 8, 128), (2, 0, 1))
    psc = np.transpose(np.asarray(inp["pool_scale"], f).reshape(DEPTH, 4, 128), (2, 0, 1))
    gb = np.transpose(np.asarray(inp["glu_b"], f).reshape(DEPTH, 4, 128), (2, 0, 1))
    shared["smallp"] = np.ascontiguousarray(np.concatenate([ng, psc, gb], axis=2), dtype=f)
    shared["finalg"] = np.ascontiguousarray(np.asarray(inp["final_g"], f).reshape(8, 128).T)

    def dup(a):
        t = np.transpose(np.asarray(a, f), (0, 2, 1))
        return np.concatenate([t, t], axis=1)
    are = dup(inp["a_re"])
    aim = dup(inp["a_im"])
    ldt = np.broadcast_to(np.asarray(inp["log_dt"], f)[:, None, :], (DEPTH, 128, G))
    dv = np.asarray(inp["d_skip"], f).reshape(DEPTH, G, 16)
    dvec = np.broadcast_to(np.transpose(dv, (0, 2, 1))[:, None, :, :], (DEPTH, 8, 16, G)).reshape(DEPTH, 128, G)
    shared["ssm_small"] = np.ascontiguousarray(
        np.transpose(np.stack([are, aim, ldt, dvec], axis=2), (1, 2, 0, 3)).reshape(128, 4, LG), dtype=f)

    def dupb(a):
        t = np.transpose(np.asarray(a, f), (0, 2, 1, 3)).reshape(DEPTH, 64, G * 16)
        return np.concatenate([t, t], axis=1)

    def dupc(a):
        t = np.transpose(np.asarray(a, f), (0, 3, 1, 2)).reshape(DEPTH, 64, G * 16)
        return np.concatenate([t, t], axis=1)
    shared["ssm_bc"] = np.ascontiguousarray(
        np.stack([dupb(inp["b_re"]), dupb(inp["b_im"]), dupc(inp["c_re"]), dupc(inp["c_im"])], axis=2), dtype=f)
    x = np.ascontiguousarray(inp["x"], dtype=f)
    maps = []
    for c in range(NCORES):
        m = dict(shared)
        m["x"] = x[c * NSEQ:(c + 1) * NSEQ]
        maps.append(m)
    return maps


def kernel(**inputs):
    maps = prep_inputs(inputs)
    nc = build_program()
    res = run_bass_kernel_spmd(nc, maps, core_ids=list(range(NCORES)))
    out = np.concatenate([np.asarray(r["out"], dtype=np.float32) for r in res.results], axis=0)
    return out
```

```python
import math
from contextlib import ExitStack

import numpy as np
import concourse.bass as bass
import concourse.mybir as mybir
from concourse.bass_utils import run_bass_kernel_spmd

F32 = mybir.dt.float32
BF16 = mybir.dt.bfloat16
I32 = mybir.dt.int32
AF = mybir.ActivationFunctionType
ALU = mybir.AluOpType

NCORES = 8
DEPTH = 4
D = 1024
SEQ = 2048
NSEQ = 2
ST = 1024
NK = ST // 8
G = 32
LG = DEPTH * G
TWO_PI = float(2.0 * math.pi)
INV_2PI = float(1.0 / (2.0 * math.pi))
HALF_PI = float(math.pi / 2.0)
SIN_SCALE = 1.0 - 4e-5
EPS = 1e-5
POOL_WINDOWS = (2, 4, 8, 16)


class Res:
    __slots__ = ("name", "w", "r")

    def __init__(self, name):
        self.name = name
        self.w = {}
        self.r = {}


class Prog:
    ENG = ("pe", "act", "dve", "pool", "sp")

    def __init__(self):
        self.ops = {e: [] for e in self.ENG}
        self.cnt = {}
        self.seen = {e: {} for e in self.ENG}
        self.pending = {e: {} for e in self.ENG}

    def fence(self):
        for e in self.ENG:
            self.pending[e] = dict(self.cnt)

    def op(self, eng, fn, reads=(), writes=(), dma=None, sig=True, acc=(), after=()):
        need = self.pending[eng]
        self.pending[eng] = {}
        for r in after:
            for k, v in r.w.items():
                if need.get(k, 0) < v:
                    need[k] = v
        for r in acc:
            for k, v in r.r.items():
                if need.get(k, 0) < v:
                    need[k] = v
        for r in reads:
            for k, v in r.w.items():
                if need.get(k, 0) < v:
                    need[k] = v
        for r in writes:
            for k, v in r.w.items():
                if need.get(k, 0) < v:
                    need[k] = v
            for k, v in r.r.items():
                if need.get(k, 0) < v:
                    need[k] = v
        waits = []
        seen = self.seen[eng]
        for k, v in need.items():
            if eng == "pe" and k == "pe":
                continue
            if seen.get(k, 0) >= v:
                continue
            seen[k] = v
            waits.append((k, v))
        tok = None
        inc = 1
        if sig:
            key = dma if dma is not None else eng
            inc = 16 if dma is not None else 1
            self.cnt[key] = self.cnt.get(key, 0) + inc
            tok = (key, self.cnt[key])
            for r in reads:
                if r.r.get(key, 0) < tok[1]:
                    r.r[key] = tok[1]
            for r in writes:
                r.w = {key: tok[1]}
            for r in acc:
                if r.w.get(key, 0) < tok[1]:
                    r.w[key] = tok[1]
        self.ops[eng].append((waits, fn, tok, inc))
        return tok

    def emit(self, nc, block, sems):
        def mk(name):
            def body(e):
                for waits, fn, tok, inc in self.ops[name]:
                    for k, v in waits:
                        e.wait_ge(sems[k], v)
                    if fn is None:
                        continue
                    ins = fn(e)
                    if tok is not None:
                        ins.then_inc(sems[tok[0]], inc)
            return body

        block.tensor(mk("pe"))
        block.scalar(mk("act"))
        block.vector(mk("dve"))
        block.gpsimd(mk("pool"))
        block.sync(mk("sp"))


def build_program(debug=False, n_layers=DEPTH, n_sub=4):
    nc = bass.Bass("TRN2", target_bir_lowering=False)
    P = Prog()
    es = ExitStack()

    def dram_in(name, shape, dt=F32):
        return nc.dram_tensor(name, list(shape), dt, kind="ExternalInput").ap()

    def dram_out(name, shape, dt=F32):
        return nc.dram_tensor(name, list(shape), dt, kind="ExternalOutput").ap()

    def dram_scr(name, shape, dt):
        kind = "ExternalOutput" if debug else "Internal"
        return nc.dram_tensor(name, list(shape), dt, kind=kind).ap()

    def sb(name, shape, dt, stack=None):
        return (stack or es).enter_context(nc.sbuf_tensor(name, list(shape), dt))

    x_d = dram_in("x", [NSEQ, SEQ, D])
    w_in_d = dram_in("w_in", [DEPTH, D, 2 * D])
    w_out_d = dram_in("w_out", [DEPTH, D, D])
    glu_w_d = dram_in("glu_w", [DEPTH, 512, 512])
    pool_w_d = dram_in("pool_w", [DEPTH, 128, 4, 128])
    smallp_d = dram_in("smallp", [128, DEPTH, 16])
    finalg_d = dram_in("finalg", [128, 8])
    ssm_small_d = dram_in("ssm_small", [128, 4, LG])
    ssm_bc_d = dram_in("ssm_bc", [DEPTH, 128, 4, 512])
    out_d = dram_out("out", [NSEQ, SEQ, D])

    ssmW_d = dram_scr("ssmW", [DEPTH, 8, 128, 5 * 4 * 128], BF16)
    tabs_d = dram_scr("tabs", [DEPTH, 8, 128, 2, 4, 256], F32)
    poolW_d = dram_scr("poolW", [DEPTH, 128, 4 * 2 * 128], BF16)
    R_SSMW = [Res(f"ssmW{l}") for l in range(DEPTH)]
    R_TABS = [Res(f"tabs{l}") for l in range(DEPTH)]
    R_POOLW = [Res(f"poolW{l}") for l in range(DEPTH)]
    R_OUT = Res("out")

    identF = sb("identF", [128, 128], F32)
    identB = sb("identB", [128, 128], BF16)
    onesB = sb("onesB", [128, 128], BF16)
    Rb_all = sb("Rb_all", [128, DEPTH, 32], F32)
    smallp = sb("smallp_sb", [128, DEPTH, 16], F32)
    finalg = sb("finalg_sb", [128, 8], F32)
    psum = es.enter_context(nc.psum_tensor("psum", [128, 8, 512], F32))
    R_CONST = Res("const")
    R_RB = Res("Rb")
    R_SMALLP = Res("smallp")
    R_PS = [Res(f"ps{i}") for i in range(8)]
    ps_rr = [0]

    def next_ps():
        i = ps_rr[0]
        ps_rr[0] = (i + 1) % 8
        return i

    P.op("pool", lambda e: e.memset(identF[:], 1.0), writes=[R_CONST])
    P.op("pool", lambda e: e.affine_select(out=identF[:], in_=identF[:], pattern=[[-1, 128]],
                                           compare_op=ALU.is_equal, fill=0.0, base=0, channel_multiplier=1),
         reads=[R_CONST], writes=[R_CONST])
    P.op("pool", lambda e: e.tensor_copy(out=identB[:], in_=identF[:]), reads=[R_CONST], writes=[R_CONST])
    P.op("pool", lambda e: e.memset(onesB[:], 1.0 / 1024.0), writes=[R_CONST])
    P.op("sp", lambda e: e.dma_start(out=smallp[:], in_=smallp_d[:]), writes=[R_SMALLP], dma="d_small")
    P.op("sp", lambda e: e.dma_start(out=finalg[:], in_=finalg_d[:]), writes=[R_SMALLP], dma="d_small")

    with ExitStack() as ps_:
        def pb(name, shape, dt):
            return sb("pl_" + name, shape, dt, ps_)
        mask = pb("mask", [128, 128], F32)
        iotaKi = pb("iotaKi", [128, 256], I32)
        iotaK = pb("iotaK", [128, 256], F32)
        JTi = pb("JTi", [128, 7, LG], I32)
        JT = pb("JT", [128, 7, LG], F32)
        sm = pb("sm", [128, 4, LG], F32)
        bcs = [pb(f"bc{i}", [128, 4, 512], F32) for i in range(2)]
        pws = [pb(f"pw{i}", [128, 4, 128], F32) for i in range(2)]
        PWo = pb("PWo", [128, 4, 2, 128], BF16)
        dtt = pb("dtt", [128, LG], F32)
        ard = pb("ard", [128, LG], F32)
        ang = pb("ang", [128, LG], F32)
        EXre = pb("EXre", [128, LG, 8], F32)
        EXim = pb("EXim", [128, LG, 8], F32)
        EYre = pb("EYre", [128, LG, 8], F32)
        EYim = pb("EYim", [128, LG, 8], F32)
        s1 = pb("s1", [128, LG], F32)
        s2 = pb("s2", [128, LG], F32)
        s3 = pb("s3", [128, LG], F32)
        s4 = pb("s4", [128, LG], F32)
        fre = pb("fre", [128, LG], F32)
        fim = pb("fim", [128, LG], F32)
        Rt = pb("Rt", [128, LG], F32)
        TH = pb("TH", [128, LG], F32)
        NI8 = pb("NI8", [128, LG], I32)
        bbA = pb("bbA", [128, 32, 16], F32)
        bbB = pb("bbB", [128, 32, 16], F32)
        cA = pb("cA", [128, 32, 16], F32)
        cB = pb("cB", [128, 32, 16], F32)
        Smat = pb("Smat", [128, 128], F32)
        hpiT = pb("hpiT", [128, 1], F32)
        T1 = pb("T1", [128, 32, 8, 16], F32)
        T2 = pb("T2", [128, 32, 8, 16], F32)

        def jview(t, k):
            return t[:].rearrange("p g t c -> p (g t c)")[:, k * 7 * LG:(k + 1) * 7 * LG].rearrange("p (j n) -> p j n", j=7)
        def bview(t, k):
            return t[:].rearrange("p g t c -> p (g t c)")[:, k * 512:(k + 1) * 512].rearrange("p (g c) -> p g c", c=16)
        ta, tb_ = bview(T1, 0), bview(T1, 1)
        tc_, td = bview(T2, 0), bview(T2, 1)
        ARJ, ANJ, MAGP, MAGN = (jview(T1, k) for k in range(4))
        NIj = JTi[:]
        RRj, SN, CS = (jview(T2, k) for k in range(1, 4))
        XBm = pb("XBm", [128, 32, 8, 16], F32)
        Gm = pb("Gm", [128, 32, 8, 16], F32)
        W5 = pb("W5", [128, 4, 5, 4, 128], BF16)
        tmpT = [pb(f"tmpT{i}", [128, 4, 128], F32) for i in range(2)]
        NTB = 2
        ANGq = [pb(f"ANGq{i}", [128, 4, 256], F32) for i in range(NTB)]
        NIq = [pb(f"NIq{i}", [128, 4, 256], I32) for i in range(NTB)]
        RRq = [pb(f"RRq{i}", [128, 4, 256], F32) for i in range(NTB)]
        ABq = [ANGq[i][:] for i in range(NTB)]
        CSq = [pb(f"CSq{i}", [128, 2, 4, 256], F32) for i in range(NTB)]
        R_TQ = [Res(f"pl_tq{i}") for i in range(NTB)]
        R_CSQ = [Res(f"pl_csq{i}") for i in range(NTB)]
        R_BB = Res("pl_bb")

        R_PC = Res("pl_const")
        R_INS = [Res("pl_in0"), Res("pl_in1")]
        R_PWS = [Res("pl_pw0"), Res("pl_pw1")]
        R_A = Res("pl_a")
        R_E = Res("pl_E")
        R_X = Res("pl_X")
        R_T = Res("pl_T")
        R_W5 = Res("pl_W5")
        R_TMPT = [Res("pl_tmpT0"), Res("pl_tmpT1")]
        R_ANG = Res("pl_ang")
        R_TAB = Res("pl_tab")
        R_PWO = Res("pl_pwo")

        P.op("pool", lambda e: e.memset(mask[:], 1.0), writes=[R_PC])
        P.op("pool", lambda e: e.affine_select(out=mask[:], in_=mask[:], pattern=[[16, 8], [0, 16]],
                                               compare_op=ALU.is_ge, fill=0.0, base=15, channel_multiplier=-1),
             reads=[R_PC], writes=[R_PC])
        P.op("pool", lambda e: e.iota(iotaKi[:], pattern=[[1, 256]], base=0, channel_multiplier=0), writes=[R_PC])
        P.op("pool", lambda e: e.tensor_copy(out=iotaK[:], in_=iotaKi[:]), reads=[R_PC], writes=[R_PC])
        P.op("pool", lambda e: e.iota(JTi[:], pattern=[[-1, 7], [0, LG]], base=7, channel_multiplier=0), writes=[R_PC])
        P.op("pool", lambda e: e.tensor_copy(out=JT[:], in_=JTi[:]), reads=[R_PC], writes=[R_PC])
        P.op("pool", lambda e: e.tensor_copy(out=Smat[:, 0:64], in_=identF[:, 64:128]), reads=[R_CONST], writes=[R_PC])
        P.op("pool", lambda e: e.tensor_scalar(out=Smat[:, 64:128], in0=identF[:, 0:64], scalar1=-1.0, scalar2=None, op0=ALU.mult),
             reads=[R_CONST, R_PC], writes=[R_PC])
        P.op("pool", lambda e: e.memset(hpiT[:], HALF_PI), reads=[R_PC], writes=[R_PC])

        def bc3(ap32):
            return ap32.unsqueeze(1).to_broadcast([128, 7, LG])

        def bcc(ap32):
            return ap32.unsqueeze(2).to_broadcast([128, 32, 16])

        def bE(apE, lo, hi):
            return apE[lo:hi].unsqueeze(3).to_broadcast([hi - lo, 32, 8, 16])

        def bB(apB, lo, hi):
            return apB[lo:hi].unsqueeze(2).to_broadcast([hi - lo, 32, 8, 16])

        def bR(apR, lo, hi):
            return apR[lo:hi].rearrange("p (a g) -> p a g", g=4).unsqueeze(3).to_broadcast([hi - lo, 8, 4, 128])

        wbf_in = dram_scr("wbf_in", [DEPTH, D, 2 * D], BF16)
        wbf_out = dram_scr("wbf_out", [DEPTH, D, D], BF16)
        wbf_glu = dram_scr("wbf_glu", [DEPTH, 512, 512], BF16)
        R_WBF = [Res(f"wbf{l}") for l in range(DEPTH)]
        NCQ = 4
        R_CASTQ = [Res(f"castq{i}") for i in range(NCQ)]
        cq = [0]

        def cast_dma(dst, src, l, reads=()):
            j = cq[0] % NCQ
            cq[0] += 1
            P.op("pool", lambda e: e.dma_start(out=dst, in_=src), after=list(reads), writes=[R_CASTQ[j]], acc=[R_WBF[l]], dma=f"d_cast{j}")
        def cast_jobs_for(l):
            jobs = []
            for a in range(8):
                jobs.append(lambda reads=(), a=a: cast_dma(wbf_in[l, a * 128:(a + 1) * 128, :], w_in_d[l, a * 128:(a + 1) * 128, :], l, reads))
            for a in range(4):
                jobs.append(lambda reads=(), a=a: cast_dma(wbf_out[l, a * 256:(a + 1) * 256, :], w_out_d[l, a * 256:(a + 1) * 256, :], l, reads))
            jobs.append(lambda reads=(): cast_dma(wbf_glu[l], glu_w_d[l], l, reads))
            return jobs
        for job in cast_jobs_for(0):
            job()

        R_SM = Res("pl_sm")
        P.op("sp", lambda e: e.dma_start(out=sm[:], in_=ssm_small_d[:]), writes=[R_SM], dma="d_plsm")
        are, aim, ldt = sm[:, 0, :], sm[:, 1, :], sm[:, 2, :]
        P.op("act", lambda e: e.activation(out=dtt[:], in_=ldt, func=AF.Exp), reads=[R_SM], writes=[R_A])
        P.op("dve", lambda e: e.tensor_tensor(out=ard[:], in0=are, in1=dtt[:], op=ALU.mult), reads=[R_SM, R_A], writes=[R_A])
        P.op("dve", lambda e: e.tensor_tensor(out=ang[:], in0=aim, in1=dtt[:], op=ALU.mult), reads=[R_SM, R_A], writes=[R_A])
        P.op("dve", lambda e: e.tensor_tensor(out=ARJ, in0=JT[:], in1=bc3(ard[:]), op=ALU.mult), reads=[R_PC, R_A], writes=[R_A])
        P.op("dve", lambda e: e.tensor_tensor(out=ANJ, in0=JT[:], in1=bc3(ang[:]), op=ALU.mult), reads=[R_PC, R_A], writes=[R_A])
        P.op("act", lambda e: e.activation(out=MAGP, in_=ARJ, func=AF.Exp), reads=[R_A], writes=[R_A])
        P.op("act", lambda e: e.activation(out=MAGN, in_=ARJ, func=AF.Exp, scale=-1.0), reads=[R_A], writes=[R_A])
        P.op("act", lambda e: e.activation(out=NIj, in_=ANJ, func=AF.Copy, scale=INV_2PI), reads=[R_A], writes=[R_A])
        P.op("dve", lambda e: e.scalar_tensor_tensor(out=RRj, in0=NIj, scalar=-TWO_PI, in1=ANJ, op0=ALU.mult, op1=ALU.add),
             reads=[R_A], writes=[R_A])
        P.op("act", lambda e: e.activation(out=SN, in_=RRj, func=AF.Sin, scale=SIN_SCALE), reads=[R_A], writes=[R_A])
        P.op("dve", lambda e: e.tensor_scalar(out=RRj, in0=ANJ, scalar1=HALF_PI, scalar2=None, op0=ALU.add), reads=[R_A], writes=[R_A])
        P.op("act", lambda e: e.activation(out=NIj, in_=RRj, func=AF.Copy, scale=INV_2PI), reads=[R_A], writes=[R_A])
        P.op("dve", lambda e: e.scalar_tensor_tensor(out=RRj, in0=NIj, scalar=-TWO_PI, in1=RRj, op0=ALU.mult, op1=ALU.add),
             reads=[R_A], writes=[R_A])
        P.op("act", lambda e: e.activation(out=CS, in_=RRj, func=AF.Sin, scale=SIN_SCALE), reads=[R_A], writes=[R_A])
        def Ev(t):
            return t[:].rearrange("p g t -> p t g")[:, 0:7, :]
        P.op("dve", lambda e: e.tensor_tensor(out=Ev(EXre), in0=MAGP, in1=CS, op=ALU.mult), reads=[R_A], writes=[R_E])
        P.op("dve", lambda e: e.tensor_tensor(out=Ev(EXim), in0=MAGP, in1=SN, op=ALU.mult), reads=[R_A], writes=[R_E])
        P.op("dve", lambda e: e.tensor_tensor(out=Ev(EYre), in0=MAGN, in1=CS, op=ALU.mult), reads=[R_A], writes=[R_E])
        P.op("dve", lambda e: e.scalar_tensor_tensor(out=Ev(EYim), in0=MAGN, scalar=-1.0, in1=SN, op0=ALU.mult, op1=ALU.mult),
             reads=[R_A], writes=[R_E])
        P.op("dve", lambda e: e.memset(EXre[:, :, 7:8], 1.0), writes=[R_E])
        P.op("dve", lambda e: e.memset(EXim[:, :, 7:8], 0.0), writes=[R_E])
        P.op("dve", lambda e: e.memset(EYre[:, :, 7:8], 1.0), writes=[R_E])
        P.op("dve", lambda e: e.memset(EYim[:, :, 7:8], 0.0), writes=[R_E])
        lre, lim = EXre[:, :, 6], EXim[:, :, 6]
        P.op("dve", lambda e: e.tensor_scalar(out=s1[:], in0=lre, scalar1=-1.0, scalar2=None, op0=ALU.add), reads=[R_E], writes=[R_A])
        P.op("dve", lambda e: e.tensor_tensor(out=s2[:], in0=are, in1=are, op=ALU.mult), reads=[R_SM], writes=[R_A])
        P.op("dve", lambda e: e.tensor_tensor(out=s3[:], in0=aim, in1=aim, op=ALU.mult), reads=[R_SM], writes=[R_A])
        P.op("dve", lambda e: e.tensor_tensor(out=s2[:], in0=s2[:], in1=s3[:], op=ALU.add), reads=[R_A], writes=[R_A])
        P.op("dve", lambda e: e.reciprocal(out=s2[:], in_=s2[:]), reads=[R_A], writes=[R_A])
        P.op("dve", lambda e: e.tensor_tensor(out=s3[:], in0=s1[:], in1=are, op=ALU.mult), reads=[R_A, R_SM], writes=[R_A])
        P.op("dve", lambda e: e.tensor_tensor(out=s4[:], in0=lim, in1=aim, op=ALU.mult), reads=[R_E, R_SM], writes=[R_A])
        P.op("dve", lambda e: e.tensor_tensor(out=s3[:], in0=s3[:], in1=s4[:], op=ALU.add), reads=[R_A], writes=[R_A])
        P.op("dve", lambda e: e.tensor_tensor(out=fre[:], in0=s3[:], in1=s2[:], op=ALU.mult), reads=[R_A], writes=[R_A])
        P.op("dve", lambda e: e.tensor_tensor(out=s3[:], in0=lim, in1=are, op=ALU.mult), reads=[R_E, R_SM], writes=[R_A])
        P.op("dve", lambda e: e.tensor_tensor(out=s4[:], in0=s1[:], in1=aim, op=ALU.mult), reads=[R_A, R_SM], writes=[R_A])
        P.op("dve", lambda e: e.tensor_tensor(out=s3[:], in0=s3[:], in1=s4[:], op=ALU.subtract), reads=[R_A], writes=[R_A])
        P.op("dve", lambda e: e.tensor_tensor(out=fim[:], in0=s3[:], in1=s2[:], op=ALU.mult), reads=[R_A], writes=[R_A])
        P.op("act", lambda e: e.activation(out=Rt[:], in_=ard[:], func=AF.Exp, scale=8.0), reads=[R_A], writes=[R_A])
        P.op("act", lambda e: e.activation(out=Rb_all[:].rearrange("p l g -> p (l g)"), in_=Rt[:], func=AF.Copy), reads=[R_A], writes=[R_RB])
        P.op("dve", lambda e: e.tensor_scalar(out=s1[:], in0=ang[:], scalar1=8.0, scalar2=None, op0=ALU.mult), reads=[R_A], writes=[R_A])
        P.op("act", lambda e: e.activation(out=NI8[:], in_=s1[:], func=AF.Copy, scale=INV_2PI), reads=[R_A], writes=[R_A])
        P.op("dve", lambda e: e.scalar_tensor_tensor(out=TH[:], in0=NI8[:], scalar=-TWO_PI, in1=s1[:], op0=ALU.mult, op1=ALU.add),
             reads=[R_A], writes=[R_A])


        for l in range(n_layers):
            def pl_loads(ll):
                P.op("sp", lambda e: e.dma_start(out=bcs[ll % 2][:], in_=ssm_bc_d[ll]), writes=[R_INS[ll % 2]], dma=f"d_plin{ll % 2}")
                P.op("sp", lambda e: e.dma_start(out=pws[ll % 2][:], in_=pool_w_d[ll]), writes=[R_PWS[ll % 2]], dma=f"d_plpw{ll % 2}")
            if l == 0:
                pl_loads(0)
            if l + 1 < n_layers:
                pl_loads(l + 1)
            bc, pw, R_IN, R_PW = bcs[l % 2], pws[l % 2], R_INS[l % 2], R_PWS[l % 2]
            ls = slice(l * 32, (l + 1) * 32)
            fre_l, fim_l = fre[:, ls], fim[:, ls]
            EXre_l, EXim_l, EYre_l, EYim_l = EXre[:, ls, :], EXim[:, ls, :], EYre[:, ls, :], EYim[:, ls, :]
            bre = bc[:, 0, :].rearrange("p (g c) -> p g c", c=16)
            bim = bc[:, 1, :].rearrange("p (g c) -> p g c", c=16)
            cre = bc[:, 2, :].rearrange("p (g c) -> p g c", c=16)
            cim = bc[:, 3, :].rearrange("p (g c) -> p g c", c=16)

            for g4 in range(4):
                P.op("act", lambda e, g4=g4, pw=pw: e.activation(out=PWo[:, g4, 0, :], in_=pw[:, g4, :], func=AF.Copy, scale=1.0 / POOL_WINDOWS[g4]),
                     reads=[R_PW], acc=[R_PWO])
            P.op("act", lambda e, pw=pw: e.activation(out=PWo[:, :, 1, :], in_=pw[:, :, :], func=AF.Copy, scale=-1.0), reads=[R_PW], acc=[R_PWO])
            P.op("sp", lambda e, l=l: e.dma_start(out=poolW_d[l], in_=PWo[:].rearrange("p a b c -> p (a b c)")),
                 reads=[R_PWO], writes=[R_POOLW[l]], dma=f"d_plpo{l}")

            P.op("dve", lambda e, fre_l=fre_l, fim_l=fim_l, bre=bre, bim=bim: e.tensor_tensor(out=ta, in0=bre, in1=bcc(fre_l), op=ALU.mult), reads=[R_A, R_IN], writes=[R_BB], acc=[R_T])
            P.op("dve", lambda e, fre_l=fre_l, fim_l=fim_l, bre=bre, bim=bim: e.tensor_tensor(out=tb_, in0=bim, in1=bcc(fim_l), op=ALU.mult), reads=[R_A, R_IN], acc=[R_BB])
            P.op("dve", lambda e, fre_l=fre_l, fim_l=fim_l, bre=bre, bim=bim: e.tensor_tensor(out=tc_, in0=bim, in1=bcc(fre_l), op=ALU.mult), reads=[R_A, R_IN], acc=[R_BB])
            P.op("dve", lambda e, fre_l=fre_l, fim_l=fim_l, bre=bre, bim=bim: e.tensor_tensor(out=td, in0=bre, in1=bcc(fim_l), op=ALU.mult), reads=[R_A, R_IN], acc=[R_BB])
            P.op("dve", lambda e: e.tensor_tensor(out=bbA[0:64], in0=ta[0:64], in1=tb_[0:64], op=ALU.subtract), reads=[R_BB], acc=[R_BB])
            P.op("dve", lambda e: e.tensor_tensor(out=bbA[64:128], in0=tc_[64:128], in1=td[64:128], op=ALU.add), reads=[R_BB], acc=[R_BB])
            P.op("dve", lambda e: e.scalar_tensor_tensor(out=bbB[0:64], in0=tc_[0:64], scalar=-1.0, in1=td[0:64], op0=ALU.mult, op1=ALU.subtract),
                 reads=[R_BB], acc=[R_BB])
            P.op("dve", lambda e: e.tensor_tensor(out=bbB[64:128], in0=ta[64:128], in1=tb_[64:128], op=ALU.subtract), reads=[R_BB], acc=[R_BB])
            P.op("act", lambda e, cre=cre, cim=cim: e.activation(out=cA[0:64], in_=cre[0:64], func=AF.Copy), reads=[R_IN], acc=[R_BB])
            P.op("act", lambda e, cre=cre, cim=cim: e.activation(out=cA[64:128], in_=cim[64:128], func=AF.Copy, scale=-1.0), reads=[R_IN], acc=[R_BB])
            P.op("act", lambda e, cre=cre, cim=cim: e.activation(out=cB[0:64], in_=cim[0:64], func=AF.Copy, scale=-1.0), reads=[R_IN], acc=[R_BB])
            P.op("act", lambda e, cre=cre, cim=cim: e.activation(out=cB[64:128], in_=cre[64:128], func=AF.Copy, scale=-1.0), reads=[R_IN], acc=[R_BB])
            tv = tabs_d[l].rearrange("a p two g k -> p a two g k")
            tq_rr = [0]

            def table_front(gb, l=l):
                i = tq_rr[0]
                tq_rr[0] = (i + 1) % NTB
                gs = slice(gb * 4, gb * 4 + 4)
                P.op("dve", lambda e: e.tensor_tensor(out=ANGq[i][:], in0=TH[:, l * 32 + gb * 4:l * 32 + gb * 4 + 4].unsqueeze(2).to_broadcast([128, 4, 256]),
                                                     in1=iotaK[:].unsqueeze(1).to_broadcast([128, 4, 256]), op=ALU.mult),
                     reads=[R_A, R_PC], writes=[R_TQ[i]])
                P.op("act", lambda e: e.activation(out=NIq[i][:], in_=ANGq[i][:], func=AF.Copy, scale=INV_2PI), reads=[R_TQ[i]], acc=[R_TQ[i]])
                return i

            def table_rest(gb, i, l=l, tv=tv):
                P.op("dve", lambda e: e.scalar_tensor_tensor(out=RRq[i][:], in0=NIq[i][:], scalar=-TWO_PI, in1=ANGq[i][:], op0=ALU.mult, op1=ALU.add),
                     reads=[R_TQ[i]], acc=[R_TQ[i]])
                P.op("act", lambda e: e.activation(out=ABq[i], in_=RRq[i][:], func=AF.Abs), reads=[R_TQ[i]], acc=[R_TQ[i]])
                P.op("act", lambda e: e.activation(out=CSq[i][:, 1], in_=RRq[i][:], func=AF.Sin, scale=SIN_SCALE), reads=[R_TQ[i]], writes=[R_CSQ[i]])
                P.op("act", lambda e: e.activation(out=CSq[i][:, 0], in_=ABq[i], func=AF.Sin, scale=-1.0, bias=hpiT[:]),
                     reads=[R_TQ[i], R_PC], acc=[R_CSQ[i]])
                P.op("sp", lambda e: e.dma_start(out=tv[:, gb], in_=CSq[i][:]), reads=[R_CSQ[i]], acc=[R_TABS[l]], dma=f"d_csq{i}")

            bigops = [
                lambda: P.op("dve", lambda e, EXre_l=EXre_l, EXim_l=EXim_l, EYre_l=EYre_l, EYim_l=EYim_l: e.tensor_tensor(out=T1[:], in0=bE(EXre_l, 0, 128), in1=bB(bbA, 0, 128), op=ALU.mult), reads=[R_E, R_BB], writes=[R_T]),
                lambda: P.op("dve", lambda e, EXre_l=EXre_l, EXim_l=EXim_l, EYre_l=EYre_l, EYim_l=EYim_l: e.tensor_tensor(out=T2[:], in0=bE(EXim_l, 0, 128), in1=bB(bbB, 0, 128), op=ALU.mult), reads=[R_E, R_BB], acc=[R_T]),
                lambda: P.op("dve", lambda e: e.tensor_tensor(out=XBm[:], in0=T1[:], in1=T2[:], op=ALU.add), reads=[R_T], writes=[R_X]),
                lambda: P.op("dve", lambda e, EXre_l=EXre_l, EXim_l=EXim_l, EYre_l=EYre_l, EYim_l=EYim_l: e.tensor_tensor(out=T1[:], in0=bE(EYre_l, 0, 128), in1=bB(cA, 0, 128), op=ALU.mult), reads=[R_E, R_BB], writes=[R_T]),
                lambda: P.op("dve", lambda e, EXre_l=EXre_l, EXim_l=EXim_l, EYre_l=EYre_l, EYim_l=EYim_l: e.tensor_tensor(out=T2[:], in0=bE(EYim_l, 0, 128), in1=bB(cB, 0, 128), op=ALU.mult), reads=[R_E, R_BB], acc=[R_T]),
                lambda: P.op("dve", lambda e: e.tensor_tensor(out=Gm[:], in0=T1[:], in1=T2[:], op=ALU.add), reads=[R_T], acc=[R_X]),
            ]
            for i_, bo in enumerate(bigops):
                ti_ = table_front(i_)
                bo()
                table_rest(i_, ti_)

            for gb in range(8):
                if gb in (2, 5):
                    tp_gb = 6 + (gb == 5)
                    tp_i = table_front(tp_gb)
                ti = gb % 2
                pi = next_ps()

                def f_toep(e, gb=gb, pi=pi):
                    ins = None
                    for g4 in range(4):
                        g = gb * 4 + g4
                        ins = e.matmul(psum[:, pi, g4 * 128:(g4 + 1) * 128],
                                       lhsT=XBm[:, g].rearrange("p t c -> p (t c)"),
                                       rhs=Gm[:, g].rearrange("p t c -> p (t c)"), start=True, stop=True)
                    return ins
                P.op("pe", f_toep, reads=[R_X], writes=[R_PS[pi]])
                P.op("dve", lambda e, pi=pi, ti=ti: e.tensor_tensor(out=tmpT[ti][:], in0=psum[:, pi, :].rearrange("p (g n) -> p g n", g=4),
                                                                   in1=mask[:].unsqueeze(1).to_broadcast([128, 4, 128]), op=ALU.mult),
                     reads=[R_PS[pi], R_PC], writes=[R_TMPT[ti]])
                for g4 in range(4):
                    P.op("dve", lambda e, gb=gb, g4=g4, ti=ti, l=l: e.scalar_tensor_tensor(
                        out=W5[:, gb % 4, 0, g4, :], in0=identF[:], scalar=sm[:, 3, l * 32 + gb * 4 + g4:l * 32 + gb * 4 + g4 + 1], in1=tmpT[ti][:, g4, :],
                        op0=ALU.mult, op1=ALU.add), reads=[R_TMPT[ti], R_CONST, R_SM], acc=[R_W5])
                if gb in (2, 5):
                    table_rest(tp_gb, tp_i)
                pi2 = next_ps()

                def f_tr(e, gb=gb, pi2=pi2):
                    ins = None
                    for g4 in range(4):
                        g = gb * 4 + g4
                        ins = e.transpose(out=psum[:, pi2, g4 * 128:(g4 + 1) * 128],
                                          in_=XBm[:, g].rearrange("p t c -> p (t c)"), identity=identF[:])
                    return ins
                P.op("pe", f_tr, reads=[R_X, R_CONST], writes=[R_PS[pi2]])
                pv = psum[:, pi2, :].rearrange("p (g n) -> p g n", g=4)
                P.op("act", lambda e, gb=gb, pv=pv: e.activation(out=W5[:, gb % 4, 1, :, :], in_=pv, func=AF.Copy),
                     reads=[R_PS[pi2]], acc=[R_W5])
                P.op("act", lambda e, gb=gb, pv=pv: e.activation(out=W5[:, gb % 4, 2, :, 0:64], in_=pv[:, :, 64:128], func=AF.Copy),
                     reads=[R_PS[pi2]], acc=[R_W5])
                P.op("act", lambda e, gb=gb, pv=pv: e.activation(out=W5[:, gb % 4, 2, :, 64:128], in_=pv[:, :, 0:64], func=AF.Copy, scale=-1.0),
                     reads=[R_PS[pi2]], acc=[R_W5])
                pi3 = next_ps()

                def f_sw(e, gb=gb, pi3=pi3):
                    ins = None
                    for g4 in range(4):
                        g = gb * 4 + g4
                        ins = e.matmul(psum[:, pi3, g4 * 128:(g4 + 1) * 128], lhsT=Smat[:],
                                       rhs=Gm[:, g].rearrange("p t c -> p (t c)"), start=True, stop=True)
                    return ins
                P.op("pe", f_sw, reads=[R_X, R_PC], writes=[R_PS[pi3]])
                for g4 in range(4):
                    g = gb * 4 + g4
                    P.op("dve", lambda e, gb=gb, g4=g4, g=g, l=l: e.tensor_scalar(out=W5[:, gb % 4, 3, g4, :], in0=Gm[:, g].rearrange("p t c -> p (t c)"),
                                                                            scalar1=Rt[:, l * 32 + g:l * 32 + g + 1], scalar2=None, op0=ALU.mult),
                         reads=[R_X, R_A], acc=[R_W5])
                    P.op("act", lambda e, gb=gb, g4=g4, g=g, pi3=pi3, l=l: e.activation(out=W5[:, gb % 4, 4, g4, :], in_=psum[:, pi3, g4 * 128:(g4 + 1) * 128],
                                                                                  func=AF.Copy, scale=Rt[:, l * 32 + g:l * 32 + g + 1]),
                         reads=[R_PS[pi3], R_A], acc=[R_W5])
                if gb % 4 == 3:
                    hb = gb // 4
                    P.op("sp", lambda e, l=l, hb=hb: e.dma_start(out=ssmW_d[l, hb * 4:(hb + 1) * 4].rearrange("a p n -> p a n"),
                                                             in_=W5[:].rearrange("p a k g n -> p a (k g n)")),
                         reads=[R_W5], acc=[R_SSMW[l]], dma=f"d_plssm{l}_{hb}")


    if debug == "prologue":
        rb_d = dram_out("rb_dbg", [128, DEPTH, 32])
        P.op("sp", lambda e: e.dma_start(out=rb_d[:], in_=Rb_all[:]), reads=[R_RB], writes=[R_OUT], dma="d_out")
        fin = [R_OUT] + R_SSMW[:n_layers] + R_TABS[:n_layers] + R_POOLW[:n_layers] + R_WBF[:n_layers]
        P.op("sp", None, reads=fin, sig=False)
        sems = {k: es.enter_context(nc.semaphore(k)) for k in P.cnt}
        with nc.Block() as block:
            P.emit(nc, block, sems)
        es.close()
        return nc

    P.fence()
    xres = sb("xres", [128, 8, ST], F32)
    h = sb("h", [128, 8, ST], BF16)
    gate = sb("gate", [128, 8, ST], BF16)
    ycat = sb("ycat", [128, 8, ST], BF16)
    upool = sb("upool", [128, 4, 16 + ST], BF16)
    spool = sb("spool", [128, 4, ST], BF16)
    ysT = sb("ysT", [128, 4, ST], BF16)
    xsq0 = spool[:].rearrange("p a n -> p (a n)").rearrange("p (a n) -> p a n", n=512)
    xsq1 = upool[:].rearrange("p a n -> p (a n)")[:, 0:4096].rearrange("p (a n) -> p a n", n=512)
    xsqs = [xsq0, xsq1]
    ZYf = sb("ZY", [128, 4096], BF16)
    ZY = ZYf[:].rearrange("p (t f) -> p t f", t=8)
    Zs = ZYf[:].rearrange("p (g t c) -> p g t c", g=32, t=8)
    U = sb("U", [128, 32, 128], BF16)
    rs = sb("rs", [128, 2, 512], F32)
    t1 = sb("t1", [128, 4, 128], F32)
    t2 = sb("t2", [128, 4, 128], F32)
    NQ = 2
    Q = [sb(f"Q{i}", [128, 4, 129], F32) for i in range(NQ)]
    Pc = [sb(f"Pc{i}", [128, 4, 128], BF16) for i in range(NQ)]
    Ps = [sb(f"Ps{i}", [128, 4, 128], BF16) for i in range(NQ)]
    sig = sb("sig", [128, 2, 512], BF16)
    xs = sb("xs", [128, 2, D], F32)
    pwk = xs[:].rearrange("p a n -> p (a n)").bitcast(BF16)[:, 0:3 * (16 + ST)].rearrange("p (a n) -> p a n", a=3)
    carry = sb("carry", [128, DEPTH, 32], F32)
    halo = sb("halo", [128, DEPTH, 4, 16], BF16)
    cfix = sb("cfix", [128, 4, 16], F32)
    epsT = sb("epsT", [128, 1], F32)
    NWS, NSW, NSTB = 4, 3, 2
    WS = [sb(f"WS{i}", [128, 8 * 512], BF16) for i in range(NWS)]
    SSw = [sb(f"SSw{i}", [128, 5, 4, 128], BF16) for i in range(NSW)]
    SSt = [sb(f"SSt{i}", [128, 2, 4, 128], F32) for i in range(NSTB)]
    R_WS = [Res(f"WS{i}") for i in range(NWS)]
    R_SW = [Res(f"SW{i}") for i in range(NSW)]
    R_STB = [Res(f"STB{i}") for i in range(NSTB)]
    R_ZY, R_U = Res("ZY"), Res("U")
    R_XRES = [Res("xres0"), Res("xres1")]
    R_RS = [Res("rs0"), Res("rs1")]
    R_H = [Res("h0"), Res("h1")]
    R_GATE = [Res("g0"), Res("g1")]
    R_YCAT = [Res("yc0"), Res("yc1")]
    R_UPOOL, R_SPOOL, R_YST = Res("upool"), Res("spool"), Res("ysT")
    R_T1, R_T2 = Res("t1"), Res("t2")
    R_Q = [Res(f"Q{i}") for i in range(NQ)]
    R_PCS = [Res(f"PcPs{i}") for i in range(NQ)]
    R_SIG = [Res("sig0"), Res("sig1")]
    R_XS = [Res("xs0"), Res("xs1")]
    R_CARRY = [[Res(f"carry{l}_{gb}") for gb in range(8)] for l in range(DEPTH)]
    R_HALO = [Res(f"halo{l}") for l in range(DEPTH)]
    R_C2 = Res("const2")
    ws_rr, sw_rr, stb_rr, xs_rr, ev_rr = [0], [0], [0], [0], [0]

    def psB(pi):
        return psum[:, pi, :].bitcast(BF16)

    def evac_eng():
        ev_rr[0] ^= 1
        return "act" if ev_rr[0] else "dve"

    def copy_op(eng, out, in_, reads, acc):
        if eng == "act":
            P.op("act", lambda e: e.activation(out=out, in_=in_, func=AF.Copy), reads=reads, acc=acc)
        else:
            P.op(eng, lambda e: e.tensor_copy(out=out, in_=in_), reads=reads, acc=acc)

    P.op("pool", lambda e: e.memset(epsT[:], EPS), writes=[R_C2])
    P.op("pool", lambda e: e.memset(cfix[:], 1.0), writes=[R_C2])
    for f in range(4):
        w = POOL_WINDOWS[f]
        for t in range(w - 1):
            P.op("pool", lambda e, f=f, t=t, w=w: e.memset(cfix[:, f, t:t + 1], float(w) / float(t + 1)), writes=[R_C2])

    def load_ws(src_ap, nparts, reads):
        i = ws_rr[0]
        ws_rr[0] = (i + 1) % NWS
        dst = WS[i][:, 0:nparts * 512].rearrange("p (a n) -> p a n", n=512)
        P.op("sp", lambda e: e.dma_start(out=dst, in_=src_ap), reads=reads, writes=[R_WS[i]], dma=f"d_ws{i}")
        return i

    def wsv(i):
        return WS[i][:].rearrange("p (a n) -> p a n", n=512)

    def mm_group(pi, lhs_fn, rhs_fn, nk, reads):
        def f(e):
            ins = None
            for kc in range(nk):
                ins = e.matmul(psum[:, pi, :], lhsT=lhs_fn(kc), rhs=rhs_fn(kc), start=(kc == 0), stop=(kc == nk - 1))
            return ins
        P.op("pe", f, reads=reads, writes=[R_PS[pi]])

    def R_XSQ(n):
        return R_SPOOL if n == 0 else R_UPOOL

    def norm_square(n, f=None):
        nr = slice(n * 512, (n + 1) * 512)
        if f is None:
            P.op("act", lambda e: e.activation(out=xsqs[n], in_=xres[:, :, nr], func=AF.Square), reads=[R_XRES[n]], writes=[R_XSQ(n)])
        elif f == 0:
            P.op("act", lambda e: e.activation(out=xsqs[n][:, 0, :], in_=xres[:, 0, nr], func=AF.Square), reads=[R_XRES[n]], writes=[R_XSQ(n)])
        else:
            P.op("act", lambda e: e.activation(out=xsqs[n][:, f, :], in_=xres[:, f, nr], func=AF.Square), reads=[R_XRES[n]], acc=[R_XSQ(n)])

    def norm_rstd(n):
        pi = next_ps()
        mm_group(pi, lambda kc: onesB[:], lambda kc: xsqs[n][:, kc, :], 8, [R_XSQ(n), R_CONST])
        P.op("act", lambda e, pi=pi: e.activation(out=rs[:, n, :], in_=psum[:, pi, :], func=AF.Ln, bias=epsT[:], scale=1.0),
             reads=[R_PS[pi], R_C2], writes=[R_RS[n]])
        P.op("act", lambda e: e.activation(out=rs[:, n, :], in_=rs[:, n, :], func=AF.Exp, scale=-0.5), reads=[R_RS[n]], writes=[R_RS[n]])

    def norm_apply(n, lnext, fs=range(8)):
        nr = slice(n * 512, (n + 1) * 512)
        for f in fs:
            if lnext is None:
                P.op("dve", lambda e, f=f: e.scalar_tensor_tensor(
                    out=xres[:, f, nr], in0=xres[:, f, nr], scalar=finalg[:, f:f + 1], in1=rs[:, n, :], op0=ALU.mult, op1=ALU.mult),
                    reads=[R_RS[n], R_SMALLP, R_XRES[n]], acc=[R_XRES[n]])
            else:
                P.op("dve", lambda e, f=f: e.scalar_tensor_tensor(
                    out=h[:, f, nr], in0=xres[:, f, nr], scalar=smallp[:, lnext, f:f + 1], in1=rs[:, n, :], op0=ALU.mult, op1=ALU.mult),
                    reads=[R_XRES[n], R_RS[n], R_SMALLP], acc=[R_H[n]])

    for st in range(n_sub):
        seq, half = st // 2, st % 2
        t0 = half * ST
        if half == 0:
            P.op("pool", lambda e: e.memset(carry[:], 0.0), writes=[r for rl in R_CARRY for r in rl])
            P.op("pool", lambda e: e.memset(halo[:], 0.0), writes=R_HALO)
        for tt in range(8):
            si = xs_rr[0]
            xs_rr[0] ^= 1
            P.op("sp", lambda e, si=si, tt=tt, seq=seq, t0=t0: e.dma_start(out=xs[:, si, :], in_=x_d[seq, t0 + tt * 128:t0 + (tt + 1) * 128, :]),
                 writes=[R_XS[si]], dma=f"d_xs{si}")
            for fh in range(2):
                pi = next_ps()

                def f_xt(e, si=si, fh=fh, pi=pi):
                    ins = None
                    for f4 in range(4):
                        f = fh * 4 + f4
                        ins = e.transpose(out=psum[:, pi, f4 * 128:(f4 + 1) * 128], in_=xs[:, si, f * 128:(f + 1) * 128], identity=identF[:])
                    return ins
                P.op("pe", f_xt, reads=[R_XS[si], R_CONST], writes=[R_PS[pi]])
                copy_op(evac_eng(), xres[:, fh * 4:(fh + 1) * 4, tt * 128:(tt + 1) * 128],
                        psum[:, pi, :].rearrange("p (f n) -> p f n", f=4), [R_PS[pi]], [R_XRES[tt // 4]])

        for l in range(n_layers):
            k0 = half * NK
            cjobs = cast_jobs_for(l + 1) if (st == 0 and l + 1 < n_layers) else []
            win = wbf_in[l].rearrange("(a p) n -> p a n", p=128)
            if l == 0:
                for n in range(2):
                    norm_square(n)
                    norm_rstd(n)
                    norm_apply(n, 0)
            wi_g = [load_ws(win[:, :, 1024 + sg * 512:1024 + (sg + 1) * 512], 8, [R_WBF[l]]) for sg in range(2)]

            def gate_group(fo, n, wi_g=wi_g):
                nr = slice(n * 512, (n + 1) * 512)
                wi, f4 = wi_g[fo // 4], fo % 4
                pi = next_ps()
                mm_group(pi, lambda kc: wsv(wi)[:, kc, f4 * 128:(f4 + 1) * 128], lambda kc: h[:, kc, nr], 8, [R_WS[wi], R_H[n]])
                P.op("act", lambda e: e.activation(out=gate[:, fo, nr], in_=psum[:, pi, :], func=AF.Silu), reads=[R_PS[pi]], acc=[R_GATE[n]])
            if l > 0:
                for fo in range(4):
                    gate_group(fo, 0)
            wi_s = load_ws(win[:, :, 512:1024], 8, [R_WBF[l]])
            for tau in range(8):
                pi = next_ps()
                mm_group(pi, lambda kc, tau=tau: h[:, kc, :].rearrange("p (k t) -> p t k", t=8)[:, tau, :],
                         lambda kc, wi_s=wi_s: wsv(wi_s)[:, kc, :], 8, [R_WS[wi_s], R_H[0], R_H[1]])
                copy_op(evac_eng(), Zs[:, :, tau, :], psum[:, pi, :].rearrange("p (g c) -> p g c", c=16), [R_PS[pi]], [R_ZY])
            for gq in range(4):
                pi = next_ps()

                def f_tr(e, gq=gq, pi=pi):
                    ins = None
                    for g8 in range(8):
                        g = gq * 8 + g8
                        ins = e.transpose(out=psB(pi)[:, g8 * 128:(g8 + 1) * 128], in_=Zs[:, g].rearrange("p t c -> p (t c)"), identity=identB[:])
                    return ins
                P.op("pe", f_tr, reads=[R_ZY, R_CONST], writes=[R_PS[pi]])
                copy_op(evac_eng(), U[:, gq * 8:(gq + 1) * 8, :], psB(pi).rearrange("p (g k) -> p g k", g=8), [R_PS[pi]], [R_U])

            wi_p = load_ws(win[:, :, 0:512], 8, [R_WBF[l]])
            fillers = []

            def pool_in_group(f, n, wi_p=wi_p, l=l):
                nr = slice(n * 512, (n + 1) * 512)
                pi = next_ps()
                mm_group(pi, lambda kc: wsv(wi_p)[:, kc, f * 128:(f + 1) * 128], lambda kc: h[:, kc, nr], 8, [R_WS[wi_p], R_H[n]])
                P.op("act", lambda e: e.activation(out=upool[:, f, 16 + n * 512:16 + (n + 1) * 512], in_=psum[:, pi, :], func=AF.Copy),
                     reads=[R_PS[pi]], acc=[R_UPOOL])

            def pool_sums(l=l, half=half):
                LT = 16 + ST
                R_PWK = R_XS
                P.op("dve", lambda e: e.tensor_copy(out=halo[:, l, :, :], in_=upool[:, :, ST:ST + 16]), reads=[R_UPOOL], writes=[R_HALO[l]])
                P.op("dve", lambda e: e.tensor_tensor(out=spool[:, 0, :], in0=upool[:, 0, 16:LT], in1=upool[:, 0, 15:LT - 1], op=ALU.add),
                     reads=[R_UPOOL], writes=[R_SPOOL])
                for f in (1, 2, 3):
                    P.op("dve", lambda e, f=f: e.tensor_tensor(out=pwk[:, 0, 1:LT], in0=upool[:, f, 1:LT], in1=upool[:, f, 0:LT - 1], op=ALU.add),
                         reads=[R_UPOOL], writes=R_PWK)
                    if f == 1:
                        P.op("dve", lambda e: e.tensor_tensor(out=spool[:, 1, :], in0=pwk[:, 0, 16:LT], in1=pwk[:, 0, 14:LT - 2], op=ALU.add),
                             reads=R_PWK, acc=[R_SPOOL])
                        continue
                    P.op("dve", lambda e: e.tensor_tensor(out=pwk[:, 1, 3:LT], in0=pwk[:, 0, 3:LT], in1=pwk[:, 0, 1:LT - 2], op=ALU.add),
                         reads=R_PWK, writes=R_PWK)
                    if f == 2:
                        P.op("dve", lambda e: e.tensor_tensor(out=spool[:, 2, :], in0=pwk[:, 1, 16:LT], in1=pwk[:, 1, 12:LT - 4], op=ALU.add),
                             reads=R_PWK, acc=[R_SPOOL])
                        continue
                    P.op("dve", lambda e: e.tensor_tensor(out=pwk[:, 2, 7:LT], in0=pwk[:, 1, 7:LT], in1=pwk[:, 1, 3:LT - 4], op=ALU.add),
                         reads=R_PWK, writes=R_PWK)
                    P.op("dve", lambda e: e.tensor_tensor(out=spool[:, 3, :], in0=pwk[:, 2, 16:LT], in1=pwk[:, 2, 8:LT - 8], op=ALU.add),
                         reads=R_PWK, acc=[R_SPOOL])
                if half == 0:
                    P.op("dve", lambda e: e.tensor_tensor(out=spool[:, :, 0:16], in0=spool[:, :, 0:16], in1=cfix[:], op=ALU.mult),
                         reads=[R_C2], writes=[R_SPOOL])

            P.op("dve", lambda e, l=l: e.tensor_copy(out=upool[:, :, 0:16], in_=halo[:, l, :, :]), reads=[R_HALO[l]], writes=[R_UPOOL])
            for fo in range(4):
                for n in range(2):
                    if l > 0 and n == 0:
                        continue
                    fillers.append(lambda fo=fo, n=n: gate_group(fo, n))
            for f in range(4):
                for n in range(2):
                    fillers.append(lambda f=f, n=n: pool_in_group(f, n))
            fillers.append(pool_sums)
            for fo in range(4, 8):
                for n in range(2):
                    fillers.append(lambda fo=fo, n=n: gate_group(fo, n))
            fillers.reverse()

            def run_fillers(k):
                for _ in range(k):
                    if fillers:
                        fillers.pop()()

            slots = {}

            def ssm_front(gb, l=l, k0=k0):
                wi = sw_rr[0]
                sw_rr[0] = (wi + 1) % NSW
                ti = stb_rr[0]
                stb_rr[0] = (ti + 1) % NSTB
                qi = gb % NQ
                slots[gb] = (wi, qi)
                P.op("sp", lambda e: e.dma_start(out=SSw[wi][:].rearrange("p a g n -> p (a g n)"), in_=ssmW_d[l, gb]),
                     reads=[R_SSMW[l]], writes=[R_SW[wi]], dma=f"d_sw{wi}")
                P.op("sp", lambda e: e.dma_start(out=SSt[ti][:], in_=tabs_d[l, gb][:, :, :, k0:k0 + NK]),
                     reads=[R_TABS[l]], writes=[R_STB[ti]], dma=f"d_stb{ti}")
                pa, pb_ = next_ps(), next_ps()

                def f_v(e):
                    ins = None
                    for kind, pi in ((1, pa), (2, pb_)):
                        for g4 in range(4):
                            ins = e.matmul(psum[:, pi, g4 * 128:(g4 + 1) * 128], lhsT=SSw[wi][:, kind, g4, :], rhs=U[:, gb * 4 + g4, :],
                                           start=True, stop=True)
                    return ins
                P.op("pe", f_v, reads=[R_SW[wi], R_U], writes=[R_PS[pa], R_PS[pb_]])

                def pv(pi):
                    return psum[:, pi, :].rearrange("p (g k) -> p g k", g=4)
                P.op("dve", lambda e: e.tensor_tensor(out=t1[:], in0=pv(pa), in1=SSt[ti][:, 0], op=ALU.mult),
                     reads=[R_PS[pa], R_STB[ti]], writes=[R_T1])
                P.op("dve", lambda e: e.tensor_tensor(out=t2[:], in0=pv(pb_), in1=SSt[ti][:, 1], op=ALU.mult),
                     reads=[R_PS[pb_], R_STB[ti]], writes=[R_T2])
                P.op("dve", lambda e: e.tensor_tensor(out=t1[:], in0=t1[:], in1=t2[:], op=ALU.add), reads=[R_T1, R_T2], writes=[R_T1])
                P.op("dve", lambda e: e.tensor_copy(out=Q[qi][:, :, 0], in_=carry[:, l, gb * 4:(gb + 1) * 4]),
                     reads=[R_CARRY[l][gb]], writes=[R_Q[qi]])
                for g4 in range(4):
                    g = gb * 4 + g4
                    P.op("dve", lambda e, g4=g4, g=g: e.tensor_tensor_scan(
                        out=Q[qi][:, g4, 1:129], data0=Rb_all[:, l, g:g + 1].to_broadcast([128, 128]), data1=t1[:, g4, :],
                        initial=carry[:, l, g:g + 1], op0=ALU.mult, op1=ALU.add),
                        reads=[R_T1, R_RB, R_CARRY[l][gb]], acc=[R_Q[qi]])
                P.op("dve", lambda e: e.tensor_copy(out=carry[:, l, gb * 4:(gb + 1) * 4], in_=Q[qi][:, :, 128]),
                     reads=[R_Q[qi]], writes=[R_CARRY[l][gb]])
                P.op("dve", lambda e: e.tensor_tensor(out=Pc[qi][:], in0=Q[qi][:, :, 0:128], in1=SSt[ti][:, 0], op=ALU.mult),
                     reads=[R_Q[qi], R_STB[ti]], writes=[R_PCS[qi]])
                P.op("dve", lambda e: e.tensor_tensor(out=Ps[qi][:], in0=Q[qi][:, :, 0:128], in1=SSt[ti][:, 1], op=ALU.mult),
                     reads=[R_Q[qi], R_STB[ti]], acc=[R_PCS[qi]])

            def ssm_back(gb):
                wi, qi = slots[gb]
                py = next_ps()

                def f_y(e):
                    ins = None
                    for g4 in range(4):
                        o = psum[:, py, g4 * 128:(g4 + 1) * 128]
                        e.matmul(o, lhsT=U[:, gb * 4 + g4, :], rhs=SSw[wi][:, 0, g4, :], start=True, stop=False)
                        e.matmul(o, lhsT=Pc[qi][:, g4, :], rhs=SSw[wi][:, 3, g4, :], start=False, stop=False)
                        ins = e.matmul(o, lhsT=Ps[qi][:, g4, :], rhs=SSw[wi][:, 4, g4, :], start=False, stop=True)
                    return ins
                P.op("pe", f_y, reads=[R_SW[wi], R_U, R_PCS[qi]], writes=[R_PS[py]])
                P.op("act", lambda e: e.activation(
                    out=ZY[:, :, gb * 64:(gb + 1) * 64].rearrange("p t (g c) -> p g t c", g=4),
                    in_=psum[:, py, :].rearrange("p (g t c) -> p g t c", g=4, t=8), func=AF.Gelu_apprx_tanh),
                    reads=[R_PS[py]], acc=[R_ZY])

            def b3_tile(f):
                pi = next_ps()

                def f_tr(e, f=f, pi=pi):
                    ins = None
                    for tau in range(8):
                        ins = e.transpose(out=psB(pi)[:, tau * 128:(tau + 1) * 128], in_=ZY[:, tau, f * 128:(f + 1) * 128], identity=identB[:])
                    return ins
                P.op("pe", f_tr, reads=[R_ZY, R_CONST], writes=[R_PS[pi]])
                copy_op(evac_eng(), ysT[:, f, :].rearrange("p (k t) -> p t k", t=8), psB(pi).rearrange("p (t k) -> p t k", t=8),
                        [R_PS[pi]], [R_YST])

            wg = None
            LAG = 1
            for s_ in range(8 + LAG):
                if s_ < 8:
                    ssm_front(s_)
                run_fillers(2)
                if s_ == 3:
                    wg = load_ws(wbf_glu[l].rearrange("(a p) n -> p a n", p=128), 4, [R_WBF[l]])
                    P.op("sp", lambda e, wg=wg, l=l: e.dma_start(out=WS[wg][:, 2048:3072], in_=poolW_d[l]),
                         reads=[R_POOLW[l]], acc=[R_WS[wg]], dma=f"d_ws{wg}")
                if s_ >= LAG:
                    ssm_back(s_ - LAG)
                    if (s_ - LAG) % 2 == 1:
                        b3_tile((s_ - LAG) // 2)
                if cjobs and s_ < 8:
                    cjobs.pop(0)(reads=[R_PCS[s_ % NQ]])
            run_fillers(len(fillers))
            while cjobs:
                cjobs.pop(0)(reads=[R_YST])

            pwv = WS[wg][:, 2048:3072].rearrange("p (g k n) -> p g k n", g=4, k=2)
            for n in range(2):
                nr = slice(n * 512, (n + 1) * 512)
                for f in range(4):
                    pi = next_ps()

                    def f_pm(e, f=f, n=n, nr=nr, pi=pi, pwv=pwv):
                        e.matmul(psum[:, pi, :], lhsT=pwv[:, f, 0, :], rhs=spool[:, f, nr], start=True, stop=False)
                        return e.matmul(psum[:, pi, :], lhsT=pwv[:, f, 1, :], rhs=upool[:, f, 16 + n * 512:16 + (n + 1) * 512], start=False, stop=True)
                    P.op("pe", f_pm, reads=[R_WS[wg], R_SPOOL, R_UPOOL], writes=[R_PS[pi]])
                    P.op("dve", lambda e, f=f, nr=nr, pi=pi, l=l: e.scalar_tensor_tensor(
                        out=ycat[:, f, nr], in0=psum[:, pi, :], scalar=smallp[:, l, 8 + f:9 + f], in1=gate[:, f, nr], op0=ALU.mult, op1=ALU.mult),
                        reads=[R_PS[pi], R_GATE[n], R_SMALLP], acc=[R_YCAT[n]])
            gluv = WS[wg][:, 0:2048].rearrange("p (a n) -> p a n", n=512)
            wov = wbf_out[l].rearrange("(a p) n -> p a n", p=128)
            wi_o = [load_ws(wov[:, :, so * 512:(so + 1) * 512], 8, [R_WBF[l]]) for so in range(2)]

            def c1_half(n, l=l, gluv=gluv, wg=wg):
                nr = slice(n * 512, (n + 1) * 512)
                for fo in range(4):
                    pi = next_ps()
                    sgi = fo % 2
                    mm_group(pi, lambda kc, fo=fo: gluv[:, kc, fo * 128:(fo + 1) * 128], lambda kc: ysT[:, kc, nr], 4, [R_WS[wg], R_YST])
                    P.op("act", lambda e, fo=fo, pi=pi, sgi=sgi: e.activation(out=sig[:, sgi, :], in_=psum[:, pi, :], func=AF.Sigmoid,
                                                                             bias=smallp[:, l, 12 + fo:13 + fo], scale=1.0),
                         reads=[R_PS[pi], R_SMALLP], writes=[R_SIG[sgi]])
                    P.op("dve", lambda e, fo=fo, sgi=sgi: e.tensor_tensor(out=sig[:, sgi, :], in0=sig[:, sgi, :], in1=ysT[:, fo, nr], op=ALU.mult),
                         reads=[R_SIG[sgi], R_YST], writes=[R_SIG[sgi]])
                    P.op("dve", lambda e, fo=fo, sgi=sgi: e.tensor_tensor(out=ycat[:, 4 + fo, nr], in0=sig[:, sgi, :], in1=gate[:, 4 + fo, nr], op=ALU.mult),
                         reads=[R_SIG[sgi], R_GATE[n]], acc=[R_YCAT[n]])

            lnext = l + 1 if l + 1 < n_layers else None

            def c2_half(n, wi_o=wi_o, lnext=lnext):
                nr = slice(n * 512, (n + 1) * 512)
                for fo in range(8):
                    if n == 1 and fo == 4:
                        norm_rstd(0)
                    wi, f4 = wi_o[fo // 4], fo % 4
                    pi = next_ps()
                    mm_group(pi, lambda kc, wi=wi, f4=f4: wsv(wi)[:, kc, f4 * 128:(f4 + 1) * 128], lambda kc: ycat[:, kc, nr], 8, [R_WS[wi], R_YCAT[n]])
                    P.op("dve", lambda e, fo=fo, pi=pi: e.tensor_tensor(out=xres[:, fo, nr], in0=xres[:, fo, nr], in1=psum[:, pi, :], op=ALU.add),
                         reads=[R_PS[pi], R_XRES[n]], acc=[R_XRES[n]])
                    norm_square(n, fo)
                    if n == 1 and fo >= 4:
                        norm_apply(0, lnext, [2 * (fo - 4), 2 * (fo - 4) + 1])
            c1_half(0)
            c1_half(1)
            c2_half(0)
            c2_half(1)
            norm_rstd(1)
            norm_apply(1, lnext)

        for tt in range(8):
            si = xs_rr[0]
            xs_rr[0] ^= 1
            for fh in range(2):
                pi = next_ps()

                def f_ot(e, tt=tt, fh=fh, pi=pi):
                    ins = None
                    for f4 in range(4):
                        f = fh * 4 + f4
                        ins = e.transpose(out=psum[:, pi, f4 * 128:(f4 + 1) * 128], in_=xres[:, f, tt * 128:(tt + 1) * 128], identity=identF[:])
                    return ins
                P.op("pe", f_ot, reads=[R_XRES[tt // 4], R_CONST], writes=[R_PS[pi]])
                copy_op(evac_eng(), xs[:, si, fh * 512:(fh + 1) * 512], psum[:, pi, :], [R_PS[pi]], [R_XS[si]])
            P.op("sp", lambda e, si=si, tt=tt, seq=seq, t0=t0: e.dma_start(out=out_d[seq, t0 + tt * 128:t0 + (tt + 1) * 128, :], in_=xs[:, si, :]),
                 reads=[R_XS[si]], acc=[R_OUT], dma=f"d_out{si}")

    P.op("sp", None, reads=[R_OUT], sig=False)
    sems = {k: es.enter_context(nc.semaphore(k)) for k in P.cnt}
    with nc.Block() as block:
        P.emit(nc, block, sems)
    es.close()
    return nc


def prep_inputs(inp):
    f = np.float32
    shared = {}
    shared["w_in"] = np.ascontiguousarray(inp["w_in"], dtype=f)
    shared["w_out"] = np.ascontiguousarray(inp["w_out"], dtype=f)
    shared["glu_w"] = np.ascontiguousarray(inp["glu_w"], dtype=f)
    shared["pool_w"] = np.ascontiguousarray(np.transpose(inp["pool_w"], (0, 2, 1, 3)), dtype=f)
    ng = np.transpose(np.asarray(inp["norm_g"], f).reshape(DEPTH, 8, 128), (2, 0, 1))
    psc = np.transpose(np.asarray(inp["pool_scale"], f).reshape(DEPTH, 4, 128), (2, 0, 1))
    gb = np.transpose(np.asarray(inp["glu_b"], f).reshape(DEPTH, 4, 128), (2, 0, 1))
    shared["smallp"] = np.ascontiguousarray(np.concatenate([ng, psc, gb], axis=2), dtype=f)
    shared["finalg"] = np.ascontiguousarray(np.asarray(inp["final_g"], f).reshape(8, 128).T)

    def dup(a):
        t = np.transpose(np.asarray(a, f), (0, 2, 1))
        return np.concatenate([t, t], axis=1)
    are = dup(inp["a_re"])
    aim = dup(inp["a_im"])
    ldt = np.broadcast_to(np.asarray(inp["log_dt"], f)[:, None, :], (DEPTH, 128, G))
    dv = np.asarray(inp["d_skip"], f).reshape(DEPTH, G, 16)
    dvec = np.broadcast_to(np.transpose(dv, (0, 2, 1))[:, None, :, :], (DEPTH, 8, 16, G)).reshape(DEPTH, 128, G)
    shared["ssm_small"] = np.ascontiguousarray(
        np.transpose(np.stack([are, aim, ldt, dvec], axis=2), (1, 2, 0, 3)).reshape(128, 4, LG), dtype=f)

    def dupb(a):
        t = np.transpose(np.asarray(a, f), (0, 2, 1, 3)).reshape(DEPTH, 64, G * 16)
        return np.concatenate([t, t], axis=1)

    def dupc(a):
        t = np.transpose(np.asarray(a, f), (0, 3, 1, 2)).reshape(DEPTH, 64, G * 16)
        return np.concatenate([t, t], axis=1)
    shared["ssm_bc"] = np.ascontiguousarray(
        np.stack([dupb(inp["b_re"]), dupb(inp["b_im"]), dupc(inp["c_re"]), dupc(inp["c_im"])], axis=2), dtype=f)
    x = np.ascontiguousarray(inp["x"], dtype=f)
    maps = []
    for c in range(NCORES):
        m = dict(shared)
        m["x"] = x[c * NSEQ:(c + 1) * NSEQ]
        maps.append(m)
    return maps


def kernel(**inputs):
    maps = prep_inputs(inputs)
    nc = build_program()
    res = run_bass_kernel_spmd(nc, maps, core_ids=list(range(NCORES)))
    out = np.concatenate([np.asarray(r["out"], dtype=np.float32) for r in res.results], axis=0)
    return out
```
